# Optimizing a Trainium2 kernel written in Bass

```python
import math
import jax
import jax.numpy as jnp
from jax import lax
import numpy as np

D_MODEL = 1024
BATCH = 32
SEQ = 2048
DEPTH = 4
DEC_BATCH = 32
DEC_SEQ = 16
PAST_LEN = 1024

CHUNK = 64
Q_BLOCK = 128
N_MIXERS = 3
BRANCH = D_MODEL
DA_HEADS = 8
DA_DIM = 64
SB_HEADS = 16
SB_DIM = 64
SW_HEADS = 16
SW_KV_HEADS = 4
SW_GROUP = SW_HEADS // SW_KV_HEADS
SW_DIM = 64
WINDOW = 128
WIN_CHUNKS = WINDOW // CHUNK
N_BUCKETS = 32
MAX_DISTANCE = 128
N_BIAS_HEADS = 16
EPS = 1e-6
NEG_INF = -1e30

DA_QK = DA_HEADS * 2 * DA_DIM
DA_V = DA_HEADS * 2 * DA_DIM
SB_W = SB_HEADS * SB_DIM
SW_Q = SW_HEADS * SW_DIM
SW_KV = SW_KV_HEADS * SW_DIM
IN_WIDTHS = (2 * DA_QK + DA_V + BRANCH, 3 * SB_W + BRANCH, SW_Q + 2 * SW_KV + BRANCH)

kernel_name = 'hybrid_diff_stickbreak_swa_stream_step'


def _rmsnorm(x, g):
    xf = x.astype(jnp.float32)
    y = xf * lax.rsqrt(jnp.mean(xf * xf, axis=-1, keepdims=True) + EPS)
    return (y * g.astype(jnp.float32)).astype(x.dtype)


def _rel_bucket(rel):
    nb = N_BUCKETS // 2
    max_exact = nb // 2
    n = jnp.abs(rel)
    nf = jnp.maximum(n, 1).astype(jnp.float32)
    large = max_exact + (jnp.log(nf / max_exact) / math.log(MAX_DISTANCE / max_exact)
                         * (nb - max_exact)).astype(jnp.int32)
    large = jnp.minimum(large, nb - 1)
    return jnp.where(rel > 0, nb, 0) + jnp.where(n < max_exact, n, large)


def _rel_bias(q_pos, k_pos, table):
    b = _rel_bucket(k_pos[None, :] - q_pos[:, None])
    return jnp.moveaxis(table.astype(jnp.float32)[b], -1, 0)


def _chunk_causal(q_pos, k_pos):
    return (k_pos[None, :] // CHUNK) <= (q_pos[:, None] // CHUNK)


def _sweep_blocks(fn, q, S):
    def blk(i):
        q0 = i * Q_BLOCK
        return fn(lax.dynamic_slice_in_dim(q, q0, Q_BLOCK, axis=1), q0 + jnp.arange(Q_BLOCK))
    o = jnp.moveaxis(lax.map(blk, jnp.arange(S // Q_BLOCK)), 0, 1)
    return o.reshape(o.shape[0], S, -1)


def _split_da(u):
    B, T = u.shape[:2]
    q = u[..., :DA_QK].reshape(B, T, DA_HEADS, 2, DA_DIM)
    k = u[..., DA_QK:2 * DA_QK].reshape(B, T, DA_HEADS, 2, DA_DIM)
    v = u[..., 2 * DA_QK:].reshape(B, T, DA_HEADS, 2 * DA_DIM)
    return q, k, v


def _diff_lambda(lam_params, lam_init):
    lp = lam_params.astype(jnp.float32)
    return jnp.exp(jnp.sum(lp[0] * lp[1])) - jnp.exp(jnp.sum(lp[2] * lp[3])) + lam_init


def _diff_attn(q, k, v, q_pos, k_pos, table, lam, lam_init, subln):
    bias = _rel_bias(q_pos, k_pos, table).reshape(DA_HEADS, 2, q_pos.shape[0], k_pos.shape[0])
    mask = _chunk_causal(q_pos, k_pos)
    s = jnp.einsum('bqhmd,bkhmd->bhmqk', q, k, preferred_element_type=jnp.float32) * (DA_DIM ** -0.5) + bias
    p = jax.nn.softmax(jnp.where(mask, s, NEG_INF), axis=-1)
    w = p[:, :, 0] - lam * p[:, :, 1]
    o = jnp.einsum('bhqk,bkhe->bqhe', w, v.astype(jnp.float32))
    o = o * lax.rsqrt(jnp.mean(o * o, axis=-1, keepdims=True) + EPS) * subln.astype(jnp.float32)
    return o * (1.0 - lam_init)


def _diff_mixer(up, us, ck, cv, table, lam_params, subln, lam_init):
    qp, kp, vp = _split_da(up)
    qs, ks, vs = _split_da(us)
    lam = _diff_lambda(lam_params, lam_init)
    S = up.shape[1]
    pos = jnp.arange(S)
    o_p = _sweep_blocks(lambda qb, qpos: _diff_attn(qb, kp, vp, qpos, pos, table, lam, lam_init, subln), qp, S)
    T = us.shape[1]
    k_all = jnp.concatenate([ck, ks], axis=1)
    v_all = jnp.concatenate([cv, vs], axis=1)
    o_s = _diff_attn(qs, k_all, v_all, PAST_LEN + jnp.arange(T), jnp.arange(PAST_LEN + T),
                     table, lam, lam_init, subln).reshape(us.shape[0], T, BRANCH)
    return o_p, o_s, (kp, vp, ks, vs)


def _split_sb(u):
    B, T = u.shape[:2]
    q = u[..., :SB_W].reshape(B, T, SB_HEADS, SB_DIM)
    k = u[..., SB_W:2 * SB_W].reshape(B, T, SB_HEADS, SB_DIM)
    v = u[..., 2 * SB_W:].reshape(B, T, SB_HEADS, SB_DIM)
    return q, k, v


def _stick_breaking(q, k, v, q_pos, k_pos):
    mask = k_pos[None, :] < q_pos[:, None]
    z = jnp.einsum('bqhd,bkhd->bhqk', q, k, preferred_element_type=jnp.float32) * (SB_DIM ** -0.5)
    log_rest = jnp.where(mask, jax.nn.log_sigmoid(-z), 0.0)
    log_after = lax.cumsum(log_rest, axis=3, reverse=True) - log_rest
    a = jnp.where(mask, jnp.exp(jax.nn.log_sigmoid(z) + log_after), 0.0)
    return jnp.einsum('bhqk,bkhd->bqhd', a, v.astype(jnp.float32))


def _sb_mixer(up, us, ck, cv):
    qp, kp, vp = _split_sb(up)
    qs, ks, vs = _split_sb(us)
    S = up.shape[1]
    pos = jnp.arange(S)
    o_p = _sweep_blocks(lambda qb, qpos: _stick_breaking(qb, kp, vp, qpos, pos), qp, S)
    T = us.shape[1]
    k_all = jnp.concatenate([ck, ks], axis=1)
    v_all = jnp.concatenate([cv, vs], axis=1)
    o_s = _stick_breaking(qs, k_all, v_all, PAST_LEN + jnp.arange(T),
                          jnp.arange(PAST_LEN + T)).reshape(us.shape[0], T, BRANCH)
    return o_p, o_s, (kp, vp, ks, vs)


def _split_sw(u):
    B, T = u.shape[:2]
    q = u[..., :SW_Q].reshape(B, T, SW_KV_HEADS, SW_GROUP, SW_DIM)
    k = u[..., SW_Q:SW_Q + SW_KV].reshape(B, T, SW_KV_HEADS, SW_DIM)
    v = u[..., SW_Q + SW_KV:].reshape(B, T, SW_KV_HEADS, SW_DIM)
    return q, k, v


def _sink_attn(q, k, v, mask, bias, sinks):
    s = jnp.einsum('nqhgd,nkhd->nhgqk', q, k, preferred_element_type=jnp.float32) * (SW_DIM ** -0.5) + bias
    s = jnp.where(mask[:, None, None], s, NEG_INF)
    sk = sinks.astype(jnp.float32)[None, :, :, None, None]
    m = jnp.maximum(jnp.max(s, axis=-1, keepdims=True), sk)
    p = jnp.exp(s - m)
    w = p / (jnp.sum(p, axis=-1, keepdims=True) + jnp.exp(sk - m))
    return jnp.einsum('nhgqk,nkhd->nqhgd', w, v.astype(jnp.float32))


def _sw_mixer(up, us, ck, cv, table, sinks):
    qp, kp, vp = _split_sw(up)
    qs, ks, vs = _split_sw(us)
    sinks = sinks.reshape(SW_KV_HEADS, SW_GROUP)
    B, S = up.shape[:2]
    nc = S // CHUNK
    band = (WIN_CHUNKS + 1) * CHUNK

    def bands(t):
        tc = t.reshape(B, nc, CHUNK, SW_KV_HEADS, SW_DIM)
        tp = jnp.pad(tc, ((0, 0), (WIN_CHUNKS, 0), (0, 0), (0, 0), (0, 0)))
        return jnp.concatenate([tp[:, j:j + nc] for j in range(WIN_CHUNKS + 1)],
                               axis=2).reshape(B * nc, band, SW_KV_HEADS, SW_DIM)

    q_off = jnp.arange(CHUNK)
    k_off = jnp.arange(band) - WIN_CHUNKS * CHUNK
    bias_p = _rel_bias(q_off, k_off, table).reshape(SW_KV_HEADS, SW_GROUP, CHUNK, band)
    k_pos = jnp.arange(nc)[:, None] * CHUNK + k_off[None, :]
    valid = jnp.broadcast_to((k_pos >= 0)[None], (B, nc, band)).reshape(B * nc, 1, band)
    o_p = _sink_attn(qp.reshape(B * nc, CHUNK, SW_KV_HEADS, SW_GROUP, SW_DIM), bands(kp), bands(vp),
                     valid, bias_p, sinks).reshape(B, S, BRANCH)

    T = us.shape[1]
    wb = ck.shape[1]
    k_all = jnp.concatenate([ck, ks], axis=1)
    v_all = jnp.concatenate([cv, vs], axis=1)
    q_pos = PAST_LEN + jnp.arange(T)
    kpos_s = PAST_LEN - wb + jnp.arange(wb + T)
    qc = q_pos[:, None] // CHUNK
    kc = kpos_s[None, :] // CHUNK
    mask_s = ((kc <= qc) & (kc >= qc - WIN_CHUNKS))[None]
    bias_s = _rel_bias(q_pos, kpos_s, table).reshape(SW_KV_HEADS, SW_GROUP, T, wb + T)
    o_s = _sink_attn(qs, k_all, v_all, mask_s, bias_s, sinks).reshape(us.shape[0], T, BRANCH)
    keep = min(WINDOW, S)
    return o_p, o_s, (kp[:, -keep:], vp[:, -keep:], k_all[:, -wb:], v_all[:, -wb:])


def _gated_out(gate, o, w_out):
    return (jax.nn.silu(gate) * o.astype(gate.dtype)) @ w_out


def setup_inputs(seed: int = 0) -> dict:
    keys = iter(jax.random.split(jax.random.key(seed), 64))

    def nrm(shape, scale):
        return scale * jax.random.normal(next(keys), shape, jnp.float32)

    wb = min(WINDOW, PAST_LEN)
    d = {}
    d['x_prompt'] = nrm((BATCH, SEQ, D_MODEL), 1.0)
    d['x_sample'] = nrm((DEC_BATCH, DEC_SEQ, D_MODEL), 1.0)
    for i in range(DEPTH):
        kind = i % N_MIXERS
        if kind == 0:
            ks = (DEC_BATCH, PAST_LEN, DA_HEADS, 2, DA_DIM)
            vs = (DEC_BATCH, PAST_LEN, DA_HEADS, 2 * DA_DIM)
        elif kind == 1:
            ks = vs = (DEC_BATCH, PAST_LEN, SB_HEADS, SB_DIM)
        else:
            ks = vs = (DEC_BATCH, wb, SW_KV_HEADS, SW_DIM)
        d[f'cache_k_{i}'] = nrm(ks, 1.0)
        d[f'cache_v_{i}'] = nrm(vs, 1.0)
    d['rel_bias_table'] = nrm((N_BUCKETS, N_BIAS_HEADS), 0.5)
    for i in range(DEPTH):
        kind = i % N_MIXERS
        d[f'norm_{i}'] = 1.0 + nrm((D_MODEL,), 0.02)
        d[f'w_in_{i}'] = nrm((D_MODEL, IN_WIDTHS[kind]), D_MODEL ** -0.5)
        d[f'w_out_{i}'] = nrm((BRANCH, D_MODEL), BRANCH ** -0.5)
        if kind == 0:
            d[f'da_lambda_{i}'] = nrm((4, DA_DIM), 0.1)
            d[f'da_subln_{i}'] = 1.0 + nrm((2 * DA_DIM,), 0.02)
        elif kind == 2:
            d[f'sw_sinks_{i}'] = nrm((SW_HEADS,), 1.0)
    d['final_norm'] = 1.0 + nrm((D_MODEL,), 0.02)
    return d


def reference(x_prompt, x_sample, cache_k_0, cache_v_0, cache_k_1, cache_v_1, cache_k_2, cache_v_2,
              cache_k_3, cache_v_3, rel_bias_table,
              norm_0, w_in_0, w_out_0, da_lambda_0, da_subln_0,
              norm_1, w_in_1, w_out_1,
              norm_2, w_in_2, w_out_2, sw_sinks_2,
              norm_3, w_in_3, w_out_3, da_lambda_3, da_subln_3,
              final_norm):
    caches = [(cache_k_0, cache_v_0), (cache_k_1, cache_v_1), (cache_k_2, cache_v_2), (cache_k_3, cache_v_3)]
    layers = [
        dict(norm=norm_0, w_in=w_in_0, w_out=w_out_0, lam=da_lambda_0, subln=da_subln_0),
        dict(norm=norm_1, w_in=w_in_1, w_out=w_out_1),
        dict(norm=norm_2, w_in=w_in_2, w_out=w_out_2, sinks=sw_sinks_2),
        dict(norm=norm_3, w_in=w_in_3, w_out=w_out_3, lam=da_lambda_3, subln=da_subln_3),
    ]
    xp, xs = x_prompt, x_sample
    states = []
    for i in range(DEPTH):
        p = layers[i]
        ck, cv = caches[i]
        kind = i % N_MIXERS
        hp = _rmsnorm(xp, p['norm']) @ p['w_in']
        hs = _rmsnorm(xs, p['norm']) @ p['w_in']
        up, gp = hp[..., :-BRANCH], hp[..., -BRANCH:]
        us, gs = hs[..., :-BRANCH], hs[..., -BRANCH:]
        if kind == 0:
            lam_init = 0.8 - 0.6 * math.exp(-0.3 * i)
            o_p, o_s, st = _diff_mixer(up, us, ck, cv, rel_bias_table, p['lam'], p['subln'], lam_init)
        elif kind == 1:
            o_p, o_s, st = _sb_mixer(up, us, ck, cv)
        else:
            o_p, o_s, st = _sw_mixer(up, us, ck, cv, rel_bias_table, p['sinks'])
        xp = xp + _gated_out(gp, o_p, p['w_out'])
        xs = xs + _gated_out(gs, o_s, p['w_out'])
        states.append(st)
    y_prompt = _rmsnorm(xp, final_norm)
    y_sample = _rmsnorm(xs, final_norm)
    return (y_prompt, y_sample,
            states[0][0], states[0][1], states[0][2], states[0][3],
            states[1][0], states[1][1], states[1][2], states[1][3],
            states[2][0], states[2][1], states[2][2], states[2][3],
            states[3][0], states[3][1], states[3][2], states[3][3])
```

```python
import math
from contextlib import ExitStack

import numpy as np
import concourse.bass as bass
import concourse.mybir as mybir
from concourse.bass_utils import run_bass_kernel_spmd

F32 = mybir.dt.float32
BF16 = mybir.dt.bfloat16
ALU = mybir.AluOpType
AF = mybir.ActivationFunctionType

D = 1024
P = 128
NCH = 8
EPS = 1e-6
NEG = -240000.0
T_DEC = 16
KINDS = (0, 1, 2, 0)
IN_W = (4096, 4096, 2560)

FULL_CFG = dict(NSEQ=4, S=2048, PAST=1024, NCORES=8)


class Buf:
    __slots__ = ("name", "last_w", "readers", "sem", "cnt")

    def __init__(self, name):
        self.name = name
        self.last_w = None
        self.readers = []
        self.sem = None
        self.cnt = 0


class Op:
    __slots__ = ("idx", "eng", "fn", "waits", "signal", "sigval", "dma", "dma_val")


ENGS = ("pe", "act", "dve", "pool", "sp")


class Sched:
    def __init__(self):
        self.ops = []
        self.by_eng = {e: [] for e in ENGS}
        self.waited = {e: {f: -1 for f in ENGS} for e in ENGS}
        self.dma_waited = {e: {} for e in ENGS}
        self.dma_bufs = []

    def add(self, eng, fn, reads=(), writes=(), dma=None):
        op = Op()
        op.idx = len(self.ops)
        op.eng = eng
        op.fn = fn
        op.waits = []
        op.signal = False
        op.sigval = 0
        op.dma = dma
        op.dma_val = 0
        deps = set()
        for b in reads:
            if b.last_w is not None:
                deps.add(b.last_w)
        for b in writes:
            if b.last_w is not None:
                deps.add(b.last_w)
            deps.update(b.readers)
        for d in sorted(deps, reverse=True):
            dop = self.ops[d]
            if dop.dma is not None:
                if self.dma_waited[eng].get(dop.dma, 0) >= dop.dma_val:
                    continue
                self.dma_waited[eng][dop.dma] = dop.dma_val
                op.waits.append(("dma", dop.dma, dop.dma_val))
            else:
                if dop.eng == "pe" and eng == "pe" and dma is None:
                    continue
                if self.waited[eng][dop.eng] >= d:
                    continue
                self.waited[eng][dop.eng] = d
                dop.signal = True
                op.waits.append(("eng", dop))
        for b in reads:
            b.readers.append(op.idx)
        for b in writes:
            b.last_w = op.idx
            b.readers = []
        if dma is not None:
            if dma.cnt == 0:
                self.dma_bufs.append(dma)
            dma.cnt += 16
            op.dma_val = dma.cnt
        self.ops.append(op)
        self.by_eng[eng].append(op)
        return op

    def emit(self, nc, es):
        for e in ENGS:
            cnt = 0
            for op in self.by_eng[e]:
                if op.signal:
                    cnt += 1
                    op.sigval = cnt
        esem = {e: es.enter_context(nc.semaphore("sem_" + e)) for e in ENGS}
        for i, b in enumerate(self.dma_bufs):
            b.sem = es.enter_context(nc.semaphore("dsem%d" % i))

        def run(ename, eng):
            for op in self.by_eng[ename]:
                for w in op.waits:
                    if w[0] == "dma":
                        eng.wait_ge(w[1].sem, w[2])
                    else:
                        eng.wait_ge(esem[w[1].eng], w[1].sigval)
                ins = op.fn(eng)
                if op.dma is not None:
                    ins.then_inc(op.dma.sem, 16)
                elif op.signal:
                    ins.then_inc(esem[ename], 1)

        with nc.Block() as block:
            @block.tensor
            def _(e):
                run("pe", e)

            @block.scalar
            def _(e):
                run("act", e)

            @block.vector
            def _(e):
                run("dve", e)

            @block.gpsimd
            def _(e):
                run("pool", e)

            @block.sync
            def _(e):
                run("sp", e)


def _bucket_np(rel):
    import jax
    import jax.numpy as jnp
    cpu = jax.devices("cpu")[0]
    with jax.default_device(cpu):
        rel = jnp.asarray(rel, dtype=jnp.int32)
        nb = 16
        max_exact = 8
        n = jnp.abs(rel)
        nf = jnp.maximum(n, 1).astype(jnp.float32)
        large = max_exact + (jnp.log(nf / max_exact) / math.log(128 / max_exact)
                             * (nb - max_exact)).astype(jnp.int32)
        large = jnp.minimum(large, nb - 1)
        out = jnp.where(rel > 0, nb, 0) + jnp.where(n < max_exact, n, large)
        return np.asarray(out)


def static_tables():
    k = np.arange(P)[:, None]
    q = np.arange(P)[None, :]
    bd = _bucket_np(k - q).astype(np.float32)
    bs = _bucket_np(k - P - q).astype(np.float32)
    maps = np.zeros((4, P, P), np.float32)
    m = bd.copy(); m[(k // 64) > (q // 64)] = -1.0
    maps[0] = m
    maps[1] = bs
    m = bd.copy(); m[(k >= 64) & (q < 64)] = -1.0
    maps[2] = m
    m = bs.copy(); m[(k < 64) & (q >= 64)] = -1.0
    maps[3] = m
    consts = np.zeros((5, P, P), np.float32)
    consts[0] = np.eye(P)
    consts[1] = (k > q)
    consts[2] = (k <= q)
    consts[3] = np.where(k < q, 0.0, NEG)
    consts[4] = 1.0
    present = [sorted(set(int(v) for v in np.unique(maps[i]) if v >= 0 and v != 15)) for i in range(4)]
    return maps, consts, present


def build(cfg):
    NSEQ, S, PAST = cfg["NSEQ"], cfg["S"], cfg["PAST"]
    NT = S // P
    NQB = S // 512
    NPT = PAST // P
    TS = NSEQ * T_DEC
    assert S % 512 == 0 and PAST % P == 0 and TS <= 64
    maps_np, consts_np, present = static_tables()

    nc = bass.Bass("TRN2", target_bir_lowering=False)

    def din(name, shape):
        return nc.dram_tensor(name, list(shape), F32, kind="ExternalInput").ap()

    def dout(name, shape):
        return nc.dram_tensor(name, list(shape), F32, kind="ExternalOutput").ap()

    xp_d = din("xp", (NSEQ, S, D))
    xs_d = din("xs", (TS, D))
    ck_d, cv_d = [], []
    for l in range(4):
        if KINDS[l] == 2:
            ck_d.append(din("ck%d" % l, (NSEQ, P, 256)))
            cv_d.append(din("cv%d" % l, (NSEQ, P, 256)))
        else:
            ck_d.append(din("ck%d" % l, (NSEQ, PAST, D)))
            cv_d.append(din("cv%d" % l, (NSEQ, PAST, D)))
    win_d = [din("win%d" % l, (D, IN_W[KINDS[l]])) for l in range(4)]
    wout_d = [din("wout%d" % l, (D, D)) for l in range(4)]
    gain_d = din("gain", (5, P, D))
    tab_d = din("tab", (P, 512))
    lam_d = din("lam", (2, P, 256))
    subln_d = din("subln", (2, P, 1))
    sink_d = din("sink", (P, 8))
    maps_d = din("maps", (4, P, P))
    consts_d = din("consts", (5, P, P))

    yp_d = dout("yp", (NSEQ, S, D))
    ys_d = dout("ys", (TS, D))
    kp_d, vp_d, ks_d, vs_d = [], [], [], []
    for l in range(4):
        if KINDS[l] == 2:
            kp_d.append(dout("kp%d" % l, (NSEQ, P, 256)))
            vp_d.append(dout("vp%d" % l, (NSEQ, P, 256)))
            ks_d.append(dout("ks%d" % l, (NSEQ, P, 256)))
            vs_d.append(dout("vs%d" % l, (NSEQ, P, 256)))
        else:
            kp_d.append(dout("kp%d" % l, (NSEQ, S, D)))
            vp_d.append(dout("vp%d" % l, (NSEQ, S, D)))
            ks_d.append(dout("ks%d" % l, (TS, D)))
            vs_d.append(dout("vs%d" % l, (TS, D)))

    es = ExitStack()
    sch = Sched()

    def sb(name, shape, dt):
        return es.enter_context(nc.sbuf_tensor(name, list(shape), dt))

    X = sb("X", (P, NT, D), F32)
    xnT = sb("xnT", (P, NCH, S), BF16)
    Gn = sb("Gn", (P, D), F32)
    WT = [sb("WT%d" % i, (P, NCH, 512), BF16) for i in range(2)]
    WO = [sb("WO%d" % i, (P, D), BF16) for i in range(2)]
    KT = [sb("KT%d" % i, (P, S), BF16) for i in range(2)]
    VB = [sb("VB%d" % i, (P, NT, P), BF16) for i in range(2)]
    KTN = [sb("KTN%d" % i, (P, T_DEC), BF16) for i in range(2)]
    VN = [sb("VN%d" % i, (T_DEC, P), BF16) for i in range(2)]
    KC = [sb("KC%d" % i, (P, max(NPT, 1), P), BF16) for i in range(2)]
    NQS = 3
    QB = [sb("QB%d" % i, (P, 512), BF16) for i in range(NQS)]
    GB = [sb("GB%d" % i, (P, 512), BF16) for i in range(NQS)]
    OT = [sb("OT%d" % i, (P, 512), BF16) for i in range(2)]
    NKVS = 3
    KVS = [sb("KVS%d" % i, (P, 256), F32) for i in range(NKVS)]
    KBT = [sb("KBT%d" % i, (P, P), BF16) for i in range(2)]
    NE = 3
    EW = [sb("EW%d" % i, (P, 512), F32) for i in range(NE)]
    SPB = [sb("SPB%d" % i, (P, 512), BF16) for i in range(NE)]
    NPB = 3
    PB = [sb("PB%d" % i, (P, 512), BF16) for i in range(NPB)]
    TP = [sb("TP%d" % i, (P, 512), F32) for i in range(2)]
    SQ = sb("SQ", (P, 512), BF16)
    XN = [sb("XN%d" % i, (P, D), BF16) for i in range(2)]
    SS = sb("SS", (P, 2 * NT + 4), F32)
    BIAS = sb("BIAS", (P, 4, 16, P), BF16)
    CST = sb("CST", (P, 5, P), BF16)
    ZER = sb("ZER", (P, P), BF16)
    LAMC = sb("LAMC", (P, 2, 8), F32)
    SCC = sb("SCC", (P, 2), F32)
    ESK = sb("ESK", (P, 8), F32)
    LAMT = sb("LAMT", (P, 8), F32)
    SCR = sb("SCR", (P, 3200), F32)
    XO = [SCR[:, i * D:(i + 1) * D] for i in range(2)]
    MAPS = SCR[:, 0:512].rearrange("p (c q) -> p c q", c=4)
    CSTF = SCR[:, 512:1152].rearrange("p (c q) -> p c q", c=5)
    OH = [SCR[:, 1152 + i * 128:1152 + (i + 1) * 128] for i in range(2)]
    ACC = SCR[:, 1408:2432].rearrange("p (h q) -> p h q", h=8)
    TABP = SCR[:, 2432:2944]
    LAMW = SCR[:, 2944:3200]

    def ps(name, dt=F32, cols=512):
        return es.enter_context(nc.psum_tensor(name, [P, cols], dt))

    SPS = [ps("SPS0"), ps("SPS1")]
    APS = [ps("APS%d" % i) for i in range(4)]
    M0 = ps("M0")
    T0 = ps("T0", BF16, 1024)

    B = {}

    def bf(name):
        if name not in B:
            B[name] = Buf(name)
        return B[name]

    ident = CST[:, 0, :]
    triu = CST[:, 1, :]
    tril = CST[:, 2, :]
    maskT = CST[:, 3, :]
    ones = CST[:, 4, :]
    bCST = bf("CST")
    bBIAS = bf("BIAS")

    rot = {}

    def nxt(name, n):
        v = rot.get(name, 0)
        rot[name] = v + 1
        return v % n

    def setup():
        sch.add("sp", lambda e: e.dma_start(out=CSTF, in_=consts_d.rearrange("c p q -> p c q")),
                writes=[bf("CSTF")], dma=bf("CSTF"))
        sch.add("sp", lambda e: e.dma_start(out=MAPS, in_=maps_d.rearrange("c p q -> p c q")),
                writes=[bf("MAPS")], dma=bf("MAPS"))
        sch.add("sp", lambda e: e.dma_start(out=TABP, in_=tab_d), writes=[bf("TABP")], dma=bf("TABP"))
        sch.add("sp", lambda e: e.dma_start(out=ESK[:, :], in_=sink_d), writes=[bf("ESK")], dma=bf("ESK"))
        sch.add("dve", lambda e: e.tensor_copy(out=CST[:, :, :], in_=CSTF),
                reads=[bf("CSTF")], writes=[bCST])
        sch.add("dve", lambda e: e.memset(ZER[:, :], 0.0), writes=[bf("ZER")])
        for b_ in range(32):
            if b_ == 15:
                continue
            sch.add("dve", lambda e, b_=b_: e.tensor_tensor(
                out=TABP[:, b_ * 16:(b_ + 1) * 16], in0=TABP[:, b_ * 16:(b_ + 1) * 16],
                in1=TABP[:, 240:256], op=ALU.subtract), reads=[bf("TABP")], writes=[bf("TABP")])
        for c in range(8):
            for j in range(2):
                r0 = j * 64
                sch.add("dve", lambda e, c=c, j=j, r0=r0: e.tensor_tensor(
                    out=ESK[r0:r0 + 64, c:c + 1], in0=ESK[r0:r0 + 64, c:c + 1],
                    in1=TABP[r0:r0 + 64, 240 + 2 * c + j:240 + 2 * c + j + 1], op=ALU.subtract),
                    reads=[bf("TABP"), bf("ESK")], writes=[bf("ESK")])
        sch.add("act", lambda e: e.activation(out=ESK[:, :], in_=ESK[:, :], func=AF.Exp),
                reads=[bf("ESK")], writes=[bf("ESK")])
        sch.add("dve", lambda e: e.tensor_scalar(out=TABP, in0=TABP, scalar1=8.0, scalar2=None,
                                                 op0=ALU.mult), reads=[bf("TABP")], writes=[bf("TABP")])
        for mi in range(4):
            eng = "dve"
            accb = bf("ACC")
            sch.add(eng, lambda e, mi=mi: e.tensor_scalar(
                out=OH[0], in0=MAPS[:, mi, :], scalar1=-1.0, scalar2=NEG, op0=ALU.is_equal, op1=ALU.mult),
                reads=[bf("MAPS")], writes=[bf("OH0")])
            for h0 in (0, 8):
                for h in range(8):
                    sch.add(eng, lambda e, h=h: e.tensor_copy(out=ACC[:, h, :], in_=OH[0]),
                            reads=[bf("OH0")], writes=[accb])
                for b_ in present[mi]:
                    sch.add(eng, lambda e, mi=mi, b_=b_: e.tensor_single_scalar(
                        out=OH[1], in_=MAPS[:, mi, :], scalar=float(b_), op=ALU.is_equal),
                        reads=[bf("MAPS")], writes=[bf("OH1")])
                    for h in range(8):
                        sch.add(eng, lambda e, h=h, b_=b_, h0=h0: e.scalar_tensor_tensor(
                            out=ACC[:, h, :], in0=OH[1],
                            scalar=TABP[:, b_ * 16 + h0 + h:b_ * 16 + h0 + h + 1],
                            in1=ACC[:, h, :], op0=ALU.mult, op1=ALU.add),
                            reads=[bf("OH1"), bf("TABP"), accb], writes=[accb])
                sch.add(eng, lambda e, mi=mi, h0=h0: e.tensor_copy(out=BIAS[:, mi, h0:h0 + 8, :], in_=ACC),
                        reads=[accb], writes=[bBIAS])
        for li, l in enumerate((0, 3)):
            lam_init = 0.8 - 0.6 * math.exp(-0.3 * l)
            sch.add("sp", lambda e, li=li: e.dma_start(out=LAMW, in_=lam_d[li]),
                    writes=[bf("LAMW")], dma=bf("LAMW"))
            sch.add("sp", lambda e, li=li: e.dma_start(out=SCC[:, li:li + 1], in_=subln_d[li]),
                    writes=[bf("SCC")], dma=bf("SCC%d" % li))
            for t in range(2):
                sch.add("dve", lambda e, t=t: e.tensor_tensor(
                    out=LAMW[:, t * 128:t * 128 + 64], in0=LAMW[:, t * 128:t * 128 + 64],
                    in1=LAMW[:, t * 128 + 64:t * 128 + 128], op=ALU.mult),
                    reads=[bf("LAMW")], writes=[bf("LAMW")])
                sch.add("dve", lambda e, t=t: e.tensor_reduce(
                    out=LAMT[:, t:t + 1], in_=LAMW[:, t * 128:t * 128 + 64], axis=mybir.AxisListType.X,
                    op=ALU.add), reads=[bf("LAMW")], writes=[bf("LAMT")])
            sch.add("act", lambda e: e.activation(out=LAMT[:, 0:2], in_=LAMT[:, 0:2], func=AF.Exp),
                    reads=[bf("LAMT")], writes=[bf("LAMT")])
            sch.add("dve", lambda e, li=li, lam_init=lam_init: e.scalar_tensor_tensor(
                out=LAMC[:, li, 0:1], in0=LAMT[:, 1:2], scalar=-lam_init, in1=LAMT[:, 0:1],
                op0=ALU.add, op1=ALU.subtract), reads=[bf("LAMT")], writes=[bf("LAMC")])
            sch.add("dve", lambda e, li=li, lam_init=lam_init: e.tensor_scalar(
                out=SCC[:, li:li + 1], in0=SCC[:, li:li + 1], scalar1=(1.0 - lam_init) * math.sqrt(128.0),
                scalar2=None, op0=ALU.mult), reads=[bf("SCC")], writes=[bf("SCC")])

    def load_weights(l, c, wset):
        kind = KINDS[l]
        W = WT[wset]
        wb = bf("WT%d" % wset)
        if kind in (0, 1):
            cols = [c * 128, 3072 + c * 128, 1024 + c * 128, 2048 + c * 128]
            for si, c0 in enumerate(cols):
                sch.add("pool", lambda e, si=si, c0=c0, W=W: e.dma_start(
                    out=W[:, :, si * 128:(si + 1) * 128],
                    in_=win_d[l][:, c0:c0 + 128].rearrange("(c p) f -> p c f", p=P)),
                    writes=[wb], dma=bf("WTd%d_%d" % (wset, si)))
        else:
            g = c // 2
            cols = [(c * 128, 128, 0), (1536 + c * 128, 128, 128), (1024 + g * 64, 64, 256),
                    (1280 + g * 64, 64, 320)]
            for si, (c0, w, o0) in enumerate(cols):
                sch.add("pool", lambda e, c0=c0, w=w, o0=o0, W=W: e.dma_start(
                    out=W[:, :, o0:o0 + w],
                    in_=win_d[l][:, c0:c0 + w].rearrange("(c p) f -> p c f", p=P)),
                    writes=[wb], dma=bf("WTd%d_%d" % (wset, si)))
        sch.add("pool", lambda e: e.dma_start(out=WO[wset][:, :], in_=wout_d[l][c * 128:(c + 1) * 128, :]),
                writes=[bf("WO%d" % wset)], dma=bf("WOd%d" % wset))

    def norm_tile(xrows_ap, nrows, col, gidx_loaded, xnT_cols):
        slot = nxt("XN", 2)
        xn = XN[slot]
        xb = bf("XN%d" % slot)
        sch.add("act", lambda e: e.activation(out=xn[0:nrows, :], in_=xrows_ap, func=AF.Square,
                                              accum_out=SS[0:nrows, col:col + 1]),
                reads=[bf("X")], writes=[xb, bf("SS")])
        sch.add("act", lambda e: e.activation(out=SS[0:nrows, col:col + 1], in_=SS[0:nrows, col:col + 1],
                                              func=AF.Ln, scale=1.0 / D, bias=EPS),
                reads=[bf("SS")], writes=[bf("SS")])
        sch.add("act", lambda e: e.activation(out=SS[0:nrows, col:col + 1], in_=SS[0:nrows, col:col + 1],
                                              func=AF.Exp, scale=-0.5),
                reads=[bf("SS")], writes=[bf("SS")])
        sch.add("dve", lambda e: e.scalar_tensor_tensor(out=xn[0:nrows, :], in0=xrows_ap,
                                                        scalar=SS[0:nrows, col:col + 1], in1=Gn[0:nrows, :],
                                                        op0=ALU.mult, op1=ALU.mult),
                reads=[bf("X"), bf("SS"), bf("Gn")], writes=[xb])
        for ch in range(NCH):
            sch.add("pe", lambda e, ch=ch: e.transpose(out=T0[:, ch * 128:ch * 128 + nrows],
                                                       in_=xn[0:nrows, ch * 128:(ch + 1) * 128],
                                                       identity=ident[0:nrows, 0:nrows]),
                    reads=[xb, bCST], writes=[bf("T0")])
        c0, c1 = xnT_cols
        sch.add("act", lambda e: e.copy(
            out=xnT[:, :, c0:c1],
            in_=T0[:, :].rearrange("p (c t) -> p c t", c=NCH)[:, :, 0:nrows]),
            reads=[bf("T0")], writes=[bf("xnT")])

    def zero_acc():
        sch.add("dve", lambda e: e.memset(SS[:, :], 0.0), writes=[bf("SS")])

    def out_proj_units(ot_slot, wset, ntok_tiles, xdst):
        units = []
        for i, (rows, xap_fn) in enumerate(xdst):
            for hf in range(2):
                def u(i=i, hf=hf, rows=rows, xap_fn=xap_fn):
                    sch.add("pe", lambda e: e.matmul(M0[0:rows, :], lhsT=OT[ot_slot][:, i * 128:i * 128 + rows],
                                                     rhs=WO[wset][:, hf * 512:(hf + 1) * 512],
                                                     start=True, stop=True),
                            reads=[bf("OT%d" % ot_slot), bf("WO%d" % wset)], writes=[bf("M0")])
                    xa = xap_fn(hf)
                    sch.add("dve", lambda e: e.tensor_tensor(out=xa, in0=M0[0:rows, :], in1=xa, op=ALU.add),
                            reads=[bf("M0"), bf("X")], writes=[bf("X")])
                units.append(u)
        return units

    def proj_q_unit(wset, tok0, ntok, slot):
        def u():
            for ch in range(NCH):
                sch.add("pe", lambda e, ch=ch: e.matmul(M0[:, 0:ntok], lhsT=WT[wset][:, ch, 0:128],
                                                        rhs=xnT[:, ch, tok0:tok0 + ntok],
                                                        start=(ch == 0), stop=(ch == NCH - 1)),
                        reads=[bf("WT%d" % wset), bf("xnT")], writes=[bf("M0")])
            sch.add("dve", lambda e: e.tensor_copy(out=QB[slot][:, 0:ntok], in_=M0[:, 0:ntok]),
                    reads=[bf("M0")], writes=[bf("QB%d" % slot)])
        return u

    def proj_g_unit(wset, tok0, ntok, slot):
        def u():
            for ch in range(NCH):
                sch.add("pe", lambda e, ch=ch: e.matmul(M0[:, 0:ntok], lhsT=WT[wset][:, ch, 128:256],
                                                        rhs=xnT[:, ch, tok0:tok0 + ntok],
                                                        start=(ch == 0), stop=(ch == NCH - 1)),
                        reads=[bf("WT%d" % wset), bf("xnT")], writes=[bf("M0")])
            sch.add("act", lambda e: e.activation(out=GB[slot][:, 0:ntok], in_=M0[:, 0:ntok], func=AF.Silu),
                    reads=[bf("M0")], writes=[bf("GB%d" % slot)])
        return u

    def proj_kv_unit(l, c, wset, kvset, tok0, nrows, kt_cols, v_dst, out_dmas, swa):
        def u():
            width = 128 if swa else 256
            for ch in range(NCH):
                sch.add("pe", lambda e, ch=ch: e.matmul(M0[0:nrows, 0:width],
                                                        lhsT=xnT[:, ch, tok0:tok0 + nrows],
                                                        rhs=WT[wset][:, ch, 256:256 + width],
                                                        start=(ch == 0), stop=(ch == NCH - 1)),
                        reads=[bf("WT%d" % wset), bf("xnT")], writes=[bf("M0")])
            ks = nxt("KVS", NKVS)
            kvs = KVS[ks]
            kb = bf("KVS%d" % ks)
            sch.add("dve", lambda e: e.tensor_copy(out=kvs[0:nrows, 0:width], in_=M0[0:nrows, 0:width]),
                    reads=[bf("M0")], writes=[kb])
            for (dst_ap, c0, w) in out_dmas:
                sch.add("sp", lambda e, dst_ap=dst_ap, c0=c0, w=w: e.dma_start(out=dst_ap,
                                                                              in_=kvs[0:nrows, c0:c0 + w]),
                        reads=[kb], dma=kb)
            kbs = nxt("KBT", 2)
            kbt = KBT[kbs]
            kbb = bf("KBT%d" % kbs)
            vb_ = v_dst[1]
            if swa:
                for hh in range(2):
                    sch.add("pool", lambda e, hh=hh: e.tensor_copy(out=kbt[0:nrows, hh * 64:(hh + 1) * 64],
                                                                   in_=kvs[0:nrows, 0:64]),
                            reads=[kb], writes=[kbb])
                    sch.add("pool", lambda e, hh=hh: e.tensor_copy(out=v_dst[0][:, hh * 64:(hh + 1) * 64],
                                                                   in_=kvs[0:nrows, 64:128]),
                            reads=[kb], writes=[vb_])
            else:
                sch.add("pool", lambda e: e.tensor_copy(out=kbt[0:nrows, :], in_=kvs[0:nrows, 0:128]),
                        reads=[kb], writes=[kbb])
                sch.add("pool", lambda e: e.tensor_copy(out=v_dst[0], in_=kvs[0:nrows, 128:256]),
                        reads=[kb], writes=[vb_])
            sch.add("pe", lambda e: e.transpose(out=T0[:, 0:nrows], in_=kbt[0:nrows, :],
                                                identity=ident[0:nrows, 0:nrows]),
                    reads=[kbb, bCST], writes=[bf("T0")])
            ktt, ktc0, ktb = kt_cols
            sch.add("act", lambda e: e.copy(out=ktt[:, ktc0:ktc0 + nrows], in_=T0[:, 0:nrows]),
                    reads=[bf("T0")], writes=[ktb])
        return u

    def attn_da(c, qslot, NQ, acol, ktiles, bg):
        hmb = 2 * c
        items = [(kt, m) for kt in range(len(ktiles)) for m in range(2)]
        nkt = len(ktiles)

        def qk(it):
            kt, m = it
            t = ktiles[kt]
            sbk = nxt("SPS", 2)
            Sb = SPS[sbk]
            sbb = bf("SPS%d" % sbk)
            c0, nk = t["c0"], t["nk"]
            nb = len(t["bias"])
            ktt, kc0 = t["kT"]
            sch.add("pe", lambda e: e.matmul(Sb[0:nk, c0:NQ], lhsT=ktt[m * 64:(m + 1) * 64, kc0:kc0 + nk],
                                             rhs=QB[qslot][m * 64:(m + 1) * 64, c0:NQ],
                                             start=True, stop=(nb == 0), skip_group_check=True),
                    reads=[t["kTb"], bf("QB%d" % qslot)], writes=[sbb])
            for bi, (co, ncol, mi, kr) in enumerate(t["bias"]):
                sch.add("pe", lambda e, co=co, ncol=ncol, mi=mi, bi=bi: e.matmul(
                    Sb[0:nk, co:co + ncol], lhsT=ident[0:nk, 0:nk], rhs=BIAS[0:nk, mi, hmb + m, 0:ncol],
                    start=False, stop=(bi == nb - 1), skip_group_check=True),
                    reads=[bCST, bBIAS], writes=[sbb])
            return Sb, sbb

        def rest(it, Sb, sbb):
            kt, m = it
            t = ktiles[kt]
            c0, nk = t["c0"], t["nk"]
            pslot = nxt("PB", NPB)
            Pt = PB[pslot]
            pbb = bf("PB%d" % pslot)
            sch.add("act", lambda e: e.activation(out=Pt[0:nk, c0:NQ], in_=Sb[0:nk, c0:NQ], func=AF.Exp,
                                                  scale=0.125), reads=[sbb], writes=[pbb])
            sch.add("pe", lambda e: e.matmul(APS[m][:, acol + c0:acol + NQ], lhsT=t["v"], rhs=Pt[0:nk, c0:NQ],
                                             start=(kt == 0), stop=(kt == nkt - 1), skip_group_check=True),
                    reads=[t["vb"], pbb], writes=[bf("APS%d" % m)])
            sch.add("pe", lambda e: e.matmul(APS[2 + m][:, acol + c0:acol + NQ], lhsT=ones[0:nk, :],
                                             rhs=Pt[0:nk, c0:NQ],
                                             start=(kt == 0), stop=(kt == nkt - 1), skip_group_check=True),
                    reads=[bCST, pbb], writes=[bf("APS%d" % (2 + m))])

        cur = qk(items[0])
        for i, it in enumerate(items):
            nx = qk(items[i + 1]) if i + 1 < len(items) else None
            rest(it, *cur)
            cur = nx
            if bg:
                bg.pop(0)()

    def post_da(li, qslot, oslot, NQ, acol):
        T_a, T_b = TP[0], TP[1]
        ba, bb_ = bf("TP0"), bf("TP1")
        sl = slice(acol, acol + NQ)
        sch.add("dve", lambda e: e.reciprocal(out=T_a[:, 0:NQ], in_=APS[2][:, sl]),
                reads=[bf("APS2")], writes=[ba])
        sch.add("dve", lambda e: e.tensor_tensor(out=T_a[:, 0:NQ], in0=APS[0][:, sl], in1=T_a[:, 0:NQ],
                                                 op=ALU.mult), reads=[bf("APS0"), ba], writes=[ba])
        sch.add("dve", lambda e: e.reciprocal(out=T_b[:, 0:NQ], in_=APS[3][:, sl]),
                reads=[bf("APS3")], writes=[bb_])
        sch.add("dve", lambda e: e.tensor_tensor(out=T_b[:, 0:NQ], in0=APS[1][:, sl], in1=T_b[:, 0:NQ],
                                                 op=ALU.mult), reads=[bf("APS1"), bb_], writes=[bb_])
        sch.add("dve", lambda e: e.scalar_tensor_tensor(out=T_a[:, 0:NQ], in0=T_b[:, 0:NQ],
                                                        scalar=LAMC[:, li, 0:1], in1=T_a[:, 0:NQ],
                                                        op0=ALU.mult, op1=ALU.add),
                reads=[ba, bb_, bf("LAMC")], writes=[ba])
        sch.add("act", lambda e: e.activation(out=SQ[:, 0:NQ], in_=T_a[:, 0:NQ], func=AF.Square),
                reads=[ba], writes=[bf("SQ")])
        sch.add("pe", lambda e: e.matmul(M0[:, 0:NQ], lhsT=ones, rhs=SQ[:, 0:NQ], start=True, stop=True),
                reads=[bCST, bf("SQ")], writes=[bf("M0")])
        sch.add("act", lambda e: e.activation(out=T_b[:, 0:NQ], in_=M0[:, 0:NQ], func=AF.Ln, bias=128.0 * EPS),
                reads=[bf("M0")], writes=[bb_])
        sch.add("act", lambda e: e.activation(out=T_b[:, 0:NQ], in_=T_b[:, 0:NQ], func=AF.Exp, scale=-0.5),
                reads=[bb_], writes=[bb_])
        sch.add("dve", lambda e: e.tensor_tensor(out=T_a[:, 0:NQ], in0=T_a[:, 0:NQ], in1=T_b[:, 0:NQ],
                                                 op=ALU.mult), reads=[ba, bb_], writes=[ba])
        sch.add("dve", lambda e: e.scalar_tensor_tensor(out=OT[oslot][:, 0:NQ], in0=T_a[:, 0:NQ],
                                                        scalar=SCC[:, li:li + 1], in1=GB[qslot][:, 0:NQ],
                                                        op0=ALU.mult, op1=ALU.mult),
                reads=[ba, bf("SCC"), bf("GB%d" % qslot)], writes=[bf("OT%d" % oslot)])

    def attn_sb(c, qslot, NQ, acol, ktiles, bg):
        nkt = len(ktiles)
        for j in range(2):
            for a in (j, 2 + j):
                sch.add("pe", lambda e, a=a: e.matmul(APS[a][:, acol:acol + NQ], lhsT=ZER[:, :],
                                                      rhs=QB[qslot][:, 0:NQ], start=True, stop=False,
                                                      skip_group_check=True),
                        reads=[bf("ZER"), bf("QB%d" % qslot)], writes=[bf("APS%d" % a)])
        items = [(kt, j) for kt in reversed(range(nkt)) for j in range(2)]

        def stage1(it):
            kt, j = it
            t = ktiles[kt]
            c0, nk = t["c0"], t["nk"]
            sbk = nxt("SPS", 2)
            Sb = SPS[sbk]
            sbb = bf("SPS%d" % sbk)
            nb = len(t["bias"])
            ktt, kc0 = t["kT"]
            sch.add("pe", lambda e: e.matmul(Sb[0:nk, c0:NQ], lhsT=ktt[j * 64:(j + 1) * 64, kc0:kc0 + nk],
                                             rhs=QB[qslot][j * 64:(j + 1) * 64, c0:NQ],
                                             start=True, stop=(nb == 0), skip_group_check=True),
                    reads=[t["kTb"], bf("QB%d" % qslot)], writes=[sbb])
            for bi, (co, ncol, mi, kr) in enumerate(t["bias"]):
                sch.add("pe", lambda e, co=co, ncol=ncol, bi=bi: e.matmul(
                    Sb[0:nk, co:co + ncol], lhsT=ident[0:nk, 0:nk], rhs=maskT[0:nk, 0:ncol],
                    start=False, stop=(bi == nb - 1), skip_group_check=True),
                    reads=[bCST], writes=[sbb])
            es_ = nxt("EW", NE)
            E = EW[es_]
            eb = bf("EW%d" % es_)
            spb = SPB[es_]
            spbb = bf("SPB%d" % es_)
            sch.add("act", lambda e: e.activation(out=E[0:nk, c0:NQ], in_=Sb[0:nk, c0:NQ], func=AF.Exp,
                                                  scale=0.125), reads=[sbb], writes=[eb])
            sch.add("act", lambda e: e.activation(out=E[0:nk, c0:NQ], in_=E[0:nk, c0:NQ], func=AF.Ln,
                                                  bias=1.0), reads=[eb], writes=[eb])
            sch.add("pool", lambda e: e.tensor_copy(out=spb[0:nk, c0:NQ], in_=E[0:nk, c0:NQ]),
                    reads=[eb], writes=[spbb])
            sch.add("dve", lambda e: e.scalar_tensor_tensor(out=E[0:nk, c0:NQ], in0=Sb[0:nk, c0:NQ],
                                                            scalar=0.125, in1=E[0:nk, c0:NQ],
                                                            op0=ALU.mult, op1=ALU.subtract),
                    reads=[sbb, eb, spbb], writes=[eb])
            return E, eb, spb, spbb

        def stage2a(it, E, eb, spb, spbb):
            kt, j = it
            t = ktiles[kt]
            c0, nk = t["c0"], t["nk"]
            rb = bf("APS%d" % j)
            sch.add("pe", lambda e: e.matmul(APS[j][:, acol + c0:acol + NQ], lhsT=triu[0:nk, :],
                                             rhs=spb[0:nk, c0:NQ], start=False, stop=False,
                                             skip_group_check=True),
                    reads=[bCST, spbb], writes=[rb])
            sch.add("dve", lambda e: e.tensor_tensor(out=E[0:nk, c0:NQ], in0=E[0:nk, c0:NQ],
                                                     in1=APS[j][0:nk, acol + c0:acol + NQ], op=ALU.subtract),
                    reads=[eb, rb], writes=[eb])
            pslot = nxt("PB", NPB)
            Ab = PB[pslot]
            pbb = bf("PB%d" % pslot)
            sch.add("act", lambda e: e.activation(out=Ab[0:nk, c0:NQ], in_=E[0:nk, c0:NQ], func=AF.Exp),
                    reads=[eb], writes=[pbb])
            return Ab, pbb

        def stage2b(it, spb, spbb, Ab, pbb):
            kt, j = it
            t = ktiles[kt]
            c0, nk = t["c0"], t["nk"]
            sch.add("pe", lambda e: e.matmul(APS[j][:, acol + c0:acol + NQ], lhsT=tril[0:nk, :],
                                             rhs=spb[0:nk, c0:NQ], start=False, stop=False,
                                             skip_group_check=True),
                    reads=[bCST, spbb], writes=[bf("APS%d" % j)])
            sch.add("pe", lambda e: e.matmul(APS[2 + j][:, acol + c0:acol + NQ], lhsT=t["v"],
                                             rhs=Ab[0:nk, c0:NQ], start=False, stop=False,
                                             skip_group_check=True),
                    reads=[t["vb"], pbb], writes=[bf("APS%d" % (2 + j))])

        n = len(items)
        s1 = {0: stage1(items[0])}
        if n > 1:
            s1[1] = stage1(items[1])
        prev = None
        for i, it in enumerate(items):
            a = stage2a(it, *s1[i])
            if prev is not None:
                stage2b(*prev)
            prev = (it, s1[i][2], s1[i][3], a[0], a[1])
            if i + 2 < n:
                s1[i + 2] = stage1(items[i + 2])
            del s1[i]
            if bg:
                bg.pop(0)()
        stage2b(*prev)

    def post_sb(qslot, oslot, NQ, acol):
        for j in range(2):
            r = slice(j * 64, (j + 1) * 64)
            sch.add("dve", lambda e, j=j, r=r: e.tensor_tensor(out=OT[oslot][r, 0:NQ],
                                                               in0=APS[2 + j][r, acol:acol + NQ],
                                                               in1=GB[qslot][r, 0:NQ], op=ALU.mult),
                    reads=[bf("APS%d" % (2 + j)), bf("GB%d" % qslot)], writes=[bf("OT%d" % oslot)])

    def attn_swa(c, qslot, qtiles, bg):
        items = []
        for qi, qt in enumerate(qtiles):
            for j in range(2):
                for ki in range(len(qt["ktiles"])):
                    items.append((qi, j, ki))

        def qk(it):
            qi, j, ki = it
            qt = qtiles[qi]
            t = qt["ktiles"][ki]
            nk, nq, qc0 = t["nk"], qt["nq"], qt["qc0"]
            sbk = nxt("SPS", 2)
            Sb = SPS[sbk]
            sbb = bf("SPS%d" % sbk)
            ktt, kc0 = t["kT"]
            sch.add("pe", lambda e: e.matmul(Sb[0:nk, 0:nq], lhsT=ktt[j * 64:(j + 1) * 64, kc0:kc0 + nk],
                                             rhs=QB[qslot][j * 64:(j + 1) * 64, qc0:qc0 + nq],
                                             start=True, stop=False, skip_group_check=True),
                    reads=[t["kTb"], bf("QB%d" % qslot)], writes=[sbb])
            sch.add("pe", lambda e: e.matmul(Sb[0:nk, 0:nq], lhsT=ident[0:nk, 0:nk],
                                             rhs=BIAS[0:nk, t["mi"], 2 * c + j, 0:nq],
                                             start=False, stop=True, skip_group_check=True),
                    reads=[bCST, bBIAS], writes=[sbb])
            return Sb, sbb

        def rest(it, Sb, sbb):
            qi, j, ki = it
            qt = qtiles[qi]
            t = qt["ktiles"][ki]
            nk, nq, ac = t["nk"], qt["nq"], qt["acol"]
            nki = len(qt["ktiles"])
            pslot = nxt("PB", NPB)
            Pt = PB[pslot]
            pbb = bf("PB%d" % pslot)
            sch.add("act", lambda e: e.activation(out=Pt[0:nk, 0:nq], in_=Sb[0:nk, 0:nq], func=AF.Exp,
                                                  scale=0.125), reads=[sbb], writes=[pbb])
            sch.add("pe", lambda e: e.matmul(APS[j][:, ac:ac + nq], lhsT=t["v"], rhs=Pt[0:nk, 0:nq],
                                             start=(ki == 0), stop=(ki == nki - 1), skip_group_check=True),
                    reads=[t["vb"], pbb], writes=[bf("APS%d" % j)])
            sch.add("pe", lambda e: e.matmul(APS[2 + j][:, ac:ac + nq], lhsT=ones[0:nk, :], rhs=Pt[0:nk, 0:nq],
                                             start=(ki == 0), stop=(ki == nki - 1), skip_group_check=True),
                    reads=[bCST, pbb], writes=[bf("APS%d" % (2 + j))])

        cur = qk(items[0])
        for i, it in enumerate(items):
            nx = qk(items[i + 1]) if i + 1 < len(items) else None
            rest(it, *cur)
            cur = nx
            if bg:
                bg.pop(0)()

    def post_swa(c, qslot, oslot, NQ, acol):
        T_a = TP[0]
        ba = bf("TP0")
        for j in range(2):
            r = slice(j * 64, (j + 1) * 64)
            sch.add("dve", lambda e, j=j, r=r: e.tensor_scalar(out=T_a[r, 0:NQ], in0=APS[2 + j][r, acol:acol + NQ],
                                                               scalar1=ESK[r, c:c + 1], scalar2=None, op0=ALU.add),
                    reads=[bf("APS%d" % (2 + j)), bf("ESK")], writes=[ba])
            sch.add("dve", lambda e, r=r: e.reciprocal(out=T_a[r, 0:NQ], in_=T_a[r, 0:NQ]),
                    reads=[ba], writes=[ba])
            sch.add("dve", lambda e, j=j, r=r: e.tensor_tensor(out=T_a[r, 0:NQ], in0=APS[j][r, acol:acol + NQ],
                                                               in1=T_a[r, 0:NQ], op=ALU.mult),
                    reads=[bf("APS%d" % j), ba], writes=[ba])
            sch.add("dve", lambda e, r=r: e.tensor_tensor(out=OT[oslot][r, 0:NQ], in0=T_a[r, 0:NQ],
                                                          in1=GB[qslot][r, 0:NQ], op=ALU.mult),
                    reads=[ba, bf("GB%d" % qslot)], writes=[bf("OT%d" % oslot)])

    def final_norm_tile(xrows_ap, nrows, col, dst_ap):
        slot = nxt("XO", 2)
        xo = XO[slot]
        xob = bf("XO%d" % slot)
        sch.add("act", lambda e: e.activation(out=xo[0:nrows, :], in_=xrows_ap, func=AF.Square,
                                              accum_out=SS[0:nrows, col:col + 1]),
                reads=[bf("X")], writes=[xob, bf("SS")])
        sch.add("act", lambda e: e.activation(out=SS[0:nrows, col:col + 1], in_=SS[0:nrows, col:col + 1],
                                              func=AF.Ln, scale=1.0 / D, bias=EPS),
                reads=[bf("SS")], writes=[bf("SS")])
        sch.add("act", lambda e: e.activation(out=SS[0:nrows, col:col + 1], in_=SS[0:nrows, col:col + 1],
                                              func=AF.Exp, scale=-0.5),
                reads=[bf("SS")], writes=[bf("SS")])
        sch.add("dve", lambda e: e.scalar_tensor_tensor(out=xo[0:nrows, :], in0=xrows_ap,
                                                        scalar=SS[0:nrows, col:col + 1], in1=Gn[0:nrows, :],
                                                        op0=ALU.mult, op1=ALU.mult),
                reads=[bf("X"), bf("SS"), bf("Gn")], writes=[xob])
        sch.add("sp", lambda e: e.dma_start(out=dst_ap, in_=xo[0:nrows, :]), reads=[xob], dma=xob)

    def load_gain(idx):
        sch.add("sp", lambda e: e.dma_start(out=Gn[:, :], in_=gain_d[idx]), writes=[bf("Gn")], dma=bf("Gn"))

    def prompt_seq(s):
        sch.add("sp", lambda e: e.dma_start(out=X[:, :, :], in_=xp_d[s].rearrange("(n p) d -> p n d", p=P)),
                writes=[bf("X")], dma=bf("Xd"))
        for l in range(4):
            kind = KINDS[l]
            li = 0 if l == 0 else 1
            swa = kind == 2
            load_gain(l)
            load_weights(l, 0, 0)
            zero_acc()
            for n in range(NT):
                norm_tile(X[:, n, :], P, n, None, (n * P, (n + 1) * P))

            def kvset_of(c):
                return (c // 2) % 2 if swa else c % 2

            def kv_units(c):
                if swa and c % 2 == 1:
                    return []
                kvset = kvset_of(c)
                wset = c % 2
                us = []
                for n in range(NT):
                    outs = []
                    if swa:
                        g = c // 2
                        if n == NT - 1:
                            outs = [(kp_d[l][s, :, g * 64:(g + 1) * 64], 0, 64),
                                    (vp_d[l][s, :, g * 64:(g + 1) * 64], 64, 64)]
                    else:
                        outs = [(kp_d[l][s, n * P:(n + 1) * P, c * 128:(c + 1) * 128], 0, 128),
                                (vp_d[l][s, n * P:(n + 1) * P, c * 128:(c + 1) * 128], 128, 128)]
                    us.append(proj_kv_unit(l, c, wset, kvset, n * P, P,
                                           (KT[kvset], n * P, bf("KT%d" % kvset)),
                                           (VB[kvset][:, n, :], bf("VB%d" % kvset)), outs, swa))
                return us

            for u in kv_units(0):
                u()
            qslots = {}

            def qg_units(c, qb):
                slot = nxt("QBslot", NQS)
                qslots[(c, qb)] = slot
                return [proj_q_unit(c % 2, qb * 512, 512, slot), proj_g_unit(c % 2, qb * 512, 512, slot)]

            for u in qg_units(0, 0):
                u()
            pend_out = []
            blocks = [(c, qb) for c in range(8) for qb in range(NQB)]
            for bi_, (c, qb) in enumerate(blocks):
                wset = c % 2
                kvset = kvset_of(c)
                bg = list(pend_out)
                pend_out = []
                if qb == min(1, NQB - 1) and c + 1 < 8:
                    if NQB == 1:
                        for u in bg:
                            u()
                        bg = []
                    load_weights(l, c + 1, (c + 1) % 2)
                if bi_ + 1 < len(blocks):
                    bg += qg_units(*blocks[bi_ + 1]) if blocks[bi_ + 1][0] == c else []
                if c + 1 < 8 and qb >= min(1, NQB - 1):
                    ku = kv_units(c + 1)
                    nshare = max(1, NQB - 1)
                    idx = qb - min(1, NQB - 1)
                    per = (len(ku) + nshare - 1) // nshare
                    bg += ku[idx * per:(idx + 1) * per]
                    if qb == NQB - 1:
                        bg += qg_units(c + 1, 0)
                qslot = qslots[(c, qb)]
                oslot = nxt("OT", 2)
                ktb, vbb = bf("KT%d" % kvset), bf("VB%d" % kvset)
                if kind in (0, 1):
                    ktiles = []
                    for kt in range(4 * qb + 4):
                        c0 = max(0, kt - 4 * qb) * P
                        bias = []
                        for i in range(4):
                            qt = 4 * qb + i
                            if kind == 0:
                                if kt == qt:
                                    bias.append((i * P, P, 0, P))
                                elif kt == qt - 1:
                                    bias.append((i * P, P, 1, P))
                            else:
                                if kt == qt:
                                    bias.append((i * P, P, 0, P))
                        ktiles.append(dict(kT=(KT[kvset], kt * P), kTb=ktb, v=VB[kvset][:, kt, :], vb=vbb,
                                           nk=P, c0=c0, bias=bias))
                    if kind == 0:
                        attn_da(c, qslot, 512, 0, ktiles, bg)
                    else:
                        attn_sb(c, qslot, 512, 0, ktiles, bg)
                else:
                    qtiles = []
                    for i in range(4):
                        qt = 4 * qb + i
                        kts = []
                        if qt > 0:
                            kts.append(dict(kT=(KT[kvset], (qt - 1) * P), kTb=ktb, v=VB[kvset][:, qt - 1, :],
                                            vb=vbb, nk=P, mi=3))
                        kts.append(dict(kT=(KT[kvset], qt * P), kTb=ktb, v=VB[kvset][:, qt, :], vb=vbb,
                                        nk=P, mi=2))
                        qtiles.append(dict(qc0=i * P, nq=P, acol=i * P, ktiles=kts))
                    attn_swa(c, qslot, qtiles, bg)
                for u in bg:
                    u()
                if kind == 0:
                    post_da(li, qslot, oslot, 512, 0)
                elif kind == 1:
                    post_sb(qslot, oslot, 512, 0)
                else:
                    post_swa(c, qslot, oslot, 512, 0)
                xdst = [(P, (lambda hf, i=i, qb=qb: X[:, 4 * qb + i, hf * 512:(hf + 1) * 512])) for i in range(4)]
                pend_out = out_proj_units(oslot, wset, 4, xdst)
            for u in pend_out:
                u()
        load_gain(4)
        zero_acc()
        for n in range(NT):
            final_norm_tile(X[:, n, :], P, n, yp_d[s, n * P:(n + 1) * P, :])

    def sample_pass():
        sch.add("sp", lambda e: e.dma_start(out=X[0:TS, 0, :], in_=xs_d), writes=[bf("X")], dma=bf("Xd"))
        for l in range(4):
            kind = KINDS[l]
            li = 0 if l == 0 else 1
            swa = kind == 2
            load_gain(l)
            zero_acc()
            norm_tile(X[0:TS, 0, :], TS, 0, None, (0, TS))
            for c in range(8):
                wset = c % 2
                load_weights(l, c, wset)
                slot = nxt("QBslot", NQS)
                proj_q_unit(wset, 0, TS, slot)()
                proj_g_unit(wset, 0, TS, slot)()
                oslot = nxt("OT", 2)
                for b in range(NSEQ):
                    st = nxt("SKV", 2)
                    ktb, vbb = bf("KT%d" % st), bf("VB%d" % st)
                    ktnb, vnb = bf("KTN%d" % st), bf("VN%d" % st)
                    if swa:
                        g = c // 2
                        for hh in range(2):
                            sch.add("pool", lambda e, hh=hh, st=st, g=g, b=b, l=l: e.dma_start(
                                out=KC[st][:, 0, hh * 64:(hh + 1) * 64], in_=ck_d[l][b, :, g * 64:(g + 1) * 64]),
                                writes=[bf("KC%d" % st)], dma=bf("KCd%d_%d" % (st, hh)))
                            sch.add("pool", lambda e, hh=hh, st=st, g=g, b=b, l=l: e.dma_start(
                                out=VB[st][:, 0, hh * 64:(hh + 1) * 64], in_=cv_d[l][b, :, g * 64:(g + 1) * 64]),
                                writes=[vbb], dma=bf("VBd%d_%d" % (st, hh)))
                        ncache = 1
                    else:
                        sch.add("pool", lambda e, st=st, b=b, l=l, c=c: e.dma_start(
                            out=KC[st][:, :, :],
                            in_=ck_d[l][b, :, c * 128:(c + 1) * 128].rearrange("(n p) f -> p n f", p=P)),
                            writes=[bf("KC%d" % st)], dma=bf("KCd%d_0" % st))
                        sch.add("pool", lambda e, st=st, b=b, l=l, c=c: e.dma_start(
                            out=VB[st][:, 0:NPT, :],
                            in_=cv_d[l][b, :, c * 128:(c + 1) * 128].rearrange("(n p) f -> p n f", p=P)),
                            writes=[vbb], dma=bf("VBd%d_0" % st))
                        ncache = NPT
                    for n in range(ncache):
                        sch.add("pe", lambda e, n=n, st=st: e.transpose(out=T0[:, n * P:(n + 1) * P],
                                                                        in_=KC[st][:, n, :], identity=ident),
                                reads=[bf("KC%d" % st), bCST], writes=[bf("T0")])
                    sch.add("act", lambda e, st=st, ncache=ncache: e.copy(out=KT[st][:, 0:ncache * P],
                                                                          in_=T0[:, 0:ncache * P]),
                            reads=[bf("T0")], writes=[ktb])
                    if swa:
                        g = c // 2
                        outs = []
                        if c % 2 == 0:
                            outs = [(ks_d[l][b, P - T_DEC:P, g * 64:(g + 1) * 64], 0, 64),
                                    (vs_d[l][b, P - T_DEC:P, g * 64:(g + 1) * 64], 64, 64)]
                    else:
                        outs = [(ks_d[l][b * T_DEC:(b + 1) * T_DEC, c * 128:(c + 1) * 128], 0, 128),
                                (vs_d[l][b * T_DEC:(b + 1) * T_DEC, c * 128:(c + 1) * 128], 128, 128)]
                    proj_kv_unit(l, c, wset, st, b * T_DEC, T_DEC, (KTN[st], 0, ktnb),
                                 (VN[st][:, :], vnb), outs, swa)()
                    acol = b * T_DEC
                    qc0 = b * T_DEC
                    if kind in (0, 1):
                        ktiles = []
                        for kt in range(NPT):
                            bias = []
                            if kind == 0 and kt == NPT - 1:
                                bias = [(0, T_DEC, 1, P)]
                            ktiles.append(dict(kT=(KT[st], kt * P), kTb=ktb, v=VB[st][:, kt, :], vb=vbb,
                                               nk=P, c0=0, bias=bias))
                        ktiles.append(dict(kT=(KTN[st], 0), kTb=ktnb, v=VN[st][:, :], vb=vnb, nk=T_DEC, c0=0,
                                           bias=[(0, T_DEC, 0, T_DEC)]))
                        if kind == 0:
                            attn_da_s(c, slot, T_DEC, acol, qc0, ktiles)
                        else:
                            attn_sb_s(c, slot, T_DEC, acol, qc0, ktiles)
                    else:
                        kts = [dict(kT=(KT[st], 0), kTb=ktb, v=VB[st][:, 0, :], vb=vbb, nk=P, mi=3),
                               dict(kT=(KTN[st], 0), kTb=ktnb, v=VN[st][:, :], vb=vnb, nk=T_DEC, mi=2)]
                        attn_swa(c, slot, [dict(qc0=qc0, nq=T_DEC, acol=acol, ktiles=kts)], None)
                if kind == 0:
                    post_da(li, slot, oslot, TS, 0)
                elif kind == 1:
                    post_sb(slot, oslot, TS, 0)
                else:
                    post_swa(c, slot, oslot, TS, 0)
                xdst = [(TS, (lambda hf: X[0:TS, 0, hf * 512:(hf + 1) * 512]))]
                for u in out_proj_units(oslot, wset, 1, xdst):
                    u()
            if swa:
                for b in range(NSEQ):
                    sch.add("sp", lambda e, b=b, l=l: e.dma_start(out=ks_d[l][b, 0:P - T_DEC, :],
                                                             in_=ck_d[l][b, T_DEC:P, :]), dma=bf("d2d"))
                    sch.add("sp", lambda e, b=b, l=l: e.dma_start(out=vs_d[l][b, 0:P - T_DEC, :],
                                                             in_=cv_d[l][b, T_DEC:P, :]), dma=bf("d2d"))
        load_gain(4)
        zero_acc()
        final_norm_tile(X[0:TS, 0, :], TS, 0, ys_d[:, :])

    def attn_da_s(c, qslot, NQ, acol, qc0, ktiles):
        hmb = 2 * c
        nkt = len(ktiles)
        for kt in range(nkt):
            for m in range(2):
                t = ktiles[kt]
                nk = t["nk"]
                sbk = nxt("SPS", 2)
                Sb = SPS[sbk]
                sbb = bf("SPS%d" % sbk)
                nb = len(t["bias"])
                ktt, kc0 = t["kT"]
                sch.add("pe", lambda e, Sb=Sb, nk=nk, m=m, ktt=ktt, kc0=kc0, nb=nb: e.matmul(
                    Sb[0:nk, 0:NQ], lhsT=ktt[m * 64:(m + 1) * 64, kc0:kc0 + nk],
                    rhs=QB[qslot][m * 64:(m + 1) * 64, qc0:qc0 + NQ], start=True, stop=(nb == 0),
                    skip_group_check=True), reads=[t["kTb"], bf("QB%d" % qslot)], writes=[sbb])
                for bi, (co, ncol, mi, kr) in enumerate(t["bias"]):
                    sch.add("pe", lambda e, Sb=Sb, nk=nk, m=m, mi=mi, bi=bi, nb=nb: e.matmul(
                        Sb[0:nk, 0:NQ], lhsT=ident[0:nk, 0:nk], rhs=BIAS[0:nk, mi, hmb + m, 0:NQ],
                        start=False, stop=(bi == nb - 1), skip_group_check=True),
                        reads=[bCST, bBIAS], writes=[sbb])
                pslot = nxt("PB", NPB)
                Pt = PB[pslot]
                pbb = bf("PB%d" % pslot)
                sch.add("act", lambda e, Pt=Pt, Sb=Sb, nk=nk: e.activation(
                    out=Pt[0:nk, 0:NQ], in_=Sb[0:nk, 0:NQ], func=AF.Exp, scale=0.125),
                    reads=[sbb], writes=[pbb])
                sch.add("pe", lambda e, Pt=Pt, nk=nk, m=m, t=t, kt=kt: e.matmul(
                    APS[m][:, acol:acol + NQ], lhsT=t["v"], rhs=Pt[0:nk, 0:NQ],
                    start=(kt == 0), stop=(kt == nkt - 1), skip_group_check=True),
                    reads=[t["vb"], pbb], writes=[bf("APS%d" % m)])
                sch.add("pe", lambda e, Pt=Pt, nk=nk, m=m, kt=kt: e.matmul(
                    APS[2 + m][:, acol:acol + NQ], lhsT=ones[0:nk, :], rhs=Pt[0:nk, 0:NQ],
                    start=(kt == 0), stop=(kt == nkt - 1), skip_group_check=True),
                    reads=[bCST, pbb], writes=[bf("APS%d" % (2 + m))])

    def attn_sb_s(c, qslot, NQ, acol, qc0, ktiles):
        nkt = len(ktiles)
        for j in range(2):
            for a in (j, 2 + j):
                sch.add("pe", lambda e, a=a: e.matmul(APS[a][:, acol:acol + NQ], lhsT=ZER[:, :],
                                                      rhs=QB[qslot][:, qc0:qc0 + NQ], start=True, stop=False,
                                                      skip_group_check=True),
                        reads=[bf("ZER"), bf("QB%d" % qslot)], writes=[bf("APS%d" % a)])
        for kt in reversed(range(nkt)):
            for j in range(2):
                t = ktiles[kt]
                nk = t["nk"]
                sbk = nxt("SPS", 2)
                Sb = SPS[sbk]
                sbb = bf("SPS%d" % sbk)
                nb = len(t["bias"])
                ktt, kc0 = t["kT"]
                sch.add("pe", lambda e, Sb=Sb, nk=nk, j=j, ktt=ktt, kc0=kc0, nb=nb: e.matmul(
                    Sb[0:nk, 0:NQ], lhsT=ktt[j * 64:(j + 1) * 64, kc0:kc0 + nk],
                    rhs=QB[qslot][j * 64:(j + 1) * 64, qc0:qc0 + NQ], start=True, stop=(nb == 0),
                    skip_group_check=True), reads=[t["kTb"], bf("QB%d" % qslot)], writes=[sbb])
                for bi, (co, ncol, mi, kr) in enumerate(t["bias"]):
                    sch.add("pe", lambda e, Sb=Sb, nk=nk, bi=bi, nb=nb: e.matmul(
                        Sb[0:nk, 0:NQ], lhsT=ident[0:nk, 0:nk], rhs=maskT[0:nk, 0:NQ],
                        start=False, stop=(bi == nb - 1), skip_group_check=True),
                        reads=[bCST], writes=[sbb])
                es_ = nxt("EW", NE)
                E = EW[es_]
                eb = bf("EW%d" % es_)
                spb = SPB[es_]
                spbb = bf("SPB%d" % es_)
                rb = bf("APS%d" % j)
                sch.add("act", lambda e, E=E, Sb=Sb, nk=nk: e.activation(
                    out=E[0:nk, 0:NQ], in_=Sb[0:nk, 0:NQ], func=AF.Exp, scale=0.125), reads=[sbb], writes=[eb])
                sch.add("act", lambda e, E=E, nk=nk: e.activation(
                    out=E[0:nk, 0:NQ], in_=E[0:nk, 0:NQ], func=AF.Ln, bias=1.0), reads=[eb], writes=[eb])
                sch.add("pool", lambda e, E=E, spb=spb, nk=nk: e.tensor_copy(out=spb[0:nk, 0:NQ], in_=E[0:nk, 0:NQ]),
                        reads=[eb], writes=[spbb])
                sch.add("dve", lambda e, E=E, Sb=Sb, nk=nk: e.scalar_tensor_tensor(
                    out=E[0:nk, 0:NQ], in0=Sb[0:nk, 0:NQ], scalar=0.125, in1=E[0:nk, 0:NQ],
                    op0=ALU.mult, op1=ALU.subtract), reads=[sbb, eb, spbb], writes=[eb])
                sch.add("pe", lambda e, spb=spb, nk=nk, j=j: e.matmul(
                    APS[j][:, acol:acol + NQ], lhsT=triu[0:nk, :], rhs=spb[0:nk, 0:NQ], start=False, stop=False,
                    skip_group_check=True), reads=[bCST, spbb], writes=[rb])
                sch.add("dve", lambda e, E=E, nk=nk, j=j: e.tensor_tensor(
                    out=E[0:nk, 0:NQ], in0=E[0:nk, 0:NQ], in1=APS[j][0:nk, acol:acol + NQ], op=ALU.subtract),
                    reads=[eb, rb], writes=[eb])
                pslot = nxt("PB", NPB)
                Ab = PB[pslot]
                pbb = bf("PB%d" % pslot)
                sch.add("act", lambda e, Ab=Ab, E=E, nk=nk: e.activation(
                    out=Ab[0:nk, 0:NQ], in_=E[0:nk, 0:NQ], func=AF.Exp), reads=[eb], writes=[pbb])
                sch.add("pe", lambda e, spb=spb, nk=nk, j=j: e.matmul(
                    APS[j][:, acol:acol + NQ], lhsT=tril[0:nk, :], rhs=spb[0:nk, 0:NQ], start=False, stop=False,
                    skip_group_check=True), reads=[bCST, spbb], writes=[rb])
                sch.add("pe", lambda e, Ab=Ab, nk=nk, j=j, t=t: e.matmul(
                    APS[2 + j][:, acol:acol + NQ], lhsT=t["v"], rhs=Ab[0:nk, 0:NQ], start=False, stop=False,
                    skip_group_check=True), reads=[t["vb"], pbb], writes=[bf("APS%d" % (2 + j))])

    setup()
    for s in range(NSEQ):
        prompt_seq(s)
    sample_pass()
    fin = sch.add("sp", lambda e: e.nop(), reads=[], writes=[])
    for b in sch.dma_bufs:
        fin.waits.append(("dma", b, b.cnt))
    sch.emit(nc, es)
    es.close()
    return nc


_NC_CACHE = {}


def _get_nc(cfg):
    key = (cfg["NSEQ"], cfg["S"], cfg["PAST"])
    if key not in _NC_CACHE:
        _NC_CACHE[key] = build(cfg)
    return _NC_CACHE[key]


def run(cfg, inputs):
    NSEQ, S, PAST, NCORES = cfg["NSEQ"], cfg["S"], cfg["PAST"], cfg["NCORES"]
    nc = _get_nc(cfg)
    f = lambda a: np.ascontiguousarray(np.asarray(a, dtype=np.float32))
    maps_np, consts_np, _ = static_tables()
    gain = np.stack([np.broadcast_to(f(inputs[k])[None, :], (P, D)) for k in
                     ("norm_0", "norm_1", "norm_2", "norm_3", "final_norm")])
    tab = np.broadcast_to(f(inputs["rel_bias_table"]).reshape(1, 512), (P, 512))
    lam = np.stack([np.broadcast_to(f(inputs["da_lambda_%d" % l]).reshape(1, 256), (P, 256)) for l in (0, 3)])
    subln = np.stack([f(inputs["da_subln_%d" % l]).reshape(P, 1) for l in (0, 3)])
    sk = f(inputs["sw_sinks_2"])
    sink = np.zeros((P, 8), np.float32)
    for c in range(8):
        sink[:64, c] = sk[2 * c]
        sink[64:, c] = sk[2 * c + 1]
    shared = dict(gain=f(gain), tab=f(tab), lam=f(lam), subln=f(subln), sink=sink, maps=maps_np, consts=consts_np)
    for l in range(4):
        shared["win%d" % l] = f(inputs["w_in_%d" % l])
        shared["wout%d" % l] = f(inputs["w_out_%d" % l])
    in_maps = []
    for core in range(NCORES):
        sl = slice(core * NSEQ, (core + 1) * NSEQ)
        m = dict(shared)
        m["xp"] = f(inputs["x_prompt"][sl])
        m["xs"] = f(inputs["x_sample"][sl]).reshape(NSEQ * T_DEC, D)
        for l in range(4):
            ck = f(inputs["cache_k_%d" % l][sl])
            cv = f(inputs["cache_v_%d" % l][sl])
            m["ck%d" % l] = ck.reshape(NSEQ, ck.shape[1], -1)
            m["cv%d" % l] = cv.reshape(NSEQ, cv.shape[1], -1)
        in_maps.append(m)
    res = run_bass_kernel_spmd(nc, in_maps, core_ids=list(range(NCORES)))
    R = res.results
    cat = lambda name: np.concatenate([np.asarray(r[name]) for r in R], axis=0)
    Bt = NSEQ * NCORES
    outs = [cat("yp").reshape(Bt, S, D), cat("ys").reshape(Bt, T_DEC, D)]
    for l in range(4):
        kind = KINDS[l]
        if kind == 0:
            outs += [cat("kp%d" % l).reshape(Bt, S, 8, 2, 64), cat("vp%d" % l).reshape(Bt, S, 8, 128),
                     cat("ks%d" % l).reshape(Bt, T_DEC, 8, 2, 64), cat("vs%d" % l).reshape(Bt, T_DEC, 8, 128)]
        elif kind == 1:
            outs += [cat("kp%d" % l).reshape(Bt, S, 16, 64), cat("vp%d" % l).reshape(Bt, S, 16, 64),
                     cat("ks%d" % l).reshape(Bt, T_DEC, 16, 64), cat("vs%d" % l).reshape(Bt, T_DEC, 16, 64)]
        else:
            outs += [cat("kp%d" % l).reshape(Bt, P, 4, 64), cat("vp%d" % l).reshape(Bt, P, 4, 64),
                     cat("ks%d" % l).reshape(Bt, P, 4, 64), cat("vs%d" % l).reshape(Bt, P, 4, 64)]
    return tuple(np.ascontiguousarray(o, dtype=np.float32) for o in outs)


def kernel(**inputs):
    return run(FULL_CFG, inputs)
```

```python
import math
from contextlib import ExitStack

import numpy as np
import concourse.bass as bass
import concourse.mybir as mybir
from concourse.bass_utils import run_bass_kernel_spmd

F32 = mybir.dt.float32
BF16 = mybir.dt.bfloat16
ALU = mybir.AluOpType
AF = mybir.ActivationFunctionType

D = 1024
P = 128
NCH = 8
EPS = 1e-6
NEG = -240000.0
T_DEC = 16
KINDS = (0, 1, 2, 0)
IN_W = (4096, 4096, 2560)

FULL_CFG = dict(NSEQ=4, S=2048, PAST=1024, NCORES=8)


class Buf:
    __slots__ = ("name", "last_w", "readers", "sem", "cnt", "psum")

    def __init__(self, name):
        self.name = name
        self.psum = False
        self.last_w = None
        self.readers = []
        self.sem = None
        self.cnt = 0


class Op:
    __slots__ = ("idx", "eng", "fn", "waits", "signal", "sigval", "dma", "dma_val")


ENGS = ("pe", "act", "dve", "pool", "sp")


class Sched:
    def __init__(self):
        self.ops = []
        self.by_eng = {e: [] for e in ENGS}
        self.waited = {e: {f: -1 for f in ENGS} for e in ENGS}
        self.dma_waited = {e: {} for e in ENGS}
        self.dma_bufs = []

    def add(self, eng, fn, reads=(), writes=(), dma=None):
        op = Op()
        op.idx = len(self.ops)
        op.eng = eng
        op.fn = fn
        op.waits = []
        op.signal = False
        op.sigval = 0
        op.dma = dma
        op.dma_val = 0
        deps = set()
        for b in reads:
            if b.last_w is not None:
                deps.add(b.last_w)
            if b.psum:
                deps.update(r for r in b.readers if self.ops[r].eng != eng)
        for b in writes:
            if b.last_w is not None:
                deps.add(b.last_w)
            deps.update(b.readers)
        for d in sorted(deps, reverse=True):
            dop = self.ops[d]
            if dop.dma is not None:
                if self.dma_waited[eng].get(dop.dma, 0) >= dop.dma_val:
                    continue
                self.dma_waited[eng][dop.dma] = dop.dma_val
                op.waits.append(("dma", dop.dma, dop.dma_val))
            else:
                if dop.eng == "pe" and eng == "pe" and dma is None:
                    continue
                if self.waited[eng][dop.eng] >= d:
                    continue
                self.waited[eng][dop.eng] = d
                dop.signal = True
                op.waits.append(("eng", dop))
        for b in reads:
            b.readers.append(op.idx)
        for b in writes:
            b.last_w = op.idx
            b.readers = []
        if dma is not None:
            if dma.cnt == 0:
                self.dma_bufs.append(dma)
            dma.cnt += 16
            op.dma_val = dma.cnt
        self.ops.append(op)
        self.by_eng[eng].append(op)
        return op

    def emit(self, nc, es):
        for e in ENGS:
            cnt = 0
            for op in self.by_eng[e]:
                if op.signal:
                    cnt += 1
                    op.sigval = cnt
        esem = {e: es.enter_context(nc.semaphore("sem_" + e)) for e in ENGS}
        for i, b in enumerate(self.dma_bufs):
            b.sem = es.enter_context(nc.semaphore("dsem%d" % i))

        def run(ename, eng):
            for op in self.by_eng[ename]:
                for w in op.waits:
                    if w[0] == "dma":
                        eng.wait_ge(w[1].sem, w[2])
                    else:
                        eng.wait_ge(esem[w[1].eng], w[1].sigval)
                ins = op.fn(eng)
                if op.dma is not None:
                    ins.then_inc(op.dma.sem, 16)
                elif op.signal:
                    ins.then_inc(esem[ename], 1)

        with nc.Block() as block:
            @block.tensor
            def _(e):
                run("pe", e)

            @block.scalar
            def _(e):
                run("act", e)

            @block.vector
            def _(e):
                run("dve", e)

            @block.gpsimd
            def _(e):
                run("pool", e)

            @block.sync
            def _(e):
                run("sp", e)


def _bucket_np(rel):
    import jax
    import jax.numpy as jnp
    cpu = jax.devices("cpu")[0]
    with jax.default_device(cpu):
        rel = jnp.asarray(rel, dtype=jnp.int32)
        nb = 16
        max_exact = 8
        n = jnp.abs(rel)
        nf = jnp.maximum(n, 1).astype(jnp.float32)
        large = max_exact + (jnp.log(nf / max_exact) / math.log(128 / max_exact)
                             * (nb - max_exact)).astype(jnp.int32)
        large = jnp.minimum(large, nb - 1)
        out = jnp.where(rel > 0, nb, 0) + jnp.where(n < max_exact, n, large)
        return np.asarray(out)


def static_tables():
    k = np.arange(P)[:, None]
    q = np.arange(P)[None, :]
    bd = _bucket_np(k - q).astype(np.float32)
    bs = _bucket_np(k - P - q).astype(np.float32)
    maps = np.zeros((4, P, P), np.float32)
    m = bd.copy(); m[(k // 64) > (q // 64)] = -1.0
    maps[0] = m
    maps[1] = bs
    m = bd.copy(); m[(k >= 64) & (q < 64)] = -1.0
    maps[2] = m
    m = bs.copy(); m[(k < 64) & (q >= 64)] = -1.0
    maps[3] = m
    consts = np.zeros((5, P, P), np.float32)
    consts[0] = np.eye(P)
    consts[1] = (k > q)
    consts[2] = (k <= q)
    consts[3] = np.where(k < q, 0.0, NEG)
    consts[4] = 1.0
    present = [sorted(set(int(v) for v in np.unique(maps[i]) if v >= 0 and v != 15)) for i in range(4)]
    return maps, consts, present


def build(cfg):
    NSEQ, S, PAST = cfg["NSEQ"], cfg["S"], cfg["PAST"]
    NT = S // P
    NQB = S // 512
    NPT = PAST // P
    TS = NSEQ * T_DEC
    assert S % 512 == 0 and PAST % P == 0 and TS <= 64
    maps_np, consts_np, present = static_tables()

    nc = bass.Bass("TRN2", target_bir_lowering=False)

    def din(name, shape):
        return nc.dram_tensor(name, list(shape), F32, kind="ExternalInput").ap()

    def dout(name, shape):
        return nc.dram_tensor(name, list(shape), F32, kind="ExternalOutput").ap()

    xp_d = din("xp", (NSEQ, S, D))
    xs_d = din("xs", (TS, D))
    ck_d, cv_d = [], []
    for l in range(4):
        if KINDS[l] == 2:
            ck_d.append(din("ck%d" % l, (NSEQ, P, 256)))
            cv_d.append(din("cv%d" % l, (NSEQ, P, 256)))
        else:
            ck_d.append(din("ck%d" % l, (NSEQ, PAST, D)))
            cv_d.append(din("cv%d" % l, (NSEQ, PAST, D)))
    win_d = [din("win%d" % l, (D, IN_W[KINDS[l]])) for l in range(4)]
    wout_d = [din("wout%d" % l, (D, D)) for l in range(4)]
    gain_d = din("gain", (5, P, D))
    tab_d = din("tab", (P, 512))
    lam_d = din("lam", (2, P, 256))
    subln_d = din("subln", (2, P, 1))
    sink_d = din("sink", (P, 8))
    maps_d = din("maps", (4, P, P))
    consts_d = din("consts", (5, P, P))

    yp_d = dout("yp", (NSEQ, S, D))
    ys_d = dout("ys", (TS, D))
    kp_d, vp_d, ks_d, vs_d = [], [], [], []
    for l in range(4):
        if KINDS[l] == 2:
            kp_d.append(dout("kp%d" % l, (NSEQ, P, 256)))
            vp_d.append(dout("vp%d" % l, (NSEQ, P, 256)))
            ks_d.append(dout("ks%d" % l, (NSEQ, P, 256)))
            vs_d.append(dout("vs%d" % l, (NSEQ, P, 256)))
        else:
            kp_d.append(dout("kp%d" % l, (NSEQ, S, D)))
            vp_d.append(dout("vp%d" % l, (NSEQ, S, D)))
            ks_d.append(dout("ks%d" % l, (TS, D)))
            vs_d.append(dout("vs%d" % l, (TS, D)))

    es = ExitStack()
    sch = Sched()

    def sb(name, shape, dt):
        return es.enter_context(nc.sbuf_tensor(name, list(shape), dt))

    X = sb("X", (P, NT, D), F32)
    xnT = sb("xnT", (P, NCH, S), BF16)
    Gn = sb("Gn", (P, D), F32)
    WT = [sb("WT%d" % i, (P, NCH, 512), BF16) for i in range(2)]
    WO = [sb("WO%d" % i, (P, D), BF16) for i in range(2)]
    KT = [sb("KT%d" % i, (P, S), BF16) for i in range(2)]
    VB = [sb("VB%d" % i, (P, NT, P), BF16) for i in range(2)]
    KTN = [sb("KTN%d" % i, (P, T_DEC), BF16) for i in range(2)]
    VN = [sb("VN%d" % i, (T_DEC, P), BF16) for i in range(2)]
    KC = [sb("KC%d" % i, (P, max(NPT, 1), P), BF16) for i in range(2)]
    NQS = 3
    QB = [sb("QB%d" % i, (P, 512), BF16) for i in range(NQS)]
    GB = [sb("GB%d" % i, (P, 512), BF16) for i in range(NQS)]
    OT = [sb("OT%d" % i, (P, 512), BF16) for i in range(2)]
    NKVS = 3
    KVS = [sb("KVS%d" % i, (P, 256), F32) for i in range(NKVS)]
    NKBT = 4
    KBT = [sb("KBT%d" % i, (P, P), BF16) for i in range(NKBT)]
    NE = 4
    EW = [sb("EW%d" % i, (P, 512), F32) for i in range(NE)]
    SPB = [sb("SPB%d" % i, (P, 512), BF16) for i in range(NE)]
    NPB = 3
    PB = [sb("PB%d" % i, (P, 512), BF16) for i in range(NPB)]
    TP = [sb("TP%d" % i, (P, 512), F32) for i in range(2)]
    SQ = sb("SQ", (P, 512), BF16)
    XN = [sb("XN%d" % i, (P, D), BF16) for i in range(2)]
    SS = sb("SS", (P, 2 * NT + 4), F32)
    BIAS = sb("BIAS", (P, 4, 16, P), BF16)
    CST = sb("CST", (P, 5, P), BF16)
    ZER = sb("ZER", (P, P), BF16)
    LAMC = sb("LAMC", (P, 2, 8), F32)
    SCC = sb("SCC", (P, 2), F32)
    ESK = sb("ESK", (P, 8), F32)
    LAMT = sb("LAMT", (P, 8), F32)
    SCR = sb("SCR", (P, 3200), F32)
    XO = [SCR[:, i * D:(i + 1) * D] for i in range(2)]
    MAPS = SCR[:, 0:512].rearrange("p (c q) -> p c q", c=4)
    CSTF = SCR[:, 512:1152].rearrange("p (c q) -> p c q", c=5)
    OH = [SCR[:, 1152 + i * 128:1152 + (i + 1) * 128] for i in range(2)]
    ACC = SCR[:, 1408:2432].rearrange("p (h q) -> p h q", h=8)
    TABP = SCR[:, 2432:2944]
    LAMW = SCR[:, 2944:3200]

    def ps(name, dt=F32, cols=512):
        return es.enter_context(nc.psum_tensor(name, [P, cols], dt))

    SPS = [ps("SPS0"), ps("SPS1")]
    APS = [ps("APS%d" % i) for i in range(4)]
    M0 = ps("M0")
    T0 = ps("T0", BF16, 1024)

    B = {}

    def bf(name):
        if name not in B:
            B[name] = Buf(name)
        return B[name]

    for nm in ("SPS0", "SPS1", "APS0", "APS1", "APS2", "APS3", "M0", "T0"):
        bf(nm).psum = True
    ident = CST[:, 0, :]
    triu = CST[:, 1, :]
    tril = CST[:, 2, :]
    maskT = CST[:, 3, :]
    ones = CST[:, 4, :]
    bCST = bf("CST")
    bBIAS = bf("BIAS")

    rot = {}

    def nxt(name, n):
        v = rot.get(name, 0)
        rot[name] = v + 1
        return v % n

    def setup():
        sch.add("sp", lambda e: e.dma_start(out=CSTF, in_=consts_d.rearrange("c p q -> p c q")),
                writes=[bf("CSTF")], dma=bf("CSTF"))
        sch.add("sp", lambda e: e.dma_start(out=MAPS, in_=maps_d.rearrange("c p q -> p c q")),
                writes=[bf("MAPS")], dma=bf("MAPS"))
        sch.add("sp", lambda e: e.dma_start(out=TABP, in_=tab_d), writes=[bf("TABP")], dma=bf("TABP"))
        sch.add("sp", lambda e: e.dma_start(out=ESK[:, :], in_=sink_d), writes=[bf("ESK")], dma=bf("ESK"))
        sch.add("dve", lambda e: e.tensor_copy(out=CST[:, :, :], in_=CSTF),
                reads=[bf("CSTF")], writes=[bCST])
        sch.add("dve", lambda e: e.memset(ZER[:, :], 0.0), writes=[bf("ZER")])
        for b_ in range(32):
            if b_ == 15:
                continue
            sch.add("dve", lambda e, b_=b_: e.tensor_tensor(
                out=TABP[:, b_ * 16:(b_ + 1) * 16], in0=TABP[:, b_ * 16:(b_ + 1) * 16],
                in1=TABP[:, 240:256], op=ALU.subtract), reads=[bf("TABP")], writes=[bf("TABP")])
        for c in range(8):
            for j in range(2):
                r0 = j * 64
                sch.add("dve", lambda e, c=c, j=j, r0=r0: e.tensor_tensor(
                    out=ESK[r0:r0 + 64, c:c + 1], in0=ESK[r0:r0 + 64, c:c + 1],
                    in1=TABP[r0:r0 + 64, 240 + 2 * c + j:240 + 2 * c + j + 1], op=ALU.subtract),
                    reads=[bf("TABP"), bf("ESK")], writes=[bf("ESK")])
        sch.add("act", lambda e: e.activation(out=ESK[:, :], in_=ESK[:, :], func=AF.Exp),
                reads=[bf("ESK")], writes=[bf("ESK")])
        sch.add("dve", lambda e: e.tensor_scalar(out=TABP, in0=TABP, scalar1=8.0, scalar2=None,
                                                 op0=ALU.mult), reads=[bf("TABP")], writes=[bf("TABP")])
        for mi in range(4):
            eng = "dve"
            accb = bf("ACC")
            sch.add(eng, lambda e, mi=mi: e.tensor_scalar(
                out=OH[0], in0=MAPS[:, mi, :], scalar1=-1.0, scalar2=NEG, op0=ALU.is_equal, op1=ALU.mult),
                reads=[bf("MAPS")], writes=[bf("OH0")])
            for h0 in (0, 8):
                for h in range(8):
                    sch.add(eng, lambda e, h=h: e.tensor_copy(out=ACC[:, h, :], in_=OH[0]),
                            reads=[bf("OH0")], writes=[accb])
                for b_ in present[mi]:
                    sch.add(eng, lambda e, mi=mi, b_=b_: e.tensor_single_scalar(
                        out=OH[1], in_=MAPS[:, mi, :], scalar=float(b_), op=ALU.is_equal),
                        reads=[bf("MAPS")], writes=[bf("OH1")])
                    for h in range(8):
                        sch.add(eng, lambda e, h=h, b_=b_, h0=h0: e.scalar_tensor_tensor(
                            out=ACC[:, h, :], in0=OH[1],
                            scalar=TABP[:, b_ * 16 + h0 + h:b_ * 16 + h0 + h + 1],
                            in1=ACC[:, h, :], op0=ALU.mult, op1=ALU.add),
                            reads=[bf("OH1"), bf("TABP"), accb], writes=[accb])
                sch.add(eng, lambda e, mi=mi, h0=h0: e.tensor_copy(out=BIAS[:, mi, h0:h0 + 8, :], in_=ACC),
                        reads=[accb], writes=[bBIAS])
        for li, l in enumerate((0, 3)):
            lam_init = 0.8 - 0.6 * math.exp(-0.3 * l)
            sch.add("sp", lambda e, li=li: e.dma_start(out=LAMW, in_=lam_d[li]),
                    writes=[bf("LAMW")], dma=bf("LAMW"))
            sch.add("sp", lambda e, li=li: e.dma_start(out=SCC[:, li:li + 1], in_=subln_d[li]),
                    writes=[bf("SCC")], dma=bf("SCC%d" % li))
            for t in range(2):
                sch.add("dve", lambda e, t=t: e.tensor_tensor(
                    out=LAMW[:, t * 128:t * 128 + 64], in0=LAMW[:, t * 128:t * 128 + 64],
                    in1=LAMW[:, t * 128 + 64:t * 128 + 128], op=ALU.mult),
                    reads=[bf("LAMW")], writes=[bf("LAMW")])
                sch.add("dve", lambda e, t=t: e.tensor_reduce(
                    out=LAMT[:, t:t + 1], in_=LAMW[:, t * 128:t * 128 + 64], axis=mybir.AxisListType.X,
                    op=ALU.add), reads=[bf("LAMW")], writes=[bf("LAMT")])
            sch.add("act", lambda e: e.activation(out=LAMT[:, 0:2], in_=LAMT[:, 0:2], func=AF.Exp),
                    reads=[bf("LAMT")], writes=[bf("LAMT")])
            sch.add("dve", lambda e, li=li, lam_init=lam_init: e.scalar_tensor_tensor(
                out=LAMC[:, li, 0:1], in0=LAMT[:, 1:2], scalar=-lam_init, in1=LAMT[:, 0:1],
                op0=ALU.add, op1=ALU.subtract), reads=[bf("LAMT")], writes=[bf("LAMC")])
            sch.add("dve", lambda e, li=li, lam_init=lam_init: e.tensor_scalar(
                out=SCC[:, li:li + 1], in0=SCC[:, li:li + 1], scalar1=(1.0 - lam_init) * math.sqrt(128.0),
                scalar2=None, op0=ALU.mult), reads=[bf("SCC")], writes=[bf("SCC")])

    def load_weights(l, c, wset):
        kind = KINDS[l]
        W = WT[wset]
        wb = bf("WT%d" % wset)
        if kind in (0, 1):
            cols = [c * 128, 3072 + c * 128, 1024 + c * 128, 2048 + c * 128]
            for si, c0 in enumerate(cols):
                sch.add("pool", lambda e, si=si, c0=c0, W=W: e.dma_start(
                    out=W[:, :, si * 128:(si + 1) * 128],
                    in_=win_d[l][:, c0:c0 + 128].rearrange("(c p) f -> p c f", p=P)),
                    writes=[wb], dma=bf("WTd%d_%d" % (wset, si)))
        else:
            g = c // 2
            cols = [(c * 128, 128, 0), (1536 + c * 128, 128, 128), (1024 + g * 64, 64, 256),
                    (1280 + g * 64, 64, 320)]
            for si, (c0, w, o0) in enumerate(cols):
                sch.add("pool", lambda e, c0=c0, w=w, o0=o0, W=W: e.dma_start(
                    out=W[:, :, o0:o0 + w],
                    in_=win_d[l][:, c0:c0 + w].rearrange("(c p) f -> p c f", p=P)),
                    writes=[wb], dma=bf("WTd%d_%d" % (wset, si)))
        sch.add("pool", lambda e: e.dma_start(out=WO[wset][:, :], in_=wout_d[l][c * 128:(c + 1) * 128, :]),
                writes=[bf("WO%d" % wset)], dma=bf("WOd%d" % wset))

    def norm_tile(xrows_ap, nrows, col, gidx_loaded, xnT_cols):
        slot = nxt("XN", 2)
        xn = XN[slot]
        xb = bf("XN%d" % slot)
        sch.add("act", lambda e: e.activation(out=xn[0:nrows, :], in_=xrows_ap, func=AF.Square,
                                              accum_out=SS[0:nrows, col:col + 1]),
                reads=[bf("X")], writes=[xb, bf("SS")])
        sch.add("act", lambda e: e.activation(out=SS[0:nrows, col:col + 1], in_=SS[0:nrows, col:col + 1],
                                              func=AF.Ln, scale=1.0 / D, bias=EPS),
                reads=[bf("SS")], writes=[bf("SS")])
        sch.add("act", lambda e: e.activation(out=SS[0:nrows, col:col + 1], in_=SS[0:nrows, col:col + 1],
                                              func=AF.Exp, scale=-0.5),
                reads=[bf("SS")], writes=[bf("SS")])
        sch.add("dve", lambda e: e.scalar_tensor_tensor(out=xn[0:nrows, :], in0=xrows_ap,
                                                        scalar=SS[0:nrows, col:col + 1], in1=Gn[0:nrows, :],
                                                        op0=ALU.mult, op1=ALU.mult),
                reads=[bf("X"), bf("SS"), bf("Gn")], writes=[xb])
        for ch in range(NCH):
            sch.add("pe", lambda e, ch=ch: e.transpose(out=T0[:, ch * 128:ch * 128 + nrows],
                                                       in_=xn[0:nrows, ch * 128:(ch + 1) * 128],
                                                       identity=ident[0:nrows, 0:nrows]),
                    reads=[xb, bCST], writes=[bf("T0")])
        c0, c1 = xnT_cols
        sch.add("act", lambda e: e.copy(
            out=xnT[:, :, c0:c1],
            in_=T0[:, :].rearrange("p (c t) -> p c t", c=NCH)[:, :, 0:nrows]),
            reads=[bf("T0")], writes=[bf("xnT")])

    def zero_acc():
        sch.add("dve", lambda e: e.memset(SS[:, :], 0.0), writes=[bf("SS")])

    def out_proj_units(ot_slot, wset, ntok_tiles, xdst):
        units = []
        for i, (rows, xap_fn) in enumerate(xdst):
            for hf in range(2):
                def u(i=i, hf=hf, rows=rows, xap_fn=xap_fn):
                    Mx, mb = M0, bf("M0")
                    sch.add("pe", lambda e: e.matmul(Mx[0:rows, :], lhsT=OT[ot_slot][:, i * 128:i * 128 + rows],
                                                     rhs=WO[wset][:, hf * 512:(hf + 1) * 512],
                                                     start=True, stop=True),
                            reads=[bf("OT%d" % ot_slot), bf("WO%d" % wset)], writes=[mb])
                    xa = xap_fn(hf)
                    sch.add("dve", lambda e: e.tensor_tensor(out=xa, in0=Mx[0:rows, :], in1=xa, op=ALU.add),
                            reads=[mb, bf("X")], writes=[bf("X")])
                units.append(u)
        return units

    def proj_q_unit(wset, tok0, ntok, slot):
        def u():
            for ch in range(NCH):
                sch.add("pe", lambda e, ch=ch: e.matmul(M0[:, 0:ntok], lhsT=WT[wset][:, ch, 0:128],
                                                        rhs=xnT[:, ch, tok0:tok0 + ntok],
                                                        start=(ch == 0), stop=(ch == NCH - 1)),
                        reads=[bf("WT%d" % wset), bf("xnT")], writes=[bf("M0")])
            sch.add("dve", lambda e: e.tensor_copy(out=QB[slot][:, 0:ntok], in_=M0[:, 0:ntok]),
                    reads=[bf("M0")], writes=[bf("QB%d" % slot)])
        return u

    def proj_g_unit(wset, tok0, ntok, slot):
        def u():
            for ch in range(NCH):
                sch.add("pe", lambda e, ch=ch: e.matmul(M0[:, 0:ntok], lhsT=WT[wset][:, ch, 128:256],
                                                        rhs=xnT[:, ch, tok0:tok0 + ntok],
                                                        start=(ch == 0), stop=(ch == NCH - 1)),
                        reads=[bf("WT%d" % wset), bf("xnT")], writes=[bf("M0")])
            sch.add("act", lambda e: e.activation(out=GB[slot][:, 0:ntok], in_=M0[:, 0:ntok], func=AF.Silu),
                    reads=[bf("M0")], writes=[bf("GB%d" % slot)])
        return u

    def proj_kv_unit(l, c, wset, kvset, tok0, nrows, kt_cols, v_dst, out_dmas, swa):
        st = {}

        def u1():
            width = 128 if swa else 256
            for ch in range(NCH):
                sch.add("pe", lambda e, ch=ch: e.matmul(M0[0:nrows, 0:width],
                                                        lhsT=xnT[:, ch, tok0:tok0 + nrows],
                                                        rhs=WT[wset][:, ch, 256:256 + width],
                                                        start=(ch == 0), stop=(ch == NCH - 1)),
                        reads=[bf("WT%d" % wset), bf("xnT")], writes=[bf("M0")])
            kbs = nxt("KBT", NKBT)
            kbt = KBT[kbs]
            kbb = bf("KBT%d" % kbs)
            st["kbt"], st["kbb"] = kbt, kbb
            vb_ = v_dst[1]
            if swa:
                for hh in range(2):
                    sch.add("act", lambda e, hh=hh: e.copy(out=kbt[0:nrows, hh * 64:(hh + 1) * 64],
                                                           in_=M0[0:nrows, 0:64]),
                            reads=[bf("M0")], writes=[kbb])
                    sch.add("act", lambda e, hh=hh: e.copy(out=v_dst[0][:, hh * 64:(hh + 1) * 64],
                                                           in_=M0[0:nrows, 64:128]),
                            reads=[bf("M0")], writes=[vb_])
            else:
                sch.add("act", lambda e: e.copy(out=kbt[0:nrows, :], in_=M0[0:nrows, 0:128]),
                        reads=[bf("M0")], writes=[kbb])
                sch.add("act", lambda e: e.copy(out=v_dst[0], in_=M0[0:nrows, 128:256]),
                        reads=[bf("M0")], writes=[vb_])
            if out_dmas:
                ks = nxt("KVS", NKVS)
                kvs = KVS[ks]
                kb = bf("KVS%d" % ks)
                sch.add("dve", lambda e: e.tensor_copy(out=kvs[0:nrows, 0:width], in_=M0[0:nrows, 0:width]),
                        reads=[bf("M0")], writes=[kb])
                for (dst_ap, c0, w) in out_dmas:
                    sch.add("sp", lambda e, dst_ap=dst_ap, c0=c0, w=w: e.dma_start(out=dst_ap,
                                                                                  in_=kvs[0:nrows, c0:c0 + w]),
                            reads=[kb], dma=kb)

        def u2():
            kbt, kbb = st["kbt"], st["kbb"]
            sch.add("pe", lambda e: e.transpose(out=T0[:, 0:nrows], in_=kbt[0:nrows, :],
                                                identity=ident[0:nrows, 0:nrows]),
                    reads=[kbb, bCST], writes=[bf("T0")])
            ktt, ktc0, ktb = kt_cols
            sch.add("act", lambda e: e.copy(out=ktt[:, ktc0:ktc0 + nrows], in_=T0[:, 0:nrows]),
                    reads=[bf("T0")], writes=[ktb])
        return u1, u2

    def attn_da(c, qslot, NQ, acol, ktiles, bg):
        hmb = 2 * c
        items = [(kt, m) for kt in range(len(ktiles)) for m in range(2)]
        nkt = len(ktiles)

        def qk(it):
            kt, m = it
            t = ktiles[kt]
            sbk = nxt("SPS", 2)
            Sb = SPS[sbk]
            sbb = bf("SPS%d" % sbk)
            c0, nk = t["c0"], t["nk"]
            nb = len(t["bias"])
            ktt, kc0 = t["kT"]
            sch.add("pe", lambda e: e.matmul(Sb[0:nk, c0:NQ], lhsT=ktt[m * 64:(m + 1) * 64, kc0:kc0 + nk],
                                             rhs=QB[qslot][m * 64:(m + 1) * 64, c0:NQ],
                                             start=True, stop=(nb == 0), skip_group_check=True),
                    reads=[t["kTb"], bf("QB%d" % qslot)], writes=[sbb])
            for bi, (co, ncol, mi, kr) in enumerate(t["bias"]):
                sch.add("pe", lambda e, co=co, ncol=ncol, mi=mi, bi=bi: e.matmul(
                    Sb[0:nk, co:co + ncol], lhsT=ident[0:nk, 0:nk], rhs=BIAS[0:nk, mi, hmb + m, 0:ncol],
                    start=False, stop=(bi == nb - 1), skip_group_check=True),
                    reads=[bCST, bBIAS], writes=[sbb])
            return Sb, sbb

        def rest(it, Sb, sbb):
            kt, m = it
            t = ktiles[kt]
            c0, nk = t["c0"], t["nk"]
            pslot = nxt("PB", NPB)
            Pt = PB[pslot]
            pbb = bf("PB%d" % pslot)
            sch.add("act", lambda e: e.activation(out=Pt[0:nk, c0:NQ], in_=Sb[0:nk, c0:NQ], func=AF.Exp,
                                                  scale=0.125), reads=[sbb], writes=[pbb])
            sch.add("pe", lambda e: e.matmul(APS[m][:, acol + c0:acol + NQ], lhsT=t["v"], rhs=Pt[0:nk, c0:NQ],
                                             start=(kt == 0), stop=(kt == nkt - 1), skip_group_check=True),
                    reads=[t["vb"], pbb], writes=[bf("APS%d" % m)])
            sch.add("pe", lambda e: e.matmul(APS[2 + m][:, acol + c0:acol + NQ], lhsT=ones[0:nk, :],
                                             rhs=Pt[0:nk, c0:NQ],
                                             start=(kt == 0), stop=(kt == nkt - 1), skip_group_check=True),
                    reads=[bCST, pbb], writes=[bf("APS%d" % (2 + m))])

        cur = qk(items[0])
        for i, it in enumerate(items):
            nx = qk(items[i + 1]) if i + 1 < len(items) else None
            rest(it, *cur)
            cur = nx
            if bg:
                bg.pop(0)()

    def post_da_a(li, qslot, oslot, NQ, acol):
        T_a, T_b = TP[0], TP[1]
        ba, bb_ = bf("TP0"), bf("TP1")
        sl = slice(acol, acol + NQ)
        sch.add("dve", lambda e: e.reciprocal(out=T_a[:, 0:NQ], in_=APS[2][:, sl]),
                reads=[bf("APS2")], writes=[ba])
        sch.add("dve", lambda e: e.tensor_tensor(out=T_a[:, 0:NQ], in0=APS[0][:, sl], in1=T_a[:, 0:NQ],
                                                 op=ALU.mult), reads=[bf("APS0"), ba], writes=[ba])
        sch.add("dve", lambda e: e.reciprocal(out=T_b[:, 0:NQ], in_=APS[3][:, sl]),
                reads=[bf("APS3")], writes=[bb_])
        sch.add("dve", lambda e: e.tensor_tensor(out=T_b[:, 0:NQ], in0=APS[1][:, sl], in1=T_b[:, 0:NQ],
                                                 op=ALU.mult), reads=[bf("APS1"), bb_], writes=[bb_])
        sch.add("dve", lambda e: e.scalar_tensor_tensor(out=T_a[:, 0:NQ], in0=T_b[:, 0:NQ],
                                                        scalar=LAMC[:, li, 0:1], in1=T_a[:, 0:NQ],
                                                        op0=ALU.mult, op1=ALU.add),
                reads=[ba, bb_, bf("LAMC")], writes=[ba])
        sch.add("act", lambda e: e.activation(out=SQ[:, 0:NQ], in_=T_a[:, 0:NQ], func=AF.Square),
                reads=[ba], writes=[bf("SQ")])

    def post_da_b(li, qslot, oslot, NQ, acol):
        T_a, T_b = TP[0], TP[1]
        ba, bb_ = bf("TP0"), bf("TP1")
        sch.add("pe", lambda e: e.matmul(M0[:, 0:NQ], lhsT=ones, rhs=SQ[:, 0:NQ], start=True, stop=True),
                reads=[bCST, bf("SQ")], writes=[bf("M0")])
        sch.add("act", lambda e: e.activation(out=T_b[:, 0:NQ], in_=M0[:, 0:NQ], func=AF.Ln, bias=128.0 * EPS),
                reads=[bf("M0")], writes=[bb_])
        sch.add("act", lambda e: e.activation(out=T_b[:, 0:NQ], in_=T_b[:, 0:NQ], func=AF.Exp, scale=-0.5),
                reads=[bb_], writes=[bb_])
        sch.add("dve", lambda e: e.tensor_tensor(out=T_a[:, 0:NQ], in0=T_a[:, 0:NQ], in1=T_b[:, 0:NQ],
                                                 op=ALU.mult), reads=[ba, bb_], writes=[ba])
        sch.add("dve", lambda e: e.scalar_tensor_tensor(out=OT[oslot][:, 0:NQ], in0=T_a[:, 0:NQ],
                                                        scalar=SCC[:, li:li + 1], in1=GB[qslot][:, 0:NQ],
                                                        op0=ALU.mult, op1=ALU.mult),
                reads=[ba, bf("SCC"), bf("GB%d" % qslot)], writes=[bf("OT%d" % oslot)])

    def post_da(li, qslot, oslot, NQ, acol):
        post_da_a(li, qslot, oslot, NQ, acol)
        post_da_b(li, qslot, oslot, NQ, acol)

    def attn_sb(c, qslot, NQ, acol, ktiles, bg):
        nkt = len(ktiles)
        for j in range(2):
            for a in (j, 2 + j):
                sch.add("pe", lambda e, a=a: e.matmul(APS[a][:, acol:acol + NQ], lhsT=ZER[:, :],
                                                      rhs=QB[qslot][:, 0:NQ], start=True, stop=False,
                                                      skip_group_check=True),
                        reads=[bf("ZER"), bf("QB%d" % qslot)], writes=[bf("APS%d" % a)])
        items = [(kt, j) for kt in reversed(range(nkt)) for j in range(2)]

        def stage1(it):
            kt, j = it
            t = ktiles[kt]
            c0, nk = t["c0"], t["nk"]
            sbk = nxt("SPS", 2)
            Sb = SPS[sbk]
            sbb = bf("SPS%d" % sbk)
            nb = len(t["bias"])
            ktt, kc0 = t["kT"]
            sch.add("pe", lambda e: e.matmul(Sb[0:nk, c0:NQ], lhsT=ktt[j * 64:(j + 1) * 64, kc0:kc0 + nk],
                                             rhs=QB[qslot][j * 64:(j + 1) * 64, c0:NQ],
                                             start=True, stop=(nb == 0), skip_group_check=True),
                    reads=[t["kTb"], bf("QB%d" % qslot)], writes=[sbb])
            for bi, (co, ncol, mi, kr) in enumerate(t["bias"]):
                sch.add("pe", lambda e, co=co, ncol=ncol, bi=bi: e.matmul(
                    Sb[0:nk, co:co + ncol], lhsT=ident[0:nk, 0:nk], rhs=maskT[0:nk, 0:ncol],
                    start=False, stop=(bi == nb - 1), skip_group_check=True),
                    reads=[bCST], writes=[sbb])
            es_ = nxt("EW", NE)
            E = EW[es_]
            eb = bf("EW%d" % es_)
            spb = SPB[es_]
            spbb = bf("SPB%d" % es_)
            sch.add("act", lambda e: e.activation(out=E[0:nk, c0:NQ], in_=Sb[0:nk, c0:NQ], func=AF.Exp,
                                                  scale=0.125), reads=[sbb], writes=[eb])
            sch.add("act", lambda e: e.activation(out=E[0:nk, c0:NQ], in_=E[0:nk, c0:NQ], func=AF.Ln,
                                                  bias=1.0), reads=[eb], writes=[eb])
            sch.add("dve", lambda e: e.tensor_copy(out=spb[0:nk, c0:NQ], in_=E[0:nk, c0:NQ]),
                    reads=[eb], writes=[spbb])
            sch.add("dve", lambda e: e.scalar_tensor_tensor(out=E[0:nk, c0:NQ], in0=Sb[0:nk, c0:NQ],
                                                            scalar=0.125, in1=E[0:nk, c0:NQ],
                                                            op0=ALU.mult, op1=ALU.subtract),
                    reads=[sbb, eb, spbb], writes=[eb])
            return E, eb, spb, spbb

        def stage2a(it, E, eb, spb, spbb):
            kt, j = it
            t = ktiles[kt]
            c0, nk = t["c0"], t["nk"]
            rb = bf("APS%d" % j)
            sch.add("pe", lambda e: e.matmul(APS[j][:, acol + c0:acol + NQ], lhsT=triu[0:nk, :],
                                             rhs=spb[0:nk, c0:NQ], start=False, stop=False,
                                             skip_group_check=True),
                    reads=[bCST, spbb], writes=[rb])
            sch.add("dve", lambda e: e.tensor_tensor(out=E[0:nk, c0:NQ], in0=E[0:nk, c0:NQ],
                                                     in1=APS[j][0:nk, acol + c0:acol + NQ], op=ALU.subtract),
                    reads=[eb, rb], writes=[eb])
            pslot = nxt("PB", NPB)
            Ab = PB[pslot]
            pbb = bf("PB%d" % pslot)
            sch.add("act", lambda e: e.activation(out=Ab[0:nk, c0:NQ], in_=E[0:nk, c0:NQ], func=AF.Exp),
                    reads=[eb], writes=[pbb])
            return Ab, pbb

        def stage2b(it, spb, spbb, Ab, pbb):
            kt, j = it
            t = ktiles[kt]
            c0, nk = t["c0"], t["nk"]
            sch.add("pe", lambda e: e.matmul(APS[j][:, acol + c0:acol + NQ], lhsT=tril[0:nk, :],
                                             rhs=spb[0:nk, c0:NQ], start=False, stop=False,
                                             skip_group_check=True),
                    reads=[bCST, spbb], writes=[bf("APS%d" % j)])
            sch.add("pe", lambda e: e.matmul(APS[2 + j][:, acol + c0:acol + NQ], lhsT=t["v"],
                                             rhs=Ab[0:nk, c0:NQ], start=False, stop=False,
                                             skip_group_check=True),
                    reads=[t["vb"], pbb], writes=[bf("APS%d" % (2 + j))])

        n = len(items)
        LA = 3
        s1 = {}
        for i in range(min(LA, n)):
            s1[i] = stage1(items[i])
        prev = None
        for i, it in enumerate(items):
            a = stage2a(it, *s1[i])
            if prev is not None:
                stage2b(*prev)
            prev = (it, s1[i][2], s1[i][3], a[0], a[1])
            if i + LA < n:
                s1[i + LA] = stage1(items[i + LA])
            del s1[i]
            if bg:
                bg.pop(0)()
        stage2b(*prev)

    def post_sb(qslot, oslot, NQ, acol):
        for j in range(2):
            r = slice(j * 64, (j + 1) * 64)
            sch.add("dve", lambda e, j=j, r=r: e.tensor_tensor(out=OT[oslot][r, 0:NQ],
                                                               in0=APS[2 + j][r, acol:acol + NQ],
                                                               in1=GB[qslot][r, 0:NQ], op=ALU.mult),
                    reads=[bf("APS%d" % (2 + j)), bf("GB%d" % qslot)], writes=[bf("OT%d" % oslot)])

    def attn_swa(c, qslot, qtiles, bg):
        items = []
        for qi, qt in enumerate(qtiles):
            for j in range(2):
                for ki in range(len(qt["ktiles"])):
                    items.append((qi, j, ki))

        def qk(it):
            qi, j, ki = it
            qt = qtiles[qi]
            t = qt["ktiles"][ki]
            nk, nq, qc0 = t["nk"], qt["nq"], qt["qc0"]
            sbk = nxt("SPS", 2)
            Sb = SPS[sbk]
            sbb = bf("SPS%d" % sbk)
            ktt, kc0 = t["kT"]
            sch.add("pe", lambda e: e.matmul(Sb[0:nk, 0:nq], lhsT=ktt[j * 64:(j + 1) * 64, kc0:kc0 + nk],
                                             rhs=QB[qslot][j * 64:(j + 1) * 64, qc0:qc0 + nq],
                                             start=True, stop=False, skip_group_check=True),
                    reads=[t["kTb"], bf("QB%d" % qslot)], writes=[sbb])
            sch.add("pe", lambda e: e.matmul(Sb[0:nk, 0:nq], lhsT=ident[0:nk, 0:nk],
                                             rhs=BIAS[0:nk, t["mi"], 2 * c + j, 0:nq],
                                             start=False, stop=True, skip_group_check=True),
                    reads=[bCST, bBIAS], writes=[sbb])
            return Sb, sbb

        def rest(it, Sb, sbb):
            qi, j, ki = it
            qt = qtiles[qi]
            t = qt["ktiles"][ki]
            nk, nq, ac = t["nk"], qt["nq"], qt["acol"]
            nki = len(qt["ktiles"])
            pslot = nxt("PB", NPB)
            Pt = PB[pslot]
            pbb = bf("PB%d" % pslot)
            sch.add("act", lambda e: e.activation(out=Pt[0:nk, 0:nq], in_=Sb[0:nk, 0:nq], func=AF.Exp,
                                                  scale=0.125), reads=[sbb], writes=[pbb])
            sch.add("pe", lambda e: e.matmul(APS[j][:, ac:ac + nq], lhsT=t["v"], rhs=Pt[0:nk, 0:nq],
                                             start=(ki == 0), stop=(ki == nki - 1), skip_group_check=True),
                    reads=[t["vb"], pbb], writes=[bf("APS%d" % j)])
            sch.add("pe", lambda e: e.matmul(APS[2 + j][:, ac:ac + nq], lhsT=ones[0:nk, :], rhs=Pt[0:nk, 0:nq],
                                             start=(ki == 0), stop=(ki == nki - 1), skip_group_check=True),
                    reads=[bCST, pbb], writes=[bf("APS%d" % (2 + j))])

        cur = qk(items[0])
        for i, it in enumerate(items):
            nx = qk(items[i + 1]) if i + 1 < len(items) else None
            rest(it, *cur)
            cur = nx
            if bg:
                bg.pop(0)()

    def post_swa(c, qslot, oslot, NQ, acol):
        T_a = TP[0]
        ba = bf("TP0")
        for j in range(2):
            r = slice(j * 64, (j + 1) * 64)
            sch.add("dve", lambda e, j=j, r=r: e.tensor_scalar(out=T_a[r, 0:NQ], in0=APS[2 + j][r, acol:acol + NQ],
                                                               scalar1=ESK[r, c:c + 1], scalar2=None, op0=ALU.add),
                    reads=[bf("APS%d" % (2 + j)), bf("ESK")], writes=[ba])
            sch.add("dve", lambda e, r=r: e.reciprocal(out=T_a[r, 0:NQ], in_=T_a[r, 0:NQ]),
                    reads=[ba], writes=[ba])
            sch.add("dve", lambda e, j=j, r=r: e.tensor_tensor(out=T_a[r, 0:NQ], in0=APS[j][r, acol:acol + NQ],
                                                               in1=T_a[r, 0:NQ], op=ALU.mult),
                    reads=[bf("APS%d" % j), ba], writes=[ba])
            sch.add("dve", lambda e, r=r: e.tensor_tensor(out=OT[oslot][r, 0:NQ], in0=T_a[r, 0:NQ],
                                                          in1=GB[qslot][r, 0:NQ], op=ALU.mult),
                    reads=[ba, bf("GB%d" % qslot)], writes=[bf("OT%d" % oslot)])

    def final_norm_tile(xrows_ap, nrows, col, dst_ap):
        slot = nxt("XO", 2)
        xo = XO[slot]
        xob = bf("XO%d" % slot)
        sch.add("act", lambda e: e.activation(out=xo[0:nrows, :], in_=xrows_ap, func=AF.Square,
                                              accum_out=SS[0:nrows, col:col + 1]),
                reads=[bf("X")], writes=[xob, bf("SS")])
        sch.add("act", lambda e: e.activation(out=SS[0:nrows, col:col + 1], in_=SS[0:nrows, col:col + 1],
                                              func=AF.Ln, scale=1.0 / D, bias=EPS),
                reads=[bf("SS")], writes=[bf("SS")])
        sch.add("act", lambda e: e.activation(out=SS[0:nrows, col:col + 1], in_=SS[0:nrows, col:col + 1],
                                              func=AF.Exp, scale=-0.5),
                reads=[bf("SS")], writes=[bf("SS")])
        sch.add("dve", lambda e: e.scalar_tensor_tensor(out=xo[0:nrows, :], in0=xrows_ap,
                                                        scalar=SS[0:nrows, col:col + 1], in1=Gn[0:nrows, :],
                                                        op0=ALU.mult, op1=ALU.mult),
                reads=[bf("X"), bf("SS"), bf("Gn")], writes=[xob])
        sch.add("sp", lambda e: e.dma_start(out=dst_ap, in_=xo[0:nrows, :]), reads=[xob], dma=xob)

    def load_gain(idx):
        sch.add("sp", lambda e: e.dma_start(out=Gn[:, :], in_=gain_d[idx]), writes=[bf("Gn")], dma=bf("Gn"))

    def prompt_seq(s):
        sch.add("sp", lambda e: e.dma_start(out=X[:, :, :], in_=xp_d[s].rearrange("(n p) d -> p n d", p=P)),
                writes=[bf("X")], dma=bf("Xd"))
        for l in range(4):
            kind = KINDS[l]
            li = 0 if l == 0 else 1
            swa = kind == 2
            load_gain(l)
            load_weights(l, 0, 0)
            zero_acc()
            for n in range(NT):
                norm_tile(X[:, n, :], P, n, None, (n * P, (n + 1) * P))

            def kvset_of(c):
                return (c // 2) % 2 if swa else c % 2

            def kv_units(c):
                if swa and c % 2 == 1:
                    return []
                kvset = kvset_of(c)
                wset = c % 2
                us = []
                for n in range(NT):
                    outs = []
                    if swa:
                        g = c // 2
                        if n == NT - 1:
                            outs = [(kp_d[l][s, :, g * 64:(g + 1) * 64], 0, 64),
                                    (vp_d[l][s, :, g * 64:(g + 1) * 64], 64, 64)]
                    else:
                        outs = [(kp_d[l][s, n * P:(n + 1) * P, c * 128:(c + 1) * 128], 0, 128),
                                (vp_d[l][s, n * P:(n + 1) * P, c * 128:(c + 1) * 128], 128, 128)]
                    us.append(proj_kv_unit(l, c, wset, kvset, n * P, P,
                                           (KT[kvset], n * P, bf("KT%d" % kvset)),
                                           (VB[kvset][:, n, :], bf("VB%d" % kvset)), outs, swa))
                flat = []
                for i_, (a_, b_) in enumerate(us):
                    flat.append(a_)
                    if i_ >= 1:
                        flat.append(us[i_ - 1][1])
                flat.append(us[-1][1])
                return flat

            for u in kv_units(0):
                u()
            qslots = {}

            def qg_units(c, qb):
                slot = nxt("QBslot", NQS)
                qslots[(c, qb)] = slot
                return [proj_q_unit(c % 2, qb * 512, 512, slot), proj_g_unit(c % 2, qb * 512, 512, slot)]

            for u in qg_units(0, 0):
                u()
            pend_out = []
            ku_cache = {}
            blocks = [(c, qb) for c in range(8) for qb in range(NQB)]
            for bi_, (c, qb) in enumerate(blocks):
                wset = c % 2
                kvset = kvset_of(c)
                bg = list(pend_out)
                pend_out = []
                if qb == min(1, NQB - 1) and c + 1 < 8:
                    if NQB == 1:
                        for u in bg:
                            u()
                        bg = []
                    load_weights(l, c + 1, (c + 1) % 2)
                if bi_ + 1 < len(blocks):
                    bg += qg_units(*blocks[bi_ + 1]) if blocks[bi_ + 1][0] == c else []
                if c + 1 < 8 and qb >= min(1, NQB - 1):
                    if (c + 1) not in ku_cache:
                        ku_cache[c + 1] = kv_units(c + 1)
                    ku = ku_cache[c + 1]
                    nshare = max(1, NQB - 1)
                    idx = qb - min(1, NQB - 1)
                    per = (len(ku) + nshare - 1) // nshare
                    bg += ku[idx * per:(idx + 1) * per]
                    if qb == NQB - 1:
                        bg += qg_units(c + 1, 0)
                qslot = qslots[(c, qb)]
                oslot = nxt("OT", 2)
                ktb, vbb = bf("KT%d" % kvset), bf("VB%d" % kvset)
                if kind in (0, 1):
                    ktiles = []
                    for kt in range(4 * qb + 4):
                        c0 = max(0, kt - 4 * qb) * P
                        bias = []
                        for i in range(4):
                            qt = 4 * qb + i
                            if kind == 0:
                                if kt == qt:
                                    bias.append((i * P, P, 0, P))
                                elif kt == qt - 1:
                                    bias.append((i * P, P, 1, P))
                            else:
                                if kt == qt:
                                    bias.append((i * P, P, 0, P))
                        ktiles.append(dict(kT=(KT[kvset], kt * P), kTb=ktb, v=VB[kvset][:, kt, :], vb=vbb,
                                           nk=P, c0=c0, bias=bias))
                    if kind == 0:
                        attn_da(c, qslot, 512, 0, ktiles, bg)
                    else:
                        attn_sb(c, qslot, 512, 0, ktiles, bg)
                else:
                    qtiles = []
                    for i in range(4):
                        qt = 4 * qb + i
                        kts = []
                        if qt > 0:
                            kts.append(dict(kT=(KT[kvset], (qt - 1) * P), kTb=ktb, v=VB[kvset][:, qt - 1, :],
                                            vb=vbb, nk=P, mi=3))
                        kts.append(dict(kT=(KT[kvset], qt * P), kTb=ktb, v=VB[kvset][:, qt, :], vb=vbb,
                                        nk=P, mi=2))
                        qtiles.append(dict(qc0=i * P, nq=P, acol=i * P, ktiles=kts))
                    attn_swa(c, qslot, qtiles, bg)
                for u in bg:
                    u()
                pre = []
                if kind == 0:
                    post_da_a(li, qslot, oslot, 512, 0)
                    pre = [lambda li=li, qslot=qslot, oslot=oslot: post_da_b(li, qslot, oslot, 512, 0)]
                elif kind == 1:
                    post_sb(qslot, oslot, 512, 0)
                else:
                    post_swa(c, qslot, oslot, 512, 0)
                xdst = [(P, (lambda hf, i=i, qb=qb: X[:, 4 * qb + i, hf * 512:(hf + 1) * 512])) for i in range(4)]
                pend_out = pre + out_proj_units(oslot, wset, 4, xdst)
            for u in pend_out:
                u()
        load_gain(4)
        zero_acc()
        for n in range(NT):
            final_norm_tile(X[:, n, :], P, n, yp_d[s, n * P:(n + 1) * P, :])

    def sample_pass():
        sch.add("sp", lambda e: e.dma_start(out=X[0:TS, 0, :], in_=xs_d), writes=[bf("X")], dma=bf("Xd"))
        for l in range(4):
            kind = KINDS[l]
            li = 0 if l == 0 else 1
            swa = kind == 2
            load_gain(l)
            zero_acc()
            norm_tile(X[0:TS, 0, :], TS, 0, None, (0, TS))
            for c in range(8):
                wset = c % 2
                load_weights(l, c, wset)
                slot = nxt("QBslot", NQS)
                proj_q_unit(wset, 0, TS, slot)()
                proj_g_unit(wset, 0, TS, slot)()
                oslot = nxt("OT", 2)
                for b in range(NSEQ):
                    st = nxt("SKV", 2)
                    ktb, vbb = bf("KT%d" % st), bf("VB%d" % st)
                    ktnb, vnb = bf("KTN%d" % st), bf("VN%d" % st)
                    if swa:
                        g = c // 2
                        for hh in range(2):
                            sch.add("pool", lambda e, hh=hh, st=st, g=g, b=b, l=l: e.dma_start(
                                out=KC[st][:, 0, hh * 64:(hh + 1) * 64], in_=ck_d[l][b, :, g * 64:(g + 1) * 64]),
                                writes=[bf("KC%d" % st)], dma=bf("KCd%d_%d" % (st, hh)))
                            sch.add("pool", lambda e, hh=hh, st=st, g=g, b=b, l=l: e.dma_start(
                                out=VB[st][:, 0, hh * 64:(hh + 1) * 64], in_=cv_d[l][b, :, g * 64:(g + 1) * 64]),
                                writes=[vbb], dma=bf("VBd%d_%d" % (st, hh)))
                        ncache = 1
                    else:
                        sch.add("pool", lambda e, st=st, b=b, l=l, c=c: e.dma_start(
                            out=KC[st][:, :, :],
                            in_=ck_d[l][b, :, c * 128:(c + 1) * 128].rearrange("(n p) f -> p n f", p=P)),
                            writes=[bf("KC%d" % st)], dma=bf("KCd%d_0" % st))
                        sch.add("pool", lambda e, st=st, b=b, l=l, c=c: e.dma_start(
                            out=VB[st][:, 0:NPT, :],
                            in_=cv_d[l][b, :, c * 128:(c + 1) * 128].rearrange("(n p) f -> p n f", p=P)),
                            writes=[vbb], dma=bf("VBd%d_0" % st))
                        ncache = NPT
                    for n in range(ncache):
                        sch.add("pe", lambda e, n=n, st=st: e.transpose(out=T0[:, n * P:(n + 1) * P],
                                                                        in_=KC[st][:, n, :], identity=ident),
                                reads=[bf("KC%d" % st), bCST], writes=[bf("T0")])
                    sch.add("act", lambda e, st=st, ncache=ncache: e.copy(out=KT[st][:, 0:ncache * P],
                                                                          in_=T0[:, 0:ncache * P]),
                            reads=[bf("T0")], writes=[ktb])
                    if swa:
                        g = c // 2
                        outs = []
                        if c % 2 == 0:
                            outs = [(ks_d[l][b, P - T_DEC:P, g * 64:(g + 1) * 64], 0, 64),
                                    (vs_d[l][b, P - T_DEC:P, g * 64:(g + 1) * 64], 64, 64)]
                    else:
                        outs = [(ks_d[l][b * T_DEC:(b + 1) * T_DEC, c * 128:(c + 1) * 128], 0, 128),
                                (vs_d[l][b * T_DEC:(b + 1) * T_DEC, c * 128:(c + 1) * 128], 128, 128)]
                    for u_ in proj_kv_unit(l, c, wset, st, b * T_DEC, T_DEC, (KTN[st], 0, ktnb),
                                           (VN[st][:, :], vnb), outs, swa):
                        u_()
                    acol = b * T_DEC
                    qc0 = b * T_DEC
                    if kind in (0, 1):
                        ktiles = []
                        for kt in range(NPT):
                            bias = []
                            if kind == 0 and kt == NPT - 1:
                                bias = [(0, T_DEC, 1, P)]
                            ktiles.append(dict(kT=(KT[st], kt * P), kTb=ktb, v=VB[st][:, kt, :], vb=vbb,
                                               nk=P, c0=0, bias=bias))
                        ktiles.append(dict(kT=(KTN[st], 0), kTb=ktnb, v=VN[st][:, :], vb=vnb, nk=T_DEC, c0=0,
                                           bias=[(0, T_DEC, 0, T_DEC)]))
                        if kind == 0:
                            attn_da_s(c, slot, T_DEC, acol, qc0, ktiles)
                        else:
                            attn_sb_s(c, slot, T_DEC, acol, qc0, ktiles)
                    else:
                        kts = [dict(kT=(KT[st], 0), kTb=ktb, v=VB[st][:, 0, :], vb=vbb, nk=P, mi=3),
                               dict(kT=(KTN[st], 0), kTb=ktnb, v=VN[st][:, :], vb=vnb, nk=T_DEC, mi=2)]
                        attn_swa(c, slot, [dict(qc0=qc0, nq=T_DEC, acol=acol, ktiles=kts)], None)
                if kind == 0:
                    post_da(li, slot, oslot, TS, 0)
                elif kind == 1:
                    post_sb(slot, oslot, TS, 0)
                else:
                    post_swa(c, slot, oslot, TS, 0)
                xdst = [(TS, (lambda hf: X[0:TS, 0, hf * 512:(hf + 1) * 512]))]
                for u in out_proj_units(oslot, wset, 1, xdst):
                    u()
            if swa:
                for b in range(NSEQ):
                    sch.add("sp", lambda e, b=b, l=l: e.dma_start(out=ks_d[l][b, 0:P - T_DEC, :],
                                                             in_=ck_d[l][b, T_DEC:P, :]), dma=bf("d2d"))
                    sch.add("sp", lambda e, b=b, l=l: e.dma_start(out=vs_d[l][b, 0:P - T_DEC, :],
                                                             in_=cv_d[l][b, T_DEC:P, :]), dma=bf("d2d"))
        load_gain(4)
        zero_acc()
        final_norm_tile(X[0:TS, 0, :], TS, 0, ys_d[:, :])

    def attn_da_s(c, qslot, NQ, acol, qc0, ktiles):
        hmb = 2 * c
        nkt = len(ktiles)
        for kt in range(nkt):
            for m in range(2):
                t = ktiles[kt]
                nk = t["nk"]
                sbk = nxt("SPS", 2)
                Sb = SPS[sbk]
                sbb = bf("SPS%d" % sbk)
                nb = len(t["bias"])
                ktt, kc0 = t["kT"]
                sch.add("pe", lambda e, Sb=Sb, nk=nk, m=m, ktt=ktt, kc0=kc0, nb=nb: e.matmul(
                    Sb[0:nk, 0:NQ], lhsT=ktt[m * 64:(m + 1) * 64, kc0:kc0 + nk],
                    rhs=QB[qslot][m * 64:(m + 1) * 64, qc0:qc0 + NQ], start=True, stop=(nb == 0),
                    skip_group_check=True), reads=[t["kTb"], bf("QB%d" % qslot)], writes=[sbb])
                for bi, (co, ncol, mi, kr) in enumerate(t["bias"]):
                    sch.add("pe", lambda e, Sb=Sb, nk=nk, m=m, mi=mi, bi=bi, nb=nb: e.matmul(
                        Sb[0:nk, 0:NQ], lhsT=ident[0:nk, 0:nk], rhs=BIAS[0:nk, mi, hmb + m, 0:NQ],
                        start=False, stop=(bi == nb - 1), skip_group_check=True),
                        reads=[bCST, bBIAS], writes=[sbb])
                pslot = nxt("PB", NPB)
                Pt = PB[pslot]
                pbb = bf("PB%d" % pslot)
                sch.add("act", lambda e, Pt=Pt, Sb=Sb, nk=nk: e.activation(
                    out=Pt[0:nk, 0:NQ], in_=Sb[0:nk, 0:NQ], func=AF.Exp, scale=0.125),
                    reads=[sbb], writes=[pbb])
                sch.add("pe", lambda e, Pt=Pt, nk=nk, m=m, t=t, kt=kt: e.matmul(
                    APS[m][:, acol:acol + NQ], lhsT=t["v"], rhs=Pt[0:nk, 0:NQ],
                    start=(kt == 0), stop=(kt == nkt - 1), skip_group_check=True),
                    reads=[t["vb"], pbb], writes=[bf("APS%d" % m)])
                sch.add("pe", lambda e, Pt=Pt, nk=nk, m=m, kt=kt: e.matmul(
                    APS[2 + m][:, acol:acol + NQ], lhsT=ones[0:nk, :], rhs=Pt[0:nk, 0:NQ],
                    start=(kt == 0), stop=(kt == nkt - 1), skip_group_check=True),
                    reads=[bCST, pbb], writes=[bf("APS%d" % (2 + m))])

    def attn_sb_s(c, qslot, NQ, acol, qc0, ktiles):
        nkt = len(ktiles)
        for j in range(2):
            for a in (j, 2 + j):
                sch.add("pe", lambda e, a=a: e.matmul(APS[a][:, acol:acol + NQ], lhsT=ZER[:, :],
                                                      rhs=QB[qslot][:, qc0:qc0 + NQ], start=True, stop=False,
                                                      skip_group_check=True),
                        reads=[bf("ZER"), bf("QB%d" % qslot)], writes=[bf("APS%d" % a)])
        for kt in reversed(range(nkt)):
            for j in range(2):
                t = ktiles[kt]
                nk = t["nk"]
                sbk = nxt("SPS", 2)
                Sb = SPS[sbk]
                sbb = bf("SPS%d" % sbk)
                nb = len(t["bias"])
                ktt, kc0 = t["kT"]
                sch.add("pe", lambda e, Sb=Sb, nk=nk, j=j, ktt=ktt, kc0=kc0, nb=nb: e.matmul(
                    Sb[0:nk, 0:NQ], lhsT=ktt[j * 64:(j + 1) * 64, kc0:kc0 + nk],
                    rhs=QB[qslot][j * 64:(j + 1) * 64, qc0:qc0 + NQ], start=True, stop=(nb == 0),
                    skip_group_check=True), reads=[t["kTb"], bf("QB%d" % qslot)], writes=[sbb])
                for bi, (co, ncol, mi, kr) in enumerate(t["bias"]):
                    sch.add("pe", lambda e, Sb=Sb, nk=nk, bi=bi, nb=nb: e.matmul(
                        Sb[0:nk, 0:NQ], lhsT=ident[0:nk, 0:nk], rhs=maskT[0:nk, 0:NQ],
                        start=False, stop=(bi == nb - 1), skip_group_check=True),
                        reads=[bCST], writes=[sbb])
                es_ = nxt("EW", NE)
                E = EW[es_]
                eb = bf("EW%d" % es_)
                spb = SPB[es_]
                spbb = bf("SPB%d" % es_)
                rb = bf("APS%d" % j)
                sch.add("act", lambda e, E=E, Sb=Sb, nk=nk: e.activation(
                    out=E[0:nk, 0:NQ], in_=Sb[0:nk, 0:NQ], func=AF.Exp, scale=0.125), reads=[sbb], writes=[eb])
                sch.add("act", lambda e, E=E, nk=nk: e.activation(
                    out=E[0:nk, 0:NQ], in_=E[0:nk, 0:NQ], func=AF.Ln, bias=1.0), reads=[eb], writes=[eb])
                sch.add("pool", lambda e, E=E, spb=spb, nk=nk: e.tensor_copy(out=spb[0:nk, 0:NQ], in_=E[0:nk, 0:NQ]),
                        reads=[eb], writes=[spbb])
                sch.add("dve", lambda e, E=E, Sb=Sb, nk=nk: e.scalar_tensor_tensor(
                    out=E[0:nk, 0:NQ], in0=Sb[0:nk, 0:NQ], scalar=0.125, in1=E[0:nk, 0:NQ],
                    op0=ALU.mult, op1=ALU.subtract), reads=[sbb, eb, spbb], writes=[eb])
                sch.add("pe", lambda e, spb=spb, nk=nk, j=j: e.matmul(
                    APS[j][:, acol:acol + NQ], lhsT=triu[0:nk, :], rhs=spb[0:nk, 0:NQ], start=False, stop=False,
                    skip_group_check=True), reads=[bCST, spbb], writes=[rb])
                sch.add("dve", lambda e, E=E, nk=nk, j=j: e.tensor_tensor(
                    out=E[0:nk, 0:NQ], in0=E[0:nk, 0:NQ], in1=APS[j][0:nk, acol:acol + NQ], op=ALU.subtract),
                    reads=[eb, rb], writes=[eb])
                pslot = nxt("PB", NPB)
                Ab = PB[pslot]
                pbb = bf("PB%d" % pslot)
                sch.add("act", lambda e, Ab=Ab, E=E, nk=nk: e.activation(
                    out=Ab[0:nk, 0:NQ], in_=E[0:nk, 0:NQ], func=AF.Exp), reads=[eb], writes=[pbb])
                sch.add("pe", lambda e, spb=spb, nk=nk, j=j: e.matmul(
                    APS[j][:, acol:acol + NQ], lhsT=tril[0:nk, :], rhs=spb[0:nk, 0:NQ], start=False, stop=False,
                    skip_group_check=True), reads=[bCST, spbb], writes=[rb])
                sch.add("pe", lambda e, Ab=Ab, nk=nk, j=j, t=t: e.matmul(
                    APS[2 + j][:, acol:acol + NQ], lhsT=t["v"], rhs=Ab[0:nk, 0:NQ], start=False, stop=False,
                    skip_group_check=True), reads=[t["vb"], pbb], writes=[bf("APS%d" % (2 + j))])

    setup()
    for s in range(NSEQ):
        prompt_seq(s)
    sample_pass()
    fin = sch.add("sp", lambda e: e.nop(), reads=[], writes=[])
    for b in sch.dma_bufs:
        fin.waits.append(("dma", b, b.cnt))
    sch.emit(nc, es)
    es.close()
    return nc


_NC_CACHE = {}


def _get_nc(cfg):
    key = (cfg["NSEQ"], cfg["S"], cfg["PAST"])
    if key not in _NC_CACHE:
        _NC_CACHE[key] = build(cfg)
    return _NC_CACHE[key]


def run(cfg, inputs):
    NSEQ, S, PAST, NCORES = cfg["NSEQ"], cfg["S"], cfg["PAST"], cfg["NCORES"]
    nc = _get_nc(cfg)
    f = lambda a: np.ascontiguousarray(np.asarray(a, dtype=np.float32))
    maps_np, consts_np, _ = static_tables()
    gain = np.stack([np.broadcast_to(f(inputs[k])[None, :], (P, D)) for k in
                     ("norm_0", "norm_1", "norm_2", "norm_3", "final_norm")])
    tab = np.broadcast_to(f(inputs["rel_bias_table"]).reshape(1, 512), (P, 512))
    lam = np.stack([np.broadcast_to(f(inputs["da_lambda_%d" % l]).reshape(1, 256), (P, 256)) for l in (0, 3)])
    subln = np.stack([f(inputs["da_subln_%d" % l]).reshape(P, 1) for l in (0, 3)])
    sk = f(inputs["sw_sinks_2"])
    sink = np.zeros((P, 8), np.float32)
    for c in range(8):
        sink[:64, c] = sk[2 * c]
        sink[64:, c] = sk[2 * c + 1]
    shared = dict(gain=f(gain), tab=f(tab), lam=f(lam), subln=f(subln), sink=sink, maps=maps_np, consts=consts_np)
    for l in range(4):
        shared["win%d" % l] = f(inputs["w_in_%d" % l])
        shared["wout%d" % l] = f(inputs["w_out_%d" % l])
    in_maps = []
    for core in range(NCORES):
        sl = slice(core * NSEQ, (core + 1) * NSEQ)
        m = dict(shared)
        m["xp"] = f(inputs["x_prompt"][sl])
        m["xs"] = f(inputs["x_sample"][sl]).reshape(NSEQ * T_DEC, D)
        for l in range(4):
            ck = f(inputs["cache_k_%d" % l][sl])
            cv = f(inputs["cache_v_%d" % l][sl])
            m["ck%d" % l] = ck.reshape(NSEQ, ck.shape[1], -1)
            m["cv%d" % l] = cv.reshape(NSEQ, cv.shape[1], -1)
        in_maps.append(m)
    res = run_bass_kernel_spmd(nc, in_maps, core_ids=list(range(NCORES)))
    R = res.results
    cat = lambda name: np.concatenate([np.asarray(r[name]) for r in R], axis=0)
    Bt = NSEQ * NCORES
    outs = [cat("yp").reshape(Bt, S, D), cat("ys").reshape(Bt, T_DEC, D)]
    for l in range(4):
        kind = KINDS[l]
        if kind == 0:
            outs += [cat("kp%d" % l).reshape(Bt, S, 8, 2, 64), cat("vp%d" % l).reshape(Bt, S, 8, 128),
                     cat("ks%d" % l).reshape(Bt, T_DEC, 8, 2, 64), cat("vs%d" % l).reshape(Bt, T_DEC, 8, 128)]
        elif kind == 1:
            outs += [cat("kp%d" % l).reshape(Bt, S, 16, 64), cat("vp%d" % l).reshape(Bt, S, 16, 64),
                     cat("ks%d" % l).reshape(Bt, T_DEC, 16, 64), cat("vs%d" % l).reshape(Bt, T_DEC, 16, 64)]
        else:
            outs += [cat("kp%d" % l).reshape(Bt, P, 4, 64), cat("vp%d" % l).reshape(Bt, P, 4, 64),
                     cat("ks%d" % l).reshape(Bt, P, 4, 64), cat("vs%d" % l).reshape(Bt, P, 4, 64)]
    return tuple(np.ascontiguousarray(o, dtype=np.float32) for o in outs)


def kernel(x_prompt, x_sample, cache_k_0, cache_v_0, cache_k_1, cache_v_1, cache_k_2, cache_v_2,
           cache_k_3, cache_v_3, rel_bias_table,
           norm_0, w_in_0, w_out_0, da_lambda_0, da_subln_0,
           norm_1, w_in_1, w_out_1,
           norm_2, w_in_2, w_out_2, sw_sinks_2,
           norm_3, w_in_3, w_out_3, da_lambda_3, da_subln_3,
           final_norm):
    inputs = {
        "x_prompt": x_prompt, "x_sample": x_sample,
        "cache_k_0": cache_k_0, "cache_v_0": cache_v_0, "cache_k_1": cache_k_1, "cache_v_1": cache_v_1,
        "cache_k_2": cache_k_2, "cache_v_2": cache_v_2, "cache_k_3": cache_k_3, "cache_v_3": cache_v_3,
        "rel_bias_table": rel_bias_table,
        "norm_0": norm_0, "w_in_0": w_in_0, "w_out_0": w_out_0, "da_lambda_0": da_lambda_0, "da_subln_0": da_subln_0,
        "norm_1": norm_1, "w_in_1": w_in_1, "w_out_1": w_out_1,
        "norm_2": norm_2, "w_in_2": w_in_2, "w_out_2": w_out_2, "sw_sinks_2": sw_sinks_2,
        "norm_3": norm_3, "w_in_3": w_in_3, "w_out_3": w_out_3, "da_lambda_3": da_lambda_3, "da_subln_3": da_subln_3,
        "final_norm": final_norm,
    }
    return run(FULL_CFG, inputs)
```

```python
import math
from contextlib import ExitStack

import numpy as np
import concourse.bass as bass
import concourse.mybir as mybir
from concourse.bass_utils import run_bass_kernel_spmd

F32 = mybir.dt.float32
BF16 = mybir.dt.bfloat16
ALU = mybir.AluOpType
AF = mybir.ActivationFunctionType

D = 1024
P = 128
NCH = 8
EPS = 1e-6
NEG = -240000.0
T_DEC = 16
KINDS = (0, 1, 2, 0)
IN_W = (4096, 4096, 2560)

FULL_CFG = dict(NSEQ=4, S=2048, PAST=1024, NCORES=8)


class Buf:
    __slots__ = ("name", "last_w", "readers", "sem", "cnt", "psum")

    def __init__(self, name):
        self.name = name
        self.psum = False
        self.last_w = None
        self.readers = []
        self.sem = None
        self.cnt = 0


class Op:
    __slots__ = ("idx", "eng", "fn", "waits", "signal", "sigval", "dma", "dma_val")


ENGS = ("pe", "act", "dve", "pool", "sp")


class Sched:
    def __init__(self):
        self.ops = []
        self.by_eng = {e: [] for e in ENGS}
        self.waited = {e: {f: -1 for f in ENGS} for e in ENGS}
        self.dma_waited = {e: {} for e in ENGS}
        self.dma_bufs = []

    def add(self, eng, fn, reads=(), writes=(), dma=None):
        op = Op()
        op.idx = len(self.ops)
        op.eng = eng
        op.fn = fn
        op.waits = []
        op.signal = False
        op.sigval = 0
        op.dma = dma
        op.dma_val = 0
        deps = set()
        for b in reads:
            if b.last_w is not None:
                deps.add(b.last_w)
            if b.psum:
                deps.update(r for r in b.readers if self.ops[r].eng != eng)
        for b in writes:
            if b.last_w is not None:
                deps.add(b.last_w)
            deps.update(b.readers)
        for d in sorted(deps, reverse=True):
            dop = self.ops[d]
            if dop.dma is not None:
                if self.dma_waited[eng].get(dop.dma, 0) >= dop.dma_val:
                    continue
                self.dma_waited[eng][dop.dma] = dop.dma_val
                op.waits.append(("dma", dop.dma, dop.dma_val))
            else:
                if dop.eng == "pe" and eng == "pe" and dma is None:
                    continue
                if self.waited[eng][dop.eng] >= d:
                    continue
                self.waited[eng][dop.eng] = d
                dop.signal = True
                op.waits.append(("eng", dop))
        for b in reads:
            b.readers.append(op.idx)
        for b in writes:
            b.last_w = op.idx
            b.readers = []
        if dma is not None:
            if dma.cnt == 0:
                self.dma_bufs.append(dma)
            dma.cnt += 16
            op.dma_val = dma.cnt
        self.ops.append(op)
        self.by_eng[eng].append(op)
        return op

    def emit(self, nc, es):
        for e in ENGS:
            cnt = 0
            for op in self.by_eng[e]:
                if op.signal:
                    cnt += 1
                    op.sigval = cnt
        esem = {e: es.enter_context(nc.semaphore("sem_" + e)) for e in ENGS}
        for i, b in enumerate(self.dma_bufs):
            b.sem = es.enter_context(nc.semaphore("dsem%d" % i))

        def run(ename, eng):
            for op in self.by_eng[ename]:
                for w in op.waits:
                    if w[0] == "dma":
                        eng.wait_ge(w[1].sem, w[2])
                    else:
                        eng.wait_ge(esem[w[1].eng], w[1].sigval)
                ins = op.fn(eng)
                if op.dma is not None:
                    ins.then_inc(op.dma.sem, 16)
                elif op.signal:
                    ins.then_inc(esem[ename], 1)

        with nc.Block() as block:
            @block.tensor
            def _(e):
                run("pe", e)

            @block.scalar
            def _(e):
                run("act", e)

            @block.vector
            def _(e):
                run("dve", e)

            @block.gpsimd
            def _(e):
                run("pool", e)

            @block.sync
            def _(e):
                run("sp", e)


def _bucket_np(rel):
    import jax
    import jax.numpy as jnp
    cpu = jax.devices("cpu")[0]
    with jax.default_device(cpu):
        rel = jnp.asarray(rel, dtype=jnp.int32)
        nb = 16
        max_exact = 8
        n = jnp.abs(rel)
        nf = jnp.maximum(n, 1).astype(jnp.float32)
        large = max_exact + (jnp.log(nf / max_exact) / math.log(128 / max_exact)
                             * (nb - max_exact)).astype(jnp.int32)
        large = jnp.minimum(large, nb - 1)
        out = jnp.where(rel > 0, nb, 0) + jnp.where(n < max_exact, n, large)
        return np.asarray(out)


def static_tables():
    k = np.arange(P)[:, None]
    q = np.arange(P)[None, :]
    bd = _bucket_np(k - q).astype(np.float32)
    bs = _bucket_np(k - P - q).astype(np.float32)
    maps = np.zeros((4, P, P), np.float32)
    m = bd.copy(); m[(k // 64) > (q // 64)] = -1.0
    maps[0] = m
    maps[1] = bs
    m = bd.copy(); m[(k >= 64) & (q < 64)] = -1.0
    maps[2] = m
    m = bs.copy(); m[(k < 64) & (q >= 64)] = -1.0
    maps[3] = m
    consts = np.zeros((5, P, P), np.float32)
    consts[0] = np.eye(P)
    consts[1] = (k > q)
    consts[2] = (k <= q)
    consts[3] = np.where(k < q, 0.0, NEG)
    consts[4] = 1.0
    present = [sorted(set(int(v) for v in np.unique(maps[i]) if v >= 0 and v != 15)) for i in range(4)]
    return maps, consts, present


def build(cfg):
    NSEQ, S, PAST = cfg["NSEQ"], cfg["S"], cfg["PAST"]
    NT = S // P
    NQB = S // 512
    NPT = PAST // P
    TS = NSEQ * T_DEC
    assert S % 512 == 0 and PAST % P == 0 and TS <= 64
    maps_np, consts_np, present = static_tables()

    nc = bass.Bass("TRN2", target_bir_lowering=False)

    def din(name, shape):
        return nc.dram_tensor(name, list(shape), F32, kind="ExternalInput").ap()

    def dout(name, shape):
        return nc.dram_tensor(name, list(shape), F32, kind="ExternalOutput").ap()

    xp_d = din("xp", (NSEQ, S, D))
    xs_d = din("xs", (TS, D))
    ck_d, cv_d = [], []
    for l in range(4):
        if KINDS[l] == 2:
            ck_d.append(din("ck%d" % l, (NSEQ, P, 256)))
            cv_d.append(din("cv%d" % l, (NSEQ, P, 256)))
        else:
            ck_d.append(din("ck%d" % l, (NSEQ, PAST, D)))
            cv_d.append(din("cv%d" % l, (NSEQ, PAST, D)))
    win_d = [din("win%d" % l, (D, IN_W[KINDS[l]])) for l in range(4)]
    wout_d = [din("wout%d" % l, (D, D)) for l in range(4)]
    gain_d = din("gain", (5, P, D))
    tab_d = din("tab", (P, 512))
    lam_d = din("lam", (2, P, 256))
    subln_d = din("subln", (2, P, 1))
    sink_d = din("sink", (P, 8))
    maps_d = din("maps", (4, P, P))
    consts_d = din("consts", (5, P, P))

    yp_d = dout("yp", (NSEQ, S, D))
    ys_d = dout("ys", (TS, D))
    kp_d, vp_d, ks_d, vs_d = [], [], [], []
    for l in range(4):
        if KINDS[l] == 2:
            kp_d.append(dout("kp%d" % l, (NSEQ, P, 256)))
            vp_d.append(dout("vp%d" % l, (NSEQ, P, 256)))
            ks_d.append(dout("ks%d" % l, (NSEQ, P, 256)))
            vs_d.append(dout("vs%d" % l, (NSEQ, P, 256)))
        else:
            kp_d.append(dout("kp%d" % l, (NSEQ, S, D)))
            vp_d.append(dout("vp%d" % l, (NSEQ, S, D)))
            ks_d.append(dout("ks%d" % l, (TS, D)))
            vs_d.append(dout("vs%d" % l, (TS, D)))

    es = ExitStack()
    sch = Sched()

    def sb(name, shape, dt):
        return es.enter_context(nc.sbuf_tensor(name, list(shape), dt))

    X = sb("X", (P, NT, D), F32)
    xnT = sb("xnT", (P, NCH, S), BF16)
    Gn = sb("Gn", (P, D), F32)
    WT = [sb("WT%d" % i, (P, NCH, 512), BF16) for i in range(2)]
    WO = [sb("WO%d" % i, (P, D), BF16) for i in range(2)]
    KT = [sb("KT%d" % i, (P, S), BF16) for i in range(2)]
    VB = [sb("VB%d" % i, (P, NT, P), BF16) for i in range(2)]
    KTN = [sb("KTN%d" % i, (P, T_DEC), BF16) for i in range(2)]
    VN = [sb("VN%d" % i, (T_DEC, P), BF16) for i in range(2)]
    KC = [sb("KC%d" % i, (P, max(NPT, 1), P), BF16) for i in range(2)]
    NQS = 3
    QB = [sb("QB%d" % i, (P, 512), BF16) for i in range(NQS)]
    GB = [sb("GB%d" % i, (P, 512), BF16) for i in range(NQS)]
    OT = [sb("OT%d" % i, (P, 512), BF16) for i in range(2)]
    NKVS = 3
    KVS = [sb("KVS%d" % i, (P, 256), F32) for i in range(NKVS)]
    NKBT = 4
    KBT = [sb("KBT%d" % i, (P, P), BF16) for i in range(NKBT)]
    NE = 4
    EW = [sb("EW%d" % i, (P, 512), F32) for i in range(NE)]
    SPB = [sb("SPB%d" % i, (P, 512), BF16) for i in range(NE)]
    NPB = 3
    PB = [sb("PB%d" % i, (P, 512), BF16) for i in range(NPB)]
    TP = [sb("TP%d" % i, (P, 512), F32) for i in range(2)]
    SQ = sb("SQ", (P, 512), BF16)
    XN = [sb("XN%d" % i, (P, D), BF16) for i in range(2)]
    SS = sb("SS", (P, 2 * NT + 4), F32)
    BIAS = sb("BIAS", (P, 4, 16, P), BF16)
    CST = sb("CST", (P, 5, P), BF16)
    ZER = sb("ZER", (P, P), BF16)
    LAMC = sb("LAMC", (P, 2, 8), F32)
    SCC = sb("SCC", (P, 2), F32)
    ESK = sb("ESK", (P, 8), F32)
    LAMT = sb("LAMT", (P, 8), F32)
    SCR = sb("SCR", (P, 3200), F32)
    XO = [SCR[:, i * D:(i + 1) * D] for i in range(2)]
    MAPS = SCR[:, 0:512].rearrange("p (c q) -> p c q", c=4)
    CSTF = SCR[:, 512:1152].rearrange("p (c q) -> p c q", c=5)
    OH = [SCR[:, 1152 + i * 128:1152 + (i + 1) * 128] for i in range(2)]
    ACC = SCR[:, 1408:2432].rearrange("p (h q) -> p h q", h=8)
    TABP = SCR[:, 2432:2944]
    LAMW = SCR[:, 2944:3200]

    def ps(name, dt=F32, cols=512):
        return es.enter_context(nc.psum_tensor(name, [P, cols], dt))

    SPS = [ps("SPS0"), ps("SPS1")]
    APS = [ps("APS%d" % i) for i in range(4)]
    M0 = ps("M0")
    T0 = ps("T0", BF16, 1024)

    B = {}

    def bf(name):
        if name not in B:
            B[name] = Buf(name)
        return B[name]

    for nm in ("SPS0", "SPS1", "APS0", "APS1", "APS2", "APS3", "M0", "T0"):
        bf(nm).psum = True
    ident = CST[:, 0, :]
    triu = CST[:, 1, :]
    tril = CST[:, 2, :]
    maskT = CST[:, 3, :]
    ones = CST[:, 4, :]
    bCST = bf("CST")
    bBIAS = bf("BIAS")

    rot = {}

    def nxt(name, n):
        v = rot.get(name, 0)
        rot[name] = v + 1
        return v % n

    def setup():
        sch.add("sp", lambda e: e.dma_start(out=CSTF, in_=consts_d.rearrange("c p q -> p c q")),
                writes=[bf("CSTF")], dma=bf("CSTF"))
        sch.add("sp", lambda e: e.dma_start(out=MAPS, in_=maps_d.rearrange("c p q -> p c q")),
                writes=[bf("MAPS")], dma=bf("MAPS"))
        sch.add("sp", lambda e: e.dma_start(out=TABP, in_=tab_d), writes=[bf("TABP")], dma=bf("TABP"))
        sch.add("sp", lambda e: e.dma_start(out=ESK[:, :], in_=sink_d), writes=[bf("ESK")], dma=bf("ESK"))
        sch.add("dve", lambda e: e.tensor_copy(out=CST[:, :, :], in_=CSTF),
                reads=[bf("CSTF")], writes=[bCST])
        sch.add("dve", lambda e: e.memset(ZER[:, :], 0.0), writes=[bf("ZER")])
        for b_ in range(32):
            if b_ == 15:
                continue
            sch.add("dve", lambda e, b_=b_: e.tensor_tensor(
                out=TABP[:, b_ * 16:(b_ + 1) * 16], in0=TABP[:, b_ * 16:(b_ + 1) * 16],
                in1=TABP[:, 240:256], op=ALU.subtract), reads=[bf("TABP")], writes=[bf("TABP")])
        for c in range(8):
            for j in range(2):
                r0 = j * 64
                sch.add("dve", lambda e, c=c, j=j, r0=r0: e.tensor_tensor(
                    out=ESK[r0:r0 + 64, c:c + 1], in0=ESK[r0:r0 + 64, c:c + 1],
                    in1=TABP[r0:r0 + 64, 240 + 2 * c + j:240 + 2 * c + j + 1], op=ALU.subtract),
                    reads=[bf("TABP"), bf("ESK")], writes=[bf("ESK")])
        sch.add("act", lambda e: e.activation(out=ESK[:, :], in_=ESK[:, :], func=AF.Exp),
                reads=[bf("ESK")], writes=[bf("ESK")])
        sch.add("dve", lambda e: e.tensor_scalar(out=TABP, in0=TABP, scalar1=8.0, scalar2=None,
                                                 op0=ALU.mult), reads=[bf("TABP")], writes=[bf("TABP")])
        for mi in range(4):
            eng = "dve"
            accb = bf("ACC")
            sch.add(eng, lambda e, mi=mi: e.tensor_scalar(
                out=OH[0], in0=MAPS[:, mi, :], scalar1=-1.0, scalar2=NEG, op0=ALU.is_equal, op1=ALU.mult),
                reads=[bf("MAPS")], writes=[bf("OH0")])
            for h0 in (0, 8):
                for h in range(8):
                    sch.add(eng, lambda e, h=h: e.tensor_copy(out=ACC[:, h, :], in_=OH[0]),
                            reads=[bf("OH0")], writes=[accb])
                for b_ in present[mi]:
                    sch.add(eng, lambda e, mi=mi, b_=b_: e.tensor_single_scalar(
                        out=OH[1], in_=MAPS[:, mi, :], scalar=float(b_), op=ALU.is_equal),
                        reads=[bf("MAPS")], writes=[bf("OH1")])
                    for h in range(8):
                        sch.add(eng, lambda e, h=h, b_=b_, h0=h0: e.scalar_tensor_tensor(
                            out=ACC[:, h, :], in0=OH[1],
                            scalar=TABP[:, b_ * 16 + h0 + h:b_ * 16 + h0 + h + 1],
                            in1=ACC[:, h, :], op0=ALU.mult, op1=ALU.add),
                            reads=[bf("OH1"), bf("TABP"), accb], writes=[accb])
                sch.add(eng, lambda e, mi=mi, h0=h0: e.tensor_copy(out=BIAS[:, mi, h0:h0 + 8, :], in_=ACC),
                        reads=[accb], writes=[bBIAS])
        for li, l in enumerate((0, 3)):
            lam_init = 0.8 - 0.6 * math.exp(-0.3 * l)
            sch.add("sp", lambda e, li=li: e.dma_start(out=LAMW, in_=lam_d[li]),
                    writes=[bf("LAMW")], dma=bf("LAMW"))
            sch.add("sp", lambda e, li=li: e.dma_start(out=SCC[:, li:li + 1], in_=subln_d[li]),
                    writes=[bf("SCC")], dma=bf("SCC%d" % li))
            for t in range(2):
                sch.add("dve", lambda e, t=t: e.tensor_tensor(
                    out=LAMW[:, t * 128:t * 128 + 64], in0=LAMW[:, t * 128:t * 128 + 64],
                    in1=LAMW[:, t * 128 + 64:t * 128 + 128], op=ALU.mult),
                    reads=[bf("LAMW")], writes=[bf("LAMW")])
                sch.add("dve", lambda e, t=t: e.tensor_reduce(
                    out=LAMT[:, t:t + 1], in_=LAMW[:, t * 128:t * 128 + 64], axis=mybir.AxisListType.X,
                    op=ALU.add), reads=[bf("LAMW")], writes=[bf("LAMT")])
            sch.add("act", lambda e: e.activation(out=LAMT[:, 0:2], in_=LAMT[:, 0:2], func=AF.Exp),
                    reads=[bf("LAMT")], writes=[bf("LAMT")])
            sch.add("dve", lambda e, li=li, lam_init=lam_init: e.scalar_tensor_tensor(
                out=LAMC[:, li, 0:1], in0=LAMT[:, 1:2], scalar=-lam_init, in1=LAMT[:, 0:1],
                op0=ALU.add, op1=ALU.subtract), reads=[bf("LAMT")], writes=[bf("LAMC")])
            sch.add("dve", lambda e, li=li, lam_init=lam_init: e.tensor_scalar(
                out=SCC[:, li:li + 1], in0=SCC[:, li:li + 1], scalar1=(1.0 - lam_init) * math.sqrt(128.0),
                scalar2=None, op0=ALU.mult), reads=[bf("SCC")], writes=[bf("SCC")])

    def load_weights(l, c, wset):
        kind = KINDS[l]
        W = WT[wset]
        wb = bf("WT%d" % wset)
        if kind in (0, 1):
            cols = [c * 128, 3072 + c * 128, 1024 + c * 128, 2048 + c * 128]
            for si, c0 in enumerate(cols):
                sch.add("pool", lambda e, si=si, c0=c0, W=W: e.dma_start(
                    out=W[:, :, si * 128:(si + 1) * 128],
                    in_=win_d[l][:, c0:c0 + 128].rearrange("(c p) f -> p c f", p=P)),
                    writes=[wb], dma=bf("WTd%d_%d" % (wset, si)))
        else:
            g = c // 2
            cols = [(c * 128, 128, 0), (1536 + c * 128, 128, 128), (1024 + g * 64, 64, 256),
                    (1280 + g * 64, 64, 320)]
            for si, (c0, w, o0) in enumerate(cols):
                sch.add("pool", lambda e, c0=c0, w=w, o0=o0, W=W: e.dma_start(
                    out=W[:, :, o0:o0 + w],
                    in_=win_d[l][:, c0:c0 + w].rearrange("(c p) f -> p c f", p=P)),
                    writes=[wb], dma=bf("WTd%d_%d" % (wset, si)))
        sch.add("pool", lambda e: e.dma_start(out=WO[wset][:, :], in_=wout_d[l][c * 128:(c + 1) * 128, :]),
                writes=[bf("WO%d" % wset)], dma=bf("WOd%d" % wset))

    def norm_tile(xrows_ap, nrows, col, gidx_loaded, xnT_cols):
        slot = nxt("XN", 2)
        xn = XN[slot]
        xb = bf("XN%d" % slot)
        sch.add("act", lambda e: e.activation(out=xn[0:nrows, :], in_=xrows_ap, func=AF.Square,
                                              accum_out=SS[0:nrows, col:col + 1]),
                reads=[bf("X")], writes=[xb, bf("SS")])
        sch.add("act", lambda e: e.activation(out=SS[0:nrows, col:col + 1], in_=SS[0:nrows, col:col + 1],
                                              func=AF.Ln, scale=1.0 / D, bias=EPS),
                reads=[bf("SS")], writes=[bf("SS")])
        sch.add("act", lambda e: e.activation(out=SS[0:nrows, col:col + 1], in_=SS[0:nrows, col:col + 1],
                                              func=AF.Exp, scale=-0.5),
                reads=[bf("SS")], writes=[bf("SS")])
        sch.add("dve", lambda e: e.scalar_tensor_tensor(out=xn[0:nrows, :], in0=xrows_ap,
                                                        scalar=SS[0:nrows, col:col + 1], in1=Gn[0:nrows, :],
                                                        op0=ALU.mult, op1=ALU.mult),
                reads=[bf("X"), bf("SS"), bf("Gn")], writes=[xb])
        for ch in range(NCH):
            sch.add("pe", lambda e, ch=ch: e.transpose(out=T0[:, ch * 128:ch * 128 + nrows],
                                                       in_=xn[0:nrows, ch * 128:(ch + 1) * 128],
                                                       identity=ident[0:nrows, 0:nrows]),
                    reads=[xb, bCST], writes=[bf("T0")])
        c0, c1 = xnT_cols
        sch.add("act", lambda e: e.copy(
            out=xnT[:, :, c0:c1],
            in_=T0[:, :].rearrange("p (c t) -> p c t", c=NCH)[:, :, 0:nrows]),
            reads=[bf("T0")], writes=[bf("xnT")])

    def zero_acc():
        sch.add("dve", lambda e: e.memset(SS[:, :], 0.0), writes=[bf("SS")])

    def out_proj_units(ot_slot, wset, ntok_tiles, xdst):
        units = []
        for i, (rows, xap_fn) in enumerate(xdst):
            for hf in range(2):
                def u(i=i, hf=hf, rows=rows, xap_fn=xap_fn):
                    Mx, mb = M0, bf("M0")
                    sch.add("pe", lambda e: e.matmul(Mx[0:rows, :], lhsT=OT[ot_slot][:, i * 128:i * 128 + rows],
                                                     rhs=WO[wset][:, hf * 512:(hf + 1) * 512],
                                                     start=True, stop=True),
                            reads=[bf("OT%d" % ot_slot), bf("WO%d" % wset)], writes=[mb])
                    xa = xap_fn(hf)
                    sch.add("dve", lambda e: e.tensor_tensor(out=xa, in0=Mx[0:rows, :], in1=xa, op=ALU.add),
                            reads=[mb, bf("X")], writes=[bf("X")])
                units.append(u)
        return units

    def proj_q_unit(wset, tok0, ntok, slot):
        def u():
            for ch in range(NCH):
                sch.add("pe", lambda e, ch=ch: e.matmul(M0[:, 0:ntok], lhsT=WT[wset][:, ch, 0:128],
                                                        rhs=xnT[:, ch, tok0:tok0 + ntok],
                                                        start=(ch == 0), stop=(ch == NCH - 1)),
                        reads=[bf("WT%d" % wset), bf("xnT")], writes=[bf("M0")])
            sch.add("dve", lambda e: e.tensor_copy(out=QB[slot][:, 0:ntok], in_=M0[:, 0:ntok]),
                    reads=[bf("M0")], writes=[bf("QB%d" % slot)])
        return u

    def proj_g_unit(wset, tok0, ntok, slot):
        def u():
            for ch in range(NCH):
                sch.add("pe", lambda e, ch=ch: e.matmul(M0[:, 0:ntok], lhsT=WT[wset][:, ch, 128:256],
                                                        rhs=xnT[:, ch, tok0:tok0 + ntok],
                                                        start=(ch == 0), stop=(ch == NCH - 1)),
                        reads=[bf("WT%d" % wset), bf("xnT")], writes=[bf("M0")])
            sch.add("act", lambda e: e.activation(out=GB[slot][:, 0:ntok], in_=M0[:, 0:ntok], func=AF.Silu),
                    reads=[bf("M0")], writes=[bf("GB%d" % slot)])
        return u

    def proj_kv_unit(l, c, wset, kvset, tok0, nrows, kt_cols, v_dst, out_dmas, swa, ev="act"):
        st = {}

        def u1():
            width = 128 if swa else 256
            for ch in range(NCH):
                sch.add("pe", lambda e, ch=ch: e.matmul(M0[0:nrows, 0:width],
                                                        lhsT=xnT[:, ch, tok0:tok0 + nrows],
                                                        rhs=WT[wset][:, ch, 256:256 + width],
                                                        start=(ch == 0), stop=(ch == NCH - 1)),
                        reads=[bf("WT%d" % wset), bf("xnT")], writes=[bf("M0")])
            kbs = nxt("KBT", NKBT)
            kbt = KBT[kbs]
            kbb = bf("KBT%d" % kbs)
            st["kbt"], st["kbb"] = kbt, kbb
            vb_ = v_dst[1]
            def cp(e, out, in_):
                return e.copy(out=out, in_=in_) if ev == "act" else e.tensor_copy(out=out, in_=in_)
            if swa:
                for hh in range(2):
                    sch.add(ev, lambda e, hh=hh: cp(e, kbt[0:nrows, hh * 64:(hh + 1) * 64], M0[0:nrows, 0:64]),
                            reads=[bf("M0")], writes=[kbb])
                    sch.add(ev, lambda e, hh=hh: cp(e, v_dst[0][:, hh * 64:(hh + 1) * 64], M0[0:nrows, 64:128]),
                            reads=[bf("M0")], writes=[vb_])
            else:
                sch.add(ev, lambda e: cp(e, kbt[0:nrows, :], M0[0:nrows, 0:128]),
                        reads=[bf("M0")], writes=[kbb])
                sch.add(ev, lambda e: cp(e, v_dst[0], M0[0:nrows, 128:256]),
                        reads=[bf("M0")], writes=[vb_])
            if out_dmas:
                ks = nxt("KVS", NKVS)
                kvs = KVS[ks]
                kb = bf("KVS%d" % ks)
                sch.add("dve", lambda e: e.tensor_copy(out=kvs[0:nrows, 0:width], in_=M0[0:nrows, 0:width]),
                        reads=[bf("M0")], writes=[kb])
                for (dst_ap, c0, w) in out_dmas:
                    sch.add("sp", lambda e, dst_ap=dst_ap, c0=c0, w=w: e.dma_start(out=dst_ap,
                                                                                  in_=kvs[0:nrows, c0:c0 + w]),
                            reads=[kb], dma=kb)

        def u2():
            kbt, kbb = st["kbt"], st["kbb"]
            sch.add("pe", lambda e: e.transpose(out=T0[:, 0:nrows], in_=kbt[0:nrows, :],
                                                identity=ident[0:nrows, 0:nrows]),
                    reads=[kbb, bCST], writes=[bf("T0")])
            ktt, ktc0, ktb = kt_cols
            if ev == "act":
                sch.add("act", lambda e: e.copy(out=ktt[:, ktc0:ktc0 + nrows], in_=T0[:, 0:nrows]),
                        reads=[bf("T0")], writes=[ktb])
            else:
                sch.add("dve", lambda e: e.tensor_copy(out=ktt[:, ktc0:ktc0 + nrows], in_=T0[:, 0:nrows]),
                        reads=[bf("T0")], writes=[ktb])
        return u1, u2

    def attn_da(c, qslot, NQ, acol, ktiles, bg):
        hmb = 2 * c
        items = [(kt, m) for kt in range(len(ktiles)) for m in range(2)]
        nkt = len(ktiles)

        def qk(it):
            kt, m = it
            t = ktiles[kt]
            sbk = nxt("SPS", 2)
            Sb = SPS[sbk]
            sbb = bf("SPS%d" % sbk)
            c0, nk = t["c0"], t["nk"]
            nb = len(t["bias"])
            ktt, kc0 = t["kT"]
            sch.add("pe", lambda e: e.matmul(Sb[0:nk, c0:NQ], lhsT=ktt[m * 64:(m + 1) * 64, kc0:kc0 + nk],
                                             rhs=QB[qslot][m * 64:(m + 1) * 64, c0:NQ],
                                             start=True, stop=(nb == 0), skip_group_check=True),
                    reads=[t["kTb"], bf("QB%d" % qslot)], writes=[sbb])
            for bi, (co, ncol, mi, kr) in enumerate(t["bias"]):
                sch.add("pe", lambda e, co=co, ncol=ncol, mi=mi, bi=bi: e.matmul(
                    Sb[0:nk, co:co + ncol], lhsT=ident[0:nk, 0:nk], rhs=BIAS[0:nk, mi, hmb + m, 0:ncol],
                    start=False, stop=(bi == nb - 1), skip_group_check=True),
                    reads=[bCST, bBIAS], writes=[sbb])
            return Sb, sbb

        def rest(it, Sb, sbb):
            kt, m = it
            t = ktiles[kt]
            c0, nk = t["c0"], t["nk"]
            pslot = nxt("PB", NPB)
            Pt = PB[pslot]
            pbb = bf("PB%d" % pslot)
            sch.add("act", lambda e: e.activation(out=Pt[0:nk, c0:NQ], in_=Sb[0:nk, c0:NQ], func=AF.Exp,
                                                  scale=0.125), reads=[sbb], writes=[pbb])
            sch.add("pe", lambda e: e.matmul(APS[m][:, acol + c0:acol + NQ], lhsT=t["v"], rhs=Pt[0:nk, c0:NQ],
                                             start=(kt == 0), stop=(kt == nkt - 1), skip_group_check=True),
                    reads=[t["vb"], pbb], writes=[bf("APS%d" % m)])
            sch.add("pe", lambda e: e.matmul(APS[2 + m][:, acol + c0:acol + NQ], lhsT=ones[0:nk, :],
                                             rhs=Pt[0:nk, c0:NQ],
                                             start=(kt == 0), stop=(kt == nkt - 1), skip_group_check=True),
                    reads=[bCST, pbb], writes=[bf("APS%d" % (2 + m))])

        cur = qk(items[0])
        for i, it in enumerate(items):
            nx = qk(items[i + 1]) if i + 1 < len(items) else None
            rest(it, *cur)
            cur = nx
            if bg:
                bg.pop(0)()

    def post_da_a(li, qslot, oslot, NQ, acol):
        T_a, T_b = TP[0], TP[1]
        ba, bb_ = bf("TP0"), bf("TP1")
        sl = slice(acol, acol + NQ)
        sch.add("act", lambda e: e.activation(out=T_a[:, 0:NQ], in_=APS[2][:, sl], func=AF.Ln),
                reads=[bf("APS2")], writes=[ba])
        sch.add("act", lambda e: e.activation(out=T_a[:, 0:NQ], in_=T_a[:, 0:NQ], func=AF.Exp, scale=-1.0),
                reads=[ba], writes=[ba])
        sch.add("dve", lambda e: e.tensor_tensor(out=T_a[:, 0:NQ], in0=APS[0][:, sl], in1=T_a[:, 0:NQ],
                                                 op=ALU.mult), reads=[bf("APS0"), ba], writes=[ba])
        sch.add("act", lambda e: e.activation(out=T_b[:, 0:NQ], in_=APS[3][:, sl], func=AF.Ln),
                reads=[bf("APS3")], writes=[bb_])
        sch.add("act", lambda e: e.activation(out=T_b[:, 0:NQ], in_=T_b[:, 0:NQ], func=AF.Exp, scale=-1.0),
                reads=[bb_], writes=[bb_])
        sch.add("dve", lambda e: e.tensor_tensor(out=T_b[:, 0:NQ], in0=APS[1][:, sl], in1=T_b[:, 0:NQ],
                                                 op=ALU.mult), reads=[bf("APS1"), bb_], writes=[bb_])
        sch.add("dve", lambda e: e.scalar_tensor_tensor(out=T_a[:, 0:NQ], in0=T_b[:, 0:NQ],
                                                        scalar=LAMC[:, li, 0:1], in1=T_a[:, 0:NQ],
                                                        op0=ALU.mult, op1=ALU.add),
                reads=[ba, bb_, bf("LAMC")], writes=[ba])
        sch.add("act", lambda e: e.activation(out=SQ[:, 0:NQ], in_=T_a[:, 0:NQ], func=AF.Square),
                reads=[ba], writes=[bf("SQ")])

    def post_da_b(li, qslot, oslot, NQ, acol):
        T_a, T_b = TP[0], TP[1]
        ba, bb_ = bf("TP0"), bf("TP1")
        sch.add("pe", lambda e: e.matmul(M0[:, 0:NQ], lhsT=ones, rhs=SQ[:, 0:NQ], start=True, stop=True),
                reads=[bCST, bf("SQ")], writes=[bf("M0")])
        sch.add("act", lambda e: e.activation(out=T_b[:, 0:NQ], in_=M0[:, 0:NQ], func=AF.Ln, bias=128.0 * EPS),
                reads=[bf("M0")], writes=[bb_])
        sch.add("act", lambda e: e.activation(out=T_b[:, 0:NQ], in_=T_b[:, 0:NQ], func=AF.Exp, scale=-0.5),
                reads=[bb_], writes=[bb_])
        sch.add("dve", lambda e: e.tensor_tensor(out=T_a[:, 0:NQ], in0=T_a[:, 0:NQ], in1=T_b[:, 0:NQ],
                                                 op=ALU.mult), reads=[ba, bb_], writes=[ba])
        sch.add("dve", lambda e: e.scalar_tensor_tensor(out=OT[oslot][:, 0:NQ], in0=T_a[:, 0:NQ],
                                                        scalar=SCC[:, li:li + 1], in1=GB[qslot][:, 0:NQ],
                                                        op0=ALU.mult, op1=ALU.mult),
                reads=[ba, bf("SCC"), bf("GB%d" % qslot)], writes=[bf("OT%d" % oslot)])

    def post_da(li, qslot, oslot, NQ, acol):
        post_da_a(li, qslot, oslot, NQ, acol)
        post_da_b(li, qslot, oslot, NQ, acol)

    def attn_sb(c, qslot, NQ, acol, ktiles, bg):
        nkt = len(ktiles)
        for j in range(2):
            for a in (j, 2 + j):
                sch.add("pe", lambda e, a=a: e.matmul(APS[a][:, acol:acol + NQ], lhsT=ZER[:, :],
                                                      rhs=QB[qslot][:, 0:NQ], start=True, stop=False,
                                                      skip_group_check=True),
                        reads=[bf("ZER"), bf("QB%d" % qslot)], writes=[bf("APS%d" % a)])
        items = [(kt, j) for kt in reversed(range(nkt)) for j in range(2)]

        def stage1(it):
            kt, j = it
            t = ktiles[kt]
            c0, nk = t["c0"], t["nk"]
            sbk = nxt("SPS", 2)
            Sb = SPS[sbk]
            sbb = bf("SPS%d" % sbk)
            nb = len(t["bias"])
            ktt, kc0 = t["kT"]
            sch.add("pe", lambda e: e.matmul(Sb[0:nk, c0:NQ], lhsT=ktt[j * 64:(j + 1) * 64, kc0:kc0 + nk],
                                             rhs=QB[qslot][j * 64:(j + 1) * 64, c0:NQ],
                                             start=True, stop=(nb == 0), skip_group_check=True),
                    reads=[t["kTb"], bf("QB%d" % qslot)], writes=[sbb])
            for bi, (co, ncol, mi, kr) in enumerate(t["bias"]):
                sch.add("pe", lambda e, co=co, ncol=ncol, bi=bi: e.matmul(
                    Sb[0:nk, co:co + ncol], lhsT=ident[0:nk, 0:nk], rhs=maskT[0:nk, 0:ncol],
                    start=False, stop=(bi == nb - 1), skip_group_check=True),
                    reads=[bCST], writes=[sbb])
            es_ = nxt("EW", NE)
            E = EW[es_]
            eb = bf("EW%d" % es_)
            spb = SPB[es_]
            spbb = bf("SPB%d" % es_)
            sch.add("act", lambda e: e.activation(out=E[0:nk, c0:NQ], in_=Sb[0:nk, c0:NQ], func=AF.Exp,
                                                  scale=0.125), reads=[sbb], writes=[eb])
            sch.add("act", lambda e: e.activation(out=E[0:nk, c0:NQ], in_=E[0:nk, c0:NQ], func=AF.Ln,
                                                  bias=1.0), reads=[eb], writes=[eb])
            sch.add("dve", lambda e: e.tensor_copy(out=spb[0:nk, c0:NQ], in_=E[0:nk, c0:NQ]),
                    reads=[eb], writes=[spbb])
            sch.add("dve", lambda e: e.scalar_tensor_tensor(out=E[0:nk, c0:NQ], in0=Sb[0:nk, c0:NQ],
                                                            scalar=0.125, in1=E[0:nk, c0:NQ],
                                                            op0=ALU.mult, op1=ALU.subtract),
                    reads=[sbb, eb, spbb], writes=[eb])
            return E, eb, spb, spbb

        def stage2a(it, E, eb, spb, spbb):
            kt, j = it
            t = ktiles[kt]
            c0, nk = t["c0"], t["nk"]
            rb = bf("APS%d" % j)
            sch.add("pe", lambda e: e.matmul(APS[j][:, acol + c0:acol + NQ], lhsT=triu[0:nk, :],
                                             rhs=spb[0:nk, c0:NQ], start=False, stop=False,
                                             skip_group_check=True),
                    reads=[bCST, spbb], writes=[rb])
            sch.add("dve", lambda e: e.tensor_tensor(out=E[0:nk, c0:NQ], in0=E[0:nk, c0:NQ],
                                                     in1=APS[j][0:nk, acol + c0:acol + NQ], op=ALU.subtract),
                    reads=[eb, rb], writes=[eb])
            pslot = nxt("PB", NPB)
            Ab = PB[pslot]
            pbb = bf("PB%d" % pslot)
            sch.add("act", lambda e: e.activation(out=Ab[0:nk, c0:NQ], in_=E[0:nk, c0:NQ], func=AF.Exp),
                    reads=[eb], writes=[pbb])
            return Ab, pbb

        def stage2b(it, spb, spbb, Ab, pbb):
            kt, j = it
            t = ktiles[kt]
            c0, nk = t["c0"], t["nk"]
            sch.add("pe", lambda e: e.matmul(APS[j][:, acol + c0:acol + NQ], lhsT=tril[0:nk, :],
                                             rhs=spb[0:nk, c0:NQ], start=False, stop=False,
                                             skip_group_check=True),
                    reads=[bCST, spbb], writes=[bf("APS%d" % j)])
            sch.add("pe", lambda e: e.matmul(APS[2 + j][:, acol + c0:acol + NQ], lhsT=t["v"],
                                             rhs=Ab[0:nk, c0:NQ], start=False, stop=False,
                                             skip_group_check=True),
                    reads=[t["vb"], pbb], writes=[bf("APS%d" % (2 + j))])

        n = len(items)
        LA = 3
        s1 = {}
        for i in range(min(LA, n)):
            s1[i] = stage1(items[i])
        prev = None
        for i, it in enumerate(items):
            a = stage2a(it, *s1[i])
            if prev is not None:
                stage2b(*prev)
            prev = (it, s1[i][2], s1[i][3], a[0], a[1])
            if i + LA < n:
                s1[i + LA] = stage1(items[i + LA])
            del s1[i]
            if bg:
                bg.pop(0)()
        stage2b(*prev)

    def post_sb(qslot, oslot, NQ, acol):
        for j in range(2):
            r = slice(j * 64, (j + 1) * 64)
            sch.add("dve", lambda e, j=j, r=r: e.tensor_tensor(out=OT[oslot][r, 0:NQ],
                                                               in0=APS[2 + j][r, acol:acol + NQ],
                                                               in1=GB[qslot][r, 0:NQ], op=ALU.mult),
                    reads=[bf("APS%d" % (2 + j)), bf("GB%d" % qslot)], writes=[bf("OT%d" % oslot)])

    def attn_swa(c, qslot, qtiles, bg):
        items = []
        for qi, qt in enumerate(qtiles):
            for j in range(2):
                for ki in range(len(qt["ktiles"])):
                    items.append((qi, j, ki))

        def qk(it):
            qi, j, ki = it
            qt = qtiles[qi]
            t = qt["ktiles"][ki]
            nk, nq, qc0 = t["nk"], qt["nq"], qt["qc0"]
            sbk = nxt("SPS", 2)
            Sb = SPS[sbk]
            sbb = bf("SPS%d" % sbk)
            ktt, kc0 = t["kT"]
            sch.add("pe", lambda e: e.matmul(Sb[0:nk, 0:nq], lhsT=ktt[j * 64:(j + 1) * 64, kc0:kc0 + nk],
                                             rhs=QB[qslot][j * 64:(j + 1) * 64, qc0:qc0 + nq],
                                             start=True, stop=False, skip_group_check=True),
                    reads=[t["kTb"], bf("QB%d" % qslot)], writes=[sbb])
            sch.add("pe", lambda e: e.matmul(Sb[0:nk, 0:nq], lhsT=ident[0:nk, 0:nk],
                                             rhs=BIAS[0:nk, t["mi"], 2 * c + j, 0:nq],
                                             start=False, stop=True, skip_group_check=True),
                    reads=[bCST, bBIAS], writes=[sbb])
            return Sb, sbb

        def rest(it, Sb, sbb):
            qi, j, ki = it
            qt = qtiles[qi]
            t = qt["ktiles"][ki]
            nk, nq, ac = t["nk"], qt["nq"], qt["acol"]
            nki = len(qt["ktiles"])
            pslot = nxt("PB", NPB)
            Pt = PB[pslot]
            pbb = bf("PB%d" % pslot)
            sch.add("act", lambda e: e.activation(out=Pt[0:nk, 0:nq], in_=Sb[0:nk, 0:nq], func=AF.Exp,
                                                  scale=0.125), reads=[sbb], writes=[pbb])
            sch.add("pe", lambda e: e.matmul(APS[j][:, ac:ac + nq], lhsT=t["v"], rhs=Pt[0:nk, 0:nq],
                                             start=(ki == 0), stop=(ki == nki - 1), skip_group_check=True),
                    reads=[t["vb"], pbb], writes=[bf("APS%d" % j)])
            sch.add("pe", lambda e: e.matmul(APS[2 + j][:, ac:ac + nq], lhsT=ones[0:nk, :], rhs=Pt[0:nk, 0:nq],
                                             start=(ki == 0), stop=(ki == nki - 1), skip_group_check=True),
                    reads=[bCST, pbb], writes=[bf("APS%d" % (2 + j))])

        cur = qk(items[0])
        for i, it in enumerate(items):
            nx = qk(items[i + 1]) if i + 1 < len(items) else None
            rest(it, *cur)
            cur = nx
            if bg:
                bg.pop(0)()

    def attn_swa_p(c, qslot, qtiles, bg):
        def qk(qi):
            qt = qtiles[qi]
            sbk = nxt("SPS", 2)
            Sb = SPS[sbk]
            sbb = bf("SPS%d" % sbk)
            nq, qc0 = qt["nq"], qt["qc0"]
            blk = 0
            for j in range(2):
                for ki, t in enumerate(qt["ktiles"]):
                    off = blk * P
                    ktt, kc0 = t["kT"]
                    sch.add("pe", lambda e, j=j, ktt=ktt, kc0=kc0, off=off: e.matmul(
                        Sb[:, off:off + nq], lhsT=ktt[j * 64:(j + 1) * 64, kc0:kc0 + P],
                        rhs=QB[qslot][j * 64:(j + 1) * 64, qc0:qc0 + nq],
                        start=True, stop=False, skip_group_check=True),
                        reads=[t["kTb"], bf("QB%d" % qslot)], writes=[sbb])
                    sch.add("pe", lambda e, j=j, t=t, off=off: e.matmul(
                        Sb[:, off:off + nq], lhsT=ident, rhs=BIAS[:, t["mi"], 2 * c + j, 0:nq],
                        start=False, stop=True, skip_group_check=True),
                        reads=[bCST, bBIAS], writes=[sbb])
                    blk += 1
            return Sb, sbb, blk

        def rest(qi, Sb, sbb, nblk):
            qt = qtiles[qi]
            nq, ac = qt["nq"], qt["acol"]
            nki = len(qt["ktiles"])
            pslot = nxt("PB", NPB)
            Pt = PB[pslot]
            pbb = bf("PB%d" % pslot)
            sch.add("act", lambda e: e.activation(out=Pt[:, 0:nblk * P], in_=Sb[:, 0:nblk * P], func=AF.Exp,
                                                  scale=0.125), reads=[sbb], writes=[pbb])
            blk = 0
            for j in range(2):
                for ki, t in enumerate(qt["ktiles"]):
                    off = blk * P
                    sch.add("pe", lambda e, j=j, t=t, off=off, ki=ki: e.matmul(
                        APS[j][:, ac:ac + nq], lhsT=t["v"], rhs=Pt[:, off:off + nq],
                        start=(ki == 0), stop=(ki == nki - 1), skip_group_check=True),
                        reads=[t["vb"], pbb], writes=[bf("APS%d" % j)])
                    sch.add("pe", lambda e, j=j, off=off, ki=ki: e.matmul(
                        APS[2 + j][:, ac:ac + nq], lhsT=ones, rhs=Pt[:, off:off + nq],
                        start=(ki == 0), stop=(ki == nki - 1), skip_group_check=True),
                        reads=[bCST, pbb], writes=[bf("APS%d" % (2 + j))])
                    blk += 1

        nqt = len(qtiles)
        cur = qk(0)
        for qi in range(nqt):
            nx = qk(qi + 1) if qi + 1 < nqt else None
            rest(qi, *cur)
            cur = nx
            for _ in range(4):
                if bg:
                    bg.pop(0)()

    def post_swa(c, qslot, oslot, NQ, acol):
        T_a = TP[0]
        ba = bf("TP0")
        for j in range(2):
            r = slice(j * 64, (j + 1) * 64)
            sch.add("act", lambda e, j=j, r=r: e.activation(out=T_a[r, 0:NQ], in_=APS[2 + j][r, acol:acol + NQ],
                                                            func=AF.Ln, bias=ESK[r, c:c + 1]),
                    reads=[bf("APS%d" % (2 + j)), bf("ESK")], writes=[ba])
            sch.add("act", lambda e, r=r: e.activation(out=T_a[r, 0:NQ], in_=T_a[r, 0:NQ], func=AF.Exp, scale=-1.0),
                    reads=[ba], writes=[ba])
            sch.add("dve", lambda e, j=j, r=r: e.tensor_tensor(out=T_a[r, 0:NQ], in0=APS[j][r, acol:acol + NQ],
                                                               in1=T_a[r, 0:NQ], op=ALU.mult),
                    reads=[bf("APS%d" % j), ba], writes=[ba])
            sch.add("dve", lambda e, r=r: e.tensor_tensor(out=OT[oslot][r, 0:NQ], in0=T_a[r, 0:NQ],
                                                          in1=GB[qslot][r, 0:NQ], op=ALU.mult),
                    reads=[ba, bf("GB%d" % qslot)], writes=[bf("OT%d" % oslot)])

    def final_norm_tile(xrows_ap, nrows, col, dst_ap):
        slot = nxt("XO", 2)
        xo = XO[slot]
        xob = bf("XO%d" % slot)
        sch.add("act", lambda e: e.activation(out=xo[0:nrows, :], in_=xrows_ap, func=AF.Square,
                                              accum_out=SS[0:nrows, col:col + 1]),
                reads=[bf("X")], writes=[xob, bf("SS")])
        sch.add("act", lambda e: e.activation(out=SS[0:nrows, col:col + 1], in_=SS[0:nrows, col:col + 1],
                                              func=AF.Ln, scale=1.0 / D, bias=EPS),
                reads=[bf("SS")], writes=[bf("SS")])
        sch.add("act", lambda e: e.activation(out=SS[0:nrows, col:col + 1], in_=SS[0:nrows, col:col + 1],
                                              func=AF.Exp, scale=-0.5),
                reads=[bf("SS")], writes=[bf("SS")])
        sch.add("dve", lambda e: e.scalar_tensor_tensor(out=xo[0:nrows, :], in0=xrows_ap,
                                                        scalar=SS[0:nrows, col:col + 1], in1=Gn[0:nrows, :],
                                                        op0=ALU.mult, op1=ALU.mult),
                reads=[bf("X"), bf("SS"), bf("Gn")], writes=[xob])
        sch.add("sp", lambda e: e.dma_start(out=dst_ap, in_=xo[0:nrows, :]), reads=[xob], dma=xob)

    def load_gain(idx):
        sch.add("sp", lambda e: e.dma_start(out=Gn[:, :], in_=gain_d[idx]), writes=[bf("Gn")], dma=bf("Gn"))

    def prompt_seq(s):
        sch.add("sp", lambda e: e.dma_start(out=X[:, :, :], in_=xp_d[s].rearrange("(n p) d -> p n d", p=P)),
                writes=[bf("X")], dma=bf("Xd"))
        for l in range(4):
            kind = KINDS[l]
            li = 0 if l == 0 else 1
            swa = kind == 2
            load_gain(l)
            load_weights(l, 0, 0)
            zero_acc()
            for n in range(NT):
                norm_tile(X[:, n, :], P, n, None, (n * P, (n + 1) * P))

            def kvset_of(c):
                return (c // 2) % 2 if swa else c % 2

            def kv_units(c):
                if swa and c % 2 == 1:
                    return []
                kvset = kvset_of(c)
                wset = c % 2
                us = []
                for n in range(NT):
                    outs = []
                    if swa:
                        g = c // 2
                        if n == NT - 1:
                            outs = [(kp_d[l][s, :, g * 64:(g + 1) * 64], 0, 64),
                                    (vp_d[l][s, :, g * 64:(g + 1) * 64], 64, 64)]
                    else:
                        outs = [(kp_d[l][s, n * P:(n + 1) * P, c * 128:(c + 1) * 128], 0, 128),
                                (vp_d[l][s, n * P:(n + 1) * P, c * 128:(c + 1) * 128], 128, 128)]
                    us.append(proj_kv_unit(l, c, wset, kvset, n * P, P,
                                           (KT[kvset], n * P, bf("KT%d" % kvset)),
                                           (VB[kvset][:, n, :], bf("VB%d" % kvset)), outs, swa,
                                           ev=("act" if kind == 1 else "dve")))
                flat = []
                for i_, (a_, b_) in enumerate(us):
                    flat.append(a_)
                    if i_ >= 1:
                        flat.append(us[i_ - 1][1])
                flat.append(us[-1][1])
                return flat

            for u in kv_units(0):
                u()
            qslots = {}

            def qg_units(c, qb):
                slot = nxt("QBslot", NQS)
                qslots[(c, qb)] = slot
                return [proj_q_unit(c % 2, qb * 512, 512, slot), proj_g_unit(c % 2, qb * 512, 512, slot)]

            for u in qg_units(0, 0):
                u()
            pend_out = []
            ku_cache = {}
            blocks = [(c, qb) for c in range(8) for qb in range(NQB)]
            for bi_, (c, qb) in enumerate(blocks):
                wset = c % 2
                kvset = kvset_of(c)
                bg = list(pend_out)
                pend_out = []
                if qb == min(1, NQB - 1) and c + 1 < 8:
                    if NQB == 1:
                        for u in bg:
                            u()
                        bg = []
                    load_weights(l, c + 1, (c + 1) % 2)
                if bi_ + 1 < len(blocks):
                    bg += qg_units(*blocks[bi_ + 1]) if blocks[bi_ + 1][0] == c else []
                if c + 1 < 8 and qb >= min(1, NQB - 1):
                    if (c + 1) not in ku_cache:
                        ku_cache[c + 1] = kv_units(c + 1)
                    ku = ku_cache[c + 1]
                    nshare = max(1, NQB - 1)
                    idx = qb - min(1, NQB - 1)
                    per = (len(ku) + nshare - 1) // nshare
                    bg += ku[idx * per:(idx + 1) * per]
                    if qb == NQB - 1:
                        bg += qg_units(c + 1, 0)
                qslot = qslots[(c, qb)]
                oslot = nxt("OT", 2)
                ktb, vbb = bf("KT%d" % kvset), bf("VB%d" % kvset)
                if kind in (0, 1):
                    ktiles = []
                    for kt in range(4 * qb + 4):
                        c0 = max(0, kt - 4 * qb) * P
                        bias = []
                        for i in range(4):
                            qt = 4 * qb + i
                            if kind == 0:
                                if kt == qt:
                                    bias.append((i * P, P, 0, P))
                                elif kt == qt - 1:
                                    bias.append((i * P, P, 1, P))
                            else:
                                if kt == qt:
                                    bias.append((i * P, P, 0, P))
                        ktiles.append(dict(kT=(KT[kvset], kt * P), kTb=ktb, v=VB[kvset][:, kt, :], vb=vbb,
                                           nk=P, c0=c0, bias=bias))
                    if kind == 0:
                        attn_da(c, qslot, 512, 0, ktiles, bg)
                    else:
                        attn_sb(c, qslot, 512, 0, ktiles, bg)
                else:
                    qtiles = []
                    for i in range(4):
                        qt = 4 * qb + i
                        kts = []
                        if qt > 0:
                            kts.append(dict(kT=(KT[kvset], (qt - 1) * P), kTb=ktb, v=VB[kvset][:, qt - 1, :],
                                            vb=vbb, nk=P, mi=3))
                        kts.append(dict(kT=(KT[kvset], qt * P), kTb=ktb, v=VB[kvset][:, qt, :], vb=vbb,
                                        nk=P, mi=2))
                        qtiles.append(dict(qc0=i * P, nq=P, acol=i * P, ktiles=kts))
                    attn_swa_p(c, qslot, qtiles, bg)
                pre = []
                if kind == 0:
                    post_da_a(li, qslot, oslot, 512, 0)
                    pre = [lambda li=li, qslot=qslot, oslot=oslot: post_da_b(li, qslot, oslot, 512, 0)]
                elif kind == 1:
                    post_sb(qslot, oslot, 512, 0)
                else:
                    post_swa(c, qslot, oslot, 512, 0)
                for u in bg:
                    u()
                xdst = [(P, (lambda hf, i=i, qb=qb: X[:, 4 * qb + i, hf * 512:(hf + 1) * 512])) for i in range(4)]
                pend_out = pre + out_proj_units(oslot, wset, 4, xdst)
            for u in pend_out:
                u()
        load_gain(4)
        zero_acc()
        for n in range(NT):
            final_norm_tile(X[:, n, :], P, n, yp_d[s, n * P:(n + 1) * P, :])

    def sample_pass():
        sch.add("sp", lambda e: e.dma_start(out=X[0:TS, 0, :], in_=xs_d), writes=[bf("X")], dma=bf("Xd"))
        for l in range(4):
            kind = KINDS[l]
            li = 0 if l == 0 else 1
            swa = kind == 2
            load_gain(l)
            zero_acc()
            norm_tile(X[0:TS, 0, :], TS, 0, None, (0, TS))
            for c in range(8):
                wset = c % 2
                load_weights(l, c, wset)
                slot = nxt("QBslot", NQS)
                proj_q_unit(wset, 0, TS, slot)()
                proj_g_unit(wset, 0, TS, slot)()
                oslot = nxt("OT", 2)
                for b in range(NSEQ):
                    st = nxt("SKV", 2)
                    ktb, vbb = bf("KT%d" % st), bf("VB%d" % st)
                    ktnb, vnb = bf("KTN%d" % st), bf("VN%d" % st)
                    if swa:
                        g = c // 2
                        for hh in range(2):
                            sch.add("pool", lambda e, hh=hh, st=st, g=g, b=b, l=l: e.dma_start(
                                out=KC[st][:, 0, hh * 64:(hh + 1) * 64], in_=ck_d[l][b, :, g * 64:(g + 1) * 64]),
                                writes=[bf("KC%d" % st)], dma=bf("KCd%d_%d" % (st, hh)))
                            sch.add("pool", lambda e, hh=hh, st=st, g=g, b=b, l=l: e.dma_start(
                                out=VB[st][:, 0, hh * 64:(hh + 1) * 64], in_=cv_d[l][b, :, g * 64:(g + 1) * 64]),
                                writes=[vbb], dma=bf("VBd%d_%d" % (st, hh)))
                        ncache = 1
                    else:
                        sch.add("pool", lambda e, st=st, b=b, l=l, c=c: e.dma_start(
                            out=KC[st][:, :, :],
                            in_=ck_d[l][b, :, c * 128:(c + 1) * 128].rearrange("(n p) f -> p n f", p=P)),
                            writes=[bf("KC%d" % st)], dma=bf("KCd%d_0" % st))
                        sch.add("pool", lambda e, st=st, b=b, l=l, c=c: e.dma_start(
                            out=VB[st][:, 0:NPT, :],
                            in_=cv_d[l][b, :, c * 128:(c + 1) * 128].rearrange("(n p) f -> p n f", p=P)),
                            writes=[vbb], dma=bf("VBd%d_0" % st))
                        ncache = NPT
                    for n in range(ncache):
                        sch.add("pe", lambda e, n=n, st=st: e.transpose(out=T0[:, n * P:(n + 1) * P],
                                                                        in_=KC[st][:, n, :], identity=ident),
                                reads=[bf("KC%d" % st), bCST], writes=[bf("T0")])
                    sch.add("act", lambda e, st=st, ncache=ncache: e.copy(out=KT[st][:, 0:ncache * P],
                                                                          in_=T0[:, 0:ncache * P]),
                            reads=[bf("T0")], writes=[ktb])
                    if swa:
                        g = c // 2
                        outs = []
                        if c % 2 == 0:
                            outs = [(ks_d[l][b, P - T_DEC:P, g * 64:(g + 1) * 64], 0, 64),
                                    (vs_d[l][b, P - T_DEC:P, g * 64:(g + 1) * 64], 64, 64)]
                    else:
                        outs = [(ks_d[l][b * T_DEC:(b + 1) * T_DEC, c * 128:(c + 1) * 128], 0, 128),
                                (vs_d[l][b * T_DEC:(b + 1) * T_DEC, c * 128:(c + 1) * 128], 128, 128)]
                    for u_ in proj_kv_unit(l, c, wset, st, b * T_DEC, T_DEC, (KTN[st], 0, ktnb),
                                           (VN[st][:, :], vnb), outs, swa):
                        u_()
                    acol = b * T_DEC
                    qc0 = b * T_DEC
                    if kind in (0, 1):
                        ktiles = []
                        for kt in range(NPT):
                            bias = []
                            if kind == 0 and kt == NPT - 1:
                                bias = [(0, T_DEC, 1, P)]
                            ktiles.append(dict(kT=(KT[st], kt * P), kTb=ktb, v=VB[st][:, kt, :], vb=vbb,
                                               nk=P, c0=0, bias=bias))
                        ktiles.append(dict(kT=(KTN[st], 0), kTb=ktnb, v=VN[st][:, :], vb=vnb, nk=T_DEC, c0=0,
                                           bias=[(0, T_DEC, 0, T_DEC)]))
                        if kind == 0:
                            attn_da_s(c, slot, T_DEC, acol, qc0, ktiles)
                        else:
                            attn_sb_s(c, slot, T_DEC, acol, qc0, ktiles)
                    else:
                        kts = [dict(kT=(KT[st], 0), kTb=ktb, v=VB[st][:, 0, :], vb=vbb, nk=P, mi=3),
                               dict(kT=(KTN[st], 0), kTb=ktnb, v=VN[st][:, :], vb=vnb, nk=T_DEC, mi=2)]
                        attn_swa(c, slot, [dict(qc0=qc0, nq=T_DEC, acol=acol, ktiles=kts)], None)
                if kind == 0:
                    post_da(li, slot, oslot, TS, 0)
                elif kind == 1:
                    post_sb(slot, oslot, TS, 0)
                else:
                    post_swa(c, slot, oslot, TS, 0)
                xdst = [(TS, (lambda hf: X[0:TS, 0, hf * 512:(hf + 1) * 512]))]
                for u in out_proj_units(oslot, wset, 1, xdst):
                    u()
            if swa:
                for b in range(NSEQ):
                    sch.add("sp", lambda e, b=b, l=l: e.dma_start(out=ks_d[l][b, 0:P - T_DEC, :],
                                                             in_=ck_d[l][b, T_DEC:P, :]), dma=bf("d2d"))
                    sch.add("sp", lambda e, b=b, l=l: e.dma_start(out=vs_d[l][b, 0:P - T_DEC, :],
                                                             in_=cv_d[l][b, T_DEC:P, :]), dma=bf("d2d"))
        load_gain(4)
        zero_acc()
        final_norm_tile(X[0:TS, 0, :], TS, 0, ys_d[:, :])

    def attn_da_s(c, qslot, NQ, acol, qc0, ktiles):
        hmb = 2 * c
        nkt = len(ktiles)
        for kt in range(nkt):
            for m in range(2):
                t = ktiles[kt]
                nk = t["nk"]
                sbk = nxt("SPS", 2)
                Sb = SPS[sbk]
                sbb = bf("SPS%d" % sbk)
                nb = len(t["bias"])
                ktt, kc0 = t["kT"]
                sch.add("pe", lambda e, Sb=Sb, nk=nk, m=m, ktt=ktt, kc0=kc0, nb=nb: e.matmul(
                    Sb[0:nk, 0:NQ], lhsT=ktt[m * 64:(m + 1) * 64, kc0:kc0 + nk],
                    rhs=QB[qslot][m * 64:(m + 1) * 64, qc0:qc0 + NQ], start=True, stop=(nb == 0),
                    skip_group_check=True), reads=[t["kTb"], bf("QB%d" % qslot)], writes=[sbb])
                for bi, (co, ncol, mi, kr) in enumerate(t["bias"]):
                    sch.add("pe", lambda e, Sb=Sb, nk=nk, m=m, mi=mi, bi=bi, nb=nb: e.matmul(
                        Sb[0:nk, 0:NQ], lhsT=ident[0:nk, 0:nk], rhs=BIAS[0:nk, mi, hmb + m, 0:NQ],
                        start=False, stop=(bi == nb - 1), skip_group_check=True),
                        reads=[bCST, bBIAS], writes=[sbb])
                pslot = nxt("PB", NPB)
                Pt = PB[pslot]
                pbb = bf("PB%d" % pslot)
                sch.add("act", lambda e, Pt=Pt, Sb=Sb, nk=nk: e.activation(
                    out=Pt[0:nk, 0:NQ], in_=Sb[0:nk, 0:NQ], func=AF.Exp, scale=0.125),
                    reads=[sbb], writes=[pbb])
                sch.add("pe", lambda e, Pt=Pt, nk=nk, m=m, t=t, kt=kt: e.matmul(
                    APS[m][:, acol:acol + NQ], lhsT=t["v"], rhs=Pt[0:nk, 0:NQ],
                    start=(kt == 0), stop=(kt == nkt - 1), skip_group_check=True),
                    reads=[t["vb"], pbb], writes=[bf("APS%d" % m)])
                sch.add("pe", lambda e, Pt=Pt, nk=nk, m=m, kt=kt: e.matmul(
                    APS[2 + m][:, acol:acol + NQ], lhsT=ones[0:nk, :], rhs=Pt[0:nk, 0:NQ],
                    start=(kt == 0), stop=(kt == nkt - 1), skip_group_check=True),
                    reads=[bCST, pbb], writes=[bf("APS%d" % (2 + m))])

    def attn_sb_s(c, qslot, NQ, acol, qc0, ktiles):
        nkt = len(ktiles)
        for j in range(2):
            for a in (j, 2 + j):
                sch.add("pe", lambda e, a=a: e.matmul(APS[a][:, acol:acol + NQ], lhsT=ZER[:, :],
                                                      rhs=QB[qslot][:, qc0:qc0 + NQ], start=True, stop=False,
                                                      skip_group_check=True),
                        reads=[bf("ZER"), bf("QB%d" % qslot)], writes=[bf("APS%d" % a)])
        for kt in reversed(range(nkt)):
            for j in range(2):
                t = ktiles[kt]
                nk = t["nk"]
                sbk = nxt("SPS", 2)
                Sb = SPS[sbk]
                sbb = bf("SPS%d" % sbk)
                nb = len(t["bias"])
                ktt, kc0 = t["kT"]
                sch.add("pe", lambda e, Sb=Sb, nk=nk, j=j, ktt=ktt, kc0=kc0, nb=nb: e.matmul(
                    Sb[0:nk, 0:NQ], lhsT=ktt[j * 64:(j + 1) * 64, kc0:kc0 + nk],
                    rhs=QB[qslot][j * 64:(j + 1) * 64, qc0:qc0 + NQ], start=True, stop=(nb == 0),
                    skip_group_check=True), reads=[t["kTb"], bf("QB%d" % qslot)], writes=[sbb])
                for bi, (co, ncol, mi, kr) in enumerate(t["bias"]):
                    sch.add("pe", lambda e, Sb=Sb, nk=nk, bi=bi, nb=nb: e.matmul(
                        Sb[0:nk, 0:NQ], lhsT=ident[0:nk, 0:nk], rhs=maskT[0:nk, 0:NQ],
                        start=False, stop=(bi == nb - 1), skip_group_check=True),
                        reads=[bCST], writes=[sbb])
                es_ = nxt("EW", NE)
                E = EW[es_]
                eb = bf("EW%d" % es_)
                spb = SPB[es_]
                spbb = bf("SPB%d" % es_)
                rb = bf("APS%d" % j)
                sch.add("act", lambda e, E=E, Sb=Sb, nk=nk: e.activation(
                    out=E[0:nk, 0:NQ], in_=Sb[0:nk, 0:NQ], func=AF.Exp, scale=0.125), reads=[sbb], writes=[eb])
                sch.add("act", lambda e, E=E, nk=nk: e.activation(
                    out=E[0:nk, 0:NQ], in_=E[0:nk, 0:NQ], func=AF.Ln, bias=1.0), reads=[eb], writes=[eb])
                sch.add("pool", lambda e, E=E, spb=spb, nk=nk: e.tensor_copy(out=spb[0:nk, 0:NQ], in_=E[0:nk, 0:NQ]),
                        reads=[eb], writes=[spbb])
                sch.add("dve", lambda e, E=E, Sb=Sb, nk=nk: e.scalar_tensor_tensor(
                    out=E[0:nk, 0:NQ], in0=Sb[0:nk, 0:NQ], scalar=0.125, in1=E[0:nk, 0:NQ],
                    op0=ALU.mult, op1=ALU.subtract), reads=[sbb, eb, spbb], writes=[eb])
                sch.add("pe", lambda e, spb=spb, nk=nk, j=j: e.matmul(
                    APS[j][:, acol:acol + NQ], lhsT=triu[0:nk, :], rhs=spb[0:nk, 0:NQ], start=False, stop=False,
                    skip_group_check=True), reads=[bCST, spbb], writes=[rb])
                sch.add("dve", lambda e, E=E, nk=nk, j=j: e.tensor_tensor(
                    out=E[0:nk, 0:NQ], in0=E[0:nk, 0:NQ], in1=APS[j][0:nk, acol:acol + NQ], op=ALU.subtract),
                    reads=[eb, rb], writes=[eb])
                pslot = nxt("PB", NPB)
                Ab = PB[pslot]
                pbb = bf("PB%d" % pslot)
                sch.add("act", lambda e, Ab=Ab, E=E, nk=nk: e.activation(
                    out=Ab[0:nk, 0:NQ], in_=E[0:nk, 0:NQ], func=AF.Exp), reads=[eb], writes=[pbb])
                sch.add("pe", lambda e, spb=spb, nk=nk, j=j: e.matmul(
                    APS[j][:, acol:acol + NQ], lhsT=tril[0:nk, :], rhs=spb[0:nk, 0:NQ], start=False, stop=False,
                    skip_group_check=True), reads=[bCST, spbb], writes=[rb])
                sch.add("pe", lambda e, Ab=Ab, nk=nk, j=j, t=t: e.matmul(
                    APS[2 + j][:, acol:acol + NQ], lhsT=t["v"], rhs=Ab[0:nk, 0:NQ], start=False, stop=False,
                    skip_group_check=True), reads=[t["vb"], pbb], writes=[bf("APS%d" % (2 + j))])

    setup()
    for s in range(NSEQ):
        prompt_seq(s)
    sample_pass()
    fin = sch.add("sp", lambda e: e.nop(), reads=[], writes=[])
    for b in sch.dma_bufs:
        fin.waits.append(("dma", b, b.cnt))
    sch.emit(nc, es)
    es.close()
    return nc


_NC_CACHE = {}


def _get_nc(cfg):
    key = (cfg["NSEQ"], cfg["S"], cfg["PAST"])
    if key not in _NC_CACHE:
        _NC_CACHE[key] = build(cfg)
    return _NC_CACHE[key]


def run(cfg, inputs):
    NSEQ, S, PAST, NCORES = cfg["NSEQ"], cfg["S"], cfg["PAST"], cfg["NCORES"]
    nc = _get_nc(cfg)
    f = lambda a: np.ascontiguousarray(np.asarray(a, dtype=np.float32))
    maps_np, consts_np, _ = static_tables()
    gain = np.stack([np.broadcast_to(f(inputs[k])[None, :], (P, D)) for k in
                     ("norm_0", "norm_1", "norm_2", "norm_3", "final_norm")])
    tab = np.broadcast_to(f(inputs["rel_bias_table"]).reshape(1, 512), (P, 512))
    lam = np.stack([np.broadcast_to(f(inputs["da_lambda_%d" % l]).reshape(1, 256), (P, 256)) for l in (0, 3)])
    subln = np.stack([f(inputs["da_subln_%d" % l]).reshape(P, 1) for l in (0, 3)])
    sk = f(inputs["sw_sinks_2"])
    sink = np.zeros((P, 8), np.float32)
    for c in range(8):
        sink[:64, c] = sk[2 * c]
        sink[64:, c] = sk[2 * c + 1]
    shared = dict(gain=f(gain), tab=f(tab), lam=f(lam), subln=f(subln), sink=sink, maps=maps_np, consts=consts_np)
    for l in range(4):
        shared["win%d" % l] = f(inputs["w_in_%d" % l])
        shared["wout%d" % l] = f(inputs["w_out_%d" % l])
    in_maps = []
    for core in range(NCORES):
        sl = slice(core * NSEQ, (core + 1) * NSEQ)
        m = dict(shared)
        m["xp"] = f(inputs["x_prompt"][sl])
        m["xs"] = f(inputs["x_sample"][sl]).reshape(NSEQ * T_DEC, D)
        for l in range(4):
            ck = f(inputs["cache_k_%d" % l][sl])
            cv = f(inputs["cache_v_%d" % l][sl])
            m["ck%d" % l] = ck.reshape(NSEQ, ck.shape[1], -1)
            m["cv%d" % l] = cv.reshape(NSEQ, cv.shape[1], -1)
        in_maps.append(m)
    res = run_bass_kernel_spmd(nc, in_maps, core_ids=list(range(NCORES)))
    R = res.results
    cat = lambda name: np.concatenate([np.asarray(r[name]) for r in R], axis=0)
    Bt = NSEQ * NCORES
    outs = [cat("yp").reshape(Bt, S, D), cat("ys").reshape(Bt, T_DEC, D)]
    for l in range(4):
        kind = KINDS[l]
        if kind == 0:
            outs += [cat("kp%d" % l).reshape(Bt, S, 8, 2, 64), cat("vp%d" % l).reshape(Bt, S, 8, 128),
                     cat("ks%d" % l).reshape(Bt, T_DEC, 8, 2, 64), cat("vs%d" % l).reshape(Bt, T_DEC, 8, 128)]
        elif kind == 1:
            outs += [cat("kp%d" % l).reshape(Bt, S, 16, 64), cat("vp%d" % l).reshape(Bt, S, 16, 64),
                     cat("ks%d" % l).reshape(Bt, T_DEC, 16, 64), cat("vs%d" % l).reshape(Bt, T_DEC, 16, 64)]
        else:
            outs += [cat("kp%d" % l).reshape(Bt, P, 4, 64), cat("vp%d" % l).reshape(Bt, P, 4, 64),
                     cat("ks%d" % l).reshape(Bt, P, 4, 64), cat("vs%d" % l).reshape(Bt, P, 4, 64)]
    return tuple(np.ascontiguousarray(o, dtype=np.float32) for o in outs)


def kernel(x_prompt, x_sample, cache_k_0, cache_v_0, cache_k_1, cache_v_1, cache_k_2, cache_v_2,
           cache_k_3, cache_v_3, rel_bias_table,
           norm_0, w_in_0, w_out_0, da_lambda_0, da_subln_0,
           norm_1, w_in_1, w_out_1,
           norm_2, w_in_2, w_out_2, sw_sinks_2,
           norm_3, w_in_3, w_out_3, da_lambda_3, da_subln_3,
           final_norm):
    inputs = {
        "x_prompt": x_prompt, "x_sample": x_sample,
        "cache_k_0": cache_k_0, "cache_v_0": cache_v_0, "cache_k_1": cache_k_1, "cache_v_1": cache_v_1,
        "cache_k_2": cache_k_2, "cache_v_2": cache_v_2, "cache_k_3": cache_k_3, "cache_v_3": cache_v_3,
        "rel_bias_table": rel_bias_table,
        "norm_0": norm_0, "w_in_0": w_in_0, "w_out_0": w_out_0, "da_lambda_0": da_lambda_0, "da_subln_0": da_subln_0,
        "norm_1": norm_1, "w_in_1": w_in_1, "w_out_1": w_out_1,
        "norm_2": norm_2, "w_in_2": w_in_2, "w_out_2": w_out_2, "sw_sinks_2": sw_sinks_2,
        "norm_3": norm_3, "w_in_3": w_in_3, "w_out_3": w_out_3, "da_lambda_3": da_lambda_3, "da_subln_3": da_subln_3,
        "final_norm": final_norm,
    }
    return run(FULL_CFG, inputs)
```

```python
import math
from contextlib import ExitStack

import numpy as np
import concourse.bass as bass
import concourse.mybir as mybir
from concourse.bass_utils import run_bass_kernel_spmd

F32 = mybir.dt.float32
BF16 = mybir.dt.bfloat16
ALU = mybir.AluOpType
AF = mybir.ActivationFunctionType

D = 1024
P = 128
NCH = 8
EPS = 1e-6
NEG = -240000.0
T_DEC = 16
KINDS = (0, 1, 2, 0)
IN_W = (4096, 4096, 2560)

FULL_CFG = dict(NSEQ=4, S=2048, PAST=1024, NCORES=8)


class Buf:
    __slots__ = ("name", "last_w", "readers", "sem", "cnt", "psum")

    def __init__(self, name):
        self.name = name
        self.psum = False
        self.last_w = None
        self.readers = []
        self.sem = None
        self.cnt = 0


class Op:
    __slots__ = ("idx", "eng", "fn", "waits", "signal", "sigval", "dma", "dma_val")


ENGS = ("pe", "act", "dve", "pool", "sp")


class Sched:
    def __init__(self):
        self.ops = []
        self.by_eng = {e: [] for e in ENGS}
        self.waited = {e: {f: -1 for f in ENGS} for e in ENGS}
        self.dma_waited = {e: {} for e in ENGS}
        self.dma_bufs = []

    def add(self, eng, fn, reads=(), writes=(), dma=None):
        op = Op()
        op.idx = len(self.ops)
        op.eng = eng
        op.fn = fn
        op.waits = []
        op.signal = False
        op.sigval = 0
        op.dma = dma
        op.dma_val = 0
        deps = set()
        for b in reads:
            if b.last_w is not None:
                deps.add(b.last_w)
            if b.psum:
                deps.update(r for r in b.readers if self.ops[r].eng != eng)
        for b in writes:
            if b.last_w is not None:
                deps.add(b.last_w)
            deps.update(b.readers)
        for d in sorted(deps, reverse=True):
            dop = self.ops[d]
            if dop.dma is not None:
                if self.dma_waited[eng].get(dop.dma, 0) >= dop.dma_val:
                    continue
                self.dma_waited[eng][dop.dma] = dop.dma_val
                op.waits.append(("dma", dop.dma, dop.dma_val))
            else:
                if dop.eng == "pe" and eng == "pe" and dma is None:
                    continue
                if self.waited[eng][dop.eng] >= d:
                    continue
                self.waited[eng][dop.eng] = d
                dop.signal = True
                op.waits.append(("eng", dop))
        for b in reads:
            b.readers.append(op.idx)
        for b in writes:
            b.last_w = op.idx
            b.readers = []
        if dma is not None:
            if dma.cnt == 0:
                self.dma_bufs.append(dma)
            dma.cnt += 16
            op.dma_val = dma.cnt
        self.ops.append(op)
        self.by_eng[eng].append(op)
        return op

    def emit(self, nc, es):
        for e in ENGS:
            cnt = 0
            for op in self.by_eng[e]:
                if op.signal:
                    cnt += 1
                    op.sigval = cnt
        esem = {e: es.enter_context(nc.semaphore("sem_" + e)) for e in ENGS}
        for i, b in enumerate(self.dma_bufs):
            b.sem = es.enter_context(nc.semaphore("dsem%d" % i))

        def run(ename, eng):
            for op in self.by_eng[ename]:
                for w in op.waits:
                    if w[0] == "dma":
                        eng.wait_ge(w[1].sem, w[2])
                    else:
                        eng.wait_ge(esem[w[1].eng], w[1].sigval)
                ins = op.fn(eng)
                if op.dma is not None:
                    ins.then_inc(op.dma.sem, 16)
                elif op.signal:
                    ins.then_inc(esem[ename], 1)

        with nc.Block() as block:
            @block.tensor
            def _(e):
                run("pe", e)

            @block.scalar
            def _(e):
                run("act", e)

            @block.vector
            def _(e):
                run("dve", e)

            @block.gpsimd
            def _(e):
                run("pool", e)

            @block.sync
            def _(e):
                run("sp", e)


def _bucket_np(rel):
    import jax
    import jax.numpy as jnp
    cpu = jax.devices("cpu")[0]
    with jax.default_device(cpu):
        rel = jnp.asarray(rel, dtype=jnp.int32)
        nb = 16
        max_exact = 8
        n = jnp.abs(rel)
        nf = jnp.maximum(n, 1).astype(jnp.float32)
        large = max_exact + (jnp.log(nf / max_exact) / math.log(128 / max_exact)
                             * (nb - max_exact)).astype(jnp.int32)
        large = jnp.minimum(large, nb - 1)
        out = jnp.where(rel > 0, nb, 0) + jnp.where(n < max_exact, n, large)
        return np.asarray(out)


def static_tables():
    k = np.arange(P)[:, None]
    q = np.arange(P)[None, :]
    bd = _bucket_np(k - q).astype(np.float32)
    bs = _bucket_np(k - P - q).astype(np.float32)
    maps = np.zeros((4, P, P), np.float32)
    m = bd.copy(); m[(k // 64) > (q // 64)] = -1.0
    maps[0] = m
    maps[1] = bs
    m = bd.copy(); m[(k >= 64) & (q < 64)] = -1.0
    maps[2] = m
    m = bs.copy(); m[(k < 64) & (q >= 64)] = -1.0
    maps[3] = m
    consts = np.zeros((5, P, P), np.float32)
    consts[0] = np.eye(P)
    consts[1] = (k > q)
    consts[2] = (k <= q)
    consts[3] = np.where(k < q, 0.0, NEG)
    consts[4] = 1.0
    present = [sorted(set(int(v) for v in np.unique(maps[i]) if v >= 0 and v != 15)) for i in range(4)]
    return maps, consts, present


def build(cfg):
    NSEQ, S, PAST = cfg["NSEQ"], cfg["S"], cfg["PAST"]
    NT = S // P
    NQB = S // 512
    NPT = PAST // P
    TS = NSEQ * T_DEC
    assert S % 512 == 0 and PAST % P == 0 and TS <= 64
    maps_np, consts_np, present = static_tables()

    nc = bass.Bass("TRN2", target_bir_lowering=False)

    def din(name, shape):
        return nc.dram_tensor(name, list(shape), F32, kind="ExternalInput").ap()

    def dout(name, shape):
        return nc.dram_tensor(name, list(shape), F32, kind="ExternalOutput").ap()

    xp_d = din("xp", (NSEQ, S, D))
    xs_d = din("xs", (TS, D))
    ck_d, cv_d = [], []
    for l in range(4):
        if KINDS[l] == 2:
            ck_d.append(din("ck%d" % l, (NSEQ, P, 256)))
            cv_d.append(din("cv%d" % l, (NSEQ, P, 256)))
        else:
            ck_d.append(din("ck%d" % l, (NSEQ, PAST, D)))
            cv_d.append(din("cv%d" % l, (NSEQ, PAST, D)))
    win_d = [din("win%d" % l, (D, IN_W[KINDS[l]])) for l in range(4)]
    wout_d = [din("wout%d" % l, (D, D)) for l in range(4)]
    gain_d = din("gain", (5, P, D))
    tab_d = din("tab", (P, 512))
    lam_d = din("lam", (2, P, 256))
    subln_d = din("subln", (2, P, 1))
    sink_d = din("sink", (P, 8))
    maps_d = din("maps", (4, P, P))
    consts_d = din("consts", (5, P, P))

    yp_d = dout("yp", (NSEQ, S, D))
    ys_d = dout("ys", (TS, D))
    kp_d, vp_d, ks_d, vs_d = [], [], [], []
    for l in range(4):
        if KINDS[l] == 2:
            kp_d.append(dout("kp%d" % l, (NSEQ, P, 256)))
            vp_d.append(dout("vp%d" % l, (NSEQ, P, 256)))
            ks_d.append(dout("ks%d" % l, (NSEQ, P, 256)))
            vs_d.append(dout("vs%d" % l, (NSEQ, P, 256)))
        else:
            kp_d.append(dout("kp%d" % l, (NSEQ, S, D)))
            vp_d.append(dout("vp%d" % l, (NSEQ, S, D)))
            ks_d.append(dout("ks%d" % l, (TS, D)))
            vs_d.append(dout("vs%d" % l, (TS, D)))

    es = ExitStack()
    sch = Sched()

    def sb(name, shape, dt):
        return es.enter_context(nc.sbuf_tensor(name, list(shape), dt))

    X = sb("X", (P, NT, D), F32)
    xnT = sb("xnT", (P, NCH, S), BF16)
    Gn = sb("Gn", (P, D), F32)
    WT = [sb("WT%d" % i, (P, NCH, 512), BF16) for i in range(2)]
    WO = [sb("WO%d" % i, (P, D), BF16) for i in range(2)]
    KT = [sb("KT%d" % i, (P, S), BF16) for i in range(2)]
    VB = [sb("VB%d" % i, (P, NT, P), BF16) for i in range(2)]
    KTN = [sb("KTN%d" % i, (P, T_DEC), BF16) for i in range(2)]
    VN = [sb("VN%d" % i, (T_DEC, P), BF16) for i in range(2)]
    KC = [sb("KC%d" % i, (P, max(NPT, 1), P), BF16) for i in range(2)]
    NQS = 3
    QB = [sb("QB%d" % i, (P, 512), BF16) for i in range(NQS)]
    GB = [sb("GB%d" % i, (P, 512), BF16) for i in range(NQS)]
    OT = [sb("OT%d" % i, (P, 512), BF16) for i in range(2)]
    NKVS = 3
    KVS = [sb("KVS%d" % i, (P, 256), F32) for i in range(NKVS)]
    NKBT = 4
    KBT = [sb("KBT%d" % i, (P, P), BF16) for i in range(NKBT)]
    NE = 4
    EW = [sb("EW%d" % i, (P, 512), F32) for i in range(NE)]
    SPB = [sb("SPB%d" % i, (P, 512), BF16) for i in range(NE)]
    NPB = 3
    PB = [sb("PB%d" % i, (P, 512), BF16) for i in range(NPB)]
    TP = [sb("TP%d" % i, (P, 512), F32) for i in range(2)]
    SQ = sb("SQ", (P, 512), BF16)
    XN = [sb("XN%d" % i, (P, D), BF16) for i in range(2)]
    SS = sb("SS", (P, 2 * NT + 4), F32)
    BIAS = sb("BIAS", (P, 4, 16, P), BF16)
    CST = sb("CST", (P, 5, P), BF16)
    ZER = sb("ZER", (P, P), BF16)
    LAMC = sb("LAMC", (P, 2, 8), F32)
    SCC = sb("SCC", (P, 2), F32)
    ESK = sb("ESK", (P, 8), F32)
    LAMT = sb("LAMT", (P, 8), F32)
    SCR = sb("SCR", (P, 3200), F32)
    XO = [SCR[:, i * D:(i + 1) * D] for i in range(2)]
    MAPS = SCR[:, 0:512].rearrange("p (c q) -> p c q", c=4)
    CSTF = SCR[:, 512:1152].rearrange("p (c q) -> p c q", c=5)
    OH = [SCR[:, 1152 + i * 128:1152 + (i + 1) * 128] for i in range(2)]
    ACC = SCR[:, 1408:2432].rearrange("p (h q) -> p h q", h=8)
    TABP = SCR[:, 2432:2944]
    LAMW = SCR[:, 2944:3200]

    def ps(name, dt=F32, cols=512):
        return es.enter_context(nc.psum_tensor(name, [P, cols], dt))

    SPS = [ps("SPS0"), ps("SPS1")]
    APS = [ps("APS%d" % i) for i in range(4)]
    M0 = ps("M0")
    T0 = ps("T0", BF16, 1024)

    B = {}

    def bf(name):
        if name not in B:
            B[name] = Buf(name)
        return B[name]

    for nm in ("SPS0", "SPS1", "APS0", "APS1", "APS2", "APS3", "M0", "T0"):
        bf(nm).psum = True
    ident = CST[:, 0, :]
    triu = CST[:, 1, :]
    tril = CST[:, 2, :]
    maskT = CST[:, 3, :]
    ones = CST[:, 4, :]
    bCST = bf("CST")
    bBIAS = bf("BIAS")

    rot = {}

    def nxt(name, n):
        v = rot.get(name, 0)
        rot[name] = v + 1
        return v % n

    def setup():
        sch.add("sp", lambda e: e.dma_start(out=CSTF, in_=consts_d.rearrange("c p q -> p c q")),
                writes=[bf("CSTF")], dma=bf("CSTF"))
        sch.add("sp", lambda e: e.dma_start(out=MAPS, in_=maps_d.rearrange("c p q -> p c q")),
                writes=[bf("MAPS")], dma=bf("MAPS"))
        sch.add("sp", lambda e: e.dma_start(out=TABP, in_=tab_d), writes=[bf("TABP")], dma=bf("TABP"))
        sch.add("sp", lambda e: e.dma_start(out=ESK[:, :], in_=sink_d), writes=[bf("ESK")], dma=bf("ESK"))
        sch.add("dve", lambda e: e.tensor_copy(out=CST[:, :, :], in_=CSTF),
                reads=[bf("CSTF")], writes=[bCST])
        sch.add("dve", lambda e: e.memset(ZER[:, :], 0.0), writes=[bf("ZER")])
        for b_ in range(32):
            if b_ == 15:
                continue
            sch.add("dve", lambda e, b_=b_: e.tensor_tensor(
                out=TABP[:, b_ * 16:(b_ + 1) * 16], in0=TABP[:, b_ * 16:(b_ + 1) * 16],
                in1=TABP[:, 240:256], op=ALU.subtract), reads=[bf("TABP")], writes=[bf("TABP")])
        for c in range(8):
            for j in range(2):
                r0 = j * 64
                sch.add("dve", lambda e, c=c, j=j, r0=r0: e.tensor_tensor(
                    out=ESK[r0:r0 + 64, c:c + 1], in0=ESK[r0:r0 + 64, c:c + 1],
                    in1=TABP[r0:r0 + 64, 240 + 2 * c + j:240 + 2 * c + j + 1], op=ALU.subtract),
                    reads=[bf("TABP"), bf("ESK")], writes=[bf("ESK")])
        sch.add("act", lambda e: e.activation(out=ESK[:, :], in_=ESK[:, :], func=AF.Exp),
                reads=[bf("ESK")], writes=[bf("ESK")])
        sch.add("dve", lambda e: e.tensor_scalar(out=TABP, in0=TABP, scalar1=8.0, scalar2=None,
                                                 op0=ALU.mult), reads=[bf("TABP")], writes=[bf("TABP")])
        for mi in range(4):
            eng = "dve"
            accb = bf("ACC")
            sch.add(eng, lambda e, mi=mi: e.tensor_scalar(
                out=OH[0], in0=MAPS[:, mi, :], scalar1=-1.0, scalar2=NEG, op0=ALU.is_equal, op1=ALU.mult),
                reads=[bf("MAPS")], writes=[bf("OH0")])
            for h0 in (0, 8):
                for h in range(8):
                    sch.add(eng, lambda e, h=h: e.tensor_copy(out=ACC[:, h, :], in_=OH[0]),
                            reads=[bf("OH0")], writes=[accb])
                for b_ in present[mi]:
                    sch.add(eng, lambda e, mi=mi, b_=b_: e.tensor_single_scalar(
                        out=OH[1], in_=MAPS[:, mi, :], scalar=float(b_), op=ALU.is_equal),
                        reads=[bf("MAPS")], writes=[bf("OH1")])
                    for h in range(8):
                        sch.add(eng, lambda e, h=h, b_=b_, h0=h0: e.scalar_tensor_tensor(
                            out=ACC[:, h, :], in0=OH[1],
                            scalar=TABP[:, b_ * 16 + h0 + h:b_ * 16 + h0 + h + 1],
                            in1=ACC[:, h, :], op0=ALU.mult, op1=ALU.add),
                            reads=[bf("OH1"), bf("TABP"), accb], writes=[accb])
                sch.add(eng, lambda e, mi=mi, h0=h0: e.tensor_copy(out=BIAS[:, mi, h0:h0 + 8, :], in_=ACC),
                        reads=[accb], writes=[bBIAS])
        for li, l in enumerate((0, 3)):
            lam_init = 0.8 - 0.6 * math.exp(-0.3 * l)
            sch.add("sp", lambda e, li=li: e.dma_start(out=LAMW, in_=lam_d[li]),
                    writes=[bf("LAMW")], dma=bf("LAMW"))
            sch.add("sp", lambda e, li=li: e.dma_start(out=SCC[:, li:li + 1], in_=subln_d[li]),
                    writes=[bf("SCC")], dma=bf("SCC%d" % li))
            for t in range(2):
                sch.add("dve", lambda e, t=t: e.tensor_tensor(
                    out=LAMW[:, t * 128:t * 128 + 64], in0=LAMW[:, t * 128:t * 128 + 64],
                    in1=LAMW[:, t * 128 + 64:t * 128 + 128], op=ALU.mult),
                    reads=[bf("LAMW")], writes=[bf("LAMW")])
                sch.add("dve", lambda e, t=t: e.tensor_reduce(
                    out=LAMT[:, t:t + 1], in_=LAMW[:, t * 128:t * 128 + 64], axis=mybir.AxisListType.X,
                    op=ALU.add), reads=[bf("LAMW")], writes=[bf("LAMT")])
            sch.add("act", lambda e: e.activation(out=LAMT[:, 0:2], in_=LAMT[:, 0:2], func=AF.Exp),
                    reads=[bf("LAMT")], writes=[bf("LAMT")])
            sch.add("dve", lambda e, li=li, lam_init=lam_init: e.scalar_tensor_tensor(
                out=LAMC[:, li, 0:1], in0=LAMT[:, 1:2], scalar=-lam_init, in1=LAMT[:, 0:1],
                op0=ALU.add, op1=ALU.subtract), reads=[bf("LAMT")], writes=[bf("LAMC")])
            sch.add("dve", lambda e, li=li, lam_init=lam_init: e.tensor_scalar(
                out=SCC[:, li:li + 1], in0=SCC[:, li:li + 1], scalar1=(1.0 - lam_init) * math.sqrt(128.0),
                scalar2=None, op0=ALU.mult), reads=[bf("SCC")], writes=[bf("SCC")])

    def load_weights(l, c, wset):
        kind = KINDS[l]
        W = WT[wset]
        wb = bf("WT%d" % wset)
        if kind in (0, 1):
            cols = [c * 128, 3072 + c * 128, 1024 + c * 128, 2048 + c * 128]
            for si, c0 in enumerate(cols):
                sch.add("pool", lambda e, si=si, c0=c0, W=W: e.dma_start(
                    out=W[:, :, si * 128:(si + 1) * 128],
                    in_=win_d[l][:, c0:c0 + 128].rearrange("(c p) f -> p c f", p=P)),
                    writes=[wb], dma=bf("WTd%d_%d" % (wset, si)))
        else:
            g = c // 2
            cols = [(c * 128, 128, 0), (1536 + c * 128, 128, 128), (1024 + g * 64, 64, 256),
                    (1280 + g * 64, 64, 320)]
            for si, (c0, w, o0) in enumerate(cols):
                sch.add("pool", lambda e, c0=c0, w=w, o0=o0, W=W: e.dma_start(
                    out=W[:, :, o0:o0 + w],
                    in_=win_d[l][:, c0:c0 + w].rearrange("(c p) f -> p c f", p=P)),
                    writes=[wb], dma=bf("WTd%d_%d" % (wset, si)))
        sch.add("pool", lambda e: e.dma_start(out=WO[wset][:, :], in_=wout_d[l][c * 128:(c + 1) * 128, :]),
                writes=[bf("WO%d" % wset)], dma=bf("WOd%d" % wset))

    def norm_tile(xrows_ap, nrows, col, gidx_loaded, xnT_cols):
        slot = nxt("XN", 2)
        xn = XN[slot]
        xb = bf("XN%d" % slot)
        sch.add("act", lambda e: e.activation(out=xn[0:nrows, :], in_=xrows_ap, func=AF.Square,
                                              accum_out=SS[0:nrows, col:col + 1]),
                reads=[bf("X")], writes=[xb, bf("SS")])
        sch.add("act", lambda e: e.activation(out=SS[0:nrows, col:col + 1], in_=SS[0:nrows, col:col + 1],
                                              func=AF.Ln, scale=1.0 / D, bias=EPS),
                reads=[bf("SS")], writes=[bf("SS")])
        sch.add("act", lambda e: e.activation(out=SS[0:nrows, col:col + 1], in_=SS[0:nrows, col:col + 1],
                                              func=AF.Exp, scale=-0.5),
                reads=[bf("SS")], writes=[bf("SS")])
        sch.add("dve", lambda e: e.scalar_tensor_tensor(out=xn[0:nrows, :], in0=xrows_ap,
                                                        scalar=SS[0:nrows, col:col + 1], in1=Gn[0:nrows, :],
                                                        op0=ALU.mult, op1=ALU.mult),
                reads=[bf("X"), bf("SS"), bf("Gn")], writes=[xb])
        for ch in range(NCH):
            sch.add("pe", lambda e, ch=ch: e.transpose(out=T0[:, ch * 128:ch * 128 + nrows],
                                                       in_=xn[0:nrows, ch * 128:(ch + 1) * 128],
                                                       identity=ident[0:nrows, 0:nrows]),
                    reads=[xb, bCST], writes=[bf("T0")])
        c0, c1 = xnT_cols
        sch.add("act", lambda e: e.copy(
            out=xnT[:, :, c0:c1],
            in_=T0[:, :].rearrange("p (c t) -> p c t", c=NCH)[:, :, 0:nrows]),
            reads=[bf("T0")], writes=[bf("xnT")])

    def zero_acc():
        sch.add("dve", lambda e: e.memset(SS[:, :], 0.0), writes=[bf("SS")])

    def out_proj_units(ot_slot, wset, ntok_tiles, xdst):
        units = []
        for i, (rows, xap_fn) in enumerate(xdst):
            for hf in range(2):
                def u(i=i, hf=hf, rows=rows, xap_fn=xap_fn):
                    Mx, mb = M0, bf("M0")
                    sch.add("pe", lambda e: e.matmul(Mx[0:rows, :], lhsT=OT[ot_slot][:, i * 128:i * 128 + rows],
                                                     rhs=WO[wset][:, hf * 512:(hf + 1) * 512],
                                                     start=True, stop=True),
                            reads=[bf("OT%d" % ot_slot), bf("WO%d" % wset)], writes=[mb])
                    xa = xap_fn(hf)
                    sch.add("dve", lambda e: e.tensor_tensor(out=xa, in0=Mx[0:rows, :], in1=xa, op=ALU.add),
                            reads=[mb, bf("X")], writes=[bf("X")])
                units.append(u)
        return units

    def proj_q_unit(wset, tok0, ntok, slot):
        def u():
            for ch in range(NCH):
                sch.add("pe", lambda e, ch=ch: e.matmul(M0[:, 0:ntok], lhsT=WT[wset][:, ch, 0:128],
                                                        rhs=xnT[:, ch, tok0:tok0 + ntok],
                                                        start=(ch == 0), stop=(ch == NCH - 1)),
                        reads=[bf("WT%d" % wset), bf("xnT")], writes=[bf("M0")])
            sch.add("dve", lambda e: e.tensor_copy(out=QB[slot][:, 0:ntok], in_=M0[:, 0:ntok]),
                    reads=[bf("M0")], writes=[bf("QB%d" % slot)])
        return u

    def proj_g_unit(wset, tok0, ntok, slot):
        def u():
            for ch in range(NCH):
                sch.add("pe", lambda e, ch=ch: e.matmul(M0[:, 0:ntok], lhsT=WT[wset][:, ch, 128:256],
                                                        rhs=xnT[:, ch, tok0:tok0 + ntok],
                                                        start=(ch == 0), stop=(ch == NCH - 1)),
                        reads=[bf("WT%d" % wset), bf("xnT")], writes=[bf("M0")])
            sch.add("act", lambda e: e.activation(out=GB[slot][:, 0:ntok], in_=M0[:, 0:ntok], func=AF.Silu),
                    reads=[bf("M0")], writes=[bf("GB%d" % slot)])
        return u

    def proj_kv_unit(l, c, wset, kvset, tok0, nrows, kt_cols, v_dst, out_dmas, swa, ev="act"):
        st = {}

        def u1():
            width = 128 if swa else 256
            for ch in range(NCH):
                sch.add("pe", lambda e, ch=ch: e.matmul(M0[0:nrows, 0:width],
                                                        lhsT=xnT[:, ch, tok0:tok0 + nrows],
                                                        rhs=WT[wset][:, ch, 256:256 + width],
                                                        start=(ch == 0), stop=(ch == NCH - 1)),
                        reads=[bf("WT%d" % wset), bf("xnT")], writes=[bf("M0")])
            kbs = nxt("KBT", NKBT)
            kbt = KBT[kbs]
            kbb = bf("KBT%d" % kbs)
            st["kbt"], st["kbb"] = kbt, kbb
            vb_ = v_dst[1]
            def cp(e, out, in_):
                return e.copy(out=out, in_=in_) if ev == "act" else e.tensor_copy(out=out, in_=in_)
            if swa:
                for hh in range(2):
                    sch.add(ev, lambda e, hh=hh: cp(e, kbt[0:nrows, hh * 64:(hh + 1) * 64], M0[0:nrows, 0:64]),
                            reads=[bf("M0")], writes=[kbb])
                    sch.add(ev, lambda e, hh=hh: cp(e, v_dst[0][:, hh * 64:(hh + 1) * 64], M0[0:nrows, 64:128]),
                            reads=[bf("M0")], writes=[vb_])
            else:
                sch.add(ev, lambda e: cp(e, kbt[0:nrows, :], M0[0:nrows, 0:128]),
                        reads=[bf("M0")], writes=[kbb])
                sch.add(ev, lambda e: cp(e, v_dst[0], M0[0:nrows, 128:256]),
                        reads=[bf("M0")], writes=[vb_])
            if out_dmas:
                ks = nxt("KVS", NKVS)
                kvs = KVS[ks]
                kb = bf("KVS%d" % ks)
                sch.add("dve", lambda e: e.tensor_copy(out=kvs[0:nrows, 0:width], in_=M0[0:nrows, 0:width]),
                        reads=[bf("M0")], writes=[kb])
                for (dst_ap, c0, w) in out_dmas:
                    sch.add("sp", lambda e, dst_ap=dst_ap, c0=c0, w=w: e.dma_start(out=dst_ap,
                                                                                  in_=kvs[0:nrows, c0:c0 + w]),
                            reads=[kb], dma=kb)

        def u2():
            kbt, kbb = st["kbt"], st["kbb"]
            sch.add("pe", lambda e: e.transpose(out=T0[:, 0:nrows], in_=kbt[0:nrows, :],
                                                identity=ident[0:nrows, 0:nrows]),
                    reads=[kbb, bCST], writes=[bf("T0")])
            ktt, ktc0, ktb = kt_cols
            if ev == "act":
                sch.add("act", lambda e: e.copy(out=ktt[:, ktc0:ktc0 + nrows], in_=T0[:, 0:nrows]),
                        reads=[bf("T0")], writes=[ktb])
            else:
                sch.add("dve", lambda e: e.tensor_copy(out=ktt[:, ktc0:ktc0 + nrows], in_=T0[:, 0:nrows]),
                        reads=[bf("T0")], writes=[ktb])
        return u1, u2

    def attn_da(c, qslot, NQ, acol, ktiles, bg):
        hmb = 2 * c
        items = [(kt, m) for kt in range(len(ktiles)) for m in range(2)]
        nkt = len(ktiles)

        def qk(it):
            kt, m = it
            t = ktiles[kt]
            sbk = nxt("SPS", 2)
            Sb = SPS[sbk]
            sbb = bf("SPS%d" % sbk)
            c0, nk = t["c0"], t["nk"]
            nb = len(t["bias"])
            ktt, kc0 = t["kT"]
            sch.add("pe", lambda e: e.matmul(Sb[0:nk, c0:NQ], lhsT=ktt[m * 64:(m + 1) * 64, kc0:kc0 + nk],
                                             rhs=QB[qslot][m * 64:(m + 1) * 64, c0:NQ],
                                             start=True, stop=(nb == 0), skip_group_check=True),
                    reads=[t["kTb"], bf("QB%d" % qslot)], writes=[sbb])
            for bi, (co, ncol, mi, kr) in enumerate(t["bias"]):
                sch.add("pe", lambda e, co=co, ncol=ncol, mi=mi, bi=bi: e.matmul(
                    Sb[0:nk, co:co + ncol], lhsT=ident[0:nk, 0:nk], rhs=BIAS[0:nk, mi, hmb + m, 0:ncol],
                    start=False, stop=(bi == nb - 1), skip_group_check=True),
                    reads=[bCST, bBIAS], writes=[sbb])
            return Sb, sbb

        def rest(it, Sb, sbb):
            kt, m = it
            t = ktiles[kt]
            c0, nk = t["c0"], t["nk"]
            pslot = nxt("PB", NPB)
            Pt = PB[pslot]
            pbb = bf("PB%d" % pslot)
            sch.add("act", lambda e: e.activation(out=Pt[0:nk, c0:NQ], in_=Sb[0:nk, c0:NQ], func=AF.Exp,
                                                  scale=0.125), reads=[sbb], writes=[pbb])
            sch.add("pe", lambda e: e.matmul(APS[m][:, acol + c0:acol + NQ], lhsT=t["v"], rhs=Pt[0:nk, c0:NQ],
                                             start=(kt == 0), stop=(kt == nkt - 1), skip_group_check=True),
                    reads=[t["vb"], pbb], writes=[bf("APS%d" % m)])
            sch.add("pe", lambda e: e.matmul(APS[2 + m][:, acol + c0:acol + NQ], lhsT=ones[0:nk, :],
                                             rhs=Pt[0:nk, c0:NQ],
                                             start=(kt == 0), stop=(kt == nkt - 1), skip_group_check=True),
                    reads=[bCST, pbb], writes=[bf("APS%d" % (2 + m))])

        cur = qk(items[0])
        for i, it in enumerate(items):
            nx = qk(items[i + 1]) if i + 1 < len(items) else None
            rest(it, *cur)
            cur = nx
            if bg:
                bg.pop(0)()

    def post_da_a(li, qslot, oslot, NQ, acol):
        T_a, T_b = TP[0], TP[1]
        ba, bb_ = bf("TP0"), bf("TP1")
        sl = slice(acol, acol + NQ)
        sch.add("act", lambda e: e.activation(out=T_a[:, 0:NQ], in_=APS[2][:, sl], func=AF.Ln),
                reads=[bf("APS2")], writes=[ba])
        sch.add("act", lambda e: e.activation(out=T_a[:, 0:NQ], in_=T_a[:, 0:NQ], func=AF.Exp, scale=-1.0),
                reads=[ba], writes=[ba])
        sch.add("dve", lambda e: e.tensor_tensor(out=T_a[:, 0:NQ], in0=APS[0][:, sl], in1=T_a[:, 0:NQ],
                                                 op=ALU.mult), reads=[bf("APS0"), ba], writes=[ba])
        sch.add("act", lambda e: e.activation(out=T_b[:, 0:NQ], in_=APS[3][:, sl], func=AF.Ln),
                reads=[bf("APS3")], writes=[bb_])
        sch.add("act", lambda e: e.activation(out=T_b[:, 0:NQ], in_=T_b[:, 0:NQ], func=AF.Exp, scale=-1.0),
                reads=[bb_], writes=[bb_])
        sch.add("dve", lambda e: e.tensor_tensor(out=T_b[:, 0:NQ], in0=APS[1][:, sl], in1=T_b[:, 0:NQ],
                                                 op=ALU.mult), reads=[bf("APS1"), bb_], writes=[bb_])
        sch.add("dve", lambda e: e.scalar_tensor_tensor(out=T_a[:, 0:NQ], in0=T_b[:, 0:NQ],
                                                        scalar=LAMC[:, li, 0:1], in1=T_a[:, 0:NQ],
                                                        op0=ALU.mult, op1=ALU.add),
                reads=[ba, bb_, bf("LAMC")], writes=[ba])
        sch.add("act", lambda e: e.activation(out=SQ[:, 0:NQ], in_=T_a[:, 0:NQ], func=AF.Square),
                reads=[ba], writes=[bf("SQ")])

    def post_da_b(li, qslot, oslot, NQ, acol):
        T_a, T_b = TP[0], TP[1]
        ba, bb_ = bf("TP0"), bf("TP1")
        sch.add("pe", lambda e: e.matmul(M0[:, 0:NQ], lhsT=ones, rhs=SQ[:, 0:NQ], start=True, stop=True),
                reads=[bCST, bf("SQ")], writes=[bf("M0")])
        sch.add("act", lambda e: e.activation(out=T_b[:, 0:NQ], in_=M0[:, 0:NQ], func=AF.Ln, bias=128.0 * EPS),
                reads=[bf("M0")], writes=[bb_])
        sch.add("act", lambda e: e.activation(out=T_b[:, 0:NQ], in_=T_b[:, 0:NQ], func=AF.Exp, scale=-0.5),
                reads=[bb_], writes=[bb_])
        sch.add("dve", lambda e: e.tensor_tensor(out=T_a[:, 0:NQ], in0=T_a[:, 0:NQ], in1=T_b[:, 0:NQ],
                                                 op=ALU.mult), reads=[ba, bb_], writes=[ba])
        sch.add("dve", lambda e: e.scalar_tensor_tensor(out=OT[oslot][:, 0:NQ], in0=T_a[:, 0:NQ],
                                                        scalar=SCC[:, li:li + 1], in1=GB[qslot][:, 0:NQ],
                                                        op0=ALU.mult, op1=ALU.mult),
                reads=[ba, bf("SCC"), bf("GB%d" % qslot)], writes=[bf("OT%d" % oslot)])

    def post_da(li, qslot, oslot, NQ, acol):
        post_da_a(li, qslot, oslot, NQ, acol)
        post_da_b(li, qslot, oslot, NQ, acol)

    def attn_sb(c, qslot, NQ, acol, ktiles, bg):
        nkt = len(ktiles)
        for j in range(2):
            for a in (j, 2 + j):
                sch.add("pe", lambda e, a=a: e.matmul(APS[a][:, acol:acol + NQ], lhsT=ZER[:, :],
                                                      rhs=QB[qslot][:, 0:NQ], start=True, stop=False,
                                                      skip_group_check=True),
                        reads=[bf("ZER"), bf("QB%d" % qslot)], writes=[bf("APS%d" % a)])
        items = [(kt, j) for kt in reversed(range(nkt)) for j in range(2)]

        def stage1(it):
            kt, j = it
            t = ktiles[kt]
            c0, nk = t["c0"], t["nk"]
            sbk = nxt("SPS", 2)
            Sb = SPS[sbk]
            sbb = bf("SPS%d" % sbk)
            nb = len(t["bias"])
            ktt, kc0 = t["kT"]
            sch.add("pe", lambda e: e.matmul(Sb[0:nk, c0:NQ], lhsT=ktt[j * 64:(j + 1) * 64, kc0:kc0 + nk],
                                             rhs=QB[qslot][j * 64:(j + 1) * 64, c0:NQ],
                                             start=True, stop=(nb == 0), skip_group_check=True),
                    reads=[t["kTb"], bf("QB%d" % qslot)], writes=[sbb])
            for bi, (co, ncol, mi, kr) in enumerate(t["bias"]):
                sch.add("pe", lambda e, co=co, ncol=ncol, bi=bi: e.matmul(
                    Sb[0:nk, co:co + ncol], lhsT=ident[0:nk, 0:nk], rhs=maskT[0:nk, 0:ncol],
                    start=False, stop=(bi == nb - 1), skip_group_check=True),
                    reads=[bCST], writes=[sbb])
            es_ = nxt("EW", NE)
            E = EW[es_]
            eb = bf("EW%d" % es_)
            spb = SPB[es_]
            spbb = bf("SPB%d" % es_)
            sch.add("act", lambda e: e.activation(out=E[0:nk, c0:NQ], in_=Sb[0:nk, c0:NQ], func=AF.Exp,
                                                  scale=0.125), reads=[sbb], writes=[eb])
            sch.add("act", lambda e: e.activation(out=E[0:nk, c0:NQ], in_=E[0:nk, c0:NQ], func=AF.Ln,
                                                  bias=1.0), reads=[eb], writes=[eb])
            return E, eb, spb, spbb, Sb, sbb

        def stage1b(it, E, eb, spb, spbb, Sb, sbb):
            kt, j = it
            t = ktiles[kt]
            c0, nk = t["c0"], t["nk"]
            sch.add("dve", lambda e: e.tensor_copy(out=spb[0:nk, c0:NQ], in_=E[0:nk, c0:NQ]),
                    reads=[eb], writes=[spbb])
            sch.add("dve", lambda e: e.scalar_tensor_tensor(out=E[0:nk, c0:NQ], in0=Sb[0:nk, c0:NQ],
                                                            scalar=0.125, in1=E[0:nk, c0:NQ],
                                                            op0=ALU.mult, op1=ALU.subtract),
                    reads=[sbb, eb, spbb], writes=[eb])

        def stage2a(it, E, eb, spb, spbb):
            kt, j = it
            t = ktiles[kt]
            c0, nk = t["c0"], t["nk"]
            rb = bf("APS%d" % j)
            sch.add("pe", lambda e: e.matmul(APS[j][:, acol + c0:acol + NQ], lhsT=triu[0:nk, :],
                                             rhs=spb[0:nk, c0:NQ], start=False, stop=False,
                                             skip_group_check=True),
                    reads=[bCST, spbb], writes=[rb])
            sch.add("dve", lambda e: e.tensor_tensor(out=E[0:nk, c0:NQ], in0=E[0:nk, c0:NQ],
                                                     in1=APS[j][0:nk, acol + c0:acol + NQ], op=ALU.subtract),
                    reads=[eb, rb], writes=[eb])
            pslot = nxt("PB", NPB)
            Ab = PB[pslot]
            pbb = bf("PB%d" % pslot)
            sch.add("act", lambda e: e.activation(out=Ab[0:nk, c0:NQ], in_=E[0:nk, c0:NQ], func=AF.Exp),
                    reads=[eb], writes=[pbb])
            return Ab, pbb

        def stage2b(it, spb, spbb, Ab, pbb):
            kt, j = it
            t = ktiles[kt]
            c0, nk = t["c0"], t["nk"]
            sch.add("pe", lambda e: e.matmul(APS[j][:, acol + c0:acol + NQ], lhsT=tril[0:nk, :],
                                             rhs=spb[0:nk, c0:NQ], start=False, stop=False,
                                             skip_group_check=True),
                    reads=[bCST, spbb], writes=[bf("APS%d" % j)])
            sch.add("pe", lambda e: e.matmul(APS[2 + j][:, acol + c0:acol + NQ], lhsT=t["v"],
                                             rhs=Ab[0:nk, c0:NQ], start=False, stop=False,
                                             skip_group_check=True),
                    reads=[t["vb"], pbb], writes=[bf("APS%d" % (2 + j))])

        n = len(items)
        LA = 2
        s1 = {}
        for i in range(-(LA + 1), 0):
            if 0 <= i + LA + 1 < n:
                s1[i + LA + 1] = stage1(items[i + LA + 1])
            if 0 <= i + LA < n:
                stage1b(items[i + LA], *s1[i + LA])
        prev = None
        for i, it in enumerate(items):
            if i + LA + 1 < n:
                s1[i + LA + 1] = stage1(items[i + LA + 1])
            if i + LA < n:
                stage1b(items[i + LA], *s1[i + LA])
            a = stage2a(it, *s1[i][0:4])
            if prev is not None:
                stage2b(*prev)
            prev = (it, s1[i][2], s1[i][3], a[0], a[1])
            del s1[i]
            if bg:
                bg.pop(0)()
        stage2b(*prev)

    def post_sb(qslot, oslot, NQ, acol):
        for j in range(2):
            r = slice(j * 64, (j + 1) * 64)
            sch.add("dve", lambda e, j=j, r=r: e.tensor_tensor(out=OT[oslot][r, 0:NQ],
                                                               in0=APS[2 + j][r, acol:acol + NQ],
                                                               in1=GB[qslot][r, 0:NQ], op=ALU.mult),
                    reads=[bf("APS%d" % (2 + j)), bf("GB%d" % qslot)], writes=[bf("OT%d" % oslot)])

    def attn_swa(c, qslot, qtiles, bg):
        items = []
        for qi, qt in enumerate(qtiles):
            for j in range(2):
                for ki in range(len(qt["ktiles"])):
                    items.append((qi, j, ki))

        def qk(it):
            qi, j, ki = it
            qt = qtiles[qi]
            t = qt["ktiles"][ki]
            nk, nq, qc0 = t["nk"], qt["nq"], qt["qc0"]
            sbk = nxt("SPS", 2)
            Sb = SPS[sbk]
            sbb = bf("SPS%d" % sbk)
            ktt, kc0 = t["kT"]
            sch.add("pe", lambda e: e.matmul(Sb[0:nk, 0:nq], lhsT=ktt[j * 64:(j + 1) * 64, kc0:kc0 + nk],
                                             rhs=QB[qslot][j * 64:(j + 1) * 64, qc0:qc0 + nq],
                                             start=True, stop=False, skip_group_check=True),
                    reads=[t["kTb"], bf("QB%d" % qslot)], writes=[sbb])
            sch.add("pe", lambda e: e.matmul(Sb[0:nk, 0:nq], lhsT=ident[0:nk, 0:nk],
                                             rhs=BIAS[0:nk, t["mi"], 2 * c + j, 0:nq],
                                             start=False, stop=True, skip_group_check=True),
                    reads=[bCST, bBIAS], writes=[sbb])
            return Sb, sbb

        def rest(it, Sb, sbb):
            qi, j, ki = it
            qt = qtiles[qi]
            t = qt["ktiles"][ki]
            nk, nq, ac = t["nk"], qt["nq"], qt["acol"]
            nki = len(qt["ktiles"])
            pslot = nxt("PB", NPB)
            Pt = PB[pslot]
            pbb = bf("PB%d" % pslot)
            sch.add("act", lambda e: e.activation(out=Pt[0:nk, 0:nq], in_=Sb[0:nk, 0:nq], func=AF.Exp,
                                                  scale=0.125), reads=[sbb], writes=[pbb])
            sch.add("pe", lambda e: e.matmul(APS[j][:, ac:ac + nq], lhsT=t["v"], rhs=Pt[0:nk, 0:nq],
                                             start=(ki == 0), stop=(ki == nki - 1), skip_group_check=True),
                    reads=[t["vb"], pbb], writes=[bf("APS%d" % j)])
            sch.add("pe", lambda e: e.matmul(APS[2 + j][:, ac:ac + nq], lhsT=ones[0:nk, :], rhs=Pt[0:nk, 0:nq],
                                             start=(ki == 0), stop=(ki == nki - 1), skip_group_check=True),
                    reads=[bCST, pbb], writes=[bf("APS%d" % (2 + j))])

        cur = qk(items[0])
        for i, it in enumerate(items):
            nx = qk(items[i + 1]) if i + 1 < len(items) else None
            rest(it, *cur)
            cur = nx
            if bg:
                bg.pop(0)()

    def attn_swa_p(c, qslot, qtiles, bg):
        def qk(qi):
            qt = qtiles[qi]
            sbk = nxt("SPS", 2)
            Sb = SPS[sbk]
            sbb = bf("SPS%d" % sbk)
            nq, qc0 = qt["nq"], qt["qc0"]
            blk = 0
            for j in range(2):
                for ki, t in enumerate(qt["ktiles"]):
                    off = blk * P
                    ktt, kc0 = t["kT"]
                    sch.add("pe", lambda e, j=j, ktt=ktt, kc0=kc0, off=off: e.matmul(
                        Sb[:, off:off + nq], lhsT=ktt[j * 64:(j + 1) * 64, kc0:kc0 + P],
                        rhs=QB[qslot][j * 64:(j + 1) * 64, qc0:qc0 + nq],
                        start=True, stop=False, skip_group_check=True),
                        reads=[t["kTb"], bf("QB%d" % qslot)], writes=[sbb])
                    sch.add("pe", lambda e, j=j, t=t, off=off: e.matmul(
                        Sb[:, off:off + nq], lhsT=ident, rhs=BIAS[:, t["mi"], 2 * c + j, 0:nq],
                        start=False, stop=True, skip_group_check=True),
                        reads=[bCST, bBIAS], writes=[sbb])
                    blk += 1
            return Sb, sbb, blk

        def rest(qi, Sb, sbb, nblk):
            qt = qtiles[qi]
            nq, ac = qt["nq"], qt["acol"]
            nki = len(qt["ktiles"])
            pslot = nxt("PB", NPB)
            Pt = PB[pslot]
            pbb = bf("PB%d" % pslot)
            sch.add("act", lambda e: e.activation(out=Pt[:, 0:nblk * P], in_=Sb[:, 0:nblk * P], func=AF.Exp,
                                                  scale=0.125), reads=[sbb], writes=[pbb])
            blk = 0
            for j in range(2):
                for ki, t in enumerate(qt["ktiles"]):
                    off = blk * P
                    sch.add("pe", lambda e, j=j, t=t, off=off, ki=ki: e.matmul(
                        APS[j][:, ac:ac + nq], lhsT=t["v"], rhs=Pt[:, off:off + nq],
                        start=(ki == 0), stop=(ki == nki - 1), skip_group_check=True),
                        reads=[t["vb"], pbb], writes=[bf("APS%d" % j)])
                    sch.add("pe", lambda e, j=j, off=off, ki=ki: e.matmul(
                        APS[2 + j][:, ac:ac + nq], lhsT=ones, rhs=Pt[:, off:off + nq],
                        start=(ki == 0), stop=(ki == nki - 1), skip_group_check=True),
                        reads=[bCST, pbb], writes=[bf("APS%d" % (2 + j))])
                    blk += 1

        nqt = len(qtiles)
        cur = qk(0)
        for qi in range(nqt):
            nx = qk(qi + 1) if qi + 1 < nqt else None
            rest(qi, *cur)
            cur = nx
            for _ in range(4):
                if bg:
                    bg.pop(0)()

    def post_swa(c, qslot, oslot, NQ, acol):
        T_a = TP[0]
        ba = bf("TP0")
        for j in range(2):
            r = slice(j * 64, (j + 1) * 64)
            sch.add("act", lambda e, j=j, r=r: e.activation(out=T_a[r, 0:NQ], in_=APS[2 + j][r, acol:acol + NQ],
                                                            func=AF.Ln, bias=ESK[r, c:c + 1]),
                    reads=[bf("APS%d" % (2 + j)), bf("ESK")], writes=[ba])
            sch.add("act", lambda e, r=r: e.activation(out=T_a[r, 0:NQ], in_=T_a[r, 0:NQ], func=AF.Exp, scale=-1.0),
                    reads=[ba], writes=[ba])
            sch.add("dve", lambda e, j=j, r=r: e.tensor_tensor(out=T_a[r, 0:NQ], in0=APS[j][r, acol:acol + NQ],
                                                               in1=T_a[r, 0:NQ], op=ALU.mult),
                    reads=[bf("APS%d" % j), ba], writes=[ba])
            sch.add("dve", lambda e, r=r: e.tensor_tensor(out=OT[oslot][r, 0:NQ], in0=T_a[r, 0:NQ],
                                                          in1=GB[qslot][r, 0:NQ], op=ALU.mult),
                    reads=[ba, bf("GB%d" % qslot)], writes=[bf("OT%d" % oslot)])

    def final_norm_tile(xrows_ap, nrows, col, dst_ap):
        slot = nxt("XO", 2)
        xo = XO[slot]
        xob = bf("XO%d" % slot)
        sch.add("act", lambda e: e.activation(out=xo[0:nrows, :], in_=xrows_ap, func=AF.Square,
                                              accum_out=SS[0:nrows, col:col + 1]),
                reads=[bf("X")], writes=[xob, bf("SS")])
        sch.add("act", lambda e: e.activation(out=SS[0:nrows, col:col + 1], in_=SS[0:nrows, col:col + 1],
                                              func=AF.Ln, scale=1.0 / D, bias=EPS),
                reads=[bf("SS")], writes=[bf("SS")])
        sch.add("act", lambda e: e.activation(out=SS[0:nrows, col:col + 1], in_=SS[0:nrows, col:col + 1],
                                              func=AF.Exp, scale=-0.5),
                reads=[bf("SS")], writes=[bf("SS")])
        sch.add("dve", lambda e: e.scalar_tensor_tensor(out=xo[0:nrows, :], in0=xrows_ap,
                                                        scalar=SS[0:nrows, col:col + 1], in1=Gn[0:nrows, :],
                                                        op0=ALU.mult, op1=ALU.mult),
                reads=[bf("X"), bf("SS"), bf("Gn")], writes=[xob])
        sch.add("sp", lambda e: e.dma_start(out=dst_ap, in_=xo[0:nrows, :]), reads=[xob], dma=xob)

    def load_gain(idx):
        sch.add("sp", lambda e: e.dma_start(out=Gn[:, :], in_=gain_d[idx]), writes=[bf("Gn")], dma=bf("Gn"))

    def prompt_seq(s):
        sch.add("sp", lambda e: e.dma_start(out=X[:, :, :], in_=xp_d[s].rearrange("(n p) d -> p n d", p=P)),
                writes=[bf("X")], dma=bf("Xd"))
        for l in range(4):
            kind = KINDS[l]
            li = 0 if l == 0 else 1
            swa = kind == 2
            load_gain(l)
            load_weights(l, 0, 0)
            zero_acc()
            for n in range(NT):
                norm_tile(X[:, n, :], P, n, None, (n * P, (n + 1) * P))

            def kvset_of(c):
                return (c // 2) % 2 if swa else c % 2

            def kv_units(c):
                if swa and c % 2 == 1:
                    return []
                kvset = kvset_of(c)
                wset = c % 2
                us = []
                for n in range(NT):
                    outs = []
                    if swa:
                        g = c // 2
                        if n == NT - 1:
                            outs = [(kp_d[l][s, :, g * 64:(g + 1) * 64], 0, 64),
                                    (vp_d[l][s, :, g * 64:(g + 1) * 64], 64, 64)]
                    else:
                        outs = [(kp_d[l][s, n * P:(n + 1) * P, c * 128:(c + 1) * 128], 0, 128),
                                (vp_d[l][s, n * P:(n + 1) * P, c * 128:(c + 1) * 128], 128, 128)]
                    us.append(proj_kv_unit(l, c, wset, kvset, n * P, P,
                                           (KT[kvset], n * P, bf("KT%d" % kvset)),
                                           (VB[kvset][:, n, :], bf("VB%d" % kvset)), outs, swa,
                                           ev=("act" if kind == 1 else "dve")))
                flat = []
                for i_, (a_, b_) in enumerate(us):
                    flat.append(a_)
                    if i_ >= 1:
                        flat.append(us[i_ - 1][1])
                flat.append(us[-1][1])
                return flat

            for u in kv_units(0):
                u()
            qslots = {}

            def qg_units(c, qb):
                slot = nxt("QBslot", NQS)
                qslots[(c, qb)] = slot
                return [proj_q_unit(c % 2, qb * 512, 512, slot), proj_g_unit(c % 2, qb * 512, 512, slot)]

            for u in qg_units(0, 0):
                u()
            pend_out = []
            ku_cache = {}
            blocks = [(c, qb) for c in range(8) for qb in range(NQB)]
            for bi_, (c, qb) in enumerate(blocks):
                wset = c % 2
                kvset = kvset_of(c)
                bg = list(pend_out)
                pend_out = []
                if qb == min(1, NQB - 1) and c + 1 < 8:
                    if NQB == 1:
                        for u in bg:
                            u()
                        bg = []
                    load_weights(l, c + 1, (c + 1) % 2)
                if bi_ + 1 < len(blocks):
                    bg += qg_units(*blocks[bi_ + 1]) if blocks[bi_ + 1][0] == c else []
                if c + 1 < 8 and qb >= min(1, NQB - 1):
                    if (c + 1) not in ku_cache:
                        ku_cache[c + 1] = kv_units(c + 1)
                    ku = ku_cache[c + 1]
                    nshare = max(1, NQB - 1)
                    idx = qb - min(1, NQB - 1)
                    per = (len(ku) + nshare - 1) // nshare
                    bg += ku[idx * per:(idx + 1) * per]
                    if qb == NQB - 1:
                        bg += qg_units(c + 1, 0)
                qslot = qslots[(c, qb)]
                oslot = nxt("OT", 2)
                ktb, vbb = bf("KT%d" % kvset), bf("VB%d" % kvset)
                if kind in (0, 1):
                    ktiles = []
                    for kt in range(4 * qb + 4):
                        c0 = max(0, kt - 4 * qb) * P
                        bias = []
                        for i in range(4):
                            qt = 4 * qb + i
                            if kind == 0:
                                if kt == qt:
                                    bias.append((i * P, P, 0, P))
                                elif kt == qt - 1:
                                    bias.append((i * P, P, 1, P))
                            else:
                                if kt == qt:
                                    bias.append((i * P, P, 0, P))
                        ktiles.append(dict(kT=(KT[kvset], kt * P), kTb=ktb, v=VB[kvset][:, kt, :], vb=vbb,
                                           nk=P, c0=c0, bias=bias))
                    if kind == 0:
                        attn_da(c, qslot, 512, 0, ktiles, bg)
                    else:
                        attn_sb(c, qslot, 512, 0, ktiles, bg)
                else:
                    qtiles = []
                    for i in range(4):
                        qt = 4 * qb + i
                        kts = []
                        if qt > 0:
                            kts.append(dict(kT=(KT[kvset], (qt - 1) * P), kTb=ktb, v=VB[kvset][:, qt - 1, :],
                                            vb=vbb, nk=P, mi=3))
                        kts.append(dict(kT=(KT[kvset], qt * P), kTb=ktb, v=VB[kvset][:, qt, :], vb=vbb,
                                        nk=P, mi=2))
                        qtiles.append(dict(qc0=i * P, nq=P, acol=i * P, ktiles=kts))
                    attn_swa_p(c, qslot, qtiles, bg)
                pre = []
                if kind == 0:
                    post_da_a(li, qslot, oslot, 512, 0)
                    pre = [lambda li=li, qslot=qslot, oslot=oslot: post_da_b(li, qslot, oslot, 512, 0)]
                elif kind == 1:
                    post_sb(qslot, oslot, 512, 0)
                else:
                    post_swa(c, qslot, oslot, 512, 0)
                for u in bg:
                    u()
                xdst = [(P, (lambda hf, i=i, qb=qb: X[:, 4 * qb + i, hf * 512:(hf + 1) * 512])) for i in range(4)]
                pend_out = pre + out_proj_units(oslot, wset, 4, xdst)
            for u in pend_out:
                u()
        load_gain(4)
        zero_acc()
        for n in range(NT):
            final_norm_tile(X[:, n, :], P, n, yp_d[s, n * P:(n + 1) * P, :])

    def sample_pass():
        sch.add("sp", lambda e: e.dma_start(out=X[0:TS, 0, :], in_=xs_d), writes=[bf("X")], dma=bf("Xd"))
        for l in range(4):
            kind = KINDS[l]
            li = 0 if l == 0 else 1
            swa = kind == 2
            load_gain(l)
            zero_acc()
            norm_tile(X[0:TS, 0, :], TS, 0, None, (0, TS))
            for c in range(8):
                wset = c % 2
                load_weights(l, c, wset)
                slot = nxt("QBslot", NQS)
                proj_q_unit(wset, 0, TS, slot)()
                proj_g_unit(wset, 0, TS, slot)()
                oslot = nxt("OT", 2)
                for b in range(NSEQ):
                    st = nxt("SKV", 2)
                    ktb, vbb = bf("KT%d" % st), bf("VB%d" % st)
                    ktnb, vnb = bf("KTN%d" % st), bf("VN%d" % st)
                    if swa:
                        g = c // 2
                        for hh in range(2):
                            sch.add("pool", lambda e, hh=hh, st=st, g=g, b=b, l=l: e.dma_start(
                                out=KC[st][:, 0, hh * 64:(hh + 1) * 64], in_=ck_d[l][b, :, g * 64:(g + 1) * 64]),
                                writes=[bf("KC%d" % st)], dma=bf("KCd%d_%d" % (st, hh)))
                            sch.add("pool", lambda e, hh=hh, st=st, g=g, b=b, l=l: e.dma_start(
                                out=VB[st][:, 0, hh * 64:(hh + 1) * 64], in_=cv_d[l][b, :, g * 64:(g + 1) * 64]),
                                writes=[vbb], dma=bf("VBd%d_%d" % (st, hh)))
                        ncache = 1
                    else:
                        sch.add("pool", lambda e, st=st, b=b, l=l, c=c: e.dma_start(
                            out=KC[st][:, :, :],
                            in_=ck_d[l][b, :, c * 128:(c + 1) * 128].rearrange("(n p) f -> p n f", p=P)),
                            writes=[bf("KC%d" % st)], dma=bf("KCd%d_0" % st))
                        sch.add("pool", lambda e, st=st, b=b, l=l, c=c: e.dma_start(
                            out=VB[st][:, 0:NPT, :],
                            in_=cv_d[l][b, :, c * 128:(c + 1) * 128].rearrange("(n p) f -> p n f", p=P)),
                            writes=[vbb], dma=bf("VBd%d_0" % st))
                        ncache = NPT
                    for n in range(ncache):
                        sch.add("pe", lambda e, n=n, st=st: e.transpose(out=T0[:, n * P:(n + 1) * P],
                                                                        in_=KC[st][:, n, :], identity=ident),
                                reads=[bf("KC%d" % st), bCST], writes=[bf("T0")])
                    sch.add("act", lambda e, st=st, ncache=ncache: e.copy(out=KT[st][:, 0:ncache * P],
                                                                          in_=T0[:, 0:ncache * P]),
                            reads=[bf("T0")], writes=[ktb])
                    if swa:
                        g = c // 2
                        outs = []
                        if c % 2 == 0:
                            outs = [(ks_d[l][b, P - T_DEC:P, g * 64:(g + 1) * 64], 0, 64),
                                    (vs_d[l][b, P - T_DEC:P, g * 64:(g + 1) * 64], 64, 64)]
                    else:
                        outs = [(ks_d[l][b * T_DEC:(b + 1) * T_DEC, c * 128:(c + 1) * 128], 0, 128),
                                (vs_d[l][b * T_DEC:(b + 1) * T_DEC, c * 128:(c + 1) * 128], 128, 128)]
                    for u_ in proj_kv_unit(l, c, wset, st, b * T_DEC, T_DEC, (KTN[st], 0, ktnb),
                                           (VN[st][:, :], vnb), outs, swa):
                        u_()
                    acol = b * T_DEC
                    qc0 = b * T_DEC
                    if kind in (0, 1):
                        ktiles = []
                        for kt in range(NPT):
                            bias = []
                            if kind == 0 and kt == NPT - 1:
                                bias = [(0, T_DEC, 1, P)]
                            ktiles.append(dict(kT=(KT[st], kt * P), kTb=ktb, v=VB[st][:, kt, :], vb=vbb,
                                               nk=P, c0=0, bias=bias))
                        ktiles.append(dict(kT=(KTN[st], 0), kTb=ktnb, v=VN[st][:, :], vb=vnb, nk=T_DEC, c0=0,
                                           bias=[(0, T_DEC, 0, T_DEC)]))
                        if kind == 0:
                            attn_da_s(c, slot, T_DEC, acol, qc0, ktiles)
                        else:
                            attn_sb_s(c, slot, T_DEC, acol, qc0, ktiles)
                    else:
                        kts = [dict(kT=(KT[st], 0), kTb=ktb, v=VB[st][:, 0, :], vb=vbb, nk=P, mi=3),
                               dict(kT=(KTN[st], 0), kTb=ktnb, v=VN[st][:, :], vb=vnb, nk=T_DEC, mi=2)]
                        attn_swa(c, slot, [dict(qc0=qc0, nq=T_DEC, acol=acol, ktiles=kts)], None)
                if kind == 0:
                    post_da(li, slot, oslot, TS, 0)
                elif kind == 1:
                    post_sb(slot, oslot, TS, 0)
                else:
                    post_swa(c, slot, oslot, TS, 0)
                xdst = [(TS, (lambda hf: X[0:TS, 0, hf * 512:(hf + 1) * 512]))]
                for u in out_proj_units(oslot, wset, 1, xdst):
                    u()
            if swa:
                for b in range(NSEQ):
                    sch.add("sp", lambda e, b=b, l=l: e.dma_start(out=ks_d[l][b, 0:P - T_DEC, :],
                                                             in_=ck_d[l][b, T_DEC:P, :]), dma=bf("d2d"))
                    sch.add("sp", lambda e, b=b, l=l: e.dma_start(out=vs_d[l][b, 0:P - T_DEC, :],
                                                             in_=cv_d[l][b, T_DEC:P, :]), dma=bf("d2d"))
        load_gain(4)
        zero_acc()
        final_norm_tile(X[0:TS, 0, :], TS, 0, ys_d[:, :])

    def attn_da_s(c, qslot, NQ, acol, qc0, ktiles):
        hmb = 2 * c
        nkt = len(ktiles)
        for kt in range(nkt):
            for m in range(2):
                t = ktiles[kt]
                nk = t["nk"]
                sbk = nxt("SPS", 2)
                Sb = SPS[sbk]
                sbb = bf("SPS%d" % sbk)
                nb = len(t["bias"])
                ktt, kc0 = t["kT"]
                sch.add("pe", lambda e, Sb=Sb, nk=nk, m=m, ktt=ktt, kc0=kc0, nb=nb: e.matmul(
                    Sb[0:nk, 0:NQ], lhsT=ktt[m * 64:(m + 1) * 64, kc0:kc0 + nk],
                    rhs=QB[qslot][m * 64:(m + 1) * 64, qc0:qc0 + NQ], start=True, stop=(nb == 0),
                    skip_group_check=True), reads=[t["kTb"], bf("QB%d" % qslot)], writes=[sbb])
                for bi, (co, ncol, mi, kr) in enumerate(t["bias"]):
                    sch.add("pe", lambda e, Sb=Sb, nk=nk, m=m, mi=mi, bi=bi, nb=nb: e.matmul(
                        Sb[0:nk, 0:NQ], lhsT=ident[0:nk, 0:nk], rhs=BIAS[0:nk, mi, hmb + m, 0:NQ],
                        start=False, stop=(bi == nb - 1), skip_group_check=True),
                        reads=[bCST, bBIAS], writes=[sbb])
                pslot = nxt("PB", NPB)
                Pt = PB[pslot]
                pbb = bf("PB%d" % pslot)
                sch.add("act", lambda e, Pt=Pt, Sb=Sb, nk=nk: e.activation(
                    out=Pt[0:nk, 0:NQ], in_=Sb[0:nk, 0:NQ], func=AF.Exp, scale=0.125),
                    reads=[sbb], writes=[pbb])
                sch.add("pe", lambda e, Pt=Pt, nk=nk, m=m, t=t, kt=kt: e.matmul(
                    APS[m][:, acol:acol + NQ], lhsT=t["v"], rhs=Pt[0:nk, 0:NQ],
                    start=(kt == 0), stop=(kt == nkt - 1), skip_group_check=True),
                    reads=[t["vb"], pbb], writes=[bf("APS%d" % m)])
                sch.add("pe", lambda e, Pt=Pt, nk=nk, m=m, kt=kt: e.matmul(
                    APS[2 + m][:, acol:acol + NQ], lhsT=ones[0:nk, :], rhs=Pt[0:nk, 0:NQ],
                    start=(kt == 0), stop=(kt == nkt - 1), skip_group_check=True),
                    reads=[bCST, pbb], writes=[bf("APS%d" % (2 + m))])

    def attn_sb_s(c, qslot, NQ, acol, qc0, ktiles):
        nkt = len(ktiles)
        for j in range(2):
            for a in (j, 2 + j):
                sch.add("pe", lambda e, a=a: e.matmul(APS[a][:, acol:acol + NQ], lhsT=ZER[:, :],
                                                      rhs=QB[qslot][:, qc0:qc0 + NQ], start=True, stop=False,
                                                      skip_group_check=True),
                        reads=[bf("ZER"), bf("QB%d" % qslot)], writes=[bf("APS%d" % a)])
        for kt in reversed(range(nkt)):
            for j in range(2):
                t = ktiles[kt]
                nk = t["nk"]
                sbk = nxt("SPS", 2)
                Sb = SPS[sbk]
                sbb = bf("SPS%d" % sbk)
                nb = len(t["bias"])
                ktt, kc0 = t["kT"]
                sch.add("pe", lambda e, Sb=Sb, nk=nk, j=j, ktt=ktt, kc0=kc0, nb=nb: e.matmul(
                    Sb[0:nk, 0:NQ], lhsT=ktt[j * 64:(j + 1) * 64, kc0:kc0 + nk],
                    rhs=QB[qslot][j * 64:(j + 1) * 64, qc0:qc0 + NQ], start=True, stop=(nb == 0),
                    skip_group_check=True), reads=[t["kTb"], bf("QB%d" % qslot)], writes=[sbb])
                for bi, (co, ncol, mi, kr) in enumerate(t["bias"]):
                    sch.add("pe", lambda e, Sb=Sb, nk=nk, bi=bi, nb=nb: e.matmul(
                        Sb[0:nk, 0:NQ], lhsT=ident[0:nk, 0:nk], rhs=maskT[0:nk, 0:NQ],
                        start=False, stop=(bi == nb - 1), skip_group_check=True),
                        reads=[bCST], writes=[sbb])
                es_ = nxt("EW", NE)
                E = EW[es_]
                eb = bf("EW%d" % es_)
                spb = SPB[es_]
                spbb = bf("SPB%d" % es_)
                rb = bf("APS%d" % j)
                sch.add("act", lambda e, E=E, Sb=Sb, nk=nk: e.activation(
                    out=E[0:nk, 0:NQ], in_=Sb[0:nk, 0:NQ], func=AF.Exp, scale=0.125), reads=[sbb], writes=[eb])
                sch.add("act", lambda e, E=E, nk=nk: e.activation(
                    out=E[0:nk, 0:NQ], in_=E[0:nk, 0:NQ], func=AF.Ln, bias=1.0), reads=[eb], writes=[eb])
                sch.add("pool", lambda e, E=E, spb=spb, nk=nk: e.tensor_copy(out=spb[0:nk, 0:NQ], in_=E[0:nk, 0:NQ]),
                        reads=[eb], writes=[spbb])
                sch.add("dve", lambda e, E=E, Sb=Sb, nk=nk: e.scalar_tensor_tensor(
                    out=E[0:nk, 0:NQ], in0=Sb[0:nk, 0:NQ], scalar=0.125, in1=E[0:nk, 0:NQ],
                    op0=ALU.mult, op1=ALU.subtract), reads=[sbb, eb, spbb], writes=[eb])
                sch.add("pe", lambda e, spb=spb, nk=nk, j=j: e.matmul(
                    APS[j][:, acol:acol + NQ], lhsT=triu[0:nk, :], rhs=spb[0:nk, 0:NQ], start=False, stop=False,
                    skip_group_check=True), reads=[bCST, spbb], writes=[rb])
                sch.add("dve", lambda e, E=E, nk=nk, j=j: e.tensor_tensor(
                    out=E[0:nk, 0:NQ], in0=E[0:nk, 0:NQ], in1=APS[j][0:nk, acol:acol + NQ], op=ALU.subtract),
                    reads=[eb, rb], writes=[eb])
                pslot = nxt("PB", NPB)
                Ab = PB[pslot]
                pbb = bf("PB%d" % pslot)
                sch.add("act", lambda e, Ab=Ab, E=E, nk=nk: e.activation(
                    out=Ab[0:nk, 0:NQ], in_=E[0:nk, 0:NQ], func=AF.Exp), reads=[eb], writes=[pbb])
                sch.add("pe", lambda e, spb=spb, nk=nk, j=j: e.matmul(
                    APS[j][:, acol:acol + NQ], lhsT=tril[0:nk, :], rhs=spb[0:nk, 0:NQ], start=False, stop=False,
                    skip_group_check=True), reads=[bCST, spbb], writes=[rb])
                sch.add("pe", lambda e, Ab=Ab, nk=nk, j=j, t=t: e.matmul(
                    APS[2 + j][:, acol:acol + NQ], lhsT=t["v"], rhs=Ab[0:nk, 0:NQ], start=False, stop=False,
                    skip_group_check=True), reads=[t["vb"], pbb], writes=[bf("APS%d" % (2 + j))])

    setup()
    for s in range(NSEQ):
        prompt_seq(s)
    sample_pass()
    fin = sch.add("sp", lambda e: e.nop(), reads=[], writes=[])
    for b in sch.dma_bufs:
        fin.waits.append(("dma", b, b.cnt))
    sch.emit(nc, es)
    es.close()
    return nc


_NC_CACHE = {}


def _get_nc(cfg):
    key = (cfg["NSEQ"], cfg["S"], cfg["PAST"])
    if key not in _NC_CACHE:
        _NC_CACHE[key] = build(cfg)
    return _NC_CACHE[key]


def run(cfg, inputs):
    NSEQ, S, PAST, NCORES = cfg["NSEQ"], cfg["S"], cfg["PAST"], cfg["NCORES"]
    nc = _get_nc(cfg)
    f = lambda a: np.ascontiguousarray(np.asarray(a, dtype=np.float32))
    maps_np, consts_np, _ = static_tables()
    gain = np.stack([np.broadcast_to(f(inputs[k])[None, :], (P, D)) for k in
                     ("norm_0", "norm_1", "norm_2", "norm_3", "final_norm")])
    tab = np.broadcast_to(f(inputs["rel_bias_table"]).reshape(1, 512), (P, 512))
    lam = np.stack([np.broadcast_to(f(inputs["da_lambda_%d" % l]).reshape(1, 256), (P, 256)) for l in (0, 3)])
    subln = np.stack([f(inputs["da_subln_%d" % l]).reshape(P, 1) for l in (0, 3)])
    sk = f(inputs["sw_sinks_2"])
    sink = np.zeros((P, 8), np.float32)
    for c in range(8):
        sink[:64, c] = sk[2 * c]
        sink[64:, c] = sk[2 * c + 1]
    shared = dict(gain=f(gain), tab=f(tab), lam=f(lam), subln=f(subln), sink=sink, maps=maps_np, consts=consts_np)
    for l in range(4):
        shared["win%d" % l] = f(inputs["w_in_%d" % l])
        shared["wout%d" % l] = f(inputs["w_out_%d" % l])
    in_maps = []
    for core in range(NCORES):
        sl = slice(core * NSEQ, (core + 1) * NSEQ)
        m = dict(shared)
        m["xp"] = f(inputs["x_prompt"][sl])
        m["xs"] = f(inputs["x_sample"][sl]).reshape(NSEQ * T_DEC, D)
        for l in range(4):
            ck = f(inputs["cache_k_%d" % l][sl])
            cv = f(inputs["cache_v_%d" % l][sl])
            m["ck%d" % l] = ck.reshape(NSEQ, ck.shape[1], -1)
            m["cv%d" % l] = cv.reshape(NSEQ, cv.shape[1], -1)
        in_maps.append(m)
    res = run_bass_kernel_spmd(nc, in_maps, core_ids=list(range(NCORES)))
    R = res.results
    cat = lambda name: np.concatenate([np.asarray(r[name]) for r in R], axis=0)
    Bt = NSEQ * NCORES
    outs = [cat("yp").reshape(Bt, S, D), cat("ys").reshape(Bt, T_DEC, D)]
    for l in range(4):
        kind = KINDS[l]
        if kind == 0:
            outs += [cat("kp%d" % l).reshape(Bt, S, 8, 2, 64), cat("vp%d" % l).reshape(Bt, S, 8, 128),
                     cat("ks%d" % l).reshape(Bt, T_DEC, 8, 2, 64), cat("vs%d" % l).reshape(Bt, T_DEC, 8, 128)]
        elif kind == 1:
            outs += [cat("kp%d" % l).reshape(Bt, S, 16, 64), cat("vp%d" % l).reshape(Bt, S, 16, 64),
                     cat("ks%d" % l).reshape(Bt, T_DEC, 16, 64), cat("vs%d" % l).reshape(Bt, T_DEC, 16, 64)]
        else:
            outs += [cat("kp%d" % l).reshape(Bt, P, 4, 64), cat("vp%d" % l).reshape(Bt, P, 4, 64),
                     cat("ks%d" % l).reshape(Bt, P, 4, 64), cat("vs%d" % l).reshape(Bt, P, 4, 64)]
    return tuple(np.ascontiguousarray(o, dtype=np.float32) for o in outs)


def kernel(x_prompt, x_sample, cache_k_0, cache_v_0, cache_k_1, cache_v_1, cache_k_2, cache_v_2,
           cache_k_3, cache_v_3, rel_bias_table,
           norm_0, w_in_0, w_out_0, da_lambda_0, da_subln_0,
           norm_1, w_in_1, w_out_1,
           norm_2, w_in_2, w_out_2, sw_sinks_2,
           norm_3, w_in_3, w_out_3, da_lambda_3, da_subln_3,
           final_norm):
    inputs = {
        "x_prompt": x_prompt, "x_sample": x_sample,
        "cache_k_0": cache_k_0, "cache_v_0": cache_v_0, "cache_k_1": cache_k_1, "cache_v_1": cache_v_1,
        "cache_k_2": cache_k_2, "cache_v_2": cache_v_2, "cache_k_3": cache_k_3, "cache_v_3": cache_v_3,
        "rel_bias_table": rel_bias_table,
        "norm_0": norm_0, "w_in_0": w_in_0, "w_out_0": w_out_0, "da_lambda_0": da_lambda_0, "da_subln_0": da_subln_0,
        "norm_1": norm_1, "w_in_1": w_in_1, "w_out_1": w_out_1,
        "norm_2": norm_2, "w_in_2": w_in_2, "w_out_2": w_out_2, "sw_sinks_2": sw_sinks_2,
        "norm_3": norm_3, "w_in_3": w_in_3, "w_out_3": w_out_3, "da_lambda_3": da_lambda_3, "da_subln_3": da_subln_3,
        "final_norm": final_norm,
    }
    return run(FULL_CFG, inputs)
```

```python
import math
from contextlib import ExitStack

import numpy as np
import concourse.bass as bass
import concourse.mybir as mybir
from concourse.bass_utils import run_bass_kernel_spmd

F32 = mybir.dt.float32
BF16 = mybir.dt.bfloat16
ALU = mybir.AluOpType
AF = mybir.ActivationFunctionType

D = 1024
P = 128
NCH = 8
EPS = 1e-6
NEG = -240000.0
T_DEC = 16
KINDS = (0, 1, 2, 0)
IN_W = (4096, 4096, 2560)

FULL_CFG = dict(NSEQ=4, S=2048, PAST=1024, NCORES=8)


class Buf:
    __slots__ = ("name", "last_w", "readers", "sem", "cnt", "psum")

    def __init__(self, name):
        self.name = name
        self.psum = False
        self.last_w = None
        self.readers = []
        self.sem = None
        self.cnt = 0


class Op:
    __slots__ = ("idx", "eng", "fn", "waits", "signal", "sigval", "dma", "dma_val")


ENGS = ("pe", "act", "dve", "pool", "sp")


class Sched:
    def __init__(self):
        self.ops = []
        self.by_eng = {e: [] for e in ENGS}
        self.waited = {e: {f: -1 for f in ENGS} for e in ENGS}
        self.dma_waited = {e: {} for e in ENGS}
        self.dma_bufs = []

    def add(self, eng, fn, reads=(), writes=(), dma=None):
        op = Op()
        op.idx = len(self.ops)
        op.eng = eng
        op.fn = fn
        op.waits = []
        op.signal = False
        op.sigval = 0
        op.dma = dma
        op.dma_val = 0
        deps = set()
        for b in reads:
            if b.last_w is not None:
                deps.add(b.last_w)
            if b.psum:
                deps.update(r for r in b.readers if self.ops[r].eng != eng)
        for b in writes:
            if b.last_w is not None:
                deps.add(b.last_w)
            deps.update(b.readers)
        for d in sorted(deps, reverse=True):
            dop = self.ops[d]
            if dop.dma is not None:
                if self.dma_waited[eng].get(dop.dma, 0) >= dop.dma_val:
                    continue
                self.dma_waited[eng][dop.dma] = dop.dma_val
                op.waits.append(("dma", dop.dma, dop.dma_val))
            else:
                if dop.eng == "pe" and eng == "pe" and dma is None:
                    continue
                if self.waited[eng][dop.eng] >= d:
                    continue
                self.waited[eng][dop.eng] = d
                dop.signal = True
                op.waits.append(("eng", dop))
        for b in reads:
            b.readers.append(op.idx)
        for b in writes:
            b.last_w = op.idx
            b.readers = []
        if dma is not None:
            if dma.cnt == 0:
                self.dma_bufs.append(dma)
            dma.cnt += 16
            op.dma_val = dma.cnt
        self.ops.append(op)
        self.by_eng[eng].append(op)
        return op

    def emit(self, nc, es):
        for e in ENGS:
            cnt = 0
            for op in self.by_eng[e]:
                if op.signal:
                    cnt += 1
                    op.sigval = cnt
        esem = {e: es.enter_context(nc.semaphore("sem_" + e)) for e in ENGS}
        for i, b in enumerate(self.dma_bufs):
            b.sem = es.enter_context(nc.semaphore("dsem%d" % i))

        def run(ename, eng):
            for op in self.by_eng[ename]:
                for w in op.waits:
                    if w[0] == "dma":
                        eng.wait_ge(w[1].sem, w[2])
                    else:
                        eng.wait_ge(esem[w[1].eng], w[1].sigval)
                ins = op.fn(eng)
                if op.dma is not None:
                    ins.then_inc(op.dma.sem, 16)
                elif op.signal:
                    ins.then_inc(esem[ename], 1)

        with nc.Block() as block:
            @block.tensor
            def _(e):
                run("pe", e)

            @block.scalar
            def _(e):
                run("act", e)

            @block.vector
            def _(e):
                run("dve", e)

            @block.gpsimd
            def _(e):
                run("pool", e)

            @block.sync
            def _(e):
                run("sp", e)


def _bucket_np(rel):
    import jax
    import jax.numpy as jnp
    cpu = jax.devices("cpu")[0]
    with jax.default_device(cpu):
        rel = jnp.asarray(rel, dtype=jnp.int32)
        nb = 16
        max_exact = 8
        n = jnp.abs(rel)
        nf = jnp.maximum(n, 1).astype(jnp.float32)
        large = max_exact + (jnp.log(nf / max_exact) / math.log(128 / max_exact)
                             * (nb - max_exact)).astype(jnp.int32)
        large = jnp.minimum(large, nb - 1)
        out = jnp.where(rel > 0, nb, 0) + jnp.where(n < max_exact, n, large)
        return np.asarray(out)


def static_tables():
    k = np.arange(P)[:, None]
    q = np.arange(P)[None, :]
    bd = _bucket_np(k - q).astype(np.float32)
    bs = _bucket_np(k - P - q).astype(np.float32)
    maps = np.zeros((4, P, P), np.float32)
    m = bd.copy(); m[(k // 64) > (q // 64)] = -1.0
    maps[0] = m
    maps[1] = bs
    m = bd.copy(); m[(k >= 64) & (q < 64)] = -1.0
    maps[2] = m
    m = bs.copy(); m[(k < 64) & (q >= 64)] = -1.0
    maps[3] = m
    consts = np.zeros((5, P, P), np.float32)
    consts[0] = np.eye(P)
    consts[1] = (k > q)
    consts[2] = (k <= q)
    consts[3] = np.where(k < q, 0.0, NEG)
    consts[4] = 1.0
    present = [sorted(set(int(v) for v in np.unique(maps[i]) if v >= 0 and v != 15)) for i in range(4)]
    return maps, consts, present


def build(cfg):
    NSEQ, S, PAST = cfg["NSEQ"], cfg["S"], cfg["PAST"]
    NT = S // P
    NQB = S // 512
    NPT = PAST // P
    TS = NSEQ * T_DEC
    assert S % 512 == 0 and PAST % P == 0 and TS <= 64
    maps_np, consts_np, present = static_tables()

    nc = bass.Bass("TRN2", target_bir_lowering=False)

    def din(name, shape):
        return nc.dram_tensor(name, list(shape), F32, kind="ExternalInput").ap()

    def dout(name, shape):
        return nc.dram_tensor(name, list(shape), F32, kind="ExternalOutput").ap()

    xp_d = din("xp", (NSEQ, S, D))
    xs_d = din("xs", (TS, D))
    ck_d, cv_d = [], []
    for l in range(4):
        if KINDS[l] == 2:
            ck_d.append(din("ck%d" % l, (NSEQ, P, 256)))
            cv_d.append(din("cv%d" % l, (NSEQ, P, 256)))
        else:
            ck_d.append(din("ck%d" % l, (NSEQ, PAST, D)))
            cv_d.append(din("cv%d" % l, (NSEQ, PAST, D)))
    win_d = [din("win%d" % l, (D, IN_W[KINDS[l]])) for l in range(4)]
    wout_d = [din("wout%d" % l, (D, D)) for l in range(4)]
    gain_d = din("gain", (5, P, D))
    tab_d = din("tab", (P, 512))
    lam_d = din("lam", (2, P, 256))
    subln_d = din("subln", (2, P, 1))
    sink_d = din("sink", (P, 8))
    maps_d = din("maps", (4, P, P))
    consts_d = din("consts", (5, P, P))

    yp_d = dout("yp", (NSEQ, S, D))
    ys_d = dout("ys", (TS, D))
    kp_d, vp_d, ks_d, vs_d = [], [], [], []
    for l in range(4):
        if KINDS[l] == 2:
            kp_d.append(dout("kp%d" % l, (NSEQ, P, 256)))
            vp_d.append(dout("vp%d" % l, (NSEQ, P, 256)))
            ks_d.append(dout("ks%d" % l, (NSEQ, P, 256)))
            vs_d.append(dout("vs%d" % l, (NSEQ, P, 256)))
        else:
            kp_d.append(dout("kp%d" % l, (NSEQ, S, D)))
            vp_d.append(dout("vp%d" % l, (NSEQ, S, D)))
            ks_d.append(dout("ks%d" % l, (TS, D)))
            vs_d.append(dout("vs%d" % l, (TS, D)))

    es = ExitStack()
    sch = Sched()

    def sb(name, shape, dt):
        return es.enter_context(nc.sbuf_tensor(name, list(shape), dt))

    X = sb("X", (P, NT, D), F32)
    xnT = sb("xnT", (P, NCH, S), BF16)
    Gn = sb("Gn", (P, D), F32)
    WT = [sb("WT%d" % i, (P, NCH, 512), BF16) for i in range(2)]
    WO = [sb("WO%d" % i, (P, D), BF16) for i in range(2)]
    KT = [sb("KT%d" % i, (P, S), BF16) for i in range(2)]
    VB = [sb("VB%d" % i, (P, NT, P), BF16) for i in range(2)]
    KTN = [sb("KTN%d" % i, (P, T_DEC), BF16) for i in range(2)]
    VN = [sb("VN%d" % i, (T_DEC, P), BF16) for i in range(2)]
    KC = [sb("KC%d" % i, (P, max(NPT, 1), P), BF16) for i in range(2)]
    NQS = 3
    QB = [sb("QB%d" % i, (P, 512), BF16) for i in range(NQS)]
    GB = [sb("GB%d" % i, (P, 512), BF16) for i in range(NQS)]
    OT = [sb("OT%d" % i, (P, 512), BF16) for i in range(2)]
    NKVS = 3
    KVS = [sb("KVS%d" % i, (P, 256), F32) for i in range(NKVS)]
    NKBT = 4
    KBT = [sb("KBT%d" % i, (P, P), BF16) for i in range(NKBT)]
    NE = 4
    EW = [sb("EW%d" % i, (P, 512), F32) for i in range(NE)]
    SPB = [sb("SPB%d" % i, (P, 512), BF16) for i in range(NE)]
    NPB = 3
    PB = [sb("PB%d" % i, (P, 512), BF16) for i in range(NPB)]
    TP = [sb("TP%d" % i, (P, 512), F32) for i in range(2)]
    SQ = sb("SQ", (P, 512), BF16)
    XN = [sb("XN%d" % i, (P, D), BF16) for i in range(2)]
    SS = sb("SS", (P, 2 * NT + 4), F32)
    BIAS = sb("BIAS", (P, 4, 16, P), BF16)
    CST = sb("CST", (P, 5, P), BF16)
    ZER = sb("ZER", (P, P), BF16)
    LAMC = sb("LAMC", (P, 2, 8), F32)
    SCC = sb("SCC", (P, 2), F32)
    ESK = sb("ESK", (P, 8), F32)
    LAMT = sb("LAMT", (P, 8), F32)
    SCR = sb("SCR", (P, 3200), F32)
    XO = [SCR[:, i * D:(i + 1) * D] for i in range(2)]
    MAPS = SCR[:, 0:512].rearrange("p (c q) -> p c q", c=4)
    CSTF = SCR[:, 512:1152].rearrange("p (c q) -> p c q", c=5)
    OH = [SCR[:, 1152 + i * 128:1152 + (i + 1) * 128] for i in range(2)]
    ACC = SCR[:, 1408:2432].rearrange("p (h q) -> p h q", h=8)
    TABP = SCR[:, 2432:2944]
    LAMW = SCR[:, 2944:3200]

    def ps(name, dt=F32, cols=512):
        return es.enter_context(nc.psum_tensor(name, [P, cols], dt))

    SPS = [ps("SPS0"), ps("SPS1")]
    APS = [ps("APS%d" % i) for i in range(4)]
    M0 = ps("M0")
    T0 = ps("T0", BF16, 1024)

    B = {}

    def bf(name):
        if name not in B:
            B[name] = Buf(name)
        return B[name]

    for nm in ("SPS0", "SPS1", "APS0", "APS1", "APS2", "APS3", "M0", "T0"):
        bf(nm).psum = True
    ident = CST[:, 0, :]
    triu = CST[:, 1, :]
    tril = CST[:, 2, :]
    maskT = CST[:, 3, :]
    ones = CST[:, 4, :]
    bCST = bf("CST")
    bBIAS = bf("BIAS")

    rot = {}

    def nxt(name, n):
        v = rot.get(name, 0)
        rot[name] = v + 1
        return v % n

    def setup():
        sch.add("sp", lambda e: e.dma_start(out=CSTF, in_=consts_d.rearrange("c p q -> p c q")),
                writes=[bf("CSTF")], dma=bf("CSTF"))
        sch.add("sp", lambda e: e.dma_start(out=MAPS, in_=maps_d.rearrange("c p q -> p c q")),
                writes=[bf("MAPS")], dma=bf("MAPS"))
        sch.add("sp", lambda e: e.dma_start(out=TABP, in_=tab_d), writes=[bf("TABP")], dma=bf("TABP"))
        sch.add("sp", lambda e: e.dma_start(out=ESK[:, :], in_=sink_d), writes=[bf("ESK")], dma=bf("ESK"))
        sch.add("dve", lambda e: e.tensor_copy(out=CST[:, :, :], in_=CSTF),
                reads=[bf("CSTF")], writes=[bCST])
        sch.add("dve", lambda e: e.memset(ZER[:, :], 0.0), writes=[bf("ZER")])
        for b_ in range(32):
            if b_ == 15:
                continue
            sch.add("dve", lambda e, b_=b_: e.tensor_tensor(
                out=TABP[:, b_ * 16:(b_ + 1) * 16], in0=TABP[:, b_ * 16:(b_ + 1) * 16],
                in1=TABP[:, 240:256], op=ALU.subtract), reads=[bf("TABP")], writes=[bf("TABP")])
        for c in range(8):
            for j in range(2):
                r0 = j * 64
                sch.add("dve", lambda e, c=c, j=j, r0=r0: e.tensor_tensor(
                    out=ESK[r0:r0 + 64, c:c + 1], in0=ESK[r0:r0 + 64, c:c + 1],
                    in1=TABP[r0:r0 + 64, 240 + 2 * c + j:240 + 2 * c + j + 1], op=ALU.subtract),
                    reads=[bf("TABP"), bf("ESK")], writes=[bf("ESK")])
        sch.add("act", lambda e: e.activation(out=ESK[:, :], in_=ESK[:, :], func=AF.Exp),
                reads=[bf("ESK")], writes=[bf("ESK")])
        sch.add("dve", lambda e: e.tensor_scalar(out=TABP, in0=TABP, scalar1=8.0, scalar2=None,
                                                 op0=ALU.mult), reads=[bf("TABP")], writes=[bf("TABP")])
        for mi in range(4):
            eng = "dve"
            accb = bf("ACC")
            sch.add(eng, lambda e, mi=mi: e.tensor_scalar(
                out=OH[0], in0=MAPS[:, mi, :], scalar1=-1.0, scalar2=NEG, op0=ALU.is_equal, op1=ALU.mult),
                reads=[bf("MAPS")], writes=[bf("OH0")])
            for h0 in (0, 8):
                for h in range(8):
                    sch.add(eng, lambda e, h=h: e.tensor_copy(out=ACC[:, h, :], in_=OH[0]),
                            reads=[bf("OH0")], writes=[accb])
                for b_ in present[mi]:
                    sch.add(eng, lambda e, mi=mi, b_=b_: e.tensor_single_scalar(
                        out=OH[1], in_=MAPS[:, mi, :], scalar=float(b_), op=ALU.is_equal),
                        reads=[bf("MAPS")], writes=[bf("OH1")])
                    for h in range(8):
                        sch.add(eng, lambda e, h=h, b_=b_, h0=h0: e.scalar_tensor_tensor(
                            out=ACC[:, h, :], in0=OH[1],
                            scalar=TABP[:, b_ * 16 + h0 + h:b_ * 16 + h0 + h + 1],
                            in1=ACC[:, h, :], op0=ALU.mult, op1=ALU.add),
                            reads=[bf("OH1"), bf("TABP"), accb], writes=[accb])
                sch.add(eng, lambda e, mi=mi, h0=h0: e.tensor_copy(out=BIAS[:, mi, h0:h0 + 8, :], in_=ACC),
                        reads=[accb], writes=[bBIAS])
        for li, l in enumerate((0, 3)):
            lam_init = 0.8 - 0.6 * math.exp(-0.3 * l)
            sch.add("sp", lambda e, li=li: e.dma_start(out=LAMW, in_=lam_d[li]),
                    writes=[bf("LAMW")], dma=bf("LAMW"))
            sch.add("sp", lambda e, li=li: e.dma_start(out=SCC[:, li:li + 1], in_=subln_d[li]),
                    writes=[bf("SCC")], dma=bf("SCC%d" % li))
            for t in range(2):
                sch.add("dve", lambda e, t=t: e.tensor_tensor(
                    out=LAMW[:, t * 128:t * 128 + 64], in0=LAMW[:, t * 128:t * 128 + 64],
                    in1=LAMW[:, t * 128 + 64:t * 128 + 128], op=ALU.mult),
                    reads=[bf("LAMW")], writes=[bf("LAMW")])
                sch.add("dve", lambda e, t=t: e.tensor_reduce(
                    out=LAMT[:, t:t + 1], in_=LAMW[:, t * 128:t * 128 + 64], axis=mybir.AxisListType.X,
                    op=ALU.add), reads=[bf("LAMW")], writes=[bf("LAMT")])
            sch.add("act", lambda e: e.activation(out=LAMT[:, 0:2], in_=LAMT[:, 0:2], func=AF.Exp),
                    reads=[bf("LAMT")], writes=[bf("LAMT")])
            sch.add("dve", lambda e, li=li, lam_init=lam_init: e.scalar_tensor_tensor(
                out=LAMC[:, li, 0:1], in0=LAMT[:, 1:2], scalar=-lam_init, in1=LAMT[:, 0:1],
                op0=ALU.add, op1=ALU.subtract), reads=[bf("LAMT")], writes=[bf("LAMC")])
            sch.add("dve", lambda e, li=li, lam_init=lam_init: e.tensor_scalar(
                out=SCC[:, li:li + 1], in0=SCC[:, li:li + 1], scalar1=(1.0 - lam_init) * math.sqrt(128.0),
                scalar2=None, op0=ALU.mult), reads=[bf("SCC")], writes=[bf("SCC")])

    def load_weights(l, c, wset):
        kind = KINDS[l]
        W = WT[wset]
        wb = bf("WT%d" % wset)
        if kind in (0, 1):
            cols = [c * 128, 3072 + c * 128, 1024 + c * 128, 2048 + c * 128]
            for si, c0 in enumerate(cols):
                sch.add("pool", lambda e, si=si, c0=c0, W=W: e.dma_start(
                    out=W[:, :, si * 128:(si + 1) * 128],
                    in_=win_d[l][:, c0:c0 + 128].rearrange("(c p) f -> p c f", p=P)),
                    writes=[wb], dma=bf("WTd%d_%d" % (wset, si)))
        else:
            g = c // 2
            cols = [(c * 128, 128, 0), (1536 + c * 128, 128, 128), (1024 + g * 64, 64, 256),
                    (1280 + g * 64, 64, 320)]
            for si, (c0, w, o0) in enumerate(cols):
                sch.add("pool", lambda e, c0=c0, w=w, o0=o0, W=W: e.dma_start(
                    out=W[:, :, o0:o0 + w],
                    in_=win_d[l][:, c0:c0 + w].rearrange("(c p) f -> p c f", p=P)),
                    writes=[wb], dma=bf("WTd%d_%d" % (wset, si)))
        sch.add("pool", lambda e: e.dma_start(out=WO[wset][:, :], in_=wout_d[l][c * 128:(c + 1) * 128, :]),
                writes=[bf("WO%d" % wset)], dma=bf("WOd%d" % wset))

    def norm_tile(xrows_ap, nrows, col, gidx_loaded, xnT_cols):
        slot = nxt("XN", 2)
        xn = XN[slot]
        xb = bf("XN%d" % slot)
        sch.add("act", lambda e: e.activation(out=xn[0:nrows, :], in_=xrows_ap, func=AF.Square,
                                              accum_out=SS[0:nrows, col:col + 1]),
                reads=[bf("X")], writes=[xb, bf("SS")])
        sch.add("act", lambda e: e.activation(out=SS[0:nrows, col:col + 1], in_=SS[0:nrows, col:col + 1],
                                              func=AF.Ln, scale=1.0 / D, bias=EPS),
                reads=[bf("SS")], writes=[bf("SS")])
        sch.add("act", lambda e: e.activation(out=SS[0:nrows, col:col + 1], in_=SS[0:nrows, col:col + 1],
                                              func=AF.Exp, scale=-0.5),
                reads=[bf("SS")], writes=[bf("SS")])
        sch.add("dve", lambda e: e.scalar_tensor_tensor(out=xn[0:nrows, :], in0=xrows_ap,
                                                        scalar=SS[0:nrows, col:col + 1], in1=Gn[0:nrows, :],
                                                        op0=ALU.mult, op1=ALU.mult),
                reads=[bf("X"), bf("SS"), bf("Gn")], writes=[xb])
        for ch in range(NCH):
            sch.add("pe", lambda e, ch=ch: e.transpose(out=T0[:, ch * 128:ch * 128 + nrows],
                                                       in_=xn[0:nrows, ch * 128:(ch + 1) * 128],
                                                       identity=ident[0:nrows, 0:nrows]),
                    reads=[xb, bCST], writes=[bf("T0")])
        c0, c1 = xnT_cols
        sch.add("act", lambda e: e.copy(
            out=xnT[:, :, c0:c1],
            in_=T0[:, :].rearrange("p (c t) -> p c t", c=NCH)[:, :, 0:nrows]),
            reads=[bf("T0")], writes=[bf("xnT")])

    def zero_acc():
        sch.add("dve", lambda e: e.memset(SS[:, :], 0.0), writes=[bf("SS")])

    def out_proj_units(ot_slot, wset, ntok_tiles, xdst):
        units = []
        for i, (rows, xap_fn) in enumerate(xdst):
            for hf in range(2):
                def u(i=i, hf=hf, rows=rows, xap_fn=xap_fn):
                    Mx, mb = M0, bf("M0")
                    sch.add("pe", lambda e: e.matmul(Mx[0:rows, :], lhsT=OT[ot_slot][:, i * 128:i * 128 + rows],
                                                     rhs=WO[wset][:, hf * 512:(hf + 1) * 512],
                                                     start=True, stop=True),
                            reads=[bf("OT%d" % ot_slot), bf("WO%d" % wset)], writes=[mb])
                    xa = xap_fn(hf)
                    sch.add("dve", lambda e: e.tensor_tensor(out=xa, in0=Mx[0:rows, :], in1=xa, op=ALU.add),
                            reads=[mb, bf("X")], writes=[bf("X")])
                units.append(u)
        return units

    def proj_q_unit(wset, tok0, ntok, slot):
        def u():
            for ch in range(NCH):
                sch.add("pe", lambda e, ch=ch: e.matmul(M0[:, 0:ntok], lhsT=WT[wset][:, ch, 0:128],
                                                        rhs=xnT[:, ch, tok0:tok0 + ntok],
                                                        start=(ch == 0), stop=(ch == NCH - 1)),
                        reads=[bf("WT%d" % wset), bf("xnT")], writes=[bf("M0")])
            sch.add("dve", lambda e: e.tensor_copy(out=QB[slot][:, 0:ntok], in_=M0[:, 0:ntok]),
                    reads=[bf("M0")], writes=[bf("QB%d" % slot)])
        return u

    def proj_g_unit(wset, tok0, ntok, slot):
        def u():
            for ch in range(NCH):
                sch.add("pe", lambda e, ch=ch: e.matmul(M0[:, 0:ntok], lhsT=WT[wset][:, ch, 128:256],
                                                        rhs=xnT[:, ch, tok0:tok0 + ntok],
                                                        start=(ch == 0), stop=(ch == NCH - 1)),
                        reads=[bf("WT%d" % wset), bf("xnT")], writes=[bf("M0")])
            sch.add("act", lambda e: e.activation(out=GB[slot][:, 0:ntok], in_=M0[:, 0:ntok], func=AF.Silu),
                    reads=[bf("M0")], writes=[bf("GB%d" % slot)])
        return u

    def proj_kv_unit(l, c, wset, kvset, tok0, nrows, kt_cols, v_dst, out_dmas, swa, ev="act"):
        st = {}

        def u1():
            width = 128 if swa else 256
            for ch in range(NCH):
                sch.add("pe", lambda e, ch=ch: e.matmul(M0[0:nrows, 0:width],
                                                        lhsT=xnT[:, ch, tok0:tok0 + nrows],
                                                        rhs=WT[wset][:, ch, 256:256 + width],
                                                        start=(ch == 0), stop=(ch == NCH - 1)),
                        reads=[bf("WT%d" % wset), bf("xnT")], writes=[bf("M0")])
            kbs = nxt("KBT", NKBT)
            kbt = KBT[kbs]
            kbb = bf("KBT%d" % kbs)
            st["kbt"], st["kbb"] = kbt, kbb
            vb_ = v_dst[1]
            def cp(e, out, in_):
                return e.copy(out=out, in_=in_) if ev == "act" else e.tensor_copy(out=out, in_=in_)
            if swa:
                for hh in range(2):
                    sch.add(ev, lambda e, hh=hh: cp(e, kbt[0:nrows, hh * 64:(hh + 1) * 64], M0[0:nrows, 0:64]),
                            reads=[bf("M0")], writes=[kbb])
                    sch.add(ev, lambda e, hh=hh: cp(e, v_dst[0][:, hh * 64:(hh + 1) * 64], M0[0:nrows, 64:128]),
                            reads=[bf("M0")], writes=[vb_])
            else:
                sch.add(ev, lambda e: cp(e, kbt[0:nrows, :], M0[0:nrows, 0:128]),
                        reads=[bf("M0")], writes=[kbb])
                sch.add(ev, lambda e: cp(e, v_dst[0], M0[0:nrows, 128:256]),
                        reads=[bf("M0")], writes=[vb_])
            if out_dmas:
                ks = nxt("KVS", NKVS)
                kvs = KVS[ks]
                kb = bf("KVS%d" % ks)
                sch.add("dve", lambda e: e.tensor_copy(out=kvs[0:nrows, 0:width], in_=M0[0:nrows, 0:width]),
                        reads=[bf("M0")], writes=[kb])
                for (dst_ap, c0, w) in out_dmas:
                    sch.add("sp", lambda e, dst_ap=dst_ap, c0=c0, w=w: e.dma_start(out=dst_ap,
                                                                                  in_=kvs[0:nrows, c0:c0 + w]),
                            reads=[kb], dma=kb)

        def u2():
            kbt, kbb = st["kbt"], st["kbb"]
            sch.add("pe", lambda e: e.transpose(out=T0[:, 0:nrows], in_=kbt[0:nrows, :],
                                                identity=ident[0:nrows, 0:nrows]),
                    reads=[kbb, bCST], writes=[bf("T0")])
            ktt, ktc0, ktb = kt_cols
            if ev == "act":
                sch.add("act", lambda e: e.copy(out=ktt[:, ktc0:ktc0 + nrows], in_=T0[:, 0:nrows]),
                        reads=[bf("T0")], writes=[ktb])
            else:
                sch.add("dve", lambda e: e.tensor_copy(out=ktt[:, ktc0:ktc0 + nrows], in_=T0[:, 0:nrows]),
                        reads=[bf("T0")], writes=[ktb])
        return u1, u2

    def attn_da(c, qslot, NQ, acol, ktiles, bg):
        hmb = 2 * c
        items = [(kt, m) for kt in range(len(ktiles)) for m in range(2)]
        nkt = len(ktiles)

        def qk(it):
            kt, m = it
            t = ktiles[kt]
            sbk = nxt("SPS", 2)
            Sb = SPS[sbk]
            sbb = bf("SPS%d" % sbk)
            c0, nk = t["c0"], t["nk"]
            nb = len(t["bias"])
            ktt, kc0 = t["kT"]
            sch.add("pe", lambda e: e.matmul(Sb[0:nk, c0:NQ], lhsT=ktt[m * 64:(m + 1) * 64, kc0:kc0 + nk],
                                             rhs=QB[qslot][m * 64:(m + 1) * 64, c0:NQ],
                                             start=True, stop=(nb == 0), skip_group_check=True),
                    reads=[t["kTb"], bf("QB%d" % qslot)], writes=[sbb])
            for bi, (co, ncol, mi, kr) in enumerate(t["bias"]):
                sch.add("pe", lambda e, co=co, ncol=ncol, mi=mi, bi=bi: e.matmul(
                    Sb[0:nk, co:co + ncol], lhsT=ident[0:nk, 0:nk], rhs=BIAS[0:nk, mi, hmb + m, 0:ncol],
                    start=False, stop=(bi == nb - 1), skip_group_check=True),
                    reads=[bCST, bBIAS], writes=[sbb])
            return Sb, sbb

        def rest(it, Sb, sbb):
            kt, m = it
            t = ktiles[kt]
            c0, nk = t["c0"], t["nk"]
            pslot = nxt("PB", NPB)
            Pt = PB[pslot]
            pbb = bf("PB%d" % pslot)
            sch.add("act", lambda e: e.activation(out=Pt[0:nk, c0:NQ], in_=Sb[0:nk, c0:NQ], func=AF.Exp,
                                                  scale=0.125), reads=[sbb], writes=[pbb])
            sch.add("pe", lambda e: e.matmul(APS[m][:, acol + c0:acol + NQ], lhsT=t["v"], rhs=Pt[0:nk, c0:NQ],
                                             start=(kt == 0), stop=(kt == nkt - 1), skip_group_check=True),
                    reads=[t["vb"], pbb], writes=[bf("APS%d" % m)])
            sch.add("pe", lambda e: e.matmul(APS[2 + m][:, acol + c0:acol + NQ], lhsT=ones[0:nk, :],
                                             rhs=Pt[0:nk, c0:NQ],
                                             start=(kt == 0), stop=(kt == nkt - 1), skip_group_check=True),
                    reads=[bCST, pbb], writes=[bf("APS%d" % (2 + m))])

        cur = qk(items[0])
        for i, it in enumerate(items):
            nx = qk(items[i + 1]) if i + 1 < len(items) else None
            rest(it, *cur)
            cur = nx
            if bg:
                bg.pop(0)()

    def post_da_a(li, qslot, oslot, NQ, acol):
        T_a, T_b = TP[0], TP[1]
        ba, bb_ = bf("TP0"), bf("TP1")
        sl = slice(acol, acol + NQ)
        sch.add("act", lambda e: e.activation(out=T_a[:, 0:NQ], in_=APS[2][:, sl], func=AF.Ln),
                reads=[bf("APS2")], writes=[ba])
        sch.add("act", lambda e: e.activation(out=T_a[:, 0:NQ], in_=T_a[:, 0:NQ], func=AF.Exp, scale=-1.0),
                reads=[ba], writes=[ba])
        sch.add("dve", lambda e: e.tensor_tensor(out=T_a[:, 0:NQ], in0=APS[0][:, sl], in1=T_a[:, 0:NQ],
                                                 op=ALU.mult), reads=[bf("APS0"), ba], writes=[ba])
        sch.add("act", lambda e: e.activation(out=T_b[:, 0:NQ], in_=APS[3][:, sl], func=AF.Ln),
                reads=[bf("APS3")], writes=[bb_])
        sch.add("act", lambda e: e.activation(out=T_b[:, 0:NQ], in_=T_b[:, 0:NQ], func=AF.Exp, scale=-1.0),
                reads=[bb_], writes=[bb_])
        sch.add("dve", lambda e: e.tensor_tensor(out=T_b[:, 0:NQ], in0=APS[1][:, sl], in1=T_b[:, 0:NQ],
                                                 op=ALU.mult), reads=[bf("APS1"), bb_], writes=[bb_])
        sch.add("dve", lambda e: e.scalar_tensor_tensor(out=T_a[:, 0:NQ], in0=T_b[:, 0:NQ],
                                                        scalar=LAMC[:, li, 0:1], in1=T_a[:, 0:NQ],
                                                        op0=ALU.mult, op1=ALU.add),
                reads=[ba, bb_, bf("LAMC")], writes=[ba])
        sch.add("act", lambda e: e.activation(out=SQ[:, 0:NQ], in_=T_a[:, 0:NQ], func=AF.Square),
                reads=[ba], writes=[bf("SQ")])

    def post_da_b(li, qslot, oslot, NQ, acol):
        T_a, T_b = TP[0], TP[1]
        ba, bb_ = bf("TP0"), bf("TP1")
        sch.add("pe", lambda e: e.matmul(M0[:, 0:NQ], lhsT=ones, rhs=SQ[:, 0:NQ], start=True, stop=True),
                reads=[bCST, bf("SQ")], writes=[bf("M0")])
        sch.add("act", lambda e: e.activation(out=T_b[:, 0:NQ], in_=M0[:, 0:NQ], func=AF.Ln, bias=128.0 * EPS),
                reads=[bf("M0")], writes=[bb_])
        sch.add("act", lambda e: e.activation(out=T_b[:, 0:NQ], in_=T_b[:, 0:NQ], func=AF.Exp, scale=-0.5),
                reads=[bb_], writes=[bb_])
        sch.add("dve", lambda e: e.tensor_tensor(out=T_a[:, 0:NQ], in0=T_a[:, 0:NQ], in1=T_b[:, 0:NQ],
                                                 op=ALU.mult), reads=[ba, bb_], writes=[ba])
        sch.add("dve", lambda e: e.scalar_tensor_tensor(out=OT[oslot][:, 0:NQ], in0=T_a[:, 0:NQ],
                                                        scalar=SCC[:, li:li + 1], in1=GB[qslot][:, 0:NQ],
                                                        op0=ALU.mult, op1=ALU.mult),
                reads=[ba, bf("SCC"), bf("GB%d" % qslot)], writes=[bf("OT%d" % oslot)])

    def post_da(li, qslot, oslot, NQ, acol):
        post_da_a(li, qslot, oslot, NQ, acol)
        post_da_b(li, qslot, oslot, NQ, acol)

    def attn_sb(c, qslot, NQ, acol, ktiles, bg):
        nkt = len(ktiles)
        for j in range(2):
            for a in (j, 2 + j):
                sch.add("pe", lambda e, a=a: e.matmul(APS[a][:, acol:acol + NQ], lhsT=ZER[:, :],
                                                      rhs=QB[qslot][:, 0:NQ], start=True, stop=False,
                                                      skip_group_check=True),
                        reads=[bf("ZER"), bf("QB%d" % qslot)], writes=[bf("APS%d" % a)])
        items = [(kt, j) for kt in reversed(range(nkt)) for j in range(2)]

        def stage1(it):
            kt, j = it
            t = ktiles[kt]
            c0, nk = t["c0"], t["nk"]
            sbk = nxt("SPS", 2)
            Sb = SPS[sbk]
            sbb = bf("SPS%d" % sbk)
            nb = len(t["bias"])
            ktt, kc0 = t["kT"]
            sch.add("pe", lambda e: e.matmul(Sb[0:nk, c0:NQ], lhsT=ktt[j * 64:(j + 1) * 64, kc0:kc0 + nk],
                                             rhs=QB[qslot][j * 64:(j + 1) * 64, c0:NQ],
                                             start=True, stop=(nb == 0), skip_group_check=True),
                    reads=[t["kTb"], bf("QB%d" % qslot)], writes=[sbb])
            for bi, (co, ncol, mi, kr) in enumerate(t["bias"]):
                sch.add("pe", lambda e, co=co, ncol=ncol, bi=bi: e.matmul(
                    Sb[0:nk, co:co + ncol], lhsT=ident[0:nk, 0:nk], rhs=maskT[0:nk, 0:ncol],
                    start=False, stop=(bi == nb - 1), skip_group_check=True),
                    reads=[bCST], writes=[sbb])
            es_ = nxt("EW", NE)
            E = EW[es_]
            eb = bf("EW%d" % es_)
            spb = SPB[es_]
            spbb = bf("SPB%d" % es_)
            sch.add("act", lambda e: e.activation(out=E[0:nk, c0:NQ], in_=Sb[0:nk, c0:NQ], func=AF.Exp,
                                                  scale=0.125), reads=[sbb], writes=[eb])
            sch.add("act", lambda e: e.activation(out=E[0:nk, c0:NQ], in_=E[0:nk, c0:NQ], func=AF.Ln,
                                                  bias=1.0), reads=[eb], writes=[eb])
            return E, eb, spb, spbb, Sb, sbb

        def stage1b(it, E, eb, spb, spbb, Sb, sbb):
            kt, j = it
            t = ktiles[kt]
            c0, nk = t["c0"], t["nk"]
            sch.add("dve", lambda e: e.tensor_copy(out=spb[0:nk, c0:NQ], in_=E[0:nk, c0:NQ]),
                    reads=[eb], writes=[spbb])
            sch.add("dve", lambda e: e.scalar_tensor_tensor(out=E[0:nk, c0:NQ], in0=Sb[0:nk, c0:NQ],
                                                            scalar=0.125, in1=E[0:nk, c0:NQ],
                                                            op0=ALU.mult, op1=ALU.subtract),
                    reads=[sbb, eb, spbb], writes=[eb])

        def stage2a(it, E, eb, spb, spbb):
            kt, j = it
            t = ktiles[kt]
            c0, nk = t["c0"], t["nk"]
            rb = bf("APS%d" % j)
            sch.add("pe", lambda e: e.matmul(APS[j][:, acol + c0:acol + NQ], lhsT=triu[0:nk, :],
                                             rhs=spb[0:nk, c0:NQ], start=False, stop=False,
                                             skip_group_check=True),
                    reads=[bCST, spbb], writes=[rb])
            sch.add("dve", lambda e: e.tensor_tensor(out=E[0:nk, c0:NQ], in0=E[0:nk, c0:NQ],
                                                     in1=APS[j][0:nk, acol + c0:acol + NQ], op=ALU.subtract),
                    reads=[eb, rb], writes=[eb])
            pslot = nxt("PB", NPB)
            Ab = PB[pslot]
            pbb = bf("PB%d" % pslot)
            sch.add("act", lambda e: e.activation(out=Ab[0:nk, c0:NQ], in_=E[0:nk, c0:NQ], func=AF.Exp),
                    reads=[eb], writes=[pbb])
            return Ab, pbb

        def stage2b(it, spb, spbb, Ab, pbb):
            kt, j = it
            t = ktiles[kt]
            c0, nk = t["c0"], t["nk"]
            sch.add("pe", lambda e: e.matmul(APS[j][:, acol + c0:acol + NQ], lhsT=tril[0:nk, :],
                                             rhs=spb[0:nk, c0:NQ], start=False, stop=False,
                                             skip_group_check=True),
                    reads=[bCST, spbb], writes=[bf("APS%d" % j)])
            sch.add("pe", lambda e: e.matmul(APS[2 + j][:, acol + c0:acol + NQ], lhsT=t["v"],
                                             rhs=Ab[0:nk, c0:NQ], start=False, stop=False,
                                             skip_group_check=True),
                    reads=[t["vb"], pbb], writes=[bf("APS%d" % (2 + j))])

        n = len(items)
        LA = 2
        s1 = {}
        for i in range(-(LA + 1), 0):
            if 0 <= i + LA + 1 < n:
                s1[i + LA + 1] = stage1(items[i + LA + 1])
            if 0 <= i + LA < n:
                stage1b(items[i + LA], *s1[i + LA])
        prev = None
        for i, it in enumerate(items):
            if i + LA + 1 < n:
                s1[i + LA + 1] = stage1(items[i + LA + 1])
            if i + LA < n:
                stage1b(items[i + LA], *s1[i + LA])
            a = stage2a(it, *s1[i][0:4])
            if prev is not None:
                stage2b(*prev)
            prev = (it, s1[i][2], s1[i][3], a[0], a[1])
            del s1[i]
            if bg:
                bg.pop(0)()
        stage2b(*prev)

    def post_sb(qslot, oslot, NQ, acol):
        for j in range(2):
            r = slice(j * 64, (j + 1) * 64)
            sch.add("dve", lambda e, j=j, r=r: e.tensor_tensor(out=OT[oslot][r, 0:NQ],
                                                               in0=APS[2 + j][r, acol:acol + NQ],
                                                               in1=GB[qslot][r, 0:NQ], op=ALU.mult),
                    reads=[bf("APS%d" % (2 + j)), bf("GB%d" % qslot)], writes=[bf("OT%d" % oslot)])

    def attn_swa(c, qslot, qtiles, bg):
        items = []
        for qi, qt in enumerate(qtiles):
            for j in range(2):
                for ki in range(len(qt["ktiles"])):
                    items.append((qi, j, ki))

        def qk(it):
            qi, j, ki = it
            qt = qtiles[qi]
            t = qt["ktiles"][ki]
            nk, nq, qc0 = t["nk"], qt["nq"], qt["qc0"]
            sbk = nxt("SPS", 2)
            Sb = SPS[sbk]
            sbb = bf("SPS%d" % sbk)
            ktt, kc0 = t["kT"]
            sch.add("pe", lambda e: e.matmul(Sb[0:nk, 0:nq], lhsT=ktt[j * 64:(j + 1) * 64, kc0:kc0 + nk],
                                             rhs=QB[qslot][j * 64:(j + 1) * 64, qc0:qc0 + nq],
                                             start=True, stop=False, skip_group_check=True),
                    reads=[t["kTb"], bf("QB%d" % qslot)], writes=[sbb])
            sch.add("pe", lambda e: e.matmul(Sb[0:nk, 0:nq], lhsT=ident[0:nk, 0:nk],
                                             rhs=BIAS[0:nk, t["mi"], 2 * c + j, 0:nq],
                                             start=False, stop=True, skip_group_check=True),
                    reads=[bCST, bBIAS], writes=[sbb])
            return Sb, sbb

        def rest(it, Sb, sbb):
            qi, j, ki = it
            qt = qtiles[qi]
            t = qt["ktiles"][ki]
            nk, nq, ac = t["nk"], qt["nq"], qt["acol"]
            nki = len(qt["ktiles"])
            pslot = nxt("PB", NPB)
            Pt = PB[pslot]
            pbb = bf("PB%d" % pslot)
            sch.add("act", lambda e: e.activation(out=Pt[0:nk, 0:nq], in_=Sb[0:nk, 0:nq], func=AF.Exp,
                                                  scale=0.125), reads=[sbb], writes=[pbb])
            sch.add("pe", lambda e: e.matmul(APS[j][:, ac:ac + nq], lhsT=t["v"], rhs=Pt[0:nk, 0:nq],
                                             start=(ki == 0), stop=(ki == nki - 1), skip_group_check=True),
                    reads=[t["vb"], pbb], writes=[bf("APS%d" % j)])
            sch.add("pe", lambda e: e.matmul(APS[2 + j][:, ac:ac + nq], lhsT=ones[0:nk, :], rhs=Pt[0:nk, 0:nq],
                                             start=(ki == 0), stop=(ki == nki - 1), skip_group_check=True),
                    reads=[bCST, pbb], writes=[bf("APS%d" % (2 + j))])

        cur = qk(items[0])
        for i, it in enumerate(items):
            nx = qk(items[i + 1]) if i + 1 < len(items) else None
            rest(it, *cur)
            cur = nx
            if bg:
                bg.pop(0)()

    def attn_swa_p(c, qslot, qtiles, bg):
        def qk(qi):
            qt = qtiles[qi]
            sbk = nxt("SPS", 2)
            Sb = SPS[sbk]
            sbb = bf("SPS%d" % sbk)
            nq, qc0 = qt["nq"], qt["qc0"]
            blk = 0
            for j in range(2):
                for ki, t in enumerate(qt["ktiles"]):
                    off = blk * P
                    ktt, kc0 = t["kT"]
                    sch.add("pe", lambda e, j=j, ktt=ktt, kc0=kc0, off=off: e.matmul(
                        Sb[:, off:off + nq], lhsT=ktt[j * 64:(j + 1) * 64, kc0:kc0 + P],
                        rhs=QB[qslot][j * 64:(j + 1) * 64, qc0:qc0 + nq],
                        start=True, stop=False, skip_group_check=True),
                        reads=[t["kTb"], bf("QB%d" % qslot)], writes=[sbb])
                    sch.add("pe", lambda e, j=j, t=t, off=off: e.matmul(
                        Sb[:, off:off + nq], lhsT=ident, rhs=BIAS[:, t["mi"], 2 * c + j, 0:nq],
                        start=False, stop=True, skip_group_check=True),
                        reads=[bCST, bBIAS], writes=[sbb])
                    blk += 1
            return Sb, sbb, blk

        def rest(qi, Sb, sbb, nblk):
            qt = qtiles[qi]
            nq, ac = qt["nq"], qt["acol"]
            nki = len(qt["ktiles"])
            pslot = nxt("PB", NPB)
            Pt = PB[pslot]
            pbb = bf("PB%d" % pslot)
            sch.add("act", lambda e: e.activation(out=Pt[:, 0:nblk * P], in_=Sb[:, 0:nblk * P], func=AF.Exp,
                                                  scale=0.125), reads=[sbb], writes=[pbb])
            blk = 0
            for j in range(2):
                for ki, t in enumerate(qt["ktiles"]):
                    off = blk * P
                    sch.add("pe", lambda e, j=j, t=t, off=off, ki=ki: e.matmul(
                        APS[j][:, ac:ac + nq], lhsT=t["v"], rhs=Pt[:, off:off + nq],
                        start=(ki == 0), stop=(ki == nki - 1), skip_group_check=True),
                        reads=[t["vb"], pbb], writes=[bf("APS%d" % j)])
                    sch.add("pe", lambda e, j=j, off=off, ki=ki: e.matmul(
                        APS[2 + j][:, ac:ac + nq], lhsT=ones, rhs=Pt[:, off:off + nq],
                        start=(ki == 0), stop=(ki == nki - 1), skip_group_check=True),
                        reads=[bCST, pbb], writes=[bf("APS%d" % (2 + j))])
                    blk += 1

        nqt = len(qtiles)
        cur = qk(0)
        for qi in range(nqt):
            nx = qk(qi + 1) if qi + 1 < nqt else None
            rest(qi, *cur)
            cur = nx
            for _ in range(4):
                if bg:
                    bg.pop(0)()

    def post_swa(c, qslot, oslot, NQ, acol):
        T_a = TP[0]
        ba = bf("TP0")
        for j in range(2):
            r = slice(j * 64, (j + 1) * 64)
            sch.add("act", lambda e, j=j, r=r: e.activation(out=T_a[r, 0:NQ], in_=APS[2 + j][r, acol:acol + NQ],
                                                            func=AF.Ln, bias=ESK[r, c:c + 1]),
                    reads=[bf("APS%d" % (2 + j)), bf("ESK")], writes=[ba])
            sch.add("act", lambda e, r=r: e.activation(out=T_a[r, 0:NQ], in_=T_a[r, 0:NQ], func=AF.Exp, scale=-1.0),
                    reads=[ba], writes=[ba])
            sch.add("dve", lambda e, j=j, r=r: e.tensor_tensor(out=T_a[r, 0:NQ], in0=APS[j][r, acol:acol + NQ],
                                                               in1=T_a[r, 0:NQ], op=ALU.mult),
                    reads=[bf("APS%d" % j), ba], writes=[ba])
            sch.add("dve", lambda e, r=r: e.tensor_tensor(out=OT[oslot][r, 0:NQ], in0=T_a[r, 0:NQ],
                                                          in1=GB[qslot][r, 0:NQ], op=ALU.mult),
                    reads=[ba, bf("GB%d" % qslot)], writes=[bf("OT%d" % oslot)])

    def final_norm_tile(xrows_ap, nrows, col, dst_ap):
        slot = nxt("XO", 2)
        xo = XO[slot]
        xob = bf("XO%d" % slot)
        sch.add("act", lambda e: e.activation(out=xo[0:nrows, :], in_=xrows_ap, func=AF.Square,
                                              accum_out=SS[0:nrows, col:col + 1]),
                reads=[bf("X")], writes=[xob, bf("SS")])
        sch.add("act", lambda e: e.activation(out=SS[0:nrows, col:col + 1], in_=SS[0:nrows, col:col + 1],
                                              func=AF.Ln, scale=1.0 / D, bias=EPS),
                reads=[bf("SS")], writes=[bf("SS")])
        sch.add("act", lambda e: e.activation(out=SS[0:nrows, col:col + 1], in_=SS[0:nrows, col:col + 1],
                                              func=AF.Exp, scale=-0.5),
                reads=[bf("SS")], writes=[bf("SS")])
        sch.add("dve", lambda e: e.scalar_tensor_tensor(out=xo[0:nrows, :], in0=xrows_ap,
                                                        scalar=SS[0:nrows, col:col + 1], in1=Gn[0:nrows, :],
                                                        op0=ALU.mult, op1=ALU.mult),
                reads=[bf("X"), bf("SS"), bf("Gn")], writes=[xob])
        sch.add("sp", lambda e: e.dma_start(out=dst_ap, in_=xo[0:nrows, :]), reads=[xob], dma=xob)

    def load_gain(idx):
        sch.add("sp", lambda e: e.dma_start(out=Gn[:, :], in_=gain_d[idx]), writes=[bf("Gn")], dma=bf("Gn"))

    def prompt_seq(s):
        sch.add("sp", lambda e: e.dma_start(out=X[:, :, :], in_=xp_d[s].rearrange("(n p) d -> p n d", p=P)),
                writes=[bf("X")], dma=bf("Xd"))
        for l in range(4):
            kind = KINDS[l]
            li = 0 if l == 0 else 1
            swa = kind == 2
            load_gain(l)
            load_weights(l, 0, 0)
            zero_acc()
            for n in range(NT):
                norm_tile(X[:, n, :], P, n, None, (n * P, (n + 1) * P))

            def kvset_of(c):
                return (c // 2) % 2 if swa else c % 2

            def kv_units(c):
                if swa and c % 2 == 1:
                    return []
                kvset = kvset_of(c)
                wset = c % 2
                us = []
                for n in range(NT):
                    outs = []
                    if swa:
                        g = c // 2
                        if n == NT - 1:
                            outs = [(kp_d[l][s, :, g * 64:(g + 1) * 64], 0, 64),
                                    (vp_d[l][s, :, g * 64:(g + 1) * 64], 64, 64)]
                    else:
                        outs = [(kp_d[l][s, n * P:(n + 1) * P, c * 128:(c + 1) * 128], 0, 128),
                                (vp_d[l][s, n * P:(n + 1) * P, c * 128:(c + 1) * 128], 128, 128)]
                    us.append(proj_kv_unit(l, c, wset, kvset, n * P, P,
                                           (KT[kvset], n * P, bf("KT%d" % kvset)),
                                           (VB[kvset][:, n, :], bf("VB%d" % kvset)), outs, swa,
                                           ev=("act" if kind == 1 else "dve")))
                flat = []
                for i_, (a_, b_) in enumerate(us):
                    flat.append(a_)
                    if i_ >= 1:
                        flat.append(us[i_ - 1][1])
                flat.append(us[-1][1])
                return flat

            for u in kv_units(0):
                u()
            qslots = {}

            def qg_units(c, qb):
                slot = nxt("QBslot", NQS)
                qslots[(c, qb)] = slot
                return [proj_q_unit(c % 2, qb * 512, 512, slot), proj_g_unit(c % 2, qb * 512, 512, slot)]

            for u in qg_units(0, 0):
                u()
            pend_out = []
            ku_cache = {}
            blocks = [(c, qb) for c in range(8) for qb in range(NQB)]
            for bi_, (c, qb) in enumerate(blocks):
                wset = c % 2
                kvset = kvset_of(c)
                bg = list(pend_out)
                pend_out = []
                if qb == min(1, NQB - 1) and c + 1 < 8:
                    if NQB == 1:
                        for u in bg:
                            u()
                        bg = []
                    load_weights(l, c + 1, (c + 1) % 2)
                if bi_ + 1 < len(blocks):
                    bg += qg_units(*blocks[bi_ + 1]) if blocks[bi_ + 1][0] == c else []
                if c + 1 < 8 and qb >= min(1, NQB - 1):
                    if (c + 1) not in ku_cache:
                        ku_cache[c + 1] = kv_units(c + 1)
                    ku = ku_cache[c + 1]
                    nshare = max(1, NQB - 1)
                    idx = qb - min(1, NQB - 1)
                    per = (len(ku) + nshare - 1) // nshare
                    bg += ku[idx * per:(idx + 1) * per]
                    if qb == NQB - 1:
                        bg += qg_units(c + 1, 0)
                qslot = qslots[(c, qb)]
                oslot = nxt("OT", 2)
                ktb, vbb = bf("KT%d" % kvset), bf("VB%d" % kvset)
                if kind in (0, 1):
                    ktiles = []
                    for kt in range(4 * qb + 4):
                        c0 = max(0, kt - 4 * qb) * P
                        bias = []
                        for i in range(4):
                            qt = 4 * qb + i
                            if kind == 0:
                                if kt == qt:
                                    bias.append((i * P, P, 0, P))
                                elif kt == qt - 1:
                                    bias.append((i * P, P, 1, P))
                            else:
                                if kt == qt:
                                    bias.append((i * P, P, 0, P))
                        ktiles.append(dict(kT=(KT[kvset], kt * P), kTb=ktb, v=VB[kvset][:, kt, :], vb=vbb,
                                           nk=P, c0=c0, bias=bias))
                    if kind == 0:
                        attn_da(c, qslot, 512, 0, ktiles, bg)
                    else:
                        attn_sb(c, qslot, 512, 0, ktiles, bg)
                else:
                    qtiles = []
                    for i in range(4):
                        qt = 4 * qb + i
                        kts = []
                        if qt > 0:
                            kts.append(dict(kT=(KT[kvset], (qt - 1) * P), kTb=ktb, v=VB[kvset][:, qt - 1, :],
                                            vb=vbb, nk=P, mi=3))
                        kts.append(dict(kT=(KT[kvset], qt * P), kTb=ktb, v=VB[kvset][:, qt, :], vb=vbb,
                                        nk=P, mi=2))
                        qtiles.append(dict(qc0=i * P, nq=P, acol=i * P, ktiles=kts))
                    attn_swa_p(c, qslot, qtiles, bg)
                pre = []
                if kind == 0:
                    post_da_a(li, qslot, oslot, 512, 0)
                    pre = [lambda li=li, qslot=qslot, oslot=oslot: post_da_b(li, qslot, oslot, 512, 0)]
                elif kind == 1:
                    post_sb(qslot, oslot, 512, 0)
                else:
                    post_swa(c, qslot, oslot, 512, 0)
                for u in bg:
                    u()
                xdst = [(P, (lambda hf, i=i, qb=qb: X[:, 4 * qb + i, hf * 512:(hf + 1) * 512])) for i in range(4)]
                pend_out = pre + out_proj_units(oslot, wset, 4, xdst)
            for u in pend_out:
                u()
        load_gain(4)
        zero_acc()
        for n in range(NT):
            final_norm_tile(X[:, n, :], P, n, yp_d[s, n * P:(n + 1) * P, :])

    def sample_pass():
        sch.add("sp", lambda e: e.dma_start(out=X[0:TS, 0, :], in_=xs_d), writes=[bf("X")], dma=bf("Xd"))
        for l in range(4):
            kind = KINDS[l]
            li = 0 if l == 0 else 1
            swa = kind == 2
            load_gain(l)
            zero_acc()
            norm_tile(X[0:TS, 0, :], TS, 0, None, (0, TS))
            def cache_loads(c, b, st, l=l, swa=swa):
                vbb_ = bf("VB%d" % st)
                if swa:
                    g = c // 2
                    for hh in range(2):
                        sch.add("pool", lambda e, hh=hh: e.dma_start(
                            out=KC[st][:, 0, hh * 64:(hh + 1) * 64], in_=ck_d[l][b, :, g * 64:(g + 1) * 64]),
                            writes=[bf("KC%d" % st)], dma=bf("KCd%d_%d" % (st, hh)))
                        sch.add("pool", lambda e, hh=hh: e.dma_start(
                            out=VB[st][:, 0, hh * 64:(hh + 1) * 64], in_=cv_d[l][b, :, g * 64:(g + 1) * 64]),
                            writes=[vbb_], dma=bf("VBd%d_%d" % (st, hh)))
                else:
                    sch.add("pool", lambda e: e.dma_start(
                        out=KC[st][:, :, :],
                        in_=ck_d[l][b, :, c * 128:(c + 1) * 128].rearrange("(n p) f -> p n f", p=P)),
                        writes=[bf("KC%d" % st)], dma=bf("KCd%d_0" % st))
                    sch.add("pool", lambda e: e.dma_start(
                        out=VB[st][:, 0:NPT, :],
                        in_=cv_d[l][b, :, c * 128:(c + 1) * 128].rearrange("(n p) f -> p n f", p=P)),
                        writes=[vbb_], dma=bf("VBd%d_0" % st))

            iters = [(c_, b_) for c_ in range(8) for b_ in range(NSEQ)]
            load_weights(l, 0, 0)
            cache_loads(0, 0, 0)
            for c in range(8):
                wset = c % 2
                if c + 1 < 8:
                    load_weights(l, c + 1, (c + 1) % 2)
                slot = nxt("QBslot", NQS)
                proj_q_unit(wset, 0, TS, slot)()
                proj_g_unit(wset, 0, TS, slot)()
                oslot = nxt("OT", 2)
                for b in range(NSEQ):
                    k_it = c * NSEQ + b
                    st = k_it % 2
                    if k_it + 1 < len(iters):
                        cache_loads(iters[k_it + 1][0], iters[k_it + 1][1], (k_it + 1) % 2)
                    ktb, vbb = bf("KT%d" % st), bf("VB%d" % st)
                    ktnb, vnb = bf("KTN%d" % st), bf("VN%d" % st)
                    ncache = 1 if swa else NPT
                    for n in range(ncache):
                        sch.add("pe", lambda e, n=n, st=st: e.transpose(out=T0[:, n * P:(n + 1) * P],
                                                                        in_=KC[st][:, n, :], identity=ident),
                                reads=[bf("KC%d" % st), bCST], writes=[bf("T0")])
                    sch.add("act", lambda e, st=st, ncache=ncache: e.copy(out=KT[st][:, 0:ncache * P],
                                                                          in_=T0[:, 0:ncache * P]),
                            reads=[bf("T0")], writes=[ktb])
                    if swa:
                        g = c // 2
                        outs = []
                        if c % 2 == 0:
                            outs = [(ks_d[l][b, P - T_DEC:P, g * 64:(g + 1) * 64], 0, 64),
                                    (vs_d[l][b, P - T_DEC:P, g * 64:(g + 1) * 64], 64, 64)]
                    else:
                        outs = [(ks_d[l][b * T_DEC:(b + 1) * T_DEC, c * 128:(c + 1) * 128], 0, 128),
                                (vs_d[l][b * T_DEC:(b + 1) * T_DEC, c * 128:(c + 1) * 128], 128, 128)]
                    for u_ in proj_kv_unit(l, c, wset, st, b * T_DEC, T_DEC, (KTN[st], 0, ktnb),
                                           (VN[st][:, :], vnb), outs, swa):
                        u_()
                    acol = b * T_DEC
                    qc0 = b * T_DEC
                    if kind in (0, 1):
                        ktiles = []
                        for kt in range(NPT):
                            bias = []
                            if kind == 0 and kt == NPT - 1:
                                bias = [(0, T_DEC, 1, P)]
                            ktiles.append(dict(kT=(KT[st], kt * P), kTb=ktb, v=VB[st][:, kt, :], vb=vbb,
                                               nk=P, c0=0, bias=bias))
                        ktiles.append(dict(kT=(KTN[st], 0), kTb=ktnb, v=VN[st][:, :], vb=vnb, nk=T_DEC, c0=0,
                                           bias=[(0, T_DEC, 0, T_DEC)]))
                        if kind == 0:
                            attn_da_s(c, slot, T_DEC, acol, qc0, ktiles)
                        else:
                            attn_sb_s(c, slot, T_DEC, acol, qc0, ktiles)
                    else:
                        kts = [dict(kT=(KT[st], 0), kTb=ktb, v=VB[st][:, 0, :], vb=vbb, nk=P, mi=3),
                               dict(kT=(KTN[st], 0), kTb=ktnb, v=VN[st][:, :], vb=vnb, nk=T_DEC, mi=2)]
                        attn_swa(c, slot, [dict(qc0=qc0, nq=T_DEC, acol=acol, ktiles=kts)], None)
                if kind == 0:
                    post_da(li, slot, oslot, TS, 0)
                elif kind == 1:
                    post_sb(slot, oslot, TS, 0)
                else:
                    post_swa(c, slot, oslot, TS, 0)
                xdst = [(TS, (lambda hf: X[0:TS, 0, hf * 512:(hf + 1) * 512]))]
                for u in out_proj_units(oslot, wset, 1, xdst):
                    u()
            if swa:
                for b in range(NSEQ):
                    sch.add("sp", lambda e, b=b, l=l: e.dma_start(out=ks_d[l][b, 0:P - T_DEC, :],
                                                             in_=ck_d[l][b, T_DEC:P, :]), dma=bf("d2d"))
                    sch.add("sp", lambda e, b=b, l=l: e.dma_start(out=vs_d[l][b, 0:P - T_DEC, :],
                                                             in_=cv_d[l][b, T_DEC:P, :]), dma=bf("d2d"))
        load_gain(4)
        zero_acc()
        final_norm_tile(X[0:TS, 0, :], TS, 0, ys_d[:, :])

    def attn_da_s(c, qslot, NQ, acol, qc0, ktiles):
        hmb = 2 * c
        nkt = len(ktiles)
        for kt in range(nkt):
            for m in range(2):
                t = ktiles[kt]
                nk = t["nk"]
                sbk = nxt("SPS", 2)
                Sb = SPS[sbk]
                sbb = bf("SPS%d" % sbk)
                nb = len(t["bias"])
                ktt, kc0 = t["kT"]
                sch.add("pe", lambda e, Sb=Sb, nk=nk, m=m, ktt=ktt, kc0=kc0, nb=nb: e.matmul(
                    Sb[0:nk, 0:NQ], lhsT=ktt[m * 64:(m + 1) * 64, kc0:kc0 + nk],
                    rhs=QB[qslot][m * 64:(m + 1) * 64, qc0:qc0 + NQ], start=True, stop=(nb == 0),
                    skip_group_check=True), reads=[t["kTb"], bf("QB%d" % qslot)], writes=[sbb])
                for bi, (co, ncol, mi, kr) in enumerate(t["bias"]):
                    sch.add("pe", lambda e, Sb=Sb, nk=nk, m=m, mi=mi, bi=bi, nb=nb: e.matmul(
                        Sb[0:nk, 0:NQ], lhsT=ident[0:nk, 0:nk], rhs=BIAS[0:nk, mi, hmb + m, 0:NQ],
                        start=False, stop=(bi == nb - 1), skip_group_check=True),
                        reads=[bCST, bBIAS], writes=[sbb])
                pslot = nxt("PB", NPB)
                Pt = PB[pslot]
                pbb = bf("PB%d" % pslot)
                sch.add("act", lambda e, Pt=Pt, Sb=Sb, nk=nk: e.activation(
                    out=Pt[0:nk, 0:NQ], in_=Sb[0:nk, 0:NQ], func=AF.Exp, scale=0.125),
                    reads=[sbb], writes=[pbb])
                sch.add("pe", lambda e, Pt=Pt, nk=nk, m=m, t=t, kt=kt: e.matmul(
                    APS[m][:, acol:acol + NQ], lhsT=t["v"], rhs=Pt[0:nk, 0:NQ],
                    start=(kt == 0), stop=(kt == nkt - 1), skip_group_check=True),
                    reads=[t["vb"], pbb], writes=[bf("APS%d" % m)])
                sch.add("pe", lambda e, Pt=Pt, nk=nk, m=m, kt=kt: e.matmul(
                    APS[2 + m][:, acol:acol + NQ], lhsT=ones[0:nk, :], rhs=Pt[0:nk, 0:NQ],
                    start=(kt == 0), stop=(kt == nkt - 1), skip_group_check=True),
                    reads=[bCST, pbb], writes=[bf("APS%d" % (2 + m))])

    def attn_sb_s(c, qslot, NQ, acol, qc0, ktiles):
        nkt = len(ktiles)
        for j in range(2):
            for a in (j, 2 + j):
                sch.add("pe", lambda e, a=a: e.matmul(APS[a][:, acol:acol + NQ], lhsT=ZER[:, :],
                                                      rhs=QB[qslot][:, qc0:qc0 + NQ], start=True, stop=False,
                                                      skip_group_check=True),
                        reads=[bf("ZER"), bf("QB%d" % qslot)], writes=[bf("APS%d" % a)])
        for kt in reversed(range(nkt)):
            for j in range(2):
                t = ktiles[kt]
                nk = t["nk"]
                sbk = nxt("SPS", 2)
                Sb = SPS[sbk]
                sbb = bf("SPS%d" % sbk)
                nb = len(t["bias"])
                ktt, kc0 = t["kT"]
                sch.add("pe", lambda e, Sb=Sb, nk=nk, j=j, ktt=ktt, kc0=kc0, nb=nb: e.matmul(
                    Sb[0:nk, 0:NQ], lhsT=ktt[j * 64:(j + 1) * 64, kc0:kc0 + nk],
                    rhs=QB[qslot][j * 64:(j + 1) * 64, qc0:qc0 + NQ], start=True, stop=(nb == 0),
                    skip_group_check=True), reads=[t["kTb"], bf("QB%d" % qslot)], writes=[sbb])
                for bi, (co, ncol, mi, kr) in enumerate(t["bias"]):
                    sch.add("pe", lambda e, Sb=Sb, nk=nk, bi=bi, nb=nb: e.matmul(
                        Sb[0:nk, 0:NQ], lhsT=ident[0:nk, 0:nk], rhs=maskT[0:nk, 0:NQ],
                        start=False, stop=(bi == nb - 1), skip_group_check=True),
                        reads=[bCST], writes=[sbb])
                es_ = nxt("EW", NE)
                E = EW[es_]
                eb = bf("EW%d" % es_)
                spb = SPB[es_]
                spbb = bf("SPB%d" % es_)
                rb = bf("APS%d" % j)
                sch.add("act", lambda e, E=E, Sb=Sb, nk=nk: e.activation(
                    out=E[0:nk, 0:NQ], in_=Sb[0:nk, 0:NQ], func=AF.Exp, scale=0.125), reads=[sbb], writes=[eb])
                sch.add("act", lambda e, E=E, nk=nk: e.activation(
                    out=E[0:nk, 0:NQ], in_=E[0:nk, 0:NQ], func=AF.Ln, bias=1.0), reads=[eb], writes=[eb])
                sch.add("pool", lambda e, E=E, spb=spb, nk=nk: e.tensor_copy(out=spb[0:nk, 0:NQ], in_=E[0:nk, 0:NQ]),
                        reads=[eb], writes=[spbb])
                sch.add("dve", lambda e, E=E, Sb=Sb, nk=nk: e.scalar_tensor_tensor(
                    out=E[0:nk, 0:NQ], in0=Sb[0:nk, 0:NQ], scalar=0.125, in1=E[0:nk, 0:NQ],
                    op0=ALU.mult, op1=ALU.subtract), reads=[sbb, eb, spbb], writes=[eb])
                sch.add("pe", lambda e, spb=spb, nk=nk, j=j: e.matmul(
                    APS[j][:, acol:acol + NQ], lhsT=triu[0:nk, :], rhs=spb[0:nk, 0:NQ], start=False, stop=False,
                    skip_group_check=True), reads=[bCST, spbb], writes=[rb])
                sch.add("dve", lambda e, E=E, nk=nk, j=j: e.tensor_tensor(
                    out=E[0:nk, 0:NQ], in0=E[0:nk, 0:NQ], in1=APS[j][0:nk, acol:acol + NQ], op=ALU.subtract),
                    reads=[eb, rb], writes=[eb])
                pslot = nxt("PB", NPB)
                Ab = PB[pslot]
                pbb = bf("PB%d" % pslot)
                sch.add("act", lambda e, Ab=Ab, E=E, nk=nk: e.activation(
                    out=Ab[0:nk, 0:NQ], in_=E[0:nk, 0:NQ], func=AF.Exp), reads=[eb], writes=[pbb])
                sch.add("pe", lambda e, spb=spb, nk=nk, j=j: e.matmul(
                    APS[j][:, acol:acol + NQ], lhsT=tril[0:nk, :], rhs=spb[0:nk, 0:NQ], start=False, stop=False,
                    skip_group_check=True), reads=[bCST, spbb], writes=[rb])
                sch.add("pe", lambda e, Ab=Ab, nk=nk, j=j, t=t: e.matmul(
                    APS[2 + j][:, acol:acol + NQ], lhsT=t["v"], rhs=Ab[0:nk, 0:NQ], start=False, stop=False,
                    skip_group_check=True), reads=[t["vb"], pbb], writes=[bf("APS%d" % (2 + j))])

    setup()
    for s in range(NSEQ):
        prompt_seq(s)
    sample_pass()
    fin = sch.add("sp", lambda e: e.nop(), reads=[], writes=[])
    for b in sch.dma_bufs:
        fin.waits.append(("dma", b, b.cnt))
    sch.emit(nc, es)
    es.close()
    return nc


_NC_CACHE = {}


def _get_nc(cfg):
    key = (cfg["NSEQ"], cfg["S"], cfg["PAST"])
    if key not in _NC_CACHE:
        _NC_CACHE[key] = build(cfg)
    return _NC_CACHE[key]


def run(cfg, inputs):
    NSEQ, S, PAST, NCORES = cfg["NSEQ"], cfg["S"], cfg["PAST"], cfg["NCORES"]
    nc = _get_nc(cfg)
    f = lambda a: np.ascontiguousarray(np.asarray(a, dtype=np.float32))
    maps_np, consts_np, _ = static_tables()
    gain = np.stack([np.broadcast_to(f(inputs[k])[None, :], (P, D)) for k in
                     ("norm_0", "norm_1", "norm_2", "norm_3", "final_norm")])
    tab = np.broadcast_to(f(inputs["rel_bias_table"]).reshape(1, 512), (P, 512))
    lam = np.stack([np.broadcast_to(f(inputs["da_lambda_%d" % l]).reshape(1, 256), (P, 256)) for l in (0, 3)])
    subln = np.stack([f(inputs["da_subln_%d" % l]).reshape(P, 1) for l in (0, 3)])
    sk = f(inputs["sw_sinks_2"])
    sink = np.zeros((P, 8), np.float32)
    for c in range(8):
        sink[:64, c] = sk[2 * c]
        sink[64:, c] = sk[2 * c + 1]
    shared = dict(gain=f(gain), tab=f(tab), lam=f(lam), subln=f(subln), sink=sink, maps=maps_np, consts=consts_np)
    for l in range(4):
        shared["win%d" % l] = f(inputs["w_in_%d" % l])
        shared["wout%d" % l] = f(inputs["w_out_%d" % l])
    in_maps = []
    for core in range(NCORES):
        sl = slice(core * NSEQ, (core + 1) * NSEQ)
        m = dict(shared)
        m["xp"] = f(inputs["x_prompt"][sl])
        m["xs"] = f(inputs["x_sample"][sl]).reshape(NSEQ * T_DEC, D)
        for l in range(4):
            ck = f(inputs["cache_k_%d" % l][sl])
            cv = f(inputs["cache_v_%d" % l][sl])
            m["ck%d" % l] = ck.reshape(NSEQ, ck.shape[1], -1)
            m["cv%d" % l] = cv.reshape(NSEQ, cv.shape[1], -1)
        in_maps.append(m)
    res = run_bass_kernel_spmd(nc, in_maps, core_ids=list(range(NCORES)))
    R = res.results
    cat = lambda name: np.concatenate([np.asarray(r[name]) for r in R], axis=0)
    Bt = NSEQ * NCORES
    outs = [cat("yp").reshape(Bt, S, D), cat("ys").reshape(Bt, T_DEC, D)]
    for l in range(4):
        kind = KINDS[l]
        if kind == 0:
            outs += [cat("kp%d" % l).reshape(Bt, S, 8, 2, 64), cat("vp%d" % l).reshape(Bt, S, 8, 128),
                     cat("ks%d" % l).reshape(Bt, T_DEC, 8, 2, 64), cat("vs%d" % l).reshape(Bt, T_DEC, 8, 128)]
        elif kind == 1:
            outs += [cat("kp%d" % l).reshape(Bt, S, 16, 64), cat("vp%d" % l).reshape(Bt, S, 16, 64),
                     cat("ks%d" % l).reshape(Bt, T_DEC, 16, 64), cat("vs%d" % l).reshape(Bt, T_DEC, 16, 64)]
        else:
            outs += [cat("kp%d" % l).reshape(Bt, P, 4, 64), cat("vp%d" % l).reshape(Bt, P, 4, 64),
                     cat("ks%d" % l).reshape(Bt, P, 4, 64), cat("vs%d" % l).reshape(Bt, P, 4, 64)]
    return tuple(np.ascontiguousarray(o, dtype=np.float32) for o in outs)


def kernel(x_prompt, x_sample, cache_k_0, cache_v_0, cache_k_1, cache_v_1, cache_k_2, cache_v_2,
           cache_k_3, cache_v_3, rel_bias_table,
           norm_0, w_in_0, w_out_0, da_lambda_0, da_subln_0,
           norm_1, w_in_1, w_out_1,
           norm_2, w_in_2, w_out_2, sw_sinks_2,
           norm_3, w_in_3, w_out_3, da_lambda_3, da_subln_3,
           final_norm):
    inputs = {
        "x_prompt": x_prompt, "x_sample": x_sample,
        "cache_k_0": cache_k_0, "cache_v_0": cache_v_0, "cache_k_1": cache_k_1, "cache_v_1": cache_v_1,
        "cache_k_2": cache_k_2, "cache_v_2": cache_v_2, "cache_k_3": cache_k_3, "cache_v_3": cache_v_3,
        "rel_bias_table": rel_bias_table,
        "norm_0": norm_0, "w_in_0": w_in_0, "w_out_0": w_out_0, "da_lambda_0": da_lambda_0, "da_subln_0": da_subln_0,
        "norm_1": norm_1, "w_in_1": w_in_1, "w_out_1": w_out_1,
        "norm_2": norm_2, "w_in_2": w_in_2, "w_out_2": w_out_2, "sw_sinks_2": sw_sinks_2,
        "norm_3": norm_3, "w_in_3": w_in_3, "w_out_3": w_out_3, "da_lambda_3": da_lambda_3, "da_subln_3": da_subln_3,
        "final_norm": final_norm,
    }
    return run(FULL_CFG, inputs)
```

```python
import math
from contextlib import ExitStack

import numpy as np
import concourse.bass as bass
import concourse.mybir as mybir
from concourse.bass_utils import run_bass_kernel_spmd

F32 = mybir.dt.float32
BF16 = mybir.dt.bfloat16
ALU = mybir.AluOpType
AF = mybir.ActivationFunctionType

D = 1024
P = 128
NCH = 8
EPS = 1e-6
NEG = -240000.0
T_DEC = 16
KINDS = (0, 1, 2, 0)
IN_W = (4096, 4096, 2560)

FULL_CFG = dict(NSEQ=4, S=2048, PAST=1024, NCORES=8)


class Buf:
    __slots__ = ("name", "last_w", "readers", "sem", "cnt", "psum")

    def __init__(self, name):
        self.name = name
        self.psum = False
        self.last_w = None
        self.readers = []
        self.sem = None
        self.cnt = 0


class Op:
    __slots__ = ("idx", "eng", "fn", "waits", "signal", "sigval", "dma", "dma_val")


ENGS = ("pe", "act", "dve", "pool", "sp")


class Sched:
    def __init__(self):
        self.ops = []
        self.by_eng = {e: [] for e in ENGS}
        self.waited = {e: {f: -1 for f in ENGS} for e in ENGS}
        self.dma_waited = {e: {} for e in ENGS}
        self.dma_bufs = []

    def add(self, eng, fn, reads=(), writes=(), dma=None):
        op = Op()
        op.idx = len(self.ops)
        op.eng = eng
        op.fn = fn
        op.waits = []
        op.signal = False
        op.sigval = 0
        op.dma = dma
        op.dma_val = 0
        deps = set()
        for b in reads:
            if b.last_w is not None:
                deps.add(b.last_w)
            if b.psum:
                deps.update(r for r in b.readers if self.ops[r].eng != eng)
        for b in writes:
            if b.last_w is not None:
                deps.add(b.last_w)
            deps.update(b.readers)
        for d in sorted(deps, reverse=True):
            dop = self.ops[d]
            if dop.dma is not None:
                if self.dma_waited[eng].get(dop.dma, 0) >= dop.dma_val:
                    continue
                self.dma_waited[eng][dop.dma] = dop.dma_val
                op.waits.append(("dma", dop.dma, dop.dma_val))
            else:
                if dop.eng == "pe" and eng == "pe" and dma is None:
                    continue
                if self.waited[eng][dop.eng] >= d:
                    continue
                self.waited[eng][dop.eng] = d
                dop.signal = True
                op.waits.append(("eng", dop))
        for b in reads:
            b.readers.append(op.idx)
        for b in writes:
            b.last_w = op.idx
            b.readers = []
        if dma is not None:
            if dma.cnt == 0:
                self.dma_bufs.append(dma)
            dma.cnt += 16
            op.dma_val = dma.cnt
        self.ops.append(op)
        self.by_eng[eng].append(op)
        return op

    def emit(self, nc, es):
        for e in ENGS:
            cnt = 0
            for op in self.by_eng[e]:
                if op.signal:
                    cnt += 1
                    op.sigval = cnt
        esem = {e: es.enter_context(nc.semaphore("sem_" + e)) for e in ENGS}
        for i, b in enumerate(self.dma_bufs):
            b.sem = es.enter_context(nc.semaphore("dsem%d" % i))

        def run(ename, eng):
            for op in self.by_eng[ename]:
                for w in op.waits:
                    if w[0] == "dma":
                        eng.wait_ge(w[1].sem, w[2])
                    else:
                        eng.wait_ge(esem[w[1].eng], w[1].sigval)
                ins = op.fn(eng)
                if op.dma is not None:
                    ins.then_inc(op.dma.sem, 16)
                elif op.signal:
                    ins.then_inc(esem[ename], 1)

        with nc.Block() as block:
            @block.tensor
            def _(e):
                run("pe", e)

            @block.scalar
            def _(e):
                run("act", e)

            @block.vector
            def _(e):
                run("dve", e)

            @block.gpsimd
            def _(e):
                run("pool", e)

            @block.sync
            def _(e):
                run("sp", e)


def _bucket_np(rel):
    import jax
    import jax.numpy as jnp
    cpu = jax.devices("cpu")[0]
    with jax.default_device(cpu):
        rel = jnp.asarray(rel, dtype=jnp.int32)
        nb = 16
        max_exact = 8
        n = jnp.abs(rel)
        nf = jnp.maximum(n, 1).astype(jnp.float32)
        large = max_exact + (jnp.log(nf / max_exact) / math.log(128 / max_exact)
                             * (nb - max_exact)).astype(jnp.int32)
        large = jnp.minimum(large, nb - 1)
        out = jnp.where(rel > 0, nb, 0) + jnp.where(n < max_exact, n, large)
        return np.asarray(out)


def static_tables():
    k = np.arange(P)[:, None]
    q = np.arange(P)[None, :]
    bd = _bucket_np(k - q).astype(np.float32)
    bs = _bucket_np(k - P - q).astype(np.float32)
    maps = np.zeros((4, P, P), np.float32)
    m = bd.copy(); m[(k // 64) > (q // 64)] = -1.0
    maps[0] = m
    maps[1] = bs
    m = bd.copy(); m[(k >= 64) & (q < 64)] = -1.0
    maps[2] = m
    m = bs.copy(); m[(k < 64) & (q >= 64)] = -1.0
    maps[3] = m
    consts = np.zeros((5, P, P), np.float32)
    consts[0] = np.eye(P)
    consts[1] = (k > q)
    consts[2] = (k <= q)
    consts[3] = np.where(k < q, 0.0, NEG)
    consts[4] = 1.0
    present = [sorted(set(int(v) for v in np.unique(maps[i]) if v >= 0 and v != 15)) for i in range(4)]
    return maps, consts, present


def build(cfg):
    NSEQ, S, PAST = cfg["NSEQ"], cfg["S"], cfg["PAST"]
    NT = S // P
    NQB = S // 512
    NPT = PAST // P
    TS = NSEQ * T_DEC
    assert S % 512 == 0 and PAST % P == 0 and TS <= 64
    maps_np, consts_np, present = static_tables()

    nc = bass.Bass("TRN2", target_bir_lowering=False)

    def din(name, shape):
        return nc.dram_tensor(name, list(shape), F32, kind="ExternalInput").ap()

    def dout(name, shape):
        return nc.dram_tensor(name, list(shape), F32, kind="ExternalOutput").ap()

    xp_d = din("xp", (NSEQ, S, D))
    xs_d = din("xs", (TS, D))
    ck_d, cv_d = [], []
    for l in range(4):
        if KINDS[l] == 2:
            ck_d.append(din("ck%d" % l, (NSEQ, P, 256)))
            cv_d.append(din("cv%d" % l, (NSEQ, P, 256)))
        else:
            ck_d.append(din("ck%d" % l, (NSEQ, PAST, D)))
            cv_d.append(din("cv%d" % l, (NSEQ, PAST, D)))
    win_d = [din("win%d" % l, (D, IN_W[KINDS[l]])) for l in range(4)]
    wout_d = [din("wout%d" % l, (D, D)) for l in range(4)]
    gain_d = din("gain", (5, P, D))
    tab_d = din("tab", (P, 512))
    lam_d = din("lam", (2, P, 256))
    subln_d = din("subln", (2, P, 1))
    sink_d = din("sink", (P, 8))
    maps_d = din("maps", (4, P, P))
    consts_d = din("consts", (5, P, P))

    yp_d = dout("yp", (NSEQ, S, D))
    ys_d = dout("ys", (TS, D))
    kp_d, vp_d, ks_d, vs_d = [], [], [], []
    for l in range(4):
        if KINDS[l] == 2:
            kp_d.append(dout("kp%d" % l, (NSEQ, P, 256)))
            vp_d.append(dout("vp%d" % l, (NSEQ, P, 256)))
            ks_d.append(dout("ks%d" % l, (NSEQ, P, 256)))
            vs_d.append(dout("vs%d" % l, (NSEQ, P, 256)))
        else:
            kp_d.append(dout("kp%d" % l, (NSEQ, S, D)))
            vp_d.append(dout("vp%d" % l, (NSEQ, S, D)))
            ks_d.append(dout("ks%d" % l, (TS, D)))
            vs_d.append(dout("vs%d" % l, (TS, D)))

    es = ExitStack()
    sch = Sched()

    def sb(name, shape, dt):
        return es.enter_context(nc.sbuf_tensor(name, list(shape), dt))

    X = sb("X", (P, NT, D), F32)
    xnT = sb("xnT", (P, NCH, S), BF16)
    Gn = sb("Gn", (P, D), F32)
    WT = [sb("WT%d" % i, (P, NCH, 512), BF16) for i in range(2)]
    WO = [sb("WO%d" % i, (P, D), BF16) for i in range(2)]
    KT = [sb("KT%d" % i, (P, S), BF16) for i in range(2)]
    VB = [sb("VB%d" % i, (P, NT, P), BF16) for i in range(2)]
    KTN = [sb("KTN%d" % i, (P, T_DEC), BF16) for i in range(2)]
    VN = [sb("VN%d" % i, (T_DEC, P), BF16) for i in range(2)]
    KC = [sb("KC%d" % i, (P, max(NPT, 1), P), BF16) for i in range(2)]
    NQS = 3
    QB = [sb("QB%d" % i, (P, 512), BF16) for i in range(NQS)]
    GB = [sb("GB%d" % i, (P, 512), BF16) for i in range(NQS)]
    OT = [sb("OT%d" % i, (P, 512), BF16) for i in range(2)]
    NKVS = 2
    KVS = [sb("KVS%d" % i, (P, 256), F32) for i in range(NKVS)]
    NKBT = 3
    KBT = [sb("KBT%d" % i, (P, P), BF16) for i in range(NKBT)]
    NE = 4
    EW = [sb("EW%d" % i, (P, 512), F32) for i in range(NE)]
    SPB = [sb("SPB%d" % i, (P, 512), BF16) for i in range(NE)]
    NPB = 3
    PB = [sb("PB%d" % i, (P, 512), BF16) for i in range(NPB)]
    TP = [sb("TP%d" % i, (P, 512), F32) for i in range(3)]
    SQ = sb("SQ", (P, 512), BF16)
    XN = [sb("XN%d" % i, (P, D), BF16) for i in range(2)]
    SS = sb("SS", (P, 2 * NT + 4), F32)
    BIAS = sb("BIAS", (P, 4, 16, P), BF16)
    CST = sb("CST", (P, 5, P), BF16)
    ZER = sb("ZER", (P, P), BF16)
    LAMC = sb("LAMC", (P, 2, 8), F32)
    SCC = sb("SCC", (P, 2), F32)
    ESK = sb("ESK", (P, 8), F32)
    LAMT = sb("LAMT", (P, 8), F32)
    SCR = sb("SCR", (P, 3200), F32)
    XO = [SCR[:, i * D:(i + 1) * D] for i in range(2)]
    MAPS = SCR[:, 0:512].rearrange("p (c q) -> p c q", c=4)
    CSTF = SCR[:, 512:1152].rearrange("p (c q) -> p c q", c=5)
    OH = [SCR[:, 1152 + i * 128:1152 + (i + 1) * 128] for i in range(2)]
    ACC = SCR[:, 1408:2432].rearrange("p (h q) -> p h q", h=8)
    TABP = SCR[:, 2432:2944]
    LAMW = SCR[:, 2944:3200]

    def ps(name, dt=F32, cols=512):
        return es.enter_context(nc.psum_tensor(name, [P, cols], dt))

    SPS = [ps("SPS0"), ps("SPS1")]
    APS = [ps("APS%d" % i) for i in range(4)]
    M0 = ps("M0")
    T0 = ps("T0", BF16, 1024)

    B = {}

    def bf(name):
        if name not in B:
            B[name] = Buf(name)
        return B[name]

    for nm in ("SPS0", "SPS1", "APS0", "APS1", "APS2", "APS3", "M0", "T0"):
        bf(nm).psum = True
    ident = CST[:, 0, :]
    triu = CST[:, 1, :]
    tril = CST[:, 2, :]
    maskT = CST[:, 3, :]
    ones = CST[:, 4, :]
    bCST = bf("CST")
    bBIAS = bf("BIAS")

    rot = {}

    def nxt(name, n):
        v = rot.get(name, 0)
        rot[name] = v + 1
        return v % n

    def setup():
        sch.add("sp", lambda e: e.dma_start(out=CSTF, in_=consts_d.rearrange("c p q -> p c q")),
                writes=[bf("CSTF")], dma=bf("CSTF"))
        sch.add("sp", lambda e: e.dma_start(out=MAPS, in_=maps_d.rearrange("c p q -> p c q")),
                writes=[bf("MAPS")], dma=bf("MAPS"))
        sch.add("sp", lambda e: e.dma_start(out=TABP, in_=tab_d), writes=[bf("TABP")], dma=bf("TABP"))
        sch.add("sp", lambda e: e.dma_start(out=ESK[:, :], in_=sink_d), writes=[bf("ESK")], dma=bf("ESK"))
        sch.add("dve", lambda e: e.tensor_copy(out=CST[:, :, :], in_=CSTF),
                reads=[bf("CSTF")], writes=[bCST])
        sch.add("dve", lambda e: e.memset(ZER[:, :], 0.0), writes=[bf("ZER")])
        for b_ in range(32):
            if b_ == 15:
                continue
            sch.add("dve", lambda e, b_=b_: e.tensor_tensor(
                out=TABP[:, b_ * 16:(b_ + 1) * 16], in0=TABP[:, b_ * 16:(b_ + 1) * 16],
                in1=TABP[:, 240:256], op=ALU.subtract), reads=[bf("TABP")], writes=[bf("TABP")])
        for c in range(8):
            for j in range(2):
                r0 = j * 64
                sch.add("dve", lambda e, c=c, j=j, r0=r0: e.tensor_tensor(
                    out=ESK[r0:r0 + 64, c:c + 1], in0=ESK[r0:r0 + 64, c:c + 1],
                    in1=TABP[r0:r0 + 64, 240 + 2 * c + j:240 + 2 * c + j + 1], op=ALU.subtract),
                    reads=[bf("TABP"), bf("ESK")], writes=[bf("ESK")])
        sch.add("act", lambda e: e.activation(out=ESK[:, :], in_=ESK[:, :], func=AF.Exp),
                reads=[bf("ESK")], writes=[bf("ESK")])
        sch.add("dve", lambda e: e.tensor_scalar(out=TABP, in0=TABP, scalar1=8.0, scalar2=None,
                                                 op0=ALU.mult), reads=[bf("TABP")], writes=[bf("TABP")])
        for mi in range(4):
            eng = "dve"
            accb = bf("ACC")
            sch.add(eng, lambda e, mi=mi: e.tensor_scalar(
                out=OH[0], in0=MAPS[:, mi, :], scalar1=-1.0, scalar2=NEG, op0=ALU.is_equal, op1=ALU.mult),
                reads=[bf("MAPS")], writes=[bf("OH0")])
            for h0 in (0, 8):
                for h in range(8):
                    sch.add(eng, lambda e, h=h: e.tensor_copy(out=ACC[:, h, :], in_=OH[0]),
                            reads=[bf("OH0")], writes=[accb])
                for b_ in present[mi]:
                    sch.add(eng, lambda e, mi=mi, b_=b_: e.tensor_single_scalar(
                        out=OH[1], in_=MAPS[:, mi, :], scalar=float(b_), op=ALU.is_equal),
                        reads=[bf("MAPS")], writes=[bf("OH1")])
                    for h in range(8):
                        sch.add(eng, lambda e, h=h, b_=b_, h0=h0: e.scalar_tensor_tensor(
                            out=ACC[:, h, :], in0=OH[1],
                            scalar=TABP[:, b_ * 16 + h0 + h:b_ * 16 + h0 + h + 1],
                            in1=ACC[:, h, :], op0=ALU.mult, op1=ALU.add),
                            reads=[bf("OH1"), bf("TABP"), accb], writes=[accb])
                sch.add(eng, lambda e, mi=mi, h0=h0: e.tensor_copy(out=BIAS[:, mi, h0:h0 + 8, :], in_=ACC),
                        reads=[accb], writes=[bBIAS])
        for li, l in enumerate((0, 3)):
            lam_init = 0.8 - 0.6 * math.exp(-0.3 * l)
            sch.add("sp", lambda e, li=li: e.dma_start(out=LAMW, in_=lam_d[li]),
                    writes=[bf("LAMW")], dma=bf("LAMW"))
            sch.add("sp", lambda e, li=li: e.dma_start(out=SCC[:, li:li + 1], in_=subln_d[li]),
                    writes=[bf("SCC")], dma=bf("SCC%d" % li))
            for t in range(2):
                sch.add("dve", lambda e, t=t: e.tensor_tensor(
                    out=LAMW[:, t * 128:t * 128 + 64], in0=LAMW[:, t * 128:t * 128 + 64],
                    in1=LAMW[:, t * 128 + 64:t * 128 + 128], op=ALU.mult),
                    reads=[bf("LAMW")], writes=[bf("LAMW")])
                sch.add("dve", lambda e, t=t: e.tensor_reduce(
                    out=LAMT[:, t:t + 1], in_=LAMW[:, t * 128:t * 128 + 64], axis=mybir.AxisListType.X,
                    op=ALU.add), reads=[bf("LAMW")], writes=[bf("LAMT")])
            sch.add("act", lambda e: e.activation(out=LAMT[:, 0:2], in_=LAMT[:, 0:2], func=AF.Exp),
                    reads=[bf("LAMT")], writes=[bf("LAMT")])
            sch.add("dve", lambda e, li=li, lam_init=lam_init: e.scalar_tensor_tensor(
                out=LAMC[:, li, 0:1], in0=LAMT[:, 1:2], scalar=-lam_init, in1=LAMT[:, 0:1],
                op0=ALU.add, op1=ALU.subtract), reads=[bf("LAMT")], writes=[bf("LAMC")])
            sch.add("dve", lambda e, li=li, lam_init=lam_init: e.tensor_scalar(
                out=SCC[:, li:li + 1], in0=SCC[:, li:li + 1], scalar1=(1.0 - lam_init) * math.sqrt(128.0),
                scalar2=None, op0=ALU.mult), reads=[bf("SCC")], writes=[bf("SCC")])

    def load_weights(l, c, wset):
        kind = KINDS[l]
        W = WT[wset]
        wb = bf("WT%d" % wset)
        if kind in (0, 1):
            cols = [c * 128, 3072 + c * 128, 1024 + c * 128, 2048 + c * 128]
            for si, c0 in enumerate(cols):
                sch.add("pool", lambda e, si=si, c0=c0, W=W: e.dma_start(
                    out=W[:, :, si * 128:(si + 1) * 128],
                    in_=win_d[l][:, c0:c0 + 128].rearrange("(c p) f -> p c f", p=P)),
                    writes=[wb], dma=bf("WTd%d_%d" % (wset, si)))
        else:
            g = c // 2
            cols = [(c * 128, 128, 0), (1536 + c * 128, 128, 128), (1024 + g * 64, 64, 256),
                    (1280 + g * 64, 64, 320)]
            for si, (c0, w, o0) in enumerate(cols):
                sch.add("pool", lambda e, c0=c0, w=w, o0=o0, W=W: e.dma_start(
                    out=W[:, :, o0:o0 + w],
                    in_=win_d[l][:, c0:c0 + w].rearrange("(c p) f -> p c f", p=P)),
                    writes=[wb], dma=bf("WTd%d_%d" % (wset, si)))
        sch.add("pool", lambda e: e.dma_start(out=WO[wset][:, :], in_=wout_d[l][c * 128:(c + 1) * 128, :]),
                writes=[bf("WO%d" % wset)], dma=bf("WOd%d" % wset))

    def norm_tile(xrows_ap, nrows, col, gidx_loaded, xnT_cols):
        slot = nxt("XN", 2)
        xn = XN[slot]
        xb = bf("XN%d" % slot)
        sch.add("act", lambda e: e.activation(out=xn[0:nrows, :], in_=xrows_ap, func=AF.Square,
                                              accum_out=SS[0:nrows, col:col + 1]),
                reads=[bf("X")], writes=[xb, bf("SS")])
        sch.add("act", lambda e: e.activation(out=SS[0:nrows, col:col + 1], in_=SS[0:nrows, col:col + 1],
                                              func=AF.Ln, scale=1.0 / D, bias=EPS),
                reads=[bf("SS")], writes=[bf("SS")])
        sch.add("act", lambda e: e.activation(out=SS[0:nrows, col:col + 1], in_=SS[0:nrows, col:col + 1],
                                              func=AF.Exp, scale=-0.5),
                reads=[bf("SS")], writes=[bf("SS")])
        sch.add("dve", lambda e: e.scalar_tensor_tensor(out=xn[0:nrows, :], in0=xrows_ap,
                                                        scalar=SS[0:nrows, col:col + 1], in1=Gn[0:nrows, :],
                                                        op0=ALU.mult, op1=ALU.mult),
                reads=[bf("X"), bf("SS"), bf("Gn")], writes=[xb])
        for ch in range(NCH):
            sch.add("pe", lambda e, ch=ch: e.transpose(out=T0[:, ch * 128:ch * 128 + nrows],
                                                       in_=xn[0:nrows, ch * 128:(ch + 1) * 128],
                                                       identity=ident[0:nrows, 0:nrows]),
                    reads=[xb, bCST], writes=[bf("T0")])
        c0, c1 = xnT_cols
        sch.add("act", lambda e: e.copy(
            out=xnT[:, :, c0:c1],
            in_=T0[:, :].rearrange("p (c t) -> p c t", c=NCH)[:, :, 0:nrows]),
            reads=[bf("T0")], writes=[bf("xnT")])

    def zero_acc():
        sch.add("dve", lambda e: e.memset(SS[:, :], 0.0), writes=[bf("SS")])

    def out_proj_units(ot_slot, wset, ntok_tiles, xdst):
        units = []
        for i, (rows, xap_fn) in enumerate(xdst):
            for hf in range(2):
                def u(i=i, hf=hf, rows=rows, xap_fn=xap_fn):
                    Mx, mb = M0, bf("M0")
                    sch.add("pe", lambda e: e.matmul(Mx[0:rows, :], lhsT=OT[ot_slot][:, i * 128:i * 128 + rows],
                                                     rhs=WO[wset][:, hf * 512:(hf + 1) * 512],
                                                     start=True, stop=True),
                            reads=[bf("OT%d" % ot_slot), bf("WO%d" % wset)], writes=[mb])
                    xa = xap_fn(hf)
                    sch.add("dve", lambda e: e.tensor_tensor(out=xa, in0=Mx[0:rows, :], in1=xa, op=ALU.add),
                            reads=[mb, bf("X")], writes=[bf("X")])
                units.append(u)
        return units

    def proj_q_unit(wset, tok0, ntok, slot):
        def u():
            for ch in range(NCH):
                sch.add("pe", lambda e, ch=ch: e.matmul(M0[:, 0:ntok], lhsT=WT[wset][:, ch, 0:128],
                                                        rhs=xnT[:, ch, tok0:tok0 + ntok],
                                                        start=(ch == 0), stop=(ch == NCH - 1)),
                        reads=[bf("WT%d" % wset), bf("xnT")], writes=[bf("M0")])
            sch.add("dve", lambda e: e.tensor_copy(out=QB[slot][:, 0:ntok], in_=M0[:, 0:ntok]),
                    reads=[bf("M0")], writes=[bf("QB%d" % slot)])
        return u

    def proj_g_unit(wset, tok0, ntok, slot):
        def u():
            for ch in range(NCH):
                sch.add("pe", lambda e, ch=ch: e.matmul(M0[:, 0:ntok], lhsT=WT[wset][:, ch, 128:256],
                                                        rhs=xnT[:, ch, tok0:tok0 + ntok],
                                                        start=(ch == 0), stop=(ch == NCH - 1)),
                        reads=[bf("WT%d" % wset), bf("xnT")], writes=[bf("M0")])
            sg = TP[2]
            sgb = bf("TP2")
            sch.add("act", lambda e: e.activation(out=sg[:, 0:ntok], in_=M0[:, 0:ntok], func=AF.Exp, scale=-1.0),
                    reads=[bf("M0")], writes=[sgb])
            sch.add("act", lambda e: e.activation(out=sg[:, 0:ntok], in_=sg[:, 0:ntok], func=AF.Ln, bias=1.0),
                    reads=[sgb], writes=[sgb])
            sch.add("act", lambda e: e.activation(out=sg[:, 0:ntok], in_=sg[:, 0:ntok], func=AF.Exp, scale=-1.0),
                    reads=[sgb], writes=[sgb])
            sch.add("dve", lambda e: e.tensor_tensor(out=GB[slot][:, 0:ntok], in0=M0[:, 0:ntok], in1=sg[:, 0:ntok],
                                                     op=ALU.mult),
                    reads=[bf("M0"), sgb], writes=[bf("GB%d" % slot)])
        return u

    def proj_kv_unit(l, c, wset, kvset, tok0, nrows, kt_cols, v_dst, out_dmas, swa, ev="act"):
        st = {}

        def u1():
            width = 128 if swa else 256
            for ch in range(NCH):
                sch.add("pe", lambda e, ch=ch: e.matmul(M0[0:nrows, 0:width],
                                                        lhsT=xnT[:, ch, tok0:tok0 + nrows],
                                                        rhs=WT[wset][:, ch, 256:256 + width],
                                                        start=(ch == 0), stop=(ch == NCH - 1)),
                        reads=[bf("WT%d" % wset), bf("xnT")], writes=[bf("M0")])
            kbs = nxt("KBT", NKBT)
            kbt = KBT[kbs]
            kbb = bf("KBT%d" % kbs)
            st["kbt"], st["kbb"] = kbt, kbb
            vb_ = v_dst[1]
            def cp(e, out, in_):
                return e.copy(out=out, in_=in_) if ev == "act" else e.tensor_copy(out=out, in_=in_)
            if swa:
                for hh in range(2):
                    sch.add(ev, lambda e, hh=hh: cp(e, kbt[0:nrows, hh * 64:(hh + 1) * 64], M0[0:nrows, 0:64]),
                            reads=[bf("M0")], writes=[kbb])
                    sch.add(ev, lambda e, hh=hh: cp(e, v_dst[0][:, hh * 64:(hh + 1) * 64], M0[0:nrows, 64:128]),
                            reads=[bf("M0")], writes=[vb_])
            else:
                sch.add(ev, lambda e: cp(e, kbt[0:nrows, :], M0[0:nrows, 0:128]),
                        reads=[bf("M0")], writes=[kbb])
                sch.add(ev, lambda e: cp(e, v_dst[0], M0[0:nrows, 128:256]),
                        reads=[bf("M0")], writes=[vb_])
            if out_dmas:
                ks = nxt("KVS", NKVS)
                kvs = KVS[ks]
                kb = bf("KVS%d" % ks)
                sch.add("dve", lambda e: e.tensor_copy(out=kvs[0:nrows, 0:width], in_=M0[0:nrows, 0:width]),
                        reads=[bf("M0")], writes=[kb])
                for (dst_ap, c0, w) in out_dmas:
                    sch.add("sp", lambda e, dst_ap=dst_ap, c0=c0, w=w: e.dma_start(out=dst_ap,
                                                                                  in_=kvs[0:nrows, c0:c0 + w]),
                            reads=[kb], dma=kb)

        def u2():
            kbt, kbb = st["kbt"], st["kbb"]
            sch.add("pe", lambda e: e.transpose(out=T0[:, 0:nrows], in_=kbt[0:nrows, :],
                                                identity=ident[0:nrows, 0:nrows]),
                    reads=[kbb, bCST], writes=[bf("T0")])
            ktt, ktc0, ktb = kt_cols
            if ev == "act":
                sch.add("act", lambda e: e.copy(out=ktt[:, ktc0:ktc0 + nrows], in_=T0[:, 0:nrows]),
                        reads=[bf("T0")], writes=[ktb])
            else:
                sch.add("dve", lambda e: e.tensor_copy(out=ktt[:, ktc0:ktc0 + nrows], in_=T0[:, 0:nrows]),
                        reads=[bf("T0")], writes=[ktb])
        return u1, u2

    def attn_da(c, qslot, NQ, acol, ktiles, bg):
        hmb = 2 * c
        items = [(kt, m) for kt in range(len(ktiles)) for m in range(2)]
        nkt = len(ktiles)

        def qk(it):
            kt, m = it
            t = ktiles[kt]
            sbk = nxt("SPS", 2)
            Sb = SPS[sbk]
            sbb = bf("SPS%d" % sbk)
            c0, nk = t["c0"], t["nk"]
            nb = len(t["bias"])
            ktt, kc0 = t["kT"]
            sch.add("pe", lambda e: e.matmul(Sb[0:nk, c0:NQ], lhsT=ktt[m * 64:(m + 1) * 64, kc0:kc0 + nk],
                                             rhs=QB[qslot][m * 64:(m + 1) * 64, c0:NQ],
                                             start=True, stop=(nb == 0), skip_group_check=True),
                    reads=[t["kTb"], bf("QB%d" % qslot)], writes=[sbb])
            for bi, (co, ncol, mi, kr) in enumerate(t["bias"]):
                sch.add("pe", lambda e, co=co, ncol=ncol, mi=mi, bi=bi: e.matmul(
                    Sb[0:nk, co:co + ncol], lhsT=ident[0:nk, 0:nk], rhs=BIAS[0:nk, mi, hmb + m, 0:ncol],
                    start=False, stop=(bi == nb - 1), skip_group_check=True),
                    reads=[bCST, bBIAS], writes=[sbb])
            return Sb, sbb

        def rest(it, Sb, sbb):
            kt, m = it
            t = ktiles[kt]
            c0, nk = t["c0"], t["nk"]
            pslot = nxt("PB", NPB)
            Pt = PB[pslot]
            pbb = bf("PB%d" % pslot)
            sch.add("act", lambda e: e.activation(out=Pt[0:nk, c0:NQ], in_=Sb[0:nk, c0:NQ], func=AF.Exp,
                                                  scale=0.125), reads=[sbb], writes=[pbb])
            sch.add("pe", lambda e: e.matmul(APS[m][:, acol + c0:acol + NQ], lhsT=t["v"], rhs=Pt[0:nk, c0:NQ],
                                             start=(kt == 0), stop=(kt == nkt - 1), skip_group_check=True),
                    reads=[t["vb"], pbb], writes=[bf("APS%d" % m)])
            sch.add("pe", lambda e: e.matmul(APS[2 + m][:, acol + c0:acol + NQ], lhsT=ones[0:nk, :],
                                             rhs=Pt[0:nk, c0:NQ],
                                             start=(kt == 0), stop=(kt == nkt - 1), skip_group_check=True),
                    reads=[bCST, pbb], writes=[bf("APS%d" % (2 + m))])

        cur = qk(items[0])
        for i, it in enumerate(items):
            nx = qk(items[i + 1]) if i + 1 < len(items) else None
            rest(it, *cur)
            cur = nx
            if bg:
                bg.pop(0)()

    def post_da_a(li, qslot, oslot, NQ, acol):
        T_a, T_b = TP[0], TP[1]
        ba, bb_ = bf("TP0"), bf("TP1")
        sl = slice(acol, acol + NQ)
        sch.add("act", lambda e: e.activation(out=T_a[:, 0:NQ], in_=APS[2][:, sl], func=AF.Ln),
                reads=[bf("APS2")], writes=[ba])
        sch.add("act", lambda e: e.activation(out=T_a[:, 0:NQ], in_=T_a[:, 0:NQ], func=AF.Exp, scale=-1.0),
                reads=[ba], writes=[ba])
        sch.add("dve", lambda e: e.tensor_tensor(out=T_a[:, 0:NQ], in0=APS[0][:, sl], in1=T_a[:, 0:NQ],
                                                 op=ALU.mult), reads=[bf("APS0"), ba], writes=[ba])
        sch.add("act", lambda e: e.activation(out=T_b[:, 0:NQ], in_=APS[3][:, sl], func=AF.Ln),
                reads=[bf("APS3")], writes=[bb_])
        sch.add("act", lambda e: e.activation(out=T_b[:, 0:NQ], in_=T_b[:, 0:NQ], func=AF.Exp, scale=-1.0),
                reads=[bb_], writes=[bb_])
        sch.add("dve", lambda e: e.tensor_tensor(out=T_b[:, 0:NQ], in0=APS[1][:, sl], in1=T_b[:, 0:NQ],
                                                 op=ALU.mult), reads=[bf("APS1"), bb_], writes=[bb_])
        sch.add("dve", lambda e: e.scalar_tensor_tensor(out=T_a[:, 0:NQ], in0=T_b[:, 0:NQ],
                                                        scalar=LAMC[:, li, 0:1], in1=T_a[:, 0:NQ],
                                                        op0=ALU.mult, op1=ALU.add),
                reads=[ba, bb_, bf("LAMC")], writes=[ba])
        sch.add("act", lambda e: e.activation(out=SQ[:, 0:NQ], in_=T_a[:, 0:NQ], func=AF.Square),
                reads=[ba], writes=[bf("SQ")])

    def post_da_b(li, qslot, oslot, NQ, acol):
        T_a, T_b = TP[0], TP[1]
        ba, bb_ = bf("TP0"), bf("TP1")
        sch.add("pe", lambda e: e.matmul(M0[:, 0:NQ], lhsT=ones, rhs=SQ[:, 0:NQ], start=True, stop=True),
                reads=[bCST, bf("SQ")], writes=[bf("M0")])
        sch.add("act", lambda e: e.activation(out=T_b[:, 0:NQ], in_=M0[:, 0:NQ], func=AF.Ln, bias=128.0 * EPS),
                reads=[bf("M0")], writes=[bb_])
        sch.add("act", lambda e: e.activation(out=T_b[:, 0:NQ], in_=T_b[:, 0:NQ], func=AF.Exp, scale=-0.5),
                reads=[bb_], writes=[bb_])
        sch.add("dve", lambda e: e.tensor_tensor(out=T_a[:, 0:NQ], in0=T_a[:, 0:NQ], in1=T_b[:, 0:NQ],
                                                 op=ALU.mult), reads=[ba, bb_], writes=[ba])
        sch.add("dve", lambda e: e.scalar_tensor_tensor(out=OT[oslot][:, 0:NQ], in0=T_a[:, 0:NQ],
                                                        scalar=SCC[:, li:li + 1], in1=GB[qslot][:, 0:NQ],
                                                        op0=ALU.mult, op1=ALU.mult),
                reads=[ba, bf("SCC"), bf("GB%d" % qslot)], writes=[bf("OT%d" % oslot)])

    def post_da(li, qslot, oslot, NQ, acol):
        post_da_a(li, qslot, oslot, NQ, acol)
        post_da_b(li, qslot, oslot, NQ, acol)

    def attn_sb(c, qslot, NQ, acol, ktiles, bg):
        nkt = len(ktiles)
        for j in range(2):
            for a in (j, 2 + j):
                sch.add("pe", lambda e, a=a: e.matmul(APS[a][:, acol:acol + NQ], lhsT=ZER[:, :],
                                                      rhs=QB[qslot][:, 0:NQ], start=True, stop=False,
                                                      skip_group_check=True),
                        reads=[bf("ZER"), bf("QB%d" % qslot)], writes=[bf("APS%d" % a)])
        items = [(kt, j) for kt in reversed(range(nkt)) for j in range(2)]

        def stage1(it):
            kt, j = it
            t = ktiles[kt]
            c0, nk = t["c0"], t["nk"]
            sbk = nxt("SPS", 2)
            Sb = SPS[sbk]
            sbb = bf("SPS%d" % sbk)
            nb = len(t["bias"])
            ktt, kc0 = t["kT"]
            sch.add("pe", lambda e: e.matmul(Sb[0:nk, c0:NQ], lhsT=ktt[j * 64:(j + 1) * 64, kc0:kc0 + nk],
                                             rhs=QB[qslot][j * 64:(j + 1) * 64, c0:NQ],
                                             start=True, stop=(nb == 0), skip_group_check=True),
                    reads=[t["kTb"], bf("QB%d" % qslot)], writes=[sbb])
            for bi, (co, ncol, mi, kr) in enumerate(t["bias"]):
                sch.add("pe", lambda e, co=co, ncol=ncol, bi=bi: e.matmul(
                    Sb[0:nk, co:co + ncol], lhsT=ident[0:nk, 0:nk], rhs=maskT[0:nk, 0:ncol],
                    start=False, stop=(bi == nb - 1), skip_group_check=True),
                    reads=[bCST], writes=[sbb])
            es_ = nxt("EW", NE)
            E = EW[es_]
            eb = bf("EW%d" % es_)
            spb = SPB[es_]
            spbb = bf("SPB%d" % es_)
            sch.add("act", lambda e: e.activation(out=E[0:nk, c0:NQ], in_=Sb[0:nk, c0:NQ], func=AF.Exp,
                                                  scale=0.125), reads=[sbb], writes=[eb])
            sch.add("act", lambda e: e.activation(out=E[0:nk, c0:NQ], in_=E[0:nk, c0:NQ], func=AF.Ln,
                                                  bias=1.0), reads=[eb], writes=[eb])
            return E, eb, spb, spbb, Sb, sbb

        def stage1b(it, E, eb, spb, spbb, Sb, sbb):
            kt, j = it
            t = ktiles[kt]
            c0, nk = t["c0"], t["nk"]
            sch.add("dve", lambda e: e.tensor_copy(out=spb[0:nk, c0:NQ], in_=E[0:nk, c0:NQ]),
                    reads=[eb], writes=[spbb])
            sch.add("dve", lambda e: e.scalar_tensor_tensor(out=E[0:nk, c0:NQ], in0=Sb[0:nk, c0:NQ],
                                                            scalar=0.125, in1=E[0:nk, c0:NQ],
                                                            op0=ALU.mult, op1=ALU.subtract),
                    reads=[sbb, eb, spbb], writes=[eb])

        def stage2a(it, E, eb, spb, spbb):
            kt, j = it
            t = ktiles[kt]
            c0, nk = t["c0"], t["nk"]
            rb = bf("APS%d" % j)
            sch.add("pe", lambda e: e.matmul(APS[j][:, acol + c0:acol + NQ], lhsT=triu[0:nk, :],
                                             rhs=spb[0:nk, c0:NQ], start=False, stop=False,
                                             skip_group_check=True),
                    reads=[bCST, spbb], writes=[rb])
            sch.add("dve", lambda e: e.tensor_tensor(out=E[0:nk, c0:NQ], in0=E[0:nk, c0:NQ],
                                                     in1=APS[j][0:nk, acol + c0:acol + NQ], op=ALU.subtract),
                    reads=[eb, rb], writes=[eb])
            pslot = nxt("PB", NPB)
            Ab = PB[pslot]
            pbb = bf("PB%d" % pslot)
            sch.add("act", lambda e: e.activation(out=Ab[0:nk, c0:NQ], in_=E[0:nk, c0:NQ], func=AF.Exp),
                    reads=[eb], writes=[pbb])
            return Ab, pbb

        def stage2b(it, spb, spbb, Ab, pbb):
            kt, j = it
            t = ktiles[kt]
            c0, nk = t["c0"], t["nk"]
            sch.add("pe", lambda e: e.matmul(APS[j][:, acol + c0:acol + NQ], lhsT=tril[0:nk, :],
                                             rhs=spb[0:nk, c0:NQ], start=False, stop=False,
                                             skip_group_check=True),
                    reads=[bCST, spbb], writes=[bf("APS%d" % j)])
            sch.add("pe", lambda e: e.matmul(APS[2 + j][:, acol + c0:acol + NQ], lhsT=t["v"],
                                             rhs=Ab[0:nk, c0:NQ], start=False, stop=False,
                                             skip_group_check=True),
                    reads=[t["vb"], pbb], writes=[bf("APS%d" % (2 + j))])

        n = len(items)
        LA = 2
        s1 = {}
        for i in range(-(LA + 1), 0):
            if 0 <= i + LA + 1 < n:
                s1[i + LA + 1] = stage1(items[i + LA + 1])
            if 0 <= i + LA < n:
                stage1b(items[i + LA], *s1[i + LA])
        prev = None
        for i, it in enumerate(items):
            if i + LA + 1 < n:
                s1[i + LA + 1] = stage1(items[i + LA + 1])
            if i + LA < n:
                stage1b(items[i + LA], *s1[i + LA])
            a = stage2a(it, *s1[i][0:4])
            if prev is not None:
                stage2b(*prev)
            prev = (it, s1[i][2], s1[i][3], a[0], a[1])
            del s1[i]
            if bg:
                bg.pop(0)()
        stage2b(*prev)

    def post_sb(qslot, oslot, NQ, acol):
        for j in range(2):
            r = slice(j * 64, (j + 1) * 64)
            sch.add("dve", lambda e, j=j, r=r: e.tensor_tensor(out=OT[oslot][r, 0:NQ],
                                                               in0=APS[2 + j][r, acol:acol + NQ],
                                                               in1=GB[qslot][r, 0:NQ], op=ALU.mult),
                    reads=[bf("APS%d" % (2 + j)), bf("GB%d" % qslot)], writes=[bf("OT%d" % oslot)])

    def attn_swa(c, qslot, qtiles, bg):
        items = []
        for qi, qt in enumerate(qtiles):
            for j in range(2):
                for ki in range(len(qt["ktiles"])):
                    items.append((qi, j, ki))

        def qk(it):
            qi, j, ki = it
            qt = qtiles[qi]
            t = qt["ktiles"][ki]
            nk, nq, qc0 = t["nk"], qt["nq"], qt["qc0"]
            sbk = nxt("SPS", 2)
            Sb = SPS[sbk]
            sbb = bf("SPS%d" % sbk)
            ktt, kc0 = t["kT"]
            sch.add("pe", lambda e: e.matmul(Sb[0:nk, 0:nq], lhsT=ktt[j * 64:(j + 1) * 64, kc0:kc0 + nk],
                                             rhs=QB[qslot][j * 64:(j + 1) * 64, qc0:qc0 + nq],
                                             start=True, stop=False, skip_group_check=True),
                    reads=[t["kTb"], bf("QB%d" % qslot)], writes=[sbb])
            sch.add("pe", lambda e: e.matmul(Sb[0:nk, 0:nq], lhsT=ident[0:nk, 0:nk],
                                             rhs=BIAS[0:nk, t["mi"], 2 * c + j, 0:nq],
                                             start=False, stop=True, skip_group_check=True),
                    reads=[bCST, bBIAS], writes=[sbb])
            return Sb, sbb

        def rest(it, Sb, sbb):
            qi, j, ki = it
            qt = qtiles[qi]
            t = qt["ktiles"][ki]
            nk, nq, ac = t["nk"], qt["nq"], qt["acol"]
            nki = len(qt["ktiles"])
            pslot = nxt("PB", NPB)
            Pt = PB[pslot]
            pbb = bf("PB%d" % pslot)
            sch.add("act", lambda e: e.activation(out=Pt[0:nk, 0:nq], in_=Sb[0:nk, 0:nq], func=AF.Exp,
                                                  scale=0.125), reads=[sbb], writes=[pbb])
            sch.add("pe", lambda e: e.matmul(APS[j][:, ac:ac + nq], lhsT=t["v"], rhs=Pt[0:nk, 0:nq],
                                             start=(ki == 0), stop=(ki == nki - 1), skip_group_check=True),
                    reads=[t["vb"], pbb], writes=[bf("APS%d" % j)])
            sch.add("pe", lambda e: e.matmul(APS[2 + j][:, ac:ac + nq], lhsT=ones[0:nk, :], rhs=Pt[0:nk, 0:nq],
                                             start=(ki == 0), stop=(ki == nki - 1), skip_group_check=True),
                    reads=[bCST, pbb], writes=[bf("APS%d" % (2 + j))])

        cur = qk(items[0])
        for i, it in enumerate(items):
            nx = qk(items[i + 1]) if i + 1 < len(items) else None
            rest(it, *cur)
            cur = nx
            if bg:
                bg.pop(0)()

    def attn_swa_p(c, qslot, qtiles, bg):
        def qk(qi):
            qt = qtiles[qi]
            sbk = nxt("SPS", 2)
            Sb = SPS[sbk]
            sbb = bf("SPS%d" % sbk)
            nq, qc0 = qt["nq"], qt["qc0"]
            blk = 0
            for j in range(2):
                for ki, t in enumerate(qt["ktiles"]):
                    off = blk * P
                    ktt, kc0 = t["kT"]
                    sch.add("pe", lambda e, j=j, ktt=ktt, kc0=kc0, off=off: e.matmul(
                        Sb[:, off:off + nq], lhsT=ktt[j * 64:(j + 1) * 64, kc0:kc0 + P],
                        rhs=QB[qslot][j * 64:(j + 1) * 64, qc0:qc0 + nq],
                        start=True, stop=False, skip_group_check=True),
                        reads=[t["kTb"], bf("QB%d" % qslot)], writes=[sbb])
                    sch.add("pe", lambda e, j=j, t=t, off=off: e.matmul(
                        Sb[:, off:off + nq], lhsT=ident, rhs=BIAS[:, t["mi"], 2 * c + j, 0:nq],
                        start=False, stop=True, skip_group_check=True),
                        reads=[bCST, bBIAS], writes=[sbb])
                    blk += 1
            return Sb, sbb, blk

        def rest(qi, Sb, sbb, nblk):
            qt = qtiles[qi]
            nq, ac = qt["nq"], qt["acol"]
            nki = len(qt["ktiles"])
            pslot = nxt("PB", NPB)
            Pt = PB[pslot]
            pbb = bf("PB%d" % pslot)
            sch.add("act", lambda e: e.activation(out=Pt[:, 0:nblk * P], in_=Sb[:, 0:nblk * P], func=AF.Exp,
                                                  scale=0.125), reads=[sbb], writes=[pbb])
            blk = 0
            for j in range(2):
                for ki, t in enumerate(qt["ktiles"]):
                    off = blk * P
                    sch.add("pe", lambda e, j=j, t=t, off=off, ki=ki: e.matmul(
                        APS[j][:, ac:ac + nq], lhsT=t["v"], rhs=Pt[:, off:off + nq],
                        start=(ki == 0), stop=(ki == nki - 1), skip_group_check=True),
                        reads=[t["vb"], pbb], writes=[bf("APS%d" % j)])
                    sch.add("pe", lambda e, j=j, off=off, ki=ki: e.matmul(
                        APS[2 + j][:, ac:ac + nq], lhsT=ones, rhs=Pt[:, off:off + nq],
                        start=(ki == 0), stop=(ki == nki - 1), skip_group_check=True),
                        reads=[bCST, pbb], writes=[bf("APS%d" % (2 + j))])
                    blk += 1

        nqt = len(qtiles)
        cur = qk(0)
        for qi in range(nqt):
            nx = qk(qi + 1) if qi + 1 < nqt else None
            rest(qi, *cur)
            cur = nx
            for _ in range(4):
                if bg:
                    bg.pop(0)()

    def post_swa(c, qslot, oslot, NQ, acol):
        T_a = TP[0]
        ba = bf("TP0")
        for j in range(2):
            r = slice(j * 64, (j + 1) * 64)
            sch.add("act", lambda e, j=j, r=r: e.activation(out=T_a[r, 0:NQ], in_=APS[2 + j][r, acol:acol + NQ],
                                                            func=AF.Ln, bias=ESK[r, c:c + 1]),
                    reads=[bf("APS%d" % (2 + j)), bf("ESK")], writes=[ba])
            sch.add("act", lambda e, r=r: e.activation(out=T_a[r, 0:NQ], in_=T_a[r, 0:NQ], func=AF.Exp, scale=-1.0),
                    reads=[ba], writes=[ba])
            sch.add("dve", lambda e, j=j, r=r: e.tensor_tensor(out=T_a[r, 0:NQ], in0=APS[j][r, acol:acol + NQ],
                                                               in1=T_a[r, 0:NQ], op=ALU.mult),
                    reads=[bf("APS%d" % j), ba], writes=[ba])
            sch.add("dve", lambda e, r=r: e.tensor_tensor(out=OT[oslot][r, 0:NQ], in0=T_a[r, 0:NQ],
                                                          in1=GB[qslot][r, 0:NQ], op=ALU.mult),
                    reads=[ba, bf("GB%d" % qslot)], writes=[bf("OT%d" % oslot)])

    def final_norm_tile(xrows_ap, nrows, col, dst_ap):
        slot = nxt("XO", 2)
        xo = XO[slot]
        xob = bf("XO%d" % slot)
        sch.add("act", lambda e: e.activation(out=xo[0:nrows, :], in_=xrows_ap, func=AF.Square,
                                              accum_out=SS[0:nrows, col:col + 1]),
                reads=[bf("X")], writes=[xob, bf("SS")])
        sch.add("act", lambda e: e.activation(out=SS[0:nrows, col:col + 1], in_=SS[0:nrows, col:col + 1],
                                              func=AF.Ln, scale=1.0 / D, bias=EPS),
                reads=[bf("SS")], writes=[bf("SS")])
        sch.add("act", lambda e: e.activation(out=SS[0:nrows, col:col + 1], in_=SS[0:nrows, col:col + 1],
                                              func=AF.Exp, scale=-0.5),
                reads=[bf("SS")], writes=[bf("SS")])
        sch.add("dve", lambda e: e.scalar_tensor_tensor(out=xo[0:nrows, :], in0=xrows_ap,
                                                        scalar=SS[0:nrows, col:col + 1], in1=Gn[0:nrows, :],
                                                        op0=ALU.mult, op1=ALU.mult),
                reads=[bf("X"), bf("SS"), bf("Gn")], writes=[xob])
        sch.add("sp", lambda e: e.dma_start(out=dst_ap, in_=xo[0:nrows, :]), reads=[xob], dma=xob)

    def load_gain(idx):
        sch.add("sp", lambda e: e.dma_start(out=Gn[:, :], in_=gain_d[idx]), writes=[bf("Gn")], dma=bf("Gn"))

    def prompt_seq(s):
        sch.add("sp", lambda e: e.dma_start(out=X[:, :, :], in_=xp_d[s].rearrange("(n p) d -> p n d", p=P)),
                writes=[bf("X")], dma=bf("Xd"))
        for l in range(4):
            kind = KINDS[l]
            li = 0 if l == 0 else 1
            swa = kind == 2
            load_gain(l)
            load_weights(l, 0, 0)
            zero_acc()
            for n in range(NT):
                norm_tile(X[:, n, :], P, n, None, (n * P, (n + 1) * P))

            def kvset_of(c):
                return (c // 2) % 2 if swa else c % 2

            def kv_units(c):
                if swa and c % 2 == 1:
                    return []
                kvset = kvset_of(c)
                wset = c % 2
                us = []
                for n in range(NT):
                    outs = []
                    if swa:
                        g = c // 2
                        if n == NT - 1:
                            outs = [(kp_d[l][s, :, g * 64:(g + 1) * 64], 0, 64),
                                    (vp_d[l][s, :, g * 64:(g + 1) * 64], 64, 64)]
                    else:
                        outs = [(kp_d[l][s, n * P:(n + 1) * P, c * 128:(c + 1) * 128], 0, 128),
                                (vp_d[l][s, n * P:(n + 1) * P, c * 128:(c + 1) * 128], 128, 128)]
                    us.append(proj_kv_unit(l, c, wset, kvset, n * P, P,
                                           (KT[kvset], n * P, bf("KT%d" % kvset)),
                                           (VB[kvset][:, n, :], bf("VB%d" % kvset)), outs, swa,
                                           ev=("act" if kind == 1 else "dve")))
                flat = []
                for i_, (a_, b_) in enumerate(us):
                    flat.append(a_)
                    if i_ >= 1:
                        flat.append(us[i_ - 1][1])
                flat.append(us[-1][1])
                return flat

            for u in kv_units(0):
                u()
            qslots = {}

            def qg_units(c, qb):
                slot = nxt("QBslot", NQS)
                qslots[(c, qb)] = slot
                return [proj_q_unit(c % 2, qb * 512, 512, slot), proj_g_unit(c % 2, qb * 512, 512, slot)]

            for u in qg_units(0, 0):
                u()
            pend_out = []
            ku_cache = {}
            blocks = [(c, qb) for c in range(8) for qb in range(NQB)]
            for bi_, (c, qb) in enumerate(blocks):
                wset = c % 2
                kvset = kvset_of(c)
                bg = list(pend_out)
                pend_out = []
                if qb == min(1, NQB - 1) and c + 1 < 8:
                    if NQB == 1:
                        for u in bg:
                            u()
                        bg = []
                    load_weights(l, c + 1, (c + 1) % 2)
                if bi_ + 1 < len(blocks):
                    bg += qg_units(*blocks[bi_ + 1]) if blocks[bi_ + 1][0] == c else []
                if c + 1 < 8 and qb >= min(1, NQB - 1):
                    if (c + 1) not in ku_cache:
                        ku_cache[c + 1] = kv_units(c + 1)
                    ku = ku_cache[c + 1]
                    nshare = max(1, NQB - 1)
                    idx = qb - min(1, NQB - 1)
                    per = (len(ku) + nshare - 1) // nshare
                    bg += ku[idx * per:(idx + 1) * per]
                    if qb == NQB - 1:
                        bg += qg_units(c + 1, 0)
                qslot = qslots[(c, qb)]
                oslot = nxt("OT", 2)
                ktb, vbb = bf("KT%d" % kvset), bf("VB%d" % kvset)
                if kind in (0, 1):
                    ktiles = []
                    for kt in range(4 * qb + 4):
                        c0 = max(0, kt - 4 * qb) * P
                        bias = []
                        for i in range(4):
                            qt = 4 * qb + i
                            if kind == 0:
                                if kt == qt:
                                    bias.append((i * P, P, 0, P))
                                elif kt == qt - 1:
                                    bias.append((i * P, P, 1, P))
                            else:
                                if kt == qt:
                                    bias.append((i * P, P, 0, P))
                        ktiles.append(dict(kT=(KT[kvset], kt * P), kTb=ktb, v=VB[kvset][:, kt, :], vb=vbb,
                                           nk=P, c0=c0, bias=bias))
                    if kind == 0:
                        attn_da(c, qslot, 512, 0, ktiles, bg)
                    else:
                        attn_sb(c, qslot, 512, 0, ktiles, bg)
                else:
                    qtiles = []
                    for i in range(4):
                        qt = 4 * qb + i
                        kts = []
                        if qt > 0:
                            kts.append(dict(kT=(KT[kvset], (qt - 1) * P), kTb=ktb, v=VB[kvset][:, qt - 1, :],
                                            vb=vbb, nk=P, mi=3))
                        kts.append(dict(kT=(KT[kvset], qt * P), kTb=ktb, v=VB[kvset][:, qt, :], vb=vbb,
                                        nk=P, mi=2))
                        qtiles.append(dict(qc0=i * P, nq=P, acol=i * P, ktiles=kts))
                    attn_swa_p(c, qslot, qtiles, bg)
                pre = []
                if kind == 0:
                    post_da_a(li, qslot, oslot, 512, 0)
                    pre = [lambda li=li, qslot=qslot, oslot=oslot: post_da_b(li, qslot, oslot, 512, 0)]
                elif kind == 1:
                    post_sb(qslot, oslot, 512, 0)
                else:
                    post_swa(c, qslot, oslot, 512, 0)
                for u in bg:
                    u()
                xdst = [(P, (lambda hf, i=i, qb=qb: X[:, 4 * qb + i, hf * 512:(hf + 1) * 512])) for i in range(4)]
                pend_out = pre + out_proj_units(oslot, wset, 4, xdst)
            for u in pend_out:
                u()
        load_gain(4)
        zero_acc()
        for n in range(NT):
            final_norm_tile(X[:, n, :], P, n, yp_d[s, n * P:(n + 1) * P, :])

    def sample_pass():
        sch.add("sp", lambda e: e.dma_start(out=X[0:TS, 0, :], in_=xs_d), writes=[bf("X")], dma=bf("Xd"))
        for l in range(4):
            kind = KINDS[l]
            li = 0 if l == 0 else 1
            swa = kind == 2
            load_gain(l)
            zero_acc()
            norm_tile(X[0:TS, 0, :], TS, 0, None, (0, TS))
            def cache_loads(c, b, st, l=l, swa=swa):
                vbb_ = bf("VB%d" % st)
                if swa:
                    g = c // 2
                    for hh in range(2):
                        sch.add("pool", lambda e, hh=hh: e.dma_start(
                            out=KC[st][:, 0, hh * 64:(hh + 1) * 64], in_=ck_d[l][b, :, g * 64:(g + 1) * 64]),
                            writes=[bf("KC%d" % st)], dma=bf("KCd%d_%d" % (st, hh)))
                        sch.add("pool", lambda e, hh=hh: e.dma_start(
                            out=VB[st][:, 0, hh * 64:(hh + 1) * 64], in_=cv_d[l][b, :, g * 64:(g + 1) * 64]),
                            writes=[vbb_], dma=bf("VBd%d_%d" % (st, hh)))
                else:
                    sch.add("pool", lambda e: e.dma_start(
                        out=KC[st][:, :, :],
                        in_=ck_d[l][b, :, c * 128:(c + 1) * 128].rearrange("(n p) f -> p n f", p=P)),
                        writes=[bf("KC%d" % st)], dma=bf("KCd%d_0" % st))
                    sch.add("pool", lambda e: e.dma_start(
                        out=VB[st][:, 0:NPT, :],
                        in_=cv_d[l][b, :, c * 128:(c + 1) * 128].rearrange("(n p) f -> p n f", p=P)),
                        writes=[vbb_], dma=bf("VBd%d_0" % st))

            iters = [(c_, b_) for c_ in range(8) for b_ in range(NSEQ)]
            load_weights(l, 0, 0)
            cache_loads(0, 0, 0)
            for c in range(8):
                wset = c % 2
                if c + 1 < 8:
                    load_weights(l, c + 1, (c + 1) % 2)
                slot = nxt("QBslot", NQS)
                proj_q_unit(wset, 0, TS, slot)()
                proj_g_unit(wset, 0, TS, slot)()
                oslot = nxt("OT", 2)
                for b in range(NSEQ):
                    k_it = c * NSEQ + b
                    st = k_it % 2
                    if k_it + 1 < len(iters):
                        cache_loads(iters[k_it + 1][0], iters[k_it + 1][1], (k_it + 1) % 2)
                    ktb, vbb = bf("KT%d" % st), bf("VB%d" % st)
                    ktnb, vnb = bf("KTN%d" % st), bf("VN%d" % st)
                    ncache = 1 if swa else NPT
                    for n in range(ncache):
                        sch.add("pe", lambda e, n=n, st=st: e.transpose(out=T0[:, n * P:(n + 1) * P],
                                                                        in_=KC[st][:, n, :], identity=ident),
                                reads=[bf("KC%d" % st), bCST], writes=[bf("T0")])
                    sch.add("act", lambda e, st=st, ncache=ncache: e.copy(out=KT[st][:, 0:ncache * P],
                                                                          in_=T0[:, 0:ncache * P]),
                            reads=[bf("T0")], writes=[ktb])
                    if swa:
                        g = c // 2
                        outs = []
                        if c % 2 == 0:
                            outs = [(ks_d[l][b, P - T_DEC:P, g * 64:(g + 1) * 64], 0, 64),
                                    (vs_d[l][b, P - T_DEC:P, g * 64:(g + 1) * 64], 64, 64)]
                    else:
                        outs = [(ks_d[l][b * T_DEC:(b + 1) * T_DEC, c * 128:(c + 1) * 128], 0, 128),
                                (vs_d[l][b * T_DEC:(b + 1) * T_DEC, c * 128:(c + 1) * 128], 128, 128)]
                    for u_ in proj_kv_unit(l, c, wset, st, b * T_DEC, T_DEC, (KTN[st], 0, ktnb),
                                           (VN[st][:, :], vnb), outs, swa):
                        u_()
                    acol = b * T_DEC
                    qc0 = b * T_DEC
                    if kind in (0, 1):
                        ktiles = []
                        for kt in range(NPT):
                            bias = []
                            if kind == 0 and kt == NPT - 1:
                                bias = [(0, T_DEC, 1, P)]
                            ktiles.append(dict(kT=(KT[st], kt * P), kTb=ktb, v=VB[st][:, kt, :], vb=vbb,
                                               nk=P, c0=0, bias=bias))
                        ktiles.append(dict(kT=(KTN[st], 0), kTb=ktnb, v=VN[st][:, :], vb=vnb, nk=T_DEC, c0=0,
                                           bias=[(0, T_DEC, 0, T_DEC)]))
                        if kind == 0:
                            attn_da_s(c, slot, T_DEC, acol, qc0, ktiles)
                        else:
                            attn_sb_s(c, slot, T_DEC, acol, qc0, ktiles)
                    else:
                        kts = [dict(kT=(KT[st], 0), kTb=ktb, v=VB[st][:, 0, :], vb=vbb, nk=P, mi=3),
                               dict(kT=(KTN[st], 0), kTb=ktnb, v=VN[st][:, :], vb=vnb, nk=T_DEC, mi=2)]
                        attn_swa(c, slot, [dict(qc0=qc0, nq=T_DEC, acol=acol, ktiles=kts)], None)
                if kind == 0:
                    post_da(li, slot, oslot, TS, 0)
                elif kind == 1:
                    post_sb(slot, oslot, TS, 0)
                else:
                    post_swa(c, slot, oslot, TS, 0)
                xdst = [(TS, (lambda hf: X[0:TS, 0, hf * 512:(hf + 1) * 512]))]
                for u in out_proj_units(oslot, wset, 1, xdst):
                    u()
            if swa:
                for b in range(NSEQ):
                    sch.add("sp", lambda e, b=b, l=l: e.dma_start(out=ks_d[l][b, 0:P - T_DEC, :],
                                                             in_=ck_d[l][b, T_DEC:P, :]), dma=bf("d2d"))
                    sch.add("sp", lambda e, b=b, l=l: e.dma_start(out=vs_d[l][b, 0:P - T_DEC, :],
                                                             in_=cv_d[l][b, T_DEC:P, :]), dma=bf("d2d"))
        load_gain(4)
        zero_acc()
        final_norm_tile(X[0:TS, 0, :], TS, 0, ys_d[:, :])

    def attn_da_s(c, qslot, NQ, acol, qc0, ktiles):
        hmb = 2 * c
        nkt = len(ktiles)
        for kt in range(nkt):
            for m in range(2):
                t = ktiles[kt]
                nk = t["nk"]
                sbk = nxt("SPS", 2)
                Sb = SPS[sbk]
                sbb = bf("SPS%d" % sbk)
                nb = len(t["bias"])
                ktt, kc0 = t["kT"]
                sch.add("pe", lambda e, Sb=Sb, nk=nk, m=m, ktt=ktt, kc0=kc0, nb=nb: e.matmul(
                    Sb[0:nk, 0:NQ], lhsT=ktt[m * 64:(m + 1) * 64, kc0:kc0 + nk],
                    rhs=QB[qslot][m * 64:(m + 1) * 64, qc0:qc0 + NQ], start=True, stop=(nb == 0),
                    skip_group_check=True), reads=[t["kTb"], bf("QB%d" % qslot)], writes=[sbb])
                for bi, (co, ncol, mi, kr) in enumerate(t["bias"]):
                    sch.add("pe", lambda e, Sb=Sb, nk=nk, m=m, mi=mi, bi=bi, nb=nb: e.matmul(
                        Sb[0:nk, 0:NQ], lhsT=ident[0:nk, 0:nk], rhs=BIAS[0:nk, mi, hmb + m, 0:NQ],
                        start=False, stop=(bi == nb - 1), skip_group_check=True),
                        reads=[bCST, bBIAS], writes=[sbb])
                pslot = nxt("PB", NPB)
                Pt = PB[pslot]
                pbb = bf("PB%d" % pslot)
                sch.add("act", lambda e, Pt=Pt, Sb=Sb, nk=nk: e.activation(
                    out=Pt[0:nk, 0:NQ], in_=Sb[0:nk, 0:NQ], func=AF.Exp, scale=0.125),
                    reads=[sbb], writes=[pbb])
                sch.add("pe", lambda e, Pt=Pt, nk=nk, m=m, t=t, kt=kt: e.matmul(
                    APS[m][:, acol:acol + NQ], lhsT=t["v"], rhs=Pt[0:nk, 0:NQ],
                    start=(kt == 0), stop=(kt == nkt - 1), skip_group_check=True),
                    reads=[t["vb"], pbb], writes=[bf("APS%d" % m)])
                sch.add("pe", lambda e, Pt=Pt, nk=nk, m=m, kt=kt: e.matmul(
                    APS[2 + m][:, acol:acol + NQ], lhsT=ones[0:nk, :], rhs=Pt[0:nk, 0:NQ],
                    start=(kt == 0), stop=(kt == nkt - 1), skip_group_check=True),
                    reads=[bCST, pbb], writes=[bf("APS%d" % (2 + m))])

    def attn_sb_s(c, qslot, NQ, acol, qc0, ktiles):
        nkt = len(ktiles)
        for j in range(2):
            for a in (j, 2 + j):
                sch.add("pe", lambda e, a=a: e.matmul(APS[a][:, acol:acol + NQ], lhsT=ZER[:, :],
                                                      rhs=QB[qslot][:, qc0:qc0 + NQ], start=True, stop=False,
                                                      skip_group_check=True),
                        reads=[bf("ZER"), bf("QB%d" % qslot)], writes=[bf("APS%d" % a)])
        for kt in reversed(range(nkt)):
            for j in range(2):
                t = ktiles[kt]
                nk = t["nk"]
                sbk = nxt("SPS", 2)
                Sb = SPS[sbk]
                sbb = bf("SPS%d" % sbk)
                nb = len(t["bias"])
                ktt, kc0 = t["kT"]
                sch.add("pe", lambda e, Sb=Sb, nk=nk, j=j, ktt=ktt, kc0=kc0, nb=nb: e.matmul(
                    Sb[0:nk, 0:NQ], lhsT=ktt[j * 64:(j + 1) * 64, kc0:kc0 + nk],
                    rhs=QB[qslot][j * 64:(j + 1) * 64, qc0:qc0 + NQ], start=True, stop=(nb == 0),
                    skip_group_check=True), reads=[t["kTb"], bf("QB%d" % qslot)], writes=[sbb])
                for bi, (co, ncol, mi, kr) in enumerate(t["bias"]):
                    sch.add("pe", lambda e, Sb=Sb, nk=nk, bi=bi, nb=nb: e.matmul(
                        Sb[0:nk, 0:NQ], lhsT=ident[0:nk, 0:nk], rhs=maskT[0:nk, 0:NQ],
                        start=False, stop=(bi == nb - 1), skip_group_check=True),
                        reads=[bCST], writes=[sbb])
                es_ = nxt("EW", NE)
                E = EW[es_]
                eb = bf("EW%d" % es_)
                spb = SPB[es_]
                spbb = bf("SPB%d" % es_)
                rb = bf("APS%d" % j)
                sch.add("act", lambda e, E=E, Sb=Sb, nk=nk: e.activation(
                    out=E[0:nk, 0:NQ], in_=Sb[0:nk, 0:NQ], func=AF.Exp, scale=0.125), reads=[sbb], writes=[eb])
                sch.add("act", lambda e, E=E, nk=nk: e.activation(
                    out=E[0:nk, 0:NQ], in_=E[0:nk, 0:NQ], func=AF.Ln, bias=1.0), reads=[eb], writes=[eb])
                sch.add("pool", lambda e, E=E, spb=spb, nk=nk: e.tensor_copy(out=spb[0:nk, 0:NQ], in_=E[0:nk, 0:NQ]),
                        reads=[eb], writes=[spbb])
                sch.add("dve", lambda e, E=E, Sb=Sb, nk=nk: e.scalar_tensor_tensor(
                    out=E[0:nk, 0:NQ], in0=Sb[0:nk, 0:NQ], scalar=0.125, in1=E[0:nk, 0:NQ],
                    op0=ALU.mult, op1=ALU.subtract), reads=[sbb, eb, spbb], writes=[eb])
                sch.add("pe", lambda e, spb=spb, nk=nk, j=j: e.matmul(
                    APS[j][:, acol:acol + NQ], lhsT=triu[0:nk, :], rhs=spb[0:nk, 0:NQ], start=False, stop=False,
                    skip_group_check=True), reads=[bCST, spbb], writes=[rb])
                sch.add("dve", lambda e, E=E, nk=nk, j=j: e.tensor_tensor(
                    out=E[0:nk, 0:NQ], in0=E[0:nk, 0:NQ], in1=APS[j][0:nk, acol:acol + NQ], op=ALU.subtract),
                    reads=[eb, rb], writes=[eb])
                pslot = nxt("PB", NPB)
                Ab = PB[pslot]
                pbb = bf("PB%d" % pslot)
                sch.add("act", lambda e, Ab=Ab, E=E, nk=nk: e.activation(
                    out=Ab[0:nk, 0:NQ], in_=E[0:nk, 0:NQ], func=AF.Exp), reads=[eb], writes=[pbb])
                sch.add("pe", lambda e, spb=spb, nk=nk, j=j: e.matmul(
                    APS[j][:, acol:acol + NQ], lhsT=tril[0:nk, :], rhs=spb[0:nk, 0:NQ], start=False, stop=False,
                    skip_group_check=True), reads=[bCST, spbb], writes=[rb])
                sch.add("pe", lambda e, Ab=Ab, nk=nk, j=j, t=t: e.matmul(
                    APS[2 + j][:, acol:acol + NQ], lhsT=t["v"], rhs=Ab[0:nk, 0:NQ], start=False, stop=False,
                    skip_group_check=True), reads=[t["vb"], pbb], writes=[bf("APS%d" % (2 + j))])

    setup()
    for s in range(NSEQ):
        prompt_seq(s)
    sample_pass()
    fin = sch.add("sp", lambda e: e.nop(), reads=[], writes=[])
    for b in sch.dma_bufs:
        fin.waits.append(("dma", b, b.cnt))
    sch.emit(nc, es)
    es.close()
    return nc


_NC_CACHE = {}


def _get_nc(cfg):
    key = (cfg["NSEQ"], cfg["S"], cfg["PAST"])
    if key not in _NC_CACHE:
        _NC_CACHE[key] = build(cfg)
    return _NC_CACHE[key]


def run(cfg, inputs):
    NSEQ, S, PAST, NCORES = cfg["NSEQ"], cfg["S"], cfg["PAST"], cfg["NCORES"]
    nc = _get_nc(cfg)
    f = lambda a: np.ascontiguousarray(np.asarray(a, dtype=np.float32))
    maps_np, consts_np, _ = static_tables()
    gain = np.stack([np.broadcast_to(f(inputs[k])[None, :], (P, D)) for k in
                     ("norm_0", "norm_1", "norm_2", "norm_3", "final_norm")])
    tab = np.broadcast_to(f(inputs["rel_bias_table"]).reshape(1, 512), (P, 512))
    lam = np.stack([np.broadcast_to(f(inputs["da_lambda_%d" % l]).reshape(1, 256), (P, 256)) for l in (0, 3)])
    subln = np.stack([f(inputs["da_subln_%d" % l]).reshape(P, 1) for l in (0, 3)])
    sk = f(inputs["sw_sinks_2"])
    sink = np.zeros((P, 8), np.float32)
    for c in range(8):
        sink[:64, c] = sk[2 * c]
        sink[64:, c] = sk[2 * c + 1]
    shared = dict(gain=f(gain), tab=f(tab), lam=f(lam), subln=f(subln), sink=sink, maps=maps_np, consts=consts_np)
    for l in range(4):
        shared["win%d" % l] = f(inputs["w_in_%d" % l])
        shared["wout%d" % l] = f(inputs["w_out_%d" % l])
    in_maps = []
    for core in range(NCORES):
        sl = slice(core * NSEQ, (core + 1) * NSEQ)
        m = dict(shared)
        m["xp"] = f(inputs["x_prompt"][sl])
        m["xs"] = f(inputs["x_sample"][sl]).reshape(NSEQ * T_DEC, D)
        for l in range(4):
            ck = f(inputs["cache_k_%d" % l][sl])
            cv = f(inputs["cache_v_%d" % l][sl])
            m["ck%d" % l] = ck.reshape(NSEQ, ck.shape[1], -1)
            m["cv%d" % l] = cv.reshape(NSEQ, cv.shape[1], -1)
        in_maps.append(m)
    res = run_bass_kernel_spmd(nc, in_maps, core_ids=list(range(NCORES)))
    R = res.results
    cat = lambda name: np.concatenate([np.asarray(r[name]) for r in R], axis=0)
    Bt = NSEQ * NCORES
    outs = [cat("yp").reshape(Bt, S, D), cat("ys").reshape(Bt, T_DEC, D)]
    for l in range(4):
        kind = KINDS[l]
        if kind == 0:
            outs += [cat("kp%d" % l).reshape(Bt, S, 8, 2, 64), cat("vp%d" % l).reshape(Bt, S, 8, 128),
                     cat("ks%d" % l).reshape(Bt, T_DEC, 8, 2, 64), cat("vs%d" % l).reshape(Bt, T_DEC, 8, 128)]
        elif kind == 1:
            outs += [cat("kp%d" % l).reshape(Bt, S, 16, 64), cat("vp%d" % l).reshape(Bt, S, 16, 64),
                     cat("ks%d" % l).reshape(Bt, T_DEC, 16, 64), cat("vs%d" % l).reshape(Bt, T_DEC, 16, 64)]
        else:
            outs += [cat("kp%d" % l).reshape(Bt, P, 4, 64), cat("vp%d" % l).reshape(Bt, P, 4, 64),
                     cat("ks%d" % l).reshape(Bt, P, 4, 64), cat("vs%d" % l).reshape(Bt, P, 4, 64)]
    return tuple(np.ascontiguousarray(o, dtype=np.float32) for o in outs)


def kernel(x_prompt, x_sample, cache_k_0, cache_v_0, cache_k_1, cache_v_1, cache_k_2, cache_v_2,
           cache_k_3, cache_v_3, rel_bias_table,
           norm_0, w_in_0, w_out_0, da_lambda_0, da_subln_0,
           norm_1, w_in_1, w_out_1,
           norm_2, w_in_2, w_out_2, sw_sinks_2,
           norm_3, w_in_3, w_out_3, da_lambda_3, da_subln_3,
           final_norm):
    inputs = {
        "x_prompt": x_prompt, "x_sample": x_sample,
        "cache_k_0": cache_k_0, "cache_v_0": cache_v_0, "cache_k_1": cache_k_1, "cache_v_1": cache_v_1,
        "cache_k_2": cache_k_2, "cache_v_2": cache_v_2, "cache_k_3": cache_k_3, "cache_v_3": cache_v_3,
        "rel_bias_table": rel_bias_table,
        "norm_0": norm_0, "w_in_0": w_in_0, "w_out_0": w_out_0, "da_lambda_0": da_lambda_0, "da_subln_0": da_subln_0,
        "norm_1": norm_1, "w_in_1": w_in_1, "w_out_1": w_out_1,
        "norm_2": norm_2, "w_in_2": w_in_2, "w_out_2": w_out_2, "sw_sinks_2": sw_sinks_2,
        "norm_3": norm_3, "w_in_3": w_in_3, "w_out_3": w_out_3, "da_lambda_3": da_lambda_3, "da_subln_3": da_subln_3,
        "final_norm": final_norm,
    }
    return run(FULL_CFG, inputs)
```
